# Optimizing a Trainium2 kernel written in Bass

```python
import jax, jax.numpy as jnp
from jax import lax
import numpy as np

D_MODEL = 1024
BATCH = 2
SEQ = 8192
DEPTH = 2

BLOCK = 128
NORM_EPS = 1e-6
SWA_WINDOW = 128
SWA_HEADS = 4
SWA_KV_HEADS = 2
SWA_HEAD_DIM = 64
CONV_WIDTH = 256
CONV_K = 3
MLA_HEADS = 4
MLA_Q_RANK = 256
MLA_KV_RANK = 128
MLA_NOPE_DIM = 64
MLA_ROPE_DIM = 32
MLA_V_DIM = 64
ROPE_THETA = 10000.0
SB_HEADS = 4
SB_HEAD_DIM = 64
GROUP_WIDTH = 256
N_GROUPS = 4
D_MIX = GROUP_WIDTH * N_GROUPS

A_Q = SWA_HEADS * SWA_HEAD_DIM
A_KV = SWA_KV_HEADS * SWA_HEAD_DIM
SB_W = SB_HEADS * SB_HEAD_DIM
IN_SIZES = (A_Q, A_KV, A_KV,
            CONV_WIDTH, CONV_WIDTH, CONV_WIDTH,
            MLA_Q_RANK, MLA_KV_RANK, MLA_ROPE_DIM,
            SB_W, SB_W, SB_W,
            D_MIX)
D_IN = int(sum(IN_SIZES))
SPLIT_IDX = [int(v) for v in np.cumsum(IN_SIZES)[:-1]]

kernel_name = "hymba_style_four_group_hybrid"


def rmsnorm(x, g):
    x32 = x.astype(jnp.float32)
    y = x32 * lax.rsqrt(jnp.mean(x32 * x32, axis=-1, keepdims=True) + NORM_EPS)
    return (y * g.astype(jnp.float32)).astype(x.dtype)


def to_blocks(t):
    b, s = t.shape[:2]
    t = t.reshape((b, s // BLOCK, BLOCK) + t.shape[2:])
    return jnp.moveaxis(t, 1, 0)


def from_blocks(t):
    t = jnp.moveaxis(t, 0, 1)
    return t.reshape((t.shape[0], t.shape[1] * t.shape[2]) + t.shape[3:])


def rope(x, pos):
    half = x.shape[-1] // 2
    freqs = ROPE_THETA ** (-jnp.arange(half, dtype=jnp.float32) / half)
    ang = pos.astype(jnp.float32)[..., None] * freqs
    ang = ang.reshape(ang.shape[:2] + (1,) * (x.ndim - 3) + (half,))
    cos, sin = jnp.cos(ang), jnp.sin(ang)
    x32 = x.astype(jnp.float32)
    x1, x2 = x32[..., :half], x32[..., half:]
    return jnp.concatenate([x1 * cos - x2 * sin, x1 * sin + x2 * cos], axis=-1).astype(x.dtype)


def swa_sink_attention(q, k, v, sinks):
    b, s, h, d = q.shape
    kvh = k.shape[2]
    g = h // kvh
    nb = s // BLOCK
    qb = q.reshape(b, nb, BLOCK, kvh, g, d).astype(jnp.float32)
    kb = k.reshape(b, nb, BLOCK, kvh, d).astype(jnp.float32)
    vb = v.reshape(b, nb, BLOCK, kvh, d).astype(jnp.float32)
    prev = lambda t: jnp.concatenate([jnp.zeros_like(t[:, :1]), t[:, :-1]], axis=1)
    kk = jnp.concatenate([prev(kb), kb], axis=2)
    vv = jnp.concatenate([prev(vb), vb], axis=2)
    scores = jnp.einsum('bnqhgd,bnkhd->bnhgqk', qb, kk) * (d ** -0.5)
    qi = jnp.arange(BLOCK)[:, None]
    kj = jnp.arange(2 * BLOCK)[None, :]
    diff = qi + BLOCK - kj
    blk = jnp.arange(nb)[:, None, None]
    valid = (diff >= 0) & (diff < SWA_WINDOW) & (blk * BLOCK + kj - BLOCK >= 0)
    scores = jnp.where(valid[None, :, None, None], scores, -jnp.inf)
    sink = jnp.broadcast_to(sinks.astype(jnp.float32).reshape(1, 1, kvh, g, 1, 1), scores.shape[:-1] + (1,))
    probs = jax.nn.softmax(jnp.concatenate([scores, sink], axis=-1), axis=-1)[..., :-1]
    out = jnp.einsum('bnhgqk,bnkhd->bnqhgd', probs, vv)
    return out.reshape(b, s, h * d).astype(q.dtype)


def short_gated_conv(bg, cg, xin, conv_w, conv_b):
    u = cg * xin
    y = lax.conv_general_dilated(u, conv_w[:, None, :], window_strides=(1,),
                                 padding=[(CONV_K - 1, 0)],
                                 dimension_numbers=('NWC', 'WIO', 'NWC'),
                                 feature_group_count=u.shape[-1])
    return bg * (y + conv_b)


def causal_softmax_attention(q, k, v):
    b, s, h, dk = q.shape
    scale = dk ** -0.5
    kf = k.astype(jnp.float32)
    vf = v.astype(jnp.float32)
    kpos = jnp.arange(s)

    def step(args):
        qblk, i = args
        sc = jnp.einsum('bqhd,bkhd->bhqk', qblk.astype(jnp.float32), kf) * scale
        qpos = i * BLOCK + jnp.arange(BLOCK)
        sc = jnp.where(kpos[None, :] <= qpos[:, None], sc, -jnp.inf)
        p = jax.nn.softmax(sc, axis=-1)
        return jnp.einsum('bhqk,bkhd->bqhd', p, vf)

    out = lax.map(step, (to_blocks(q), jnp.arange(s // BLOCK)))
    return from_blocks(out).reshape(b, s, -1).astype(q.dtype)


def mla(cq, ckv, kr, pos, g_q, w_uq, g_kv, w_ukv):
    b, s, _ = cq.shape
    q = (rmsnorm(cq, g_q) @ w_uq).reshape(b, s, MLA_HEADS, MLA_NOPE_DIM + MLA_ROPE_DIM)
    q = jnp.concatenate([q[..., :MLA_NOPE_DIM], rope(q[..., MLA_NOPE_DIM:], pos)], axis=-1)
    kv = (rmsnorm(ckv, g_kv) @ w_ukv).reshape(b, s, MLA_HEADS, MLA_NOPE_DIM + MLA_V_DIM)
    k_nope, v = kv[..., :MLA_NOPE_DIM], kv[..., MLA_NOPE_DIM:]
    k_rope = jnp.broadcast_to(rope(kr, pos)[:, :, None, :], (b, s, MLA_HEADS, MLA_ROPE_DIM))
    k = jnp.concatenate([k_nope, k_rope], axis=-1)
    return causal_softmax_attention(q, k, v)


def stick_breaking_attention(q, k, v):
    b, s, h, d = q.shape
    scale = d ** -0.5
    kf = k.astype(jnp.float32)
    vf = v.astype(jnp.float32)
    kpos = jnp.arange(s)

    def step(args):
        qblk, i = args
        z = jnp.einsum('bqhd,bkhd->bhqk', qblk.astype(jnp.float32), kf) * scale
        qpos = i * BLOCK + jnp.arange(BLOCK)
        mask = kpos[None, :] < qpos[:, None]
        log_keep = jnp.where(mask, jax.nn.log_sigmoid(-z), 0.0)
        after = lax.cumsum(log_keep, axis=3, reverse=True) - log_keep
        a = jnp.where(mask, jnp.exp(jax.nn.log_sigmoid(z) + after), 0.0)
        return jnp.einsum('bhqk,bkhd->bqhd', a, vf)

    out = lax.map(step, (to_blocks(q), jnp.arange(s // BLOCK)))
    return from_blocks(out).reshape(b, s, -1).astype(q.dtype)


def hybrid_layer(x, pos, g_pre, w_in, sinks, conv_w, conv_b, g_cq, w_uq, g_ckv, w_ukv, g_grp, w_out, g_post):
    b, s, _ = x.shape
    h = rmsnorm(x, g_pre) @ w_in
    (a_q, a_k, a_v, b_b, b_c, b_x, c_q, c_kv, c_kr, d_q, d_k, d_v, gate) = jnp.split(h, SPLIT_IDX, axis=-1)
    ya = swa_sink_attention(a_q.reshape(b, s, SWA_HEADS, SWA_HEAD_DIM),
                            a_k.reshape(b, s, SWA_KV_HEADS, SWA_HEAD_DIM),
                            a_v.reshape(b, s, SWA_KV_HEADS, SWA_HEAD_DIM), sinks)
    yb = short_gated_conv(b_b, b_c, b_x, conv_w, conv_b)
    yc = mla(c_q, c_kv, c_kr, pos, g_cq, w_uq, g_ckv, w_ukv)
    yd = stick_breaking_attention(d_q.reshape(b, s, SB_HEADS, SB_HEAD_DIM),
                                  d_k.reshape(b, s, SB_HEADS, SB_HEAD_DIM),
                                  d_v.reshape(b, s, SB_HEADS, SB_HEAD_DIM))
    y = jnp.stack([ya, yb, yc, yd], axis=2)
    y = rmsnorm(y, g_grp.reshape(N_GROUPS, GROUP_WIDTH)).reshape(b, s, D_MIX)
    y = y * jax.nn.silu(gate)
    return x + rmsnorm(y @ w_out, g_post)


def setup_inputs(seed: int = 0) -> dict:
    key = jax.random.key(seed)
    ks = jax.random.split(key, 16)
    f32 = jnp.float32
    nrm = lambda k, shape, scale: jax.random.normal(k, shape, f32) * scale
    gain = lambda k, shape: 1.0 + 0.02 * jax.random.normal(k, shape, f32)
    x = jax.random.normal(ks[0], (BATCH, SEQ, D_MODEL), f32)
    positions = jnp.broadcast_to(jnp.arange(SEQ, dtype=jnp.int32)[None, :], (BATCH, SEQ))
    return {
        "x": x,
        "positions": positions,
        "norm_pre": gain(ks[1], (DEPTH, D_MODEL)),
        "w_in": nrm(ks[2], (DEPTH, D_MODEL, D_IN), D_MODEL ** -0.5),
        "attn_sinks": nrm(ks[3], (DEPTH, SWA_HEADS), 0.5),
        "conv_w": nrm(ks[4], (DEPTH, CONV_K, CONV_WIDTH), CONV_K ** -0.5),
        "conv_b": nrm(ks[5], (DEPTH, CONV_WIDTH), 0.01),
        "mla_q_norm": gain(ks[6], (DEPTH, MLA_Q_RANK)),
        "mla_w_uq": nrm(ks[7], (DEPTH, MLA_Q_RANK, MLA_HEADS * (MLA_NOPE_DIM + MLA_ROPE_DIM)), MLA_Q_RANK ** -0.5),
        "mla_kv_norm": gain(ks[8], (DEPTH, MLA_KV_RANK)),
        "mla_w_ukv": nrm(ks[9], (DEPTH, MLA_KV_RANK, MLA_HEADS * (MLA_NOPE_DIM + MLA_V_DIM)), MLA_KV_RANK ** -0.5),
        "group_norm": gain(ks[10], (DEPTH, D_MIX)),
        "w_out": nrm(ks[11], (DEPTH, D_MIX, D_MODEL), D_MIX ** -0.5),
        "norm_post": gain(ks[12], (DEPTH, D_MODEL)),
    }


def reference(x, positions, norm_pre, w_in, attn_sinks, conv_w, conv_b, mla_q_norm, mla_w_uq,
              mla_kv_norm, mla_w_ukv, group_norm, w_out, norm_post):
    for l in range(DEPTH):
        x = hybrid_layer(x, positions, norm_pre[l], w_in[l], attn_sinks[l], conv_w[l], conv_b[l],
                         mla_q_norm[l], mla_w_uq[l], mla_kv_norm[l], mla_w_ukv[l],
                         group_norm[l], w_out[l], norm_post[l])
    return x
```

```python
import math
from contextlib import ExitStack

import numpy as np
import ml_dtypes

import concourse.bass as bass
import concourse.mybir as mybir
from concourse.bass_utils import run_bass_kernel_spmd

F32 = mybir.dt.float32
BF16 = mybir.dt.bfloat16
I32 = mybir.dt.int32
AF = mybir.ActivationFunctionType
ALU = mybir.AluOpType

D = 1024
DIN = 3488
DINX = 3520
DEPTH = 2
NB_BATCH = 2
NR = 4
EPS = 1e-6
KVROWS = 1416
C_AQ, C_AK, C_AV = 0, 256, 384
C_BB, C_BC, C_BX = 512, 768, 1024
C_CQ, C_CKV, C_CKR = 1280, 1536, 1664
C_DQ, C_DK, C_DV = 1696, 1952, 2208
C_GATE = 2464
C_KRSW = 3488
R_SBK, R_MLAK, R_SWAK, R_SBV, R_MLAV, R_SWAV, R_UT = 0, 256, 640, 768, 1024, 1284, 1414


HR = (772, 644)


def kvmap(oldrow):
    if oldrow < 256:
        return 0, oldrow
    if oldrow < 640:
        return 1, oldrow - 256
    if oldrow < 768:
        return 0, 512 + oldrow - 640
    if oldrow < 1024:
        return 0, 256 + oldrow - 768
    if oldrow < 1284:
        return 1, 384 + oldrow - 1024
    if oldrow < 1414:
        return 0, 640 + oldrow - 1284
    return 0, 770 + oldrow - 1414


def own_groups(r, ngl):
    return [r + NR * i for i in range(ngl)]


class Tk:
    __slots__ = ("w", "r", "dsem", "dcnt", "name", "excl")

    def __init__(self, name="", excl=False):
        self.excl = excl
        self.w = None
        self.r = {}
        self.dsem = None
        self.dcnt = 0
        self.name = name


class _Rec:
    def __init__(self):
        self.call = None

    def __getattr__(self, name):
        def f(*a, **kw):
            self.call = (name, a, kw)
            return self
        return f


class Prog:
    ENG = ("pe", "act", "dve", "pool", "sp")

    def __init__(self, nc, es):
        self.nc = nc
        self.es = es
        self.streams = {e: [] for e in self.ENG}
        self.seen = {e: {} for e in self.ENG}
        self.sems = {}
        self.cnt = {}
        self.nsem = 0
        self.all_tickets = {}
        for e in ("pe", "act", "dve", "pool"):
            self._newsem(e)

    def _newsem(self, key):
        s = self.es.enter_context(self.nc.semaphore("s%d" % self.nsem))
        self.nsem += 1
        self.sems[key] = s
        self.cnt[key] = 0
        return s

    def _dsem(self, st, q):
        cls = "sw" if q == "pool" else "hw"
        if st.dsem is not None:
            assert st.dsem[0] == cls, "tile %s used by both DMA classes" % st.name
            return
        free = getattr(self, "free_dsems", {}).get(cls)
        if free:
            st.dsem = free.pop()
        else:
            st.dsem = (cls, self.nsem)
            self._newsem(st.dsem)
        if getattr(self, "phase_dsems", None) is not None:
            self.phase_dsems.append(st.dsem)

    def phase_begin(self):
        self.phase_dsems = []

    def phase_end(self):
        if not hasattr(self, "free_dsems"):
            self.free_dsems = {"sw": [], "hw": []}
        for k in self.phase_dsems:
            self.free_dsems[k[0]].append(k)
        self.phase_dsems = None

    def tile(self, name, shape, dt):
        return self.es.enter_context(self.nc.sbuf_tensor(name, list(shape), dt))

    def psum(self, name):
        return self.es.enter_context(self.nc.psum_tensor(name, [128, 512], F32))

    def _collect(self, eng, reads, writes):
        deps = {}

        def add(tk):
            if tk is None:
                return
            k, v = tk
            if deps.get(k, 0) < v:
                deps[k] = v

        for t in reads:
            add(t.w)
            if t.excl:
                for k, v in t.r.items():
                    if k != eng:
                        add((k, v))
        for t in writes:
            add(t.w)
            for k, v in t.r.items():
                add((k, v))
        waits = []
        seen = self.seen[eng]
        for k, v in deps.items():
            if k == eng and eng == "pe":
                continue
            if seen.get(k, 0) >= v:
                continue
            seen[k] = v
            waits.append((self.sems[k], v))
        return waits

    def _record(self, tk, reads, writes):
        k, v = tk
        for t in reads:
            if t.r.get(k, 0) < v:
                t.r[k] = v
        for t in writes:
            t.w = tk
            t.r = {}
        self.all_tickets[k] = v

    def op(self, eng, fn, reads=(), writes=()):
        waits = self._collect(eng, reads, writes)
        self.cnt[eng] += 1
        tk = (eng, self.cnt[eng])
        rec = _Rec()
        fn(rec)
        name, a, kw = rec.call
        self.streams[eng].append((waits, (lambda e, name=name, a=a, kw=kw: getattr(e, name)(*a, **kw)), (self.sems[eng], 1)))
        self._record(tk, reads, writes)
        return tk

    def dma(self, q, out, in_, st, reads=(), writes=()):
        self._dsem(st, q)
        waits = self._collect(q, reads, writes)
        self.cnt[st.dsem] += 16
        tk = (st.dsem, self.cnt[st.dsem])
        self.streams[q].append((waits, lambda e, o=out, i=in_: e.dma_start(out=o, in_=i),
                                (self.sems[st.dsem], 16)))
        self._record(tk, reads, writes)
        return tk

    def dyn_dma(self, q, out, handle, tab_ap, dims, st, reads=(), writes=()):
        self._dsem(st, q)
        if not hasattr(self, "regs"):
            self.regs = {}
            self.regi = {}
        if q not in self.regs:
            eng = {"sp": self.nc.sync, "act": self.nc.scalar, "pool": self.nc.gpsimd}[q]
            self.regs[q] = [self.es.enter_context(eng.register("dr%s%d" % (q, i))) for i in range(2)]
            self.regi[q] = 0
        reg = self.regs[q][self.regi[q] % 2]
        self.regi[q] += 1
        waits = self._collect(q, reads, writes)
        self.streams[q].append((waits, lambda e, reg=reg, t=tab_ap: e.reg_load(reg, t), None))
        self.cnt[st.dsem] += 16
        tk = (st.dsem, self.cnt[st.dsem])
        src = bass.AP(handle, reg, [list(d) for d in dims])
        self.streams[q].append(([], lambda e, o=out, i=src: e.dma_start(out=o, in_=i), (self.sems[st.dsem], 16)))
        self._record(tk, reads, writes)
        return tk

    def raw(self, eng, fn, key, reads=(), writes=()):
        if key not in self.sems:
            self._newsem(key)
        waits = self._collect(eng, reads, writes)
        self.cnt[key] += 1
        tk = (key, self.cnt[key])
        self.streams[eng].append((waits, fn, (self.sems[key], 1)))
        self._record(tk, reads, writes)
        return tk

    def barrier(self):
        for e in self.ENG:
            waits = []
            for k, v in self.all_tickets.items():
                if self.seen[e].get(k, 0) < v:
                    self.seen[e][k] = v
                    waits.append((self.sems[k], v))
            if waits:
                self.streams[e].append((waits, None, None))

    def emit(self):
        nc = self.nc

        def run(e, lst):
            for waits, fn, inc in lst:
                for s, v in waits:
                    e.wait_ge(s, v)
                if fn is not None:
                    ins = fn(e)
                    if inc is not None:
                        ins.then_inc(inc[0], inc[1])

        with nc.Block() as block:
            @block.tensor
            def _(e):
                run(e, self.streams["pe"])

            @block.scalar
            def _(e):
                run(e, self.streams["act"])

            @block.vector
            def _(e):
                run(e, self.streams["dve"])

            @block.gpsimd
            def _(e):
                run(e, self.streams["pool"])

            @block.sync
            def _(e):
                run(e, self.streams["sp"])


def dap(h, off, dims):
    return bass.AP(h, off, [list(d) for d in dims])


class Builder:
    def __init__(self, mode, S):
        self.mode = mode
        self.S = S
        self.NG = S // 512
        self.NGL = self.NG // NR
        self.NTL = self.NGL * 512
        self.NBL = self.NGL * 4
        self.nc = bass.Bass("TRN2", target_bir_lowering=False)
        self.es = ExitStack()
        self.p = Prog(self.nc, self.es)
        self.dram = {}
        self.dh = {}
        self.rr = {}

    def din(self, name, shape, dt):
        self.dh[name] = self.nc.dram_tensor(name, list(shape), dt, kind="ExternalInput")
        self.dram[name] = self.dh[name].ap()

    def dout(self, name, shape, dt):
        self.dh[name] = self.nc.dram_tensor(name, list(shape), dt, kind="ExternalOutput")
        self.dram[name] = self.dh[name].ap()

    def dint(self, name, shape, dt):
        self.dh[name] = self.nc.dram_tensor(name, list(shape), dt)
        self.dram[name] = self.dh[name].ap()

    def declare(self):
        mode, NTL, NBL, NGL = self.mode, self.NTL, self.NBL, self.NGL
        L = DEPTH if mode == 'F' else 1
        self.din("consts_f", [128, 640], F32)
        self.din("consts_b", [128, 896], BF16)
        self.din("kaug", [64, 512], BF16)
        if mode in ("A", "F"):
            self.din("pos", [1, NTL], I32)
            self.din("gpre_t", [L, 128, 8], F32)
            self.din("w_in_x", [L, D, DINX], F32)
            self.din("gcq_t", [L, 128, 2], F32)
            self.din("w_uq_x", [L, 256, 512], F32)
            self.din("gkv_t", [L, 128, 1], F32)
            self.din("w_ukv_x", [L, 128, 512], F32)
        if mode in ("B", "F"):
            self.din("sinks", [L, 4], F32)
            self.din("convw_t", [L, 128, 6], F32)
            self.din("convb_t", [L, 128, 2], F32)
            self.din("ggrp", [L, D], F32)
            self.din("w_out", [L, D, D], F32)
            self.din("gpost", [L, D], F32)
        self.din("x", [NTL, D], F32)
        scr = [("sc_sbq", [256, NTL], BF16), ("sc_mlaq", [384, NTL], BF16), ("sc_swaq", [256, NTL], BF16),
               ("sc_u", [256, NTL], F32), ("sc_bb", [256, NTL], F32), ("sc_gate", [NBL * 128, D], BF16)]
        if mode == "A":
            for n, s, d in scr:
                self.dout(n, s, d)
            self.dout("kvb_loc", [KVROWS, NTL], BF16)
            self.dout("kvf_loc", [256, NGL * 2], F32)
        elif mode == "B":
            for n, s, d in scr:
                self.din(n, s, d)
            NPOS = self.S // 128 + 12
            self.din("sbK", [256, NPOS * 128], BF16)
            self.din("sbV", [NPOS * 128, 256], BF16)
            self.din("mlaK", [384, NPOS * 128], BF16)
            self.din("mlaV", [NPOS * 128, 260], BF16)
            self.din("swK", [128, NGL * 640], BF16)
            self.din("swV", [NGL * 5 * 128, 130], BF16)
            self.din("utail", [256, NGL * 2], F32)
            self.dint("sc_y", [NBL * 128, D], F32)
            self.dout("out", [NTL, D], F32)
        else:
            for n, s, d in scr:
                self.dint(n, s, d)
            for f in range(2):
                self.dint("kvb_loc%d" % f, [NGL * HR[f], 512], BF16)
                self.dint("kvb_all%d" % f, [(self.NG + 6) * HR[f], 512], BF16)
            self.dh["kvb_loc"] = None
            self.din("tab", [1, 128], I32)
            self.dint("sc_y", [NBL * 128, D], F32)
            self.dint("sc_x1", [NTL, D], F32)
            self.dint("sc_cos", [128, NTL], F32)
            self.dint("sc_sin", [128, NTL], F32)
            self.dout("out", [NTL, D], F32)

    def common(self):
        p = self.p
        self.cf = p.tile("cf", [128, 640], F32)
        self.cb = p.tile("cb", [128, 896], BF16)
        self.t_cf = Tk("cf")
        self.t_cb = Tk("cb")
        p.dma("sp", self.cf[:], self.dram["consts_f"], self.t_cf, writes=[self.t_cf])
        p.dma("sp", self.cb[:], self.dram["consts_b"], self.t_cb, writes=[self.t_cb])
        self.ident_f = self.cf[:, 0:128]
        self.ones_f = self.cf[:, 128:256]
        self.eps_c = self.cf[:, 256:257]
        self.one_c = self.cf[:, 257:258]
        self.freq_c = self.cf[:, 258:259]
        self.sgn_c = self.cf[:, 259:260]
        self.negtri = self.cb[:, 0:128]
        self.m_lt = self.cb[:, 128:256]
        self.m_le = self.cb[:, 256:384]
        self.m_gt = self.cb[:, 384:512]
        self.ones98 = self.cb[:, 512:610]
        self.zero_b = self.cb[:, 640:768]
        self.ones_b = self.cb[:, 768:896]
        self.pp = [self.es.enter_context(self.nc.psum_tensor("pp%d" % i, [128, 1024], F32)) for i in range(4)]
        self.ps = [self.pp[i // 2][:, (i % 2) * 512:(i % 2 + 1) * 512] for i in range(8)]
        self.t_ps = [Tk("ps%d" % i, excl=True) for i in range(8)]

    def rot(self, name, n):
        i = self.rr.get(name, 0)
        self.rr[name] = i + 1
        return i % n

    def phaseA(self, L, x_src):
        p, nc = self.p, self.nc
        NTL, NGL, NBL = self.NTL, self.NGL, self.NBL
        dr = self.dram
        cf_, cb_ = self.t_cf, self.t_cb
        with ExitStack() as es:
            def tile(name, shape, dt):
                return es.enter_context(nc.sbuf_tensor("A%d_%s" % (L, name), list(shape), dt))

            Wp = tile("Wp", [128, 8, DINX], BF16); t_Wp = Tk(); t_Wp2 = Tk()
            wuq = tile("wuq", [128, 2, 512], BF16); t_wuq = Tk()
            wukv = tile("wukv", [128, 512], BF16); t_wukv = Tk()
            gcols = tile("gcols", [128, 16], F32); t_gc = Tk()
            wst = [tile("wst%d" % i, [128, 1760], F32) for i in range(4)]; t_wst = [Tk() for _ in range(4)]
            cosT = tile("cosT", [128, NTL], F32); t_cos = Tk()
            sinS = tile("sinS", [128, NTL], F32); t_sin = Tk()
            posi = tile("posi", [128, NTL], I32); t_posi = Tk()
            ang = tile("ang", [128, NTL], F32); t_ang = Tk()
            tr1 = tile("tr1", [128, NTL], F32); t_tr1 = Tk()
            tr2 = tile("tr2", [128, NTL], F32); t_tr2 = Tk()
            tri = tile("tri", [128, NTL], I32); t_tri = Tk()
            xs = [tile("xs%d" % i, [128, D], F32) for i in range(2)]; t_xs = [Tk(), Tk()]
            xn = [tile("xn%d" % i, [128, D], F32) for i in range(2)]; t_xn = [Tk(), Tk()]
            junk = tile("junk", [128, D], F32); t_junk = Tk()
            sst = [tile("ss%d" % i, [128, 4], F32) for i in range(2)]; t_ss = [Tk(), Tk()]
            xnT = tile("xnT", [128, 8, 512], BF16); t_xnT = Tk()
            cqT = tile("cqT", [128, 2, 512], BF16); t_cqT = Tk()
            sq = [tile("sq%d" % i, [128, 512], BF16) for i in range(2)]; t_sq = [Tk(), Tk()]
            ckvT = tile("ckvT", [128, 512], BF16); t_ckvT = Tk()
            rq_bc = tile("rq_bc", [128, 512], F32); t_rq = Tk()
            rkv_bc = tile("rkv_bc", [128, 512], F32); t_rkv = Tk()
            rkv_t = tile("rkv_t", [128, 8], F32); t_rkvt = Tk()
            tmpf = [tile("tmpf%d" % i, [128, 512], F32) for i in range(3)]; t_tmpf = [Tk(), Tk(), Tk()]
            NSTB, NSTF = 5, 3
            stb = [tile("stb%d" % i, [128, 1024], BF16) for i in range(NSTB)]; t_stb = [Tk() for _ in range(NSTB)]
            stf = [tile("stf%d" % i, [128, 512], F32) for i in range(NSTF)]; t_stf = [Tk() for _ in range(NSTF)]

            p.dma("sp", gcols[:, 0:8], dr["gpre_t"][L], t_gc, writes=[t_gc])
            p.dma("sp", gcols[:, 8:10], dr["gcq_t"][L], t_gc, writes=[t_gc])
            p.dma("sp", gcols[:, 10:11], dr["gkv_t"][L], t_gc, writes=[t_gc])
            k = 0
            for kc in range(8):
                for hf in range(2):
                    s = k % 4; k += 1
                    p.dma("sp", wst[s][:], dr["w_in_x"][L, kc * 128:(kc + 1) * 128, hf * 1760:(hf + 1) * 1760],
                          t_wst[s], writes=[t_wst[s]])
                    if s % 2 == 0:
                        p.op("dve", lambda e, o=Wp[:, kc, hf * 1760:(hf + 1) * 1760], i=wst[s][:], sc=gcols[:, kc:kc + 1]:
                             e.tensor_scalar(out=o, in0=i, scalar1=sc, scalar2=None, op0=ALU.mult),
                             reads=[t_wst[s], t_gc], writes=[t_Wp])
                    else:
                        p.op("act", lambda e, o=Wp[:, kc, hf * 1760:(hf + 1) * 1760], i=wst[s][:], sc=gcols[:, kc:kc + 1]:
                             e.activation(out=o, in_=i, func=AF.Copy, scale=sc),
                             reads=[t_wst[s], t_gc], writes=[t_Wp2])
            for kc in range(2):
                s = k % 2; k += 1
                p.dma("sp", wst[s][:, 0:512], dr["w_uq_x"][L, kc * 128:(kc + 1) * 128, :], t_wst[s], writes=[t_wst[s]])
                p.op("dve", lambda e, o=wuq[:, kc, :], i=wst[s][:, 0:512], sc=gcols[:, 8 + kc:9 + kc]:
                     e.tensor_scalar(out=o, in0=i, scalar1=sc, scalar2=float(96 ** -0.5), op0=ALU.mult, op1=ALU.mult),
                     reads=[t_wst[s], t_gc], writes=[t_wuq])
            s = k % 2; k += 1
            p.dma("sp", wst[s][:, 0:512], dr["w_ukv_x"][L], t_wst[s], writes=[t_wst[s]])
            p.op("dve", lambda e, o=wukv[:], i=wst[s][:, 0:512], sc=gcols[:, 10:11]:
                 e.tensor_scalar(out=o, in0=i, scalar1=sc, scalar2=None, op0=ALU.mult),
                 reads=[t_wst[s], t_gc], writes=[t_wukv])

            import os
            KSTOP = float(os.environ.get("KSTOP", "99"))
            if KSTOP <= 1:
                p.barrier(); return
            REUSE_TABLES = (self.mode == "F" and L > 0)
            if REUSE_TABLES:
                p.dma("sp", cosT[:], dr["sc_cos"], t_cos, writes=[t_cos])
                p.dma("sp", sinS[:], dr["sc_sin"], t_sin, writes=[t_sin])
            else:
                p.dma("sp", posi[:], dap(self.dh["pos"], 0, [[0, 128], [1, NTL]]), t_posi, writes=[t_posi])
                p.op("dve", lambda e: e.tensor_copy(out=ang[:], in_=posi[:]), reads=[t_posi], writes=[t_ang])
                p.op("dve", lambda e: e.tensor_scalar(out=ang[:], in0=ang[:], scalar1=self.freq_c, scalar2=None, op0=ALU.mult),
                     reads=[t_ang, cf_], writes=[t_ang])
                TWO_PI = 2.0 * math.pi
                C_HI = 6.28125
                C_LO = TWO_PI - C_HI

                def sin_of(dst, t_dst, shift):
                    p.op("dve", lambda e: e.tensor_scalar(out=tr1[:], in0=ang[:], scalar1=float(shift), scalar2=float(1.0 / TWO_PI),
                                                          op0=ALU.add, op1=ALU.mult), reads=[t_ang], writes=[t_tr1])
                    p.op("dve", lambda e: e.tensor_copy(out=tri[:], in_=tr1[:]), reads=[t_tr1], writes=[t_tri])
                    p.op("dve", lambda e: e.tensor_copy(out=tr1[:], in_=tri[:]), reads=[t_tri], writes=[t_tr1])
                    p.op("dve", lambda e: e.tensor_scalar(out=tr2[:], in0=ang[:], scalar1=float(shift), scalar2=None, op0=ALU.add),
                         reads=[t_ang], writes=[t_tr2])
                    p.op("dve", lambda e: e.scalar_tensor_tensor(out=tr2[:], in0=tr1[:], scalar=float(-C_HI), in1=tr2[:],
                                                                 op0=ALU.mult, op1=ALU.add), reads=[t_tr1, t_tr2], writes=[t_tr2])
                    p.op("dve", lambda e: e.scalar_tensor_tensor(out=tr2[:], in0=tr1[:], scalar=float(-C_LO), in1=tr2[:],
                                                                 op0=ALU.mult, op1=ALU.add), reads=[t_tr1, t_tr2], writes=[t_tr2])
                    p.op("dve", lambda e: e.tensor_scalar(out=tr1[:], in0=tr2[:], scalar1=float(math.pi), scalar2=float(-TWO_PI),
                                                          op0=ALU.is_gt, op1=ALU.mult), reads=[t_tr2], writes=[t_tr1])
                    p.op("dve", lambda e: e.tensor_tensor(out=tr2[:], in0=tr2[:], in1=tr1[:], op=ALU.add),
                         reads=[t_tr1, t_tr2], writes=[t_tr2])
                    p.op("dve", lambda e: e.tensor_scalar(out=tr1[:], in0=tr2[:], scalar1=float(-math.pi), scalar2=float(TWO_PI),
                                                          op0=ALU.is_lt, op1=ALU.mult), reads=[t_tr2], writes=[t_tr1])
                    p.op("dve", lambda e: e.tensor_tensor(out=tr2[:], in0=tr2[:], in1=tr1[:], op=ALU.add),
                         reads=[t_tr1, t_tr2], writes=[t_tr2])
                    p.op("dve", lambda e: e.tensor_scalar(out=tr2[:], in0=tr2[:], scalar1=float(-3.14159), scalar2=float(3.14159),
                                                          op0=ALU.max, op1=ALU.min), reads=[t_tr2], writes=[t_tr2])
                    p.op("act", lambda e: e.activation(out=dst[:], in_=tr2[:], func=AF.Sin), reads=[t_tr2], writes=[t_dst])

                sin_of(cosT, t_cos, math.pi / 2.0)
                sin_of(sinS, t_sin, 0.0)
                p.op("dve", lambda e: e.tensor_scalar(out=sinS[:], in0=sinS[:], scalar1=self.sgn_c, scalar2=None, op0=ALU.mult),
                     reads=[t_sin, cf_], writes=[t_sin])
                if self.mode == "F":
                    p.dma("pool", dr["sc_cos"], cosT[:], t_cos, reads=[t_cos])
                    p.dma("pool", dr["sc_sin"], sinS[:], t_sin, reads=[t_sin])

            if KSTOP <= 2:
                p.barrier(); return
            if self.mode == "F" and L == 0:
                for f in range(2):
                    for m in (0, 1, 2, self.NG + 3, self.NG + 4, self.NG + 5):
                        for r0 in range(0, HR[f], 128):
                            rows = min(128, HR[f] - r0)
                            p.dma("pool", dap(self.dh["kvb_all%d" % f], (m * HR[f] + r0) * 512, [[512, rows], [1, 512]]),
                                  self.zt[0:rows, :], self.t_zt, reads=[self.t_zt])
            PSA = [2, 3, 4, 5, 6, 7]

            def acc_bank():
                return PSA[self.rot("A_acc", len(PSA))]

            def stage_b():
                i = self.rot("A_stb", NSTB)
                return stb[i], t_stb[i]

            def stage_f():
                i = self.rot("A_stf", NSTF)
                return stf[i], t_stf[i]

            def evac(eng_hint, out, in_, reads, writes, scale=None):
                e = eng_hint if eng_hint else ("act" if self.rot("A_ev", 2) == 0 else "dve")
                if e == "act":
                    if scale is None:
                        p.op("act", lambda en: en.copy(out=out, in_=in_), reads=reads, writes=writes)
                    else:
                        p.op("act", lambda en: en.mul(out=out, in_=in_, mul=float(scale)), reads=reads, writes=writes)
                else:
                    if scale is None:
                        p.op("dve", lambda en: en.tensor_copy(out=out, in_=in_), reads=reads, writes=writes)
                    else:
                        p.op("dve", lambda en: en.tensor_scalar(out=out, in0=in_, scalar1=float(scale), scalar2=None,
                                                                op0=ALU.mult), reads=reads, writes=writes)

            def fmm(col0, M, bank=None):
                b = acc_bank() if bank is None else bank
                for kc in range(8):
                    p.op("pe", lambda e, b=b, kc=kc: e.matmul(self.ps[b][0:M, :], lhsT=Wp[:, kc, col0:col0 + M],
                                                             rhs=xnT[:, kc, :], start=(kc == 0), stop=(kc == 7)),
                         reads=[t_Wp, t_Wp2, t_xnT], writes=[self.t_ps[b]])
                return b

            FUSED = (self.mode == "F")

            def store2d(dst_h, row0, M, lg, src, t_src, prow0=0, q="pool"):
                if FUSED and dst_h is kvb:
                    f, rr = kvmap(row0)
                    d_ = dr["kvb_loc%d" % f]
                    p.dma(q, d_[lg * HR[f] + rr:lg * HR[f] + rr + M, 0:512], src[prow0:prow0 + M, 0:512], t_src, reads=[t_src])
                else:
                    p.dma(q, dst_h[row0:row0 + M, lg * 512:(lg + 1) * 512], src[prow0:prow0 + M, 0:512], t_src, reads=[t_src])

            def vdst(R, lg, blk, W):
                if FUSED:
                    f, rr = kvmap(R)
                    return dap(self.dh["kvb_loc%d" % f], (lg * HR[f] + rr) * 512 + blk * 128 * W, [[W, 128], [1, W]])
                return dap(self.dh["kvb_loc"], R * NTL + (lg * 4 + blk) * 128 * W, [[W, 128], [1, W]])

            def store_nat(dst_h, row0, lg, src, t_src):
                for c in range(4):
                    p.dma("pool", dst_h[row0:row0 + 128, lg * 512 + (3 - c) * 128:lg * 512 + (4 - c) * 128], src[:, c * 128:(c + 1) * 128],
                          t_src, reads=[t_src])

            def ones_cols(st, t_st, nh):
                p.op("pool", lambda e: e.memset(st[:, 0:nh * 65].rearrange("p (h c) -> p h c", h=nh)[:, :, 64:65], 1.0), writes=[t_st])

            kvb = dr.get("kvb_loc", "KVB")
            for lg in range(NGL):
                for blk in range(4):
                    lb = lg * 4 + blk
                    s = self.rot("A_x", 2)
                    p.dma("sp", xs[s][:], x_src[lb * 128:(lb + 1) * 128, :], t_xs[s], writes=[t_xs[s]])
                    p.op("act", lambda e, s=s: e.activation(out=junk[:], in_=xs[s][:], func=AF.Square, accum_out=sst[s][:, 0:1]),
                         reads=[t_xs[s]], writes=[t_junk, t_ss[s]])
                    p.op("act", lambda e, s=s: e.activation(out=sst[s][:, 1:2], in_=sst[s][:, 0:1], func=AF.Sqrt,
                                                           bias=self.eps_c, scale=float(1.0 / D)),
                         reads=[t_ss[s], cf_], writes=[t_ss[s]])
                    p.op("dve", lambda e, s=s: e.reciprocal(out=sst[s][:, 2:3], in_=sst[s][:, 1:2]), reads=[t_ss[s]], writes=[t_ss[s]])
                    p.op("dve", lambda e, s=s: e.tensor_scalar(out=xn[s][:], in0=xs[s][:], scalar1=sst[s][:, 2:3], scalar2=None,
                                                               op0=ALU.mult), reads=[t_xs[s], t_ss[s]], writes=[t_xn[s]])
                    for half in range(2):
                        for i in range(4):
                            kc = half * 4 + i
                            p.op("pe", lambda e, s=s, kc=kc, half=half, i=i: e.transpose(
                                out=self.ps[half][:, i * 128:(i + 1) * 128], in_=xn[s][:, kc * 128:(kc + 1) * 128],
                                identity=self.ident_f), reads=[t_xn[s], cf_], writes=[self.t_ps[half]])
                        evac(None, xnT[:, half * 4:half * 4 + 4, (3 - blk) * 128:(4 - blk) * 128],
                             self.ps[half][:].rearrange("p (a b) -> p a b", a=4), [self.t_ps[half]], [t_xnT])

                if KSTOP <= 3:
                    continue
                for (c0, scale, dst, r0) in ((C_AQ, 0.125, dr["sc_swaq"], 0), (C_AQ + 128, 0.125, dr["sc_swaq"], 128),
                                             (C_AK, None, kvb, R_SWAK),
                                             (C_DQ, 0.125, dr["sc_sbq"], 0), (C_DQ + 128, 0.125, dr["sc_sbq"], 128),
                                             (C_DK, None, kvb, R_SBK), (C_DK + 128, None, kvb, R_SBK + 128)):
                    b = fmm(c0, 128)
                    st, t_st = stage_b()
                    evac(None, st[:, 0:512], self.ps[b][:], [self.t_ps[b]], [t_st], scale)
                    store2d(dst, r0, 128, lg, st, t_st)
                if KSTOP <= 4:
                    continue
                for ct in range(2):
                    b = fmm(C_BB + ct * 128, 128)
                    st, t_st = stage_f()
                    evac(None, st[:], self.ps[b][:], [self.t_ps[b]], [t_st])
                    store_nat(dr["sc_bb"], ct * 128, lg, st, t_st)
                    b1 = fmm(C_BC + ct * 128, 128)
                    b2 = fmm(C_BX + ct * 128, 128)
                    i = self.rot("A_tmpf", 3)
                    evac("act", tmpf[i][:], self.ps[b1][:], [self.t_ps[b1]], [t_tmpf[i]])
                    st, t_st = stage_f()
                    p.op("dve", lambda e, st=st, i=i, b2=b2: e.tensor_tensor(out=st[:], in0=tmpf[i][:], in1=self.ps[b2][:], op=ALU.mult),
                         reads=[t_tmpf[i], self.t_ps[b2]], writes=[t_st])
                    store_nat(dr["sc_u"], ct * 128, lg, st, t_st)
                    if FUSED:
                        tl, t_tl = stage_b()
                        p.op("dve", lambda e: e.tensor_copy(out=tl[:, 0:2], in_=st[:, 126:128]), reads=[t_st], writes=[t_tl])
                        p.op("dve", lambda e: e.tensor_tensor(out=tl[:, 2:4], in0=st[:, 126:128], in1=tl[:, 0:2], op=ALU.subtract),
                             reads=[t_st, t_tl], writes=[t_tl])
                        p.dma("pool", dap(self.dh["kvb_loc0"], (lg * HR[0] + 770) * 512 + ct * 128 * 4, [[4, 128], [1, 4]]),
                              tl[:, 0:4], t_tl, reads=[t_tl])
                    else:
                        p.dma("pool", dr["kvf_loc"][ct * 128:(ct + 1) * 128, lg * 2:lg * 2 + 2], st[:, 126:128], t_st, reads=[t_st])
                if KSTOP <= 5:
                    continue
                for kc in range(2):
                    b = fmm(C_CQ + kc * 128, 128)
                    evac("dve", cqT[:, kc, :], self.ps[b][:], [self.t_ps[b]], [t_cqT])
                    p.op("act", lambda e, b=b, kc=kc: e.activation(out=sq[kc][:], in_=self.ps[b][:], func=AF.Square),
                         reads=[self.t_ps[b]], writes=[t_sq[kc]])
                b = acc_bank()
                for kc in range(2):
                    p.op("pe", lambda e, b=b, kc=kc: e.matmul(self.ps[b][:], lhsT=self.ones_b, rhs=sq[kc][:], start=(kc == 0), stop=(kc == 1)),
                         reads=[t_sq[kc], cb_], writes=[self.t_ps[b]])
                p.op("act", lambda e, b=b: e.activation(out=rq_bc[:], in_=self.ps[b][:], func=AF.Sqrt, bias=self.eps_c, scale=float(1.0 / 256)),
                     reads=[self.t_ps[b], cf_], writes=[t_rq])
                p.op("dve", lambda e: e.reciprocal(out=rq_bc[:], in_=rq_bc[:]), reads=[t_rq], writes=[t_rq])
                if KSTOP <= 5.1:
                    continue
                b = fmm(C_CKV, 128)
                evac("dve", ckvT[:], self.ps[b][:], [self.t_ps[b]], [t_ckvT])
                p.op("act", lambda e, b=b: e.activation(out=sq[0][:], in_=self.ps[b][:], func=AF.Square),
                     reads=[self.t_ps[b]], writes=[t_sq[0]])
                b = acc_bank()
                p.op("pe", lambda e, b=b: e.matmul(self.ps[b][:], lhsT=self.ones_b, rhs=sq[0][:], start=True, stop=True),
                     reads=[t_sq[0], cb_], writes=[self.t_ps[b]])
                p.op("act", lambda e, b=b: e.activation(out=rkv_bc[:], in_=self.ps[b][:], func=AF.Sqrt, bias=self.eps_c, scale=float(1.0 / 128)),
                     reads=[self.t_ps[b], cf_], writes=[t_rkv])
                p.op("dve", lambda e: e.reciprocal(out=rkv_bc[:], in_=rkv_bc[:]), reads=[t_rkv], writes=[t_rkv])
                if KSTOP <= 5.2:
                    continue
                b = acc_bank()
                for blk in range(4):
                    p.op("pe", lambda e, b=b, blk=blk: e.matmul(self.ps[b][:, 2 * blk:2 * blk + 2], lhsT=sq[0][:, blk * 128:(blk + 1) * 128],
                                                               rhs=self.ones_b[:, 0:2], start=True, stop=True),
                         reads=[t_sq[0], cb_], writes=[self.t_ps[b]])
                p.op("act", lambda e, b=b: e.activation(out=rkv_t[:], in_=self.ps[b][:, 0:8], func=AF.Sqrt, bias=self.eps_c, scale=float(1.0 / 128)),
                     reads=[self.t_ps[b], cf_], writes=[t_rkvt])
                p.op("dve", lambda e: e.reciprocal(out=rkv_t[:], in_=rkv_t[:]), reads=[t_rkvt], writes=[t_rkvt])
                if KSTOP <= 6:
                    continue
                b1 = fmm(C_CKR, 32)
                b2 = fmm(C_KRSW, 32)
                i1 = self.rot("A_tmpf", 3)
                p.op("dve", lambda e, i1=i1, b1=b1: e.tensor_tensor(out=tmpf[i1][0:32, :], in0=self.ps[b1][0:32, :],
                                                                    in1=cosT[0:32, lg * 512:(lg + 1) * 512], op=ALU.mult),
                     reads=[self.t_ps[b1], t_cos], writes=[t_tmpf[i1]])
                i2 = self.rot("A_tmpf", 3)
                p.op("dve", lambda e, i2=i2, b2=b2: e.tensor_tensor(out=tmpf[i2][0:32, :], in0=self.ps[b2][0:32, :],
                                                                    in1=sinS[0:32, lg * 512:(lg + 1) * 512], op=ALU.mult),
                     reads=[self.t_ps[b2], t_sin], writes=[t_tmpf[i2]])
                st, t_st = stage_b()
                p.op("dve", lambda e, st=st, i1=i1, i2=i2: e.tensor_tensor(out=st[0:32, 0:512], in0=tmpf[i1][0:32, :], in1=tmpf[i2][0:32, :], op=ALU.add),
                     reads=[t_tmpf[i1], t_tmpf[i2]], writes=[t_st])
                for h in range(4):
                    store2d(kvb, R_MLAK + h * 96 + 64, 32, lg, st, t_st)
                if KSTOP <= 7:
                    continue
                for pair in range(2):
                    b = acc_bank()
                    for kc in range(2):
                        p.op("pe", lambda e, b=b, kc=kc, pair=pair: e.matmul(self.ps[b][:], lhsT=wuq[:, kc, pair * 128:(pair + 1) * 128],
                                                                            rhs=cqT[:, kc, :], start=(kc == 0), stop=(kc == 1)),
                             reads=[t_wuq, t_cqT], writes=[self.t_ps[b]])
                    st, t_st = stage_b()
                    p.op("dve", lambda e, st=st, b=b: e.tensor_tensor(out=st[:, 0:512], in0=self.ps[b][:], in1=rq_bc[:], op=ALU.mult),
                         reads=[self.t_ps[b], t_rq], writes=[t_st])
                    for i in range(2):
                        store2d(dr["sc_mlaq"], (2 * pair + i) * 96, 64, lg, st, t_st, prow0=i * 64)
                    b = acc_bank()
                    p.op("pe", lambda e, b=b, pair=pair: e.matmul(self.ps[b][:], lhsT=wukv[:, pair * 128:(pair + 1) * 128], rhs=ckvT[:],
                                                                 start=True, stop=True), reads=[t_wukv, t_ckvT], writes=[self.t_ps[b]])
                    st, t_st = stage_b()
                    p.op("dve", lambda e, st=st, b=b: e.tensor_tensor(out=st[:, 0:512], in0=self.ps[b][:], in1=rkv_bc[:], op=ALU.mult),
                         reads=[self.t_ps[b], t_rkv], writes=[t_st])
                    for i in range(2):
                        store2d(kvb, R_MLAK + (2 * pair + i) * 96, 64, lg, st, t_st, prow0=i * 64)
                b1 = acc_bank()
                b2 = acc_bank()
                for (b, c0) in ((b1, 256), (b2, 384)):
                    for kc in range(2):
                        p.op("pe", lambda e, b=b, kc=kc, c0=c0: e.matmul(self.ps[b][:], lhsT=wuq[:, kc, c0:c0 + 128], rhs=cqT[:, kc, :],
                                                                        start=(kc == 0), stop=(kc == 1)),
                             reads=[t_wuq, t_cqT], writes=[self.t_ps[b]])
                i1 = self.rot("A_tmpf", 3)
                p.op("dve", lambda e, i1=i1, b1=b1: e.tensor_tensor(out=tmpf[i1][:], in0=self.ps[b1][:], in1=cosT[:, lg * 512:(lg + 1) * 512], op=ALU.mult),
                     reads=[self.t_ps[b1], t_cos], writes=[t_tmpf[i1]])
                i2 = self.rot("A_tmpf", 3)
                p.op("dve", lambda e, i2=i2, b2=b2: e.tensor_tensor(out=tmpf[i2][:], in0=self.ps[b2][:], in1=sinS[:, lg * 512:(lg + 1) * 512], op=ALU.mult),
                     reads=[self.t_ps[b2], t_sin], writes=[t_tmpf[i2]])
                p.op("pool", lambda e, i1=i1, i2=i2: e.tensor_tensor(out=tmpf[i1][:], in0=tmpf[i1][:], in1=tmpf[i2][:], op=ALU.add),
                     reads=[t_tmpf[i1], t_tmpf[i2]], writes=[t_tmpf[i1]])
                st, t_st = stage_b()
                p.op("dve", lambda e, st=st, i1=i1: e.tensor_tensor(out=st[:, 0:512], in0=tmpf[i1][:], in1=rq_bc[:], op=ALU.mult),
                     reads=[t_tmpf[i1], t_rq], writes=[t_st])
                for h in range(4):
                    store2d(dr["sc_mlaq"], h * 96 + 64, 32, lg, st, t_st, prow0=h * 32)

                if KSTOP <= 8:
                    continue
                for blk in range(4):
                    lb = lg * 4 + blk
                    tok = slice(blk * 128, (blk + 1) * 128)
                    b = acc_bank()
                    p.op("pe", lambda e, b=b, tok=tok: e.matmul(self.ps[b][:, 0:256], lhsT=ckvT[:, tok], rhs=wukv[:, 256:512], start=True, stop=True),
                         reads=[t_ckvT, t_wukv], writes=[self.t_ps[b]])
                    st, t_st = stage_b()
                    p.op("dve", lambda e, st=st, b=b, blk=blk: e.tensor_scalar(
                        out=st[:, 0:260].rearrange("p (h c) -> p h c", h=4)[:, :, 0:64], in0=self.ps[b][:, 0:256].rearrange("p (h c) -> p h c", h=4),
                        scalar1=rkv_t[:, 2 * blk:2 * blk + 1], scalar2=None, op0=ALU.mult),
                         reads=[self.t_ps[b], t_rkvt], writes=[t_st])
                    ones_cols(st, t_st, 4)
                    p.dma("pool", vdst(R_MLAV, lg, blk, 260), st[:, 0:260], t_st, reads=[t_st])
                    b = acc_bank()
                    for (c0, n, o0) in ((C_AV, 128, 0), (C_DV, 256, 128)):
                        for kc in range(8):
                            p.op("pe", lambda e, b=b, kc=kc, c0=c0, n=n, o0=o0, tok=tok: e.matmul(
                                self.ps[b][:, o0:o0 + n], lhsT=xnT[:, kc, tok], rhs=Wp[:, kc, c0:c0 + n], start=(kc == 0), stop=(kc == 7)),
                                 reads=[t_Wp, t_Wp2, t_xnT], writes=[self.t_ps[b]])
                    st, t_st = stage_b()
                    evac(None, st[:, 0:130].rearrange("p (h c) -> p h c", h=2)[:, :, 0:64], self.ps[b][:, 0:128].rearrange("p (h c) -> p h c", h=2),
                         [self.t_ps[b]], [t_st])
                    evac(None, st[:, 256:512], self.ps[b][:, 128:384], [self.t_ps[b]], [t_st])
                    ones_cols(st, t_st, 2)
                    p.dma("pool", vdst(R_SWAV, lg, blk, 130), st[:, 0:130], t_st, reads=[t_st])
                    p.dma("pool", vdst(R_SBV, lg, blk, 256), st[:, 256:512], t_st, reads=[t_st])
                    st, t_st = stage_b()
                    for half in range(2):
                        b = acc_bank()
                        c0 = C_GATE + half * 512
                        for kc in range(8):
                            p.op("pe", lambda e, b=b, kc=kc, c0=c0, tok=tok: e.matmul(self.ps[b][:], lhsT=xnT[:, kc, tok], rhs=Wp[:, kc, c0:c0 + 512],
                                                                                     start=(kc == 0), stop=(kc == 7)),
                                 reads=[t_Wp, t_Wp2, t_xnT], writes=[self.t_ps[b]])
                        p.op("act", lambda e, st=st, b=b, half=half: e.activation(out=st[:, half * 512:(half + 1) * 512], in_=self.ps[b][:], func=AF.Silu),
                             reads=[self.t_ps[b]], writes=[t_st])
                    p.dma("pool", dr["sc_gate"][lb * 128:(lb + 1) * 128, :], st[:, :], t_st, reads=[t_st])
                if FUSED:
                    self.gather_group(lg)
            p.barrier()

    def gather_group(self, lg):
        p = self.p
        groups = [[b * NR + r for r in range(NR)] for b in range(NB_BATCH)]
        waits = []
        for k, v in p.all_tickets.items():
            if isinstance(k, tuple) and k[0] in ("sw", "hw") and p.seen["pool"].get(k, 0) < v:
                p.seen["pool"][k] = v
                waits.append((p.sems[k], v))
        p.streams["pool"].append((waits, None, None))
        t_g = Tk()
        for f in range(2):
            p.raw("pool", lambda e, lg=lg, f=f: e.collective_compute(
                "AllGather", ALU.bypass, replica_groups=groups,
                ins=[self.dh["kvb_loc%d" % f].ap()[lg * HR[f]:(lg + 1) * HR[f], :].opt()],
                outs=[self.dh["kvb_all%d" % f].ap()[(3 + 4 * lg) * HR[f]:(7 + 4 * lg) * HR[f], :].opt()]),
                  ("cc", lg * 2 + f), writes=[t_g])

    def phaseB(self, L, x_src, x_dst):
        p, nc = self.p, self.nc
        S, NG, NTL, NGL, NBL = self.S, self.NG, self.NTL, self.NGL, self.NBL
        NBK = S // 128
        NPOS = NBK + 12
        dr = self.dram
        cf_, cb_ = self.t_cf, self.t_cb
        z4 = self.zero4
        with ExitStack() as es:
            def tile(name, shape, dt):
                return es.enter_context(nc.sbuf_tensor("B%d_%s" % (L, name), list(shape), dt))

            yt = [tile("yt%d" % i, [128, 4, 256], F32) for i in range(3)]; t_yt = [Tk(), Tk(), Tk()]
            small = tile("small", [128, 64], F32); t_small = Tk()
            Wo = tile("Wo", [128, 8, D], BF16); t_Wo = Tk()
            wst = [tile("wst%d" % i, [128, D], F32) for i in range(2)]; t_wst = [Tk(), Tk()]
            ggrp_bc = tile("ggrp", [128, D], F32); t_gg = Tk()
            gpost_bc = tile("gpost", [128, D], F32); t_gp = Tk()
            cwb = tile("cwb", [128, 8], F32); t_cwb = Tk()
            esink = tile("esink", [128, 4], F32); t_es = Tk()
            es_att = ExitStack()

            def atile(name, shape, dt):
                return es_att.enter_context(nc.sbuf_tensor("B%d_%s" % (L, name), list(shape), dt))
            _tile_outer = tile
            tile = atile
            KtAll = tile("KtAll", [128, 4, NPOS * 128], BF16)
            Kt = [KtAll[:, h, :] for h in range(4)]; t_K = [Tk() for _ in range(4)]
            Vt = tile("Vt", [128, NPOS, 264], BF16); t_V = Tk()
            Qa = [tile("Qa%d" % h, [128, 512], BF16) for h in range(4)]
            Qm = [tile("Qm%d" % h, [128, 512], BF16) for h in range(4)]
            t_Qm = [Tk() for _ in range(4)]; t_Qx = [Tk() for _ in range(4)]
            NE = 2
            Et2 = [tile("E%d" % i, [128, 2, 512], F32) for i in range(NE)]; t_E = [Tk() for _ in range(NE)]
            Lt2 = [tile("Lp%d" % i, [128, 2, 512], BF16) for i in range(NE)]; t_L = [Tk() for _ in range(NE)]
            NA = 3
            At2 = [tile("At%d" % i, [128, 2, 512], BF16) for i in range(NA)]; t_A = [Tk() for _ in range(NA)]
            tile = _tile_outer

            p.dma("sp", ggrp_bc[:], dap(self.dh["ggrp"], L * D, [[0, 128], [1, D]]), t_gg, writes=[t_gg])
            p.dma("sp", gpost_bc[:], dap(self.dh["gpost"], L * D, [[0, 128], [1, D]]), t_gp, writes=[t_gp])
            p.dma("sp", cwb[:, 0:6], dr["convw_t"][L], t_cwb, writes=[t_cwb])
            p.dma("sp", cwb[:, 6:8], dr["convb_t"][L], t_cwb, writes=[t_cwb])
            p.dma("sp", esink[:], dap(self.dh["sinks"], L * 4, [[0, 128], [1, 4]]), t_es, writes=[t_es])
            p.op("act", lambda e: e.activation(out=esink[:], in_=esink[:], func=AF.Exp), reads=[t_es], writes=[t_es])

            NCH = 4

            FUSED = (self.mode == "F")
            NQG = NPOS // 4
            FAM = dict(sbK=0, sbV=0, mlaK=1, mlaV=1)
            DQ = "sp" if L == 0 else "act"
            TB = dict(sbK=0, sbV=4, mlaK=8, mlaV=12, swK=16, swV=18, ut=19)

            def load_K(name, nrows, heads=(0, 1, 2, 3)):
                if FUSED:
                    for h in heads:
                        i = TB[name] + h
                        f = FAM[name]
                        p.dyn_dma(DQ, KtAll[0:nrows, h, :], self.dh["kvb_all%d" % f], self.tabt[0:1, i:i + 1],
                                  [[512, nrows], [HR[f] * 512, NQG], [1, 512]], t_K[h], reads=[self.t_tab],
                                  writes=[t_K[h]] + ([t_Kx[h]] if nrows > 64 else []))
                    return
                w = NPOS * 128
                cw = (w + NCH - 1) // NCH
                for h in range(4):
                    for c in range(NCH):
                        c0, c1 = c * cw, min(w, (c + 1) * cw)
                        p.dma("sp", Kt[h][0:nrows, c0:c1], dr[name][h * nrows:(h + 1) * nrows, c0:c1], t_K[h], writes=[t_K[h]])

            def load_V(name, wcols):
                if FUSED:
                    for blk in range(4):
                        i = TB[name] + blk
                        f = FAM[name]
                        p.dyn_dma(DQ, Vt[:, blk:NPOS:4, 0:wcols], self.dh["kvb_all%d" % f], self.tabt[0:1, i:i + 1],
                                  [[wcols, 128], [HR[f] * 512, NQG], [1, wcols]], t_V, reads=[self.t_tab], writes=[t_V])
                    return
                step = 8
                for q0 in range(0, NPOS, step):
                    q1 = min(NPOS, q0 + step)
                    p.dma("sp", Vt[:, q0:q1, 0:wcols], dap(self.dh[name], q0 * 128 * wcols, [[wcols, 128], [128 * wcols, q1 - q0], [1, wcols]]),
                          t_V, writes=[t_V])

            def pos_of(lg, d):
                if FUSED:
                    return 4 * (4 * lg + 3 - d // 4) + d % 4
                return NBK - 4 - 16 * lg + d

            def nstep(lg):
                return 16 * lg + 16

            def acols(d):
                if d < 4:
                    return (d + 1) * 128, True
                return 512, False

            t_Kx = [Tk() for _ in range(4)]
            for h in range(4):
                en = "dve" if h % 2 == 0 else "pool"
                p.op(en, lambda e: e.memset(Kt[h][64:128, :], 0.0), writes=[t_Kx[h]])
                p.op(en, lambda e: e.memset(Kt[h][64:65, :], 1.0), writes=[t_Kx[h]])
                p.op(en, lambda e: e.memset(Kt[h][96:97, :], 1.0), writes=[t_Kx[h]])
            load_K("sbK", 64, (0, 1))
            load_V("sbV", 256)
            load_K("sbK", 64, (2, 3))

            for h in range(4):
                p.op("pool", lambda e, h=h: e.memset(Qm[h][64:128, :], 0.0), writes=[t_Qm[h]])
                p.op("pool", lambda e, h=h: e.memset(Qa[h][64:128, :], 0.0), writes=[t_Qx[h]])
            t_Qd = [Tk() for _ in range(4)]
            for pair in range(2):
              if pair == 1:
                load_K("mlaK", 96, (0, 1))
              for lg in range(NGL):
                ys = self.rot("B_yt", 3)
                if True:
                    hs = (2 * pair, 2 * pair + 1)
                    bank = {}
                    for i, h in enumerate(hs):
                        bank[h] = dict(A=i, B=2 + i, C=4 + i, O=6 + i)
                        qsrc = dr["sc_sbq"][h * 64:(h + 1) * 64, lg * 512:(lg + 1) * 512]
                        p.dma("sp", Qm[h][0:64, :], qsrc, t_Qm[h], writes=[t_Qm[h]])
                        p.dma("sp", Qa[h][0:64, :], qsrc, t_Qd[h], writes=[t_Qd[h]])
                        p.op("pool", lambda e, h=h: e.memset(Qa[h][64:98, :], 0.0), writes=[t_Qx[h]])
                        bC, bO = bank[h]["C"], bank[h]["O"]
                        p.op("pe", lambda e, bC=bC: e.matmul(self.ps[bC][0:98, :], lhsT=self.ones98, rhs=z4, start=True, stop=False, skip_group_check=True),
                             reads=[cb_], writes=[self.t_ps[bC]])
                        p.op("pe", lambda e, bO=bO: e.matmul(self.ps[bO][:, 0:256], lhsT=self.zero_b, rhs=z4[:, 0:256], start=True, stop=False, skip_group_check=True),
                             reads=[cb_], writes=[self.t_ps[bO]])
                    ND = nstep(lg)
                    slot = {}
                    aslot = {}

                    def mm1(d):
                        n, dg = acols(d)
                        P_ = pos_of(lg, d)
                        for h in hs:
                            bA = bank[h]["A"]
                            p.op("pe", lambda e, h=h, bA=bA, P_=P_, n=n: e.matmul(self.ps[bA][:, 0:n], lhsT=Kt[h][:, P_ * 128:(P_ + 1) * 128],
                                                                                rhs=Qm[h][:, 0:n], start=True, stop=True),
                                 reads=[t_K[h], t_Kx[h], t_Qm[h]], writes=[self.t_ps[bA]])

                    def EL(d):
                        n, dg = acols(d)
                        s = self.rot("B_E", 2)
                        slot[d] = s
                        kA = bank[hs[0]]["A"] // 2
                        p.op("act", lambda e: e.activation(out=Et2[s][:, :, 0:n], in_=self.pp[kA][:].rearrange("p (h c) -> p h c", h=2)[:, :, 0:n], func=AF.Exp),
                             reads=[self.t_ps[2 * kA], self.t_ps[2 * kA + 1]], writes=[t_E[s]])
                        p.op("act", lambda e: e.activation(out=Lt2[s][:, :, 0:n], in_=Et2[s][:, :, 0:n], func=AF.Ln, bias=1.0),
                             reads=[t_E[s]], writes=[t_L[s]])
                        if dg:
                            for hi in range(2):
                                p.op("pool", lambda e: e.tensor_tensor(out=Lt2[s][:, hi, n - 128:n], in0=Lt2[s][:, hi, n - 128:n], in1=self.m_lt, op=ALU.mult),
                                     reads=[t_L[s], cb_], writes=[t_L[s]])

                    def mm2tc(d):
                        n, dg = acols(d)
                        P_ = pos_of(lg, d)
                        last = (d == ND - 1)
                        s = slot[d]
                        for hi, h in enumerate(hs):
                            bB, bC = bank[h]["B"], bank[h]["C"]
                            p.op("pe", lambda e, h=h, bB=bB, P_=P_, n=n: e.matmul(self.ps[bB][:, 0:n], lhsT=Kt[h][:, P_ * 128:(P_ + 1) * 128],
                                                                                rhs=Qa[h][:, 0:n], start=True, stop=False),
                                 reads=[t_K[h], t_Kx[h], t_Qd[h], t_Qx[h]], writes=[self.t_ps[bB]])
                            p.op("pe", lambda e, bB=bB, s=s, n=n: e.matmul(self.ps[bB][:, 0:n], lhsT=self.negtri, rhs=Lt2[s][:, hi, 0:n], start=False, stop=True),
                                 reads=[t_L[s], cb_], writes=[self.t_ps[bB]])
                            if not last:
                                p.op("pe", lambda e, bC=bC, s=s, n=n: e.matmul(self.ps[bC][0:98, 0:n], lhsT=self.ones98, rhs=Lt2[s][:, hi, 0:n],
                                                                              start=False, stop=False, skip_group_check=True),
                                     reads=[t_L[s], cb_], writes=[self.t_ps[bC]])
                        if not last:
                            for h in hs:
                                bC = bank[h]["C"]
                                p.op("dve", lambda e, h=h, bC=bC, n=n: e.tensor_scalar(out=Qa[h][64:98, 0:n], in0=self.ps[bC][64:98, 0:n], scalar1=-1.0,
                                                                                      scalar2=None, op0=ALU.mult),
                                     reads=[self.t_ps[bC]], writes=[t_Qx[h]])
                                p.op("dve", lambda e, h=h, bC=bC, n=n: e.scalar_tensor_tensor(out=Qa[h][96:98, 0:n], in0=self.ps[bC][96:98, 0:n], scalar=-1.0,
                                                                                             in1=Qa[h][96:98, 0:n], op0=ALU.mult, op1=ALU.subtract),
                                     reads=[self.t_ps[bC], t_Qx[h]], writes=[t_Qx[h]])

                    def Bexp(d):
                        n, dg = acols(d)
                        a = self.rot("B_A", NA)
                        aslot[d] = a
                        kB = bank[hs[0]]["B"] // 2
                        p.op("act", lambda e: e.activation(out=At2[a][:, :, 0:n], in_=self.pp[kB][:].rearrange("p (h c) -> p h c", h=2)[:, :, 0:n], func=AF.Exp),
                             reads=[self.t_ps[2 * kB], self.t_ps[2 * kB + 1]], writes=[t_A[a]])
                        if dg:
                            for hi in range(2):
                                p.op("pool", lambda e: e.tensor_tensor(out=At2[a][:, hi, n - 128:n], in0=At2[a][:, hi, n - 128:n], in1=self.m_lt, op=ALU.mult),
                                     reads=[t_A[a], cb_], writes=[t_A[a]])

                    def PV(d):
                        n, dg = acols(d)
                        P_ = pos_of(lg, d)
                        last = (d == ND - 1)
                        a = aslot[d]
                        for hi, h in enumerate(hs):
                            bO = bank[h]["O"]
                            for c in range(n // 128):
                                p.op("pe", lambda e, h=h, bO=bO, a=a, c=c, P_=P_: e.matmul(self.ps[bO][:, c * 64:(c + 1) * 64], lhsT=At2[a][:, hi, c * 128:(c + 1) * 128],
                                                                                         rhs=Vt[:, P_, h * 64:(h + 1) * 64], start=False, stop=last,
                                                                                         skip_group_check=True),
                                     reads=[t_A[a], t_V], writes=[self.t_ps[bO]])

                    mm1(0)
                    EL(0)
                    if ND > 1:
                        mm1(1)
                    for d in range(ND):
                        mm2tc(d)
                        if d >= 1:
                            PV(d - 1)
                        if d + 1 < ND:
                            EL(d + 1)
                        if d + 2 < ND:
                            mm1(d + 2)
                        Bexp(d)
                    PV(ND - 1)
                    for h in hs:
                        bO = bank[h]["O"]
                        p.op("dve", lambda e, h=h, bO=bO, ys=ys: e.tensor_copy(out=yt[ys][:, :, h * 64:(h + 1) * 64],
                                                                               in_=self.ps[bO][:, 0:256].rearrange("p (j c) -> p j c", j=4)),
                             reads=[self.t_ps[bO]], writes=[t_yt[ys]])
                p.dma("pool", dap(self.dh["sc_y"], lg * 4 * 128 * D + 768 + pair * 128, [[D, 128], [128 * D, 4], [1, 128]]), yt[ys][:, :, pair * 128:(pair + 1) * 128], t_yt[ys], reads=[t_yt[ys]])

            load_V("mlaV", 260)
            load_K("mlaK", 96, (2, 3))
            for kc in range(8):
                s = kc % 2
                p.dma("sp", wst[s][:], dr["w_out"][L, kc * 128:(kc + 1) * 128, :], t_wst[s], writes=[t_wst[s]])
                p.op("dve", lambda e, s=s, kc=kc: e.tensor_copy(out=Wo[:, kc, :], in_=wst[s][:]), reads=[t_wst[s]], writes=[t_Wo])
            for pair in range(2):
              for lg in range(NGL):
                ys = self.rot("B_yt", 3)
                if True:
                    hs = (2 * pair, 2 * pair + 1)
                    bank = {}
                    for i, h in enumerate(hs):
                        bank[h] = dict(S=(i, 2 + i, 6 + i), O=4 + i)
                        p.dma("sp", Qa[h][0:96, :], dr["sc_mlaq"][h * 96:(h + 1) * 96, lg * 512:(lg + 1) * 512], t_Qm[h], writes=[t_Qm[h], t_Qx[h], t_Qd[h]])
                        bO = bank[h]["O"]
                        p.op("pe", lambda e, bO=bO: e.matmul(self.ps[bO][:, 0:260], lhsT=self.zero_b, rhs=z4[:, 0:260], start=True, stop=False, skip_group_check=True),
                             reads=[cb_], writes=[self.t_ps[bO]])
                    ND = nstep(lg)

                    def mmS(d):
                        n, dg = acols(d)
                        P_ = pos_of(lg, d)
                        for h in hs:
                            bS = bank[h]["S"][d % 3]
                            p.op("pe", lambda e, h=h, bS=bS, P_=P_, n=n: e.matmul(self.ps[bS][:, 0:n], lhsT=Kt[h][0:96, P_ * 128:(P_ + 1) * 128],
                                                                                rhs=Qa[h][0:96, 0:n], start=True, stop=True),
                                 reads=[t_K[h], t_Qm[h]], writes=[self.t_ps[bS]])

                    aslot = {}

                    def Pexp(d):
                        n, dg = acols(d)
                        a = self.rot("B_A", NA)
                        aslot[d] = a
                        kS = (0, 1, 3)[d % 3]
                        p.op("act", lambda e: e.activation(out=At2[a][:, :, 0:n], in_=self.pp[kS][:].rearrange("p (h c) -> p h c", h=2)[:, :, 0:n], func=AF.Exp),
                             reads=[self.t_ps[2 * kS], self.t_ps[2 * kS + 1]], writes=[t_A[a]])
                        if dg:
                            for hi in range(2):
                                p.op("pool", lambda e: e.tensor_tensor(out=At2[a][:, hi, n - 128:n], in0=At2[a][:, hi, n - 128:n], in1=self.m_le, op=ALU.mult),
                                     reads=[t_A[a], cb_], writes=[t_A[a]])

                    def PVm(d):
                        n, dg = acols(d)
                        P_ = pos_of(lg, d)
                        last = (d == ND - 1)
                        a = aslot[d]
                        for hi, h in enumerate(hs):
                            bO = bank[h]["O"]
                            for c in range(n // 128):
                                p.op("pe", lambda e: e.matmul(self.ps[bO][:, c * 65:(c + 1) * 65], lhsT=At2[a][:, hi, c * 128:(c + 1) * 128],
                                                              rhs=Vt[:, P_, h * 65:(h + 1) * 65], start=False, stop=last, skip_group_check=True),
                                     reads=[t_A[a], t_V], writes=[self.t_ps[bO]])

                    mmS(0)
                    if ND > 1:
                        mmS(1)
                    for d in range(ND):
                        if d + 2 < ND:
                            mmS(d + 2)
                        if d >= 1:
                            PVm(d - 1)
                        Pexp(d)
                    PVm(ND - 1)
                    for h in hs:
                        bO = bank[h]["O"]
                        p.op("dve", lambda e, bO=bO, h=h: e.reciprocal(out=small[:, 8 + h * 4:12 + h * 4],
                                                                       in_=self.ps[bO][:, 0:260].rearrange("p (j c) -> p j c", j=4)[:, :, 64]),
                             reads=[self.t_ps[bO]], writes=[t_small])
                        for c in range(4):
                            p.op("dve", lambda e, bO=bO, h=h, c=c, ys=ys: e.tensor_scalar(out=yt[ys][:, c, h * 64:(h + 1) * 64], in0=self.ps[bO][:, c * 65:c * 65 + 64],
                                                                                         scalar1=small[:, 8 + h * 4 + c:9 + h * 4 + c], scalar2=None, op0=ALU.mult),
                                 reads=[self.t_ps[bO], t_small], writes=[t_yt[ys]])
                p.dma("pool", dap(self.dh["sc_y"], lg * 4 * 128 * D + 512 + pair * 128, [[D, 128], [128 * D, 4], [1, 128]]), yt[ys][:, :, pair * 128:(pair + 1) * 128], t_yt[ys], reads=[t_yt[ys]])

            p.barrier()
            es_att.close()
            es_sw = ExitStack()

            def stile(name, shape, dt):
                return es_sw.enter_context(nc.sbuf_tensor("B%d_%s" % (L, name), list(shape), dt))
            tile = stile
            swQ = [tile("swQ%d" % i, [64, 4, 512], BF16) for i in range(2)]; t_swQ = [Tk(), Tk()]
            swK = [tile("swK%d" % i, [64, 2, 640], BF16) for i in range(2)]; t_swK = [Tk(), Tk()]
            swV = [tile("swV%d" % i, [128, 5, 130], BF16) for i in range(2)]; t_swV = [Tk(), Tk()]
            Pc = [tile("Pc%d" % i, [128, 512], BF16) for i in range(2)]; t_Pc = [Tk(), Tk()]
            Pp = [tile("Pp%d" % i, [128, 512], BF16) for i in range(2)]; t_Pp = [Tk(), Tk()]
            ut = [tile("ut%d" % i, [128, 2, 514], F32) for i in range(2)]; t_ut = [Tk(), Tk()]
            bbt = [tile("bbt%d" % i, [128, 2, 512], F32) for i in range(2)]; t_bbt = [Tk(), Tk()]
            cvt = [tile("cvt%d" % i, [128, 2, 512], F32) for i in range(2)]; t_cvt = [Tk(), Tk()]
            if FUSED:
                swKp = tile("swKp", [64, 2, NGL, 128], BF16); t_swKp = Tk()
                swVp = tile("swVp", [128, NGL, 130], BF16); t_swVp = Tk()
                utl = tile("utl", [128, 2, NGL, 4], BF16); t_utl = Tk()
                for kvh in range(2):
                    i = TB["swK"] + kvh
                    p.dyn_dma("pool", swKp[:, kvh, :, :], self.dh["kvb_all0"], self.tabt[0:1, i:i + 1], [[512, 64], [4 * HR[0] * 512, NGL], [1, 128]], t_swKp,
                              reads=[self.t_tab], writes=[t_swKp])
                i = TB["swV"]
                p.dyn_dma("pool", swVp[:], self.dh["kvb_all0"], self.tabt[0:1, i:i + 1], [[130, 128], [4 * HR[0] * 512, NGL], [1, 130]], t_swVp,
                          reads=[self.t_tab], writes=[t_swVp])
                for ct in range(2):
                    i = TB["ut"] + ct
                    p.dyn_dma("pool", utl[:, ct, :, :], self.dh["kvb_all0"], self.tabt[0:1, i:i + 1], [[4, 128], [4 * HR[0] * 512, NGL], [1, 4]], t_utl,
                              reads=[self.t_tab], writes=[t_utl])
            ys_of = {}

            def swa_load(lg):
                s = lg % 2
                ys_of[lg] = self.rot("B_yt", 3)
                p.dma("sp", swQ[s][:], dap(self.dh["sc_swaq"], lg * 512, [[NTL, 64], [64 * NTL, 4], [1, 512]]), t_swQ[s], writes=[t_swQ[s]])
                if FUSED:
                    kl = self.dh["kvb_loc0"]
                    p.dma("sp", swK[s][:, :, 0:512], dap(kl, (lg * HR[0] + 512) * 512, [[512, 64], [64 * 512, 2], [1, 512]]), t_swK[s], writes=[t_swK[s]])
                    p.dma("sp", swV[s][:, 0:4, :], dap(kl, (lg * HR[0] + 640) * 512, [[130, 128], [128 * 130, 4], [1, 130]]), t_swV[s], writes=[t_swV[s]])
                    p.op("pool", lambda e: e.tensor_copy(out=swK[s][:, :, 512:640], in_=swKp[:, :, lg, :]), reads=[t_swKp], writes=[t_swK[s]])
                    p.op("pool", lambda e: e.tensor_copy(out=swV[s][:, 4, :], in_=swVp[:, lg, :]), reads=[t_swVp], writes=[t_swV[s]])
                else:
                    p.dma("sp", swK[s][:], dap(self.dh["swK"], lg * 640, [[NGL * 640, 64], [64 * NGL * 640, 2], [1, 640]]), t_swK[s], writes=[t_swK[s]])
                    p.dma("sp", swV[s][:], dap(self.dh["swV"], lg * 5 * 128 * 130, [[130, 128], [128 * 130, 5], [1, 130]]), t_swV[s], writes=[t_swV[s]])

            def swa_A(lg, c, idx):
                s = lg % 2
                q = idx % 2
                bC, bP = 0, 1
                for h in range(4):
                    p.op("pe", lambda e: e.matmul(self.ps[bC][:, h * 128:(h + 1) * 128], lhsT=swK[s][:, h // 2, c * 128:(c + 1) * 128],
                                                  rhs=swQ[s][:, h, c * 128:(c + 1) * 128], start=True, stop=True),
                         reads=[t_swK[s], t_swQ[s]], writes=[self.t_ps[bC]])
                for h in range(4):
                    p.op("pe", lambda e: e.matmul(self.ps[bP][:, h * 128:(h + 1) * 128], lhsT=swK[s][:, h // 2, (c + 1) * 128:(c + 2) * 128],
                                                  rhs=swQ[s][:, h, c * 128:(c + 1) * 128], start=True, stop=True),
                         reads=[t_swK[s], t_swQ[s]], writes=[self.t_ps[bP]])
                p.op("act", lambda e: e.activation(out=Pc[q][:], in_=self.ps[bC][:], func=AF.Exp), reads=[self.t_ps[bC]], writes=[t_Pc[q]])
                p.op("act", lambda e: e.activation(out=Pp[q][:], in_=self.ps[bP][:], func=AF.Exp), reads=[self.t_ps[bP]], writes=[t_Pp[q]])
                m_le4 = bass.AP(self.cb, 256, [[896, 128], [0, 4], [1, 128]])
                m_gt4 = bass.AP(self.cb, 384, [[896, 128], [0, 4], [1, 128]])
                p.op("pool", lambda e: e.tensor_tensor(out=Pc[q][:].rearrange("p (h c) -> p h c", h=4), in0=Pc[q][:].rearrange("p (h c) -> p h c", h=4),
                                                       in1=m_le4, op=ALU.mult), reads=[t_Pc[q], cb_], writes=[t_Pc[q]])
                p.op("dve", lambda e: e.tensor_tensor(out=Pp[q][:].rearrange("p (h c) -> p h c", h=4), in0=Pp[q][:].rearrange("p (h c) -> p h c", h=4),
                                                      in1=m_gt4, op=ALU.mult), reads=[t_Pp[q], cb_], writes=[t_Pp[q]])

            def swa_B(lg, c, idx):
                s = lg % 2
                q = idx % 2
                ys = ys_of[lg]
                bO = 2 + (idx % 2)
                p.op("pe", lambda e: e.matmul(self.ps[bO][:, 0:260], lhsT=self.zero_b, rhs=z4[:, 0:260], start=True, stop=False,
                                              skip_group_check=True), reads=[cb_], writes=[self.t_ps[bO]])
                for h in range(4):
                    kvh = h // 2
                    p.op("pe", lambda e: e.matmul(self.ps[bO][:, h * 65:(h + 1) * 65], lhsT=Pc[q][:, h * 128:(h + 1) * 128],
                                                  rhs=swV[s][:, c, kvh * 65:(kvh + 1) * 65], start=False, stop=False, skip_group_check=True),
                         reads=[t_Pc[q], t_swV[s]], writes=[self.t_ps[bO]])
                    p.op("pe", lambda e: e.matmul(self.ps[bO][:, h * 65:(h + 1) * 65], lhsT=Pp[q][:, h * 128:(h + 1) * 128],
                                                  rhs=swV[s][:, c + 1, kvh * 65:(kvh + 1) * 65], start=False, stop=True, skip_group_check=True),
                         reads=[t_Pp[q], t_swV[s]], writes=[self.t_ps[bO]])
                p.op("dve", lambda e: e.tensor_tensor(out=small[:, 0:4], in0=self.ps[bO][:, 0:260].rearrange("p (h c) -> p h c", h=4)[:, :, 64],
                                                      in1=esink[:], op=ALU.add), reads=[self.t_ps[bO], t_es], writes=[t_small])
                p.op("dve", lambda e: e.reciprocal(out=small[:, 4:8], in_=small[:, 0:4]), reads=[t_small], writes=[t_small])
                rec4 = bass.AP(small, 4, [[64, 128], [1, 4], [0, 64]])
                p.op("dve", lambda e: e.tensor_tensor(out=yt[ys][:, c, :].rearrange("p (h c) -> p h c", h=4),
                                                      in0=self.ps[bO][:, 0:260].rearrange("p (h c) -> p h c", h=4)[:, :, 0:64], in1=rec4, op=ALU.mult),
                     reads=[self.t_ps[bO], t_small], writes=[t_yt[ys]])
                if c == 3:
                    p.dma("pool", dap(self.dh["sc_y"], lg * 4 * 128 * D + 0, [[D, 128], [128 * D, 4], [1, 256]]), yt[ys][:], t_yt[ys], reads=[t_yt[ys]])

            def conv_compute(lg):
                s = lg % 2
                p.dma("sp", ut[s][:, :, 2:514], dap(self.dh["sc_u"], lg * 512, [[NTL, 128], [128 * NTL, 2], [1, 512]]), t_ut[s], writes=[t_ut[s]])
                p.dma("sp", bbt[s][:], dap(self.dh["sc_bb"], lg * 512, [[NTL, 128], [128 * NTL, 2], [1, 512]]), t_bbt[s], writes=[t_bbt[s]])
                if FUSED:
                    p.op("pool", lambda e: e.tensor_tensor(out=ut[s][:, :, 0:2], in0=utl[:, :, lg, 0:2], in1=utl[:, :, lg, 2:4], op=ALU.add),
                         reads=[t_utl], writes=[t_ut[s]])
                else:
                    p.dma("sp", ut[s][:, :, 0:2], dap(self.dh["utail"], lg * 2, [[NGL * 2, 128], [128 * NGL * 2, 2], [1, 2]]), t_ut[s], writes=[t_ut[s]])
                for ct in range(2):
                    p.op("dve", lambda e, s=s, ct=ct: e.tensor_scalar(out=cvt[s][:, ct, :], in0=ut[s][:, ct, 0:512], scalar1=cwb[:, ct * 3:ct * 3 + 1], scalar2=None, op0=ALU.mult),
                         reads=[t_ut[s], t_cwb], writes=[t_cvt[s]])
                    for kk in (1, 2):
                        p.op("dve", lambda e, s=s, ct=ct, kk=kk: e.scalar_tensor_tensor(out=cvt[s][:, ct, :], in0=ut[s][:, ct, kk:kk + 512], scalar=cwb[:, ct * 3 + kk:ct * 3 + kk + 1],
                                                                                       in1=cvt[s][:, ct, :], op0=ALU.mult, op1=ALU.add),
                             reads=[t_ut[s], t_cwb, t_cvt[s]], writes=[t_cvt[s]])
                    p.op("dve", lambda e, s=s, ct=ct: e.scalar_tensor_tensor(out=cvt[s][:, ct, :], in0=cvt[s][:, ct, :], scalar=cwb[:, 6 + ct:7 + ct], in1=bbt[s][:, ct, :],
                                                                            op0=ALU.add, op1=ALU.mult),
                         reads=[t_cvt[s], t_cwb, t_bbt[s]], writes=[t_cvt[s]])

            def conv_out(lg):
                s = lg % 2
                ys2 = self.rot("B_yt", 3)
                for c in range(4):
                    bT = 4 + (c % 2)
                    blk = 3 - c
                    for ct in range(2):
                        p.op("pe", lambda e, s=s, ct=ct, blk=blk, bT=bT: e.transpose(out=self.ps[bT][:, ct * 128:(ct + 1) * 128], in_=cvt[s][:, ct, blk * 128:(blk + 1) * 128],
                                                                                    identity=self.ident_f),
                             reads=[t_cvt[s], cf_], writes=[self.t_ps[bT]])
                    p.op("act", lambda e, bT=bT, c=c, ys2=ys2: e.copy(out=yt[ys2][:, c, :], in_=self.ps[bT][:, 0:256]), reads=[self.t_ps[bT]], writes=[t_yt[ys2]])
                p.dma("pool", dap(self.dh["sc_y"], lg * 4 * 128 * D + 256, [[D, 128], [128 * D, 4], [1, 256]]), yt[ys2][:], t_yt[ys2], reads=[t_yt[ys2]])


            items = [(lg, c) for lg in range(NGL) for c in range(4)]
            swa_load(0)
            conv_compute(0)
            swa_A(0, 0, 0)
            for idx, (lg, c) in enumerate(items):
                if c == 0 and lg + 1 < NGL:
                    conv_compute(lg + 1)
                if idx + 1 < len(items):
                    nlg, nc_ = items[idx + 1]
                    if nc_ == 0:
                        swa_load(nlg)
                    swa_A(nlg, nc_, idx + 1)
                swa_B(lg, c, idx)
                if c == 3:
                    conv_out(lg)

            p.barrier()
            es_sw.close()
            tile = _tile_outer

            NS = 3
            yb = [tile("yb%d" % i, [128, D], F32) for i in range(NS)]; t_yb = [Tk() for _ in range(NS)]
            gb_ = [tile("gb%d" % i, [128, D], BF16) for i in range(NS)]; t_gb = [Tk() for _ in range(NS)]
            xb = [tile("xb%d" % i, [128, D], F32) for i in range(NS)]; t_xb = [Tk() for _ in range(NS)]
            ob = [tile("ob%d" % i, [128, D], F32) for i in range(2)]; t_ob = [Tk(), Tk()]
            junk = tile("junkB", [128, D], F32); t_junk = Tk()
            junk2 = tile("junkB2", [128, D], F32); t_junk2 = Tk()
            ygT = [tile("ygT%d" % i, [128, 8, 128], BF16) for i in range(2)]; t_ygT = [Tk(), Tk()]
            sm = [tile("sm%d" % i, [128, 16], F32) for i in range(NS)]; t_sm = [Tk() for _ in range(NS)]
            sm2 = [tile("sm2_%d" % i, [128, 8], F32) for i in range(2)]; t_sm2 = [Tk(), Tk()]

            def rows_of(sl):
                lg, c = sl // 4, sl % 4
                lb = lg * 4 + (3 - c)
                return slice(sl * 128, (sl + 1) * 128), slice(lb * 128, (lb + 1) * 128)

            def S1a(sl):
                s = sl % NS
                srow, rows = rows_of(sl)
                p.dma("sp", yb[s][:], dr["sc_y"][srow, :], t_yb[s], writes=[t_yb[s]])
                p.dma("sp", gb_[s][:], dr["sc_gate"][srow, :], t_gb[s], writes=[t_gb[s]])
                p.dma("sp", xb[s][:], x_src[rows, :], t_xb[s], writes=[t_xb[s]])
                for g in range(4):
                    p.op("act", lambda e: e.activation(out=junk[:, g * 256:(g + 1) * 256], in_=yb[s][:, g * 256:(g + 1) * 256], func=AF.Square,
                                                       accum_out=sm[s][:, g:g + 1]),
                         reads=[t_yb[s]], writes=[t_junk, t_sm[s]])
                p.op("act", lambda e: e.activation(out=sm[s][:, 4:8], in_=sm[s][:, 0:4], func=AF.Sqrt, bias=self.eps_c, scale=float(1.0 / 256)),
                     reads=[t_sm[s], cf_], writes=[t_sm[s]])
                p.op("dve", lambda e: e.reciprocal(out=sm[s][:, 8:12], in_=sm[s][:, 4:8]), reads=[t_sm[s]], writes=[t_sm[s]])
                for g in range(4):
                    p.op("dve", lambda e: e.scalar_tensor_tensor(out=yb[s][:, g * 256:(g + 1) * 256], in0=yb[s][:, g * 256:(g + 1) * 256], scalar=sm[s][:, 8 + g:9 + g],
                                                                 in1=ggrp_bc[:, g * 256:(g + 1) * 256], op0=ALU.mult, op1=ALU.mult),
                         reads=[t_yb[s], t_sm[s], t_gg], writes=[t_yb[s]])
                p.op("pool", lambda e: e.tensor_tensor(out=yb[s][:], in0=yb[s][:], in1=gb_[s][:], op=ALU.mult), reads=[t_yb[s], t_gb[s]], writes=[t_yb[s]])

            def S1b(sl):
                s = sl % NS
                u = sl % 2
                for half in range(2):
                    for i in range(4):
                        kc = half * 4 + i
                        p.op("pe", lambda e: e.transpose(out=self.ps[half][:, i * 128:(i + 1) * 128], in_=yb[s][:, kc * 128:(kc + 1) * 128],
                                                         identity=self.ident_f), reads=[t_yb[s], cf_], writes=[self.t_ps[half]])
                    if half == 0:
                        p.op("act", lambda e: e.copy(out=ygT[u][:, 0:4, :], in_=self.ps[0][:].rearrange("p (a b) -> p a b", a=4)),
                             reads=[self.t_ps[0]], writes=[t_ygT[u]])
                    else:
                        p.op("dve", lambda e: e.tensor_copy(out=ygT[u][:, 4:8, :], in_=self.ps[1][:].rearrange("p (a b) -> p a b", a=4)),
                             reads=[self.t_ps[1]], writes=[t_ygT[u]])

            def S2(sl):
                s = sl % NS
                u = sl % 2
                srow, rows = rows_of(sl)
                bo = (2 + 2 * u, 3 + 2 * u)
                for half in range(2):
                    b = bo[half]
                    for kc in range(8):
                        p.op("pe", lambda e: e.matmul(self.ps[b][:], lhsT=ygT[u][:, kc, :], rhs=Wo[:, kc, half * 512:(half + 1) * 512],
                                                      start=(kc == 0), stop=(kc == 7)),
                             reads=[t_ygT[u], t_Wo], writes=[self.t_ps[b]])
                    p.op("act", lambda e: e.activation(out=junk2[:, half * 512:(half + 1) * 512], in_=self.ps[b][:], func=AF.Square,
                                                       accum_out=sm2[u][:, half:half + 1]),
                         reads=[self.t_ps[b]], writes=[t_junk2, t_sm2[u]])
                p.op("dve", lambda e: e.tensor_tensor(out=sm2[u][:, 2:3], in0=sm2[u][:, 0:1], in1=sm2[u][:, 1:2], op=ALU.add), reads=[t_sm2[u]], writes=[t_sm2[u]])
                p.op("act", lambda e: e.activation(out=sm2[u][:, 3:4], in_=sm2[u][:, 2:3], func=AF.Sqrt, bias=self.eps_c, scale=float(1.0 / D)),
                     reads=[t_sm2[u], cf_], writes=[t_sm2[u]])
                p.op("dve", lambda e: e.reciprocal(out=sm2[u][:, 4:5], in_=sm2[u][:, 3:4]), reads=[t_sm2[u]], writes=[t_sm2[u]])
                for half in range(2):
                    b = bo[half]
                    p.op("dve", lambda e: e.scalar_tensor_tensor(out=ob[u][:, half * 512:(half + 1) * 512], in0=self.ps[b][:], scalar=sm2[u][:, 4:5],
                                                                 in1=gpost_bc[:, half * 512:(half + 1) * 512], op0=ALU.mult, op1=ALU.mult),
                         reads=[self.t_ps[b], t_sm2[u], t_gp], writes=[t_ob[u]])
                p.op("pool", lambda e: e.tensor_tensor(out=ob[u][:], in0=ob[u][:], in1=xb[s][:], op=ALU.add), reads=[t_ob[u], t_xb[s]], writes=[t_ob[u]])
                p.dma("pool", x_dst[rows, :], ob[u][:], t_ob[u], reads=[t_ob[u]])

            S1a(0)
            if NBL > 1:
                S1a(1)
            S1b(0)
            for sl in range(NBL):
                if sl + 2 < NBL:
                    S1a(sl + 2)
                if sl + 1 < NBL:
                    S1b(sl + 1)
                S2(sl)
            p.barrier()

    def build(self):
        self.declare()
        self.common()
        p = self.p
        self.zero4_t = p.tile("zero4", [128, 512], BF16)
        self.zero4 = self.zero4_t[:]
        p.op("pool", lambda e: e.memset(self.zero4_t[:], 0.0), reads=[self.t_cb], writes=[self.t_cb])
        dr = self.dram
        if self.mode == "A":
            self.phaseA(0, dr["x"])
        elif self.mode == "B":
            self.phaseB(0, dr["x"], dr["out"])
        else:
            NTL, NGL = self.NTL, self.NGL
            self.tabt = p.tile("tabt", [1, 128], I32)
            self.t_tab = Tk()
            p.dma("sp", self.tabt[:], dr["tab"], self.t_tab, writes=[self.t_tab])
            self.zt = p.tile("zt", [128, 512], BF16)
            self.t_zt = Tk()
            p.op("pool", lambda e: e.memset(self.zt[:], 0.0), writes=[self.t_zt])
            NG = self.NG
            groups = [[b * NR + r for r in range(NR)] for b in range(NB_BATCH)]
            import os
            KF = os.environ.get("KFSTOP", "")
            for l in range(DEPTH):
                if KF and l >= int(KF[0]):
                    break
                x_src = dr["x"] if l == 0 else dr["sc_x1"]
                x_dst = dr["out"] if l == DEPTH - 1 else dr["sc_x1"]
                p.phase_begin()
                self.phaseA(l, x_src)
                p.barrier()
                p.phase_end()
                if KF.endswith("a"):
                    break
                p.barrier()
                if KF.endswith("g"):
                    break
                p.phase_begin()
                self.phaseB(l, x_src, x_dst)
                p.barrier()
                p.phase_end()
        p.barrier()
        p.emit()
        self.es.close()
        return self.nc


def make_consts():
    cf = np.zeros((128, 640), np.float32)
    cf[:, 0:128] = np.eye(128, dtype=np.float32)
    cf[:, 128:256] = 1.0
    cf[:, 256] = EPS
    cf[:, 257] = 1.0
    half = 16
    freqs = (10000.0 ** (-np.arange(half, dtype=np.float32) / half)).astype(np.float32)
    pidx = np.arange(128)
    cf[:, 258] = freqs[(pidx % 32) % 16]
    cf[:, 259] = np.where((pidx % 32) < 16, -1.0, 1.0)
    cb = np.zeros((128, 896), np.float32)
    cb[:, 768:896] = 1.0
    j = np.arange(128)[:, None]
    i = np.arange(128)[None, :]
    cb[:, 0:128] = -1.0 * (j >= i)
    cb[:, 128:256] = (j < i)
    cb[:, 256:384] = (j <= i)
    cb[:, 384:512] = (j > i)
    for m in (64, 65, 96, 97):
        cb[:, 512 + m] = 1.0
    return cf, cb.astype(ml_dtypes.bfloat16)


def f_order_rows(ngl):
    idx = []
    for lg in range(ngl):
        for c in range(4):
            b = lg * 4 + (3 - c)
            idx.extend(range(b * 128, (b + 1) * 128))
    return np.array(idx)


def prep_weights(inp):
    w = {}
    w["gpre_t"] = np.ascontiguousarray(inp["norm_pre"].reshape(DEPTH, 8, 128).transpose(0, 2, 1))
    w_in = inp["w_in"]
    sw = np.concatenate([w_in[:, :, C_CKR + 16:C_CKR + 32], w_in[:, :, C_CKR:C_CKR + 16]], axis=2)
    w["w_in_x"] = np.ascontiguousarray(np.concatenate([w_in, sw], axis=2))
    w["gcq_t"] = np.ascontiguousarray(inp["mla_q_norm"].reshape(DEPTH, 2, 128).transpose(0, 2, 1))
    uq = inp["mla_w_uq"].reshape(DEPTH, 256, 4, 96)
    nope = uq[..., 0:64].reshape(DEPTH, 256, 256)
    rope = uq[..., 64:96]
    rope_sw = np.concatenate([rope[..., 16:32], rope[..., 0:16]], axis=-1)
    w["w_uq_x"] = np.ascontiguousarray(np.concatenate([nope, rope.reshape(DEPTH, 256, 128), rope_sw.reshape(DEPTH, 256, 128)], axis=2))
    w["gkv_t"] = np.ascontiguousarray(inp["mla_kv_norm"].reshape(DEPTH, 1, 128).transpose(0, 2, 1))
    ukv = inp["mla_w_ukv"].reshape(DEPTH, 128, 4, 128)
    w["w_ukv_x"] = np.ascontiguousarray(np.concatenate([ukv[..., 0:64].reshape(DEPTH, 128, 256), ukv[..., 64:128].reshape(DEPTH, 128, 256)], axis=2))
    w["sinks"] = np.ascontiguousarray(inp["attn_sinks"])
    w["convw_t"] = np.ascontiguousarray(inp["conv_w"].reshape(DEPTH, 3, 2, 128).transpose(0, 3, 2, 1).reshape(DEPTH, 128, 6))
    w["convb_t"] = np.ascontiguousarray(inp["conv_b"].reshape(DEPTH, 2, 128).transpose(0, 2, 1))
    w["ggrp"] = np.ascontiguousarray(inp["group_norm"])
    w["w_out"] = np.ascontiguousarray(inp["w_out"])
    w["gpost"] = np.ascontiguousarray(inp["norm_post"])
    return {k: np.asarray(v, np.float32) for k, v in w.items()}


_CACHE = {}


def get_prog(mode, S):
    key = (mode, S)
    if key not in _CACHE:
        _CACHE[key] = Builder(mode, S).build()
    return _CACHE[key]


A_WKEYS = ("gpre_t", "w_in_x", "gcq_t", "w_uq_x", "gkv_t", "w_ukv_x")
B_WKEYS = ("sinks", "convw_t", "convb_t", "ggrp", "w_out", "gpost")
SC_KEYS = ("sc_sbq", "sc_mlaq", "sc_swaq", "sc_u", "sc_bb", "sc_gate")


def arrange_B(resA, S):
    NG = S // 512
    NGL = NG // NR
    NTL = NGL * 512
    NBK = S // 128
    NPOS = NBK + 12
    outs = []
    for core in range(NB_BATCH * NR):
        b, r = divmod(core, NR)
        sbK = np.zeros((256, NPOS * 128), ml_dtypes.bfloat16)
        mlaK = np.zeros((384, NPOS * 128), ml_dtypes.bfloat16)
        sbV = np.zeros((NPOS * 128, 256), ml_dtypes.bfloat16)
        mlaV = np.zeros((NPOS * 128, 260), ml_dtypes.bfloat16)
        for qg in range(NPOS // 4):
            G = NG - 1 + r - qg
            if 0 <= G < NG:
                src = resA[b * NR + (G % NR)]
                lgs = G // NR
                kv = src["kvb_loc"]
                flat = kv.reshape(-1)
                sbK[:, qg * 512:(qg + 1) * 512] = kv[R_SBK:R_SBK + 256, lgs * 512:(lgs + 1) * 512]
                mlaK[:, qg * 512:(qg + 1) * 512] = kv[R_MLAK:R_MLAK + 384, lgs * 512:(lgs + 1) * 512]
                o = R_SBV * NTL + lgs * 512 * 256
                sbV[qg * 512:(qg + 1) * 512, :] = flat[o:o + 512 * 256].reshape(512, 256)
                o = R_MLAV * NTL + lgs * 512 * 260
                mlaV[qg * 512:(qg + 1) * 512, :] = flat[o:o + 512 * 260].reshape(512, 260)
        swK = np.zeros((128, NGL, 640), ml_dtypes.bfloat16)
        swV = np.zeros((NGL, 5, 128, 130), ml_dtypes.bfloat16)
        utail = np.zeros((256, NGL, 2), np.float32)
        own = resA[core]
        flat_own = own["kvb_loc"].reshape(-1)
        for lg in range(NGL):
            G = NR * lg + r
            swK[:, lg, 0:512] = own["kvb_loc"][R_SWAK:R_SWAK + 128, lg * 512:(lg + 1) * 512]
            o = R_SWAV * NTL + lg * 512 * 130
            swV[lg, 0:4] = flat_own[o:o + 512 * 130].reshape(4, 128, 130)
            if G > 0:
                src = resA[b * NR + ((G - 1) % NR)]
                lgs = (G - 1) // NR
                swK[:, lg, 512:640] = src["kvb_loc"][R_SWAK:R_SWAK + 128, lgs * 512:lgs * 512 + 128]
                o = R_SWAV * NTL + lgs * 512 * 130
                swV[lg, 4] = src["kvb_loc"].reshape(-1)[o:o + 128 * 130].reshape(128, 130)
                utail[:, lg, :] = src["kvf_loc"].reshape(256, NGL, 2)[:, lgs, :]
        outs.append(dict(sbK=sbK, sbV=sbV, mlaK=mlaK, mlaV=mlaV, swK=swK.reshape(128, NGL * 640),
                         swV=swV.reshape(NGL * 5 * 128, 130), utail=utail.reshape(256, NGL * 2)))
    return outs


def run_forward(inp, S):
    NG = S // 512
    NGL = NG // NR
    NTL = NGL * 512
    ncores = NB_BATCH * NR
    x = np.asarray(inp["x"], np.float32)
    pos = np.asarray(inp["positions"], np.int32)
    W = prep_weights(inp)
    cf, cb = make_consts()
    ford = f_order_rows(NGL)
    tok = []
    for core in range(ncores):
        b, r = divmod(core, NR)
        t = np.concatenate([np.arange((NR * lg + r) * 512, (NR * lg + r + 1) * 512) for lg in range(NGL)])
        tok.append((b, t))
    xs = [np.ascontiguousarray(x[b][t]) for (b, t) in tok]
    ps = [np.ascontiguousarray(pos[b][t][ford][None, :]) for (b, t) in tok]
    progA = get_prog("A", S)
    progB = get_prog("B", S)
    for l in range(DEPTH):
        wa = {k: W[k][l:l + 1] for k in A_WKEYS}
        wb = {k: W[k][l:l + 1] for k in B_WKEYS}
        inA = [dict(consts_f=cf, consts_b=cb, pos=ps[c], x=xs[c], **wa) for c in range(ncores)]
        resA = run_bass_kernel_spmd(progA, inA, core_ids=list(range(ncores))).results
        arr = arrange_B(resA, S)
        inB = []
        for c in range(ncores):
            d = dict(consts_f=cf, consts_b=cb, x=xs[c], **wb)
            for k in SC_KEYS:
                d[k] = resA[c][k]
            d.update(arr[c])
            inB.append(d)
        resB = run_bass_kernel_spmd(progB, inB, core_ids=list(range(ncores))).results
        xs = [np.asarray(resB[c]["out"], np.float32) for c in range(ncores)]
    out = np.zeros_like(x)
    for c, (b, t) in enumerate(tok):
        out[b][t] = xs[c]
    return out


def make_table(r, S):
    tab = np.zeros((1, 128), np.int32)
    Y0, Y1 = HR[0] * 512, HR[1] * 512
    for h in range(4):
        tab[0, 0 + h] = r * Y0 + (0 + h * 64) * 512
        tab[0, 4 + h] = r * Y0 + 256 * 512 + h * 128 * 256
        tab[0, 8 + h] = r * Y1 + (0 + h * 96) * 512
        tab[0, 12 + h] = r * Y1 + 384 * 512 + h * 128 * 260
    for kvh in range(2):
        tab[0, 16 + kvh] = (2 + r) * Y0 + (512 + kvh * 64) * 512
        tab[0, 19 + kvh] = (2 + r) * Y0 + 770 * 512 + kvh * 128 * 4
    tab[0, 18] = (2 + r) * Y0 + 640 * 512
    return tab


def run_fused(inp, S):
    NG = S // 512
    NGL = NG // NR
    ncores = NB_BATCH * NR
    x = np.asarray(inp["x"], np.float32)
    pos = np.asarray(inp["positions"], np.int32)
    W = prep_weights(inp)
    cf, cb = make_consts()
    ford = f_order_rows(NGL)
    kaug = np.zeros((64, 512), ml_dtypes.bfloat16)
    kaug[0, :] = 1.0
    kaug[32, :] = 1.0
    in_maps = []
    tok = []
    for core in range(ncores):
        b, r = divmod(core, NR)
        t = np.concatenate([np.arange((NR * lg + r) * 512, (NR * lg + r + 1) * 512) for lg in range(NGL)])
        tok.append((b, t))
        d = dict(consts_f=cf, consts_b=cb, kaug=kaug, x=np.ascontiguousarray(x[b][t]),
                 pos=np.ascontiguousarray(pos[b][t][ford][None, :]), tab=make_table(r, S))
        d.update(W)
        in_maps.append(d)
    res = run_bass_kernel_spmd(get_prog("F", S), in_maps, core_ids=list(range(ncores))).results
    out = np.zeros_like(x)
    for c, (b, t) in enumerate(tok):
        out[b][t] = np.asarray(res[c]["out"], np.float32)
    return out


def kernel(**inputs):
    inp = {k: np.asarray(v) for k, v in inputs.items()}
    S = inp["x"].shape[1]
    return run_fused(inp, S)
```

```python
import math
from contextlib import ExitStack

import numpy as np
import ml_dtypes

import concourse.bass as bass
import concourse.mybir as mybir
from concourse.bass_utils import run_bass_kernel_spmd

F32 = mybir.dt.float32
BF16 = mybir.dt.bfloat16
I32 = mybir.dt.int32
AF = mybir.ActivationFunctionType
ALU = mybir.AluOpType

D = 1024
DIN = 3488
DINX = 3520
DEPTH = 2
NB_BATCH = 2
NR = 4
EPS = 1e-6
KVROWS = 1416
C_AQ, C_AK, C_AV = 0, 256, 384
C_BB, C_BC, C_BX = 512, 768, 1024
C_CQ, C_CKV, C_CKR = 1280, 1536, 1664
C_DQ, C_DK, C_DV = 1696, 1952, 2208
C_GATE = 2464
C_KRSW = 3488
R_SBK, R_MLAK, R_SWAK, R_SBV, R_MLAV, R_SWAV, R_UT = 0, 256, 640, 768, 1024, 1284, 1414


HR = (772, 644)


def kvmap(oldrow):
    if oldrow < 256:
        return 0, oldrow
    if oldrow < 640:
        return 1, oldrow - 256
    if oldrow < 768:
        return 0, 512 + oldrow - 640
    if oldrow < 1024:
        return 0, 256 + oldrow - 768
    if oldrow < 1284:
        return 1, 384 + oldrow - 1024
    if oldrow < 1414:
        return 0, 640 + oldrow - 1284
    return 0, 770 + oldrow - 1414


def own_groups(r, ngl):
    return [r + NR * i for i in range(ngl)]


class Tk:
    __slots__ = ("w", "r", "dsem", "dcnt", "name", "excl")

    def __init__(self, name="", excl=False):
        self.excl = excl
        self.w = None
        self.r = {}
        self.dsem = None
        self.dcnt = 0
        self.name = name


class _Rec:
    def __init__(self):
        self.call = None

    def __getattr__(self, name):
        def f(*a, **kw):
            self.call = (name, a, kw)
            return self
        return f


class Prog:
    ENG = ("pe", "act", "dve", "pool", "sp")

    def __init__(self, nc, es):
        self.nc = nc
        self.es = es
        self.streams = {e: [] for e in self.ENG}
        self.seen = {e: {} for e in self.ENG}
        self.sems = {}
        self.cnt = {}
        self.nsem = 0
        self.all_tickets = {}
        for e in ("pe", "act", "dve", "pool"):
            self._newsem(e)

    def _newsem(self, key):
        s = self.es.enter_context(self.nc.semaphore("s%d" % self.nsem))
        self.nsem += 1
        self.sems[key] = s
        self.cnt[key] = 0
        return s

    def _dsem(self, st, q):
        cls = "sw" if q == "pool" else "hw"
        if st.dsem is not None:
            assert st.dsem[0] == cls, "tile %s used by both DMA classes" % st.name
            return
        free = getattr(self, "free_dsems", {}).get(cls)
        if free:
            st.dsem = free.pop()
        else:
            st.dsem = (cls, self.nsem)
            self._newsem(st.dsem)
        if getattr(self, "phase_dsems", None) is not None:
            self.phase_dsems.append(st.dsem)

    def phase_begin(self):
        self.phase_dsems = []

    def phase_end(self):
        if not hasattr(self, "free_dsems"):
            self.free_dsems = {"sw": [], "hw": []}
        for k in self.phase_dsems:
            self.free_dsems[k[0]].append(k)
        self.phase_dsems = None

    def tile(self, name, shape, dt):
        return self.es.enter_context(self.nc.sbuf_tensor(name, list(shape), dt))

    def psum(self, name):
        return self.es.enter_context(self.nc.psum_tensor(name, [128, 512], F32))

    def _collect(self, eng, reads, writes):
        deps = {}

        def add(tk):
            if tk is None:
                return
            k, v = tk
            if deps.get(k, 0) < v:
                deps[k] = v

        for t in reads:
            add(t.w)
            if t.excl:
                for k, v in t.r.items():
                    if k != eng:
                        add((k, v))
        for t in writes:
            add(t.w)
            for k, v in t.r.items():
                add((k, v))
        waits = []
        seen = self.seen[eng]
        for k, v in deps.items():
            if k == eng and eng == "pe":
                continue
            if seen.get(k, 0) >= v:
                continue
            seen[k] = v
            waits.append((self.sems[k], v))
        return waits

    def _record(self, tk, reads, writes):
        k, v = tk
        for t in reads:
            if t.r.get(k, 0) < v:
                t.r[k] = v
        for t in writes:
            t.w = tk
            t.r = {}
        self.all_tickets[k] = v

    def op(self, eng, fn, reads=(), writes=()):
        waits = self._collect(eng, reads, writes)
        self.cnt[eng] += 1
        tk = (eng, self.cnt[eng])
        rec = _Rec()
        fn(rec)
        name, a, kw = rec.call
        self.streams[eng].append((waits, (lambda e, name=name, a=a, kw=kw: getattr(e, name)(*a, **kw)), (self.sems[eng], 1)))
        self._record(tk, reads, writes)
        return tk

    def dma(self, q, out, in_, st, reads=(), writes=()):
        self._dsem(st, q)
        waits = self._collect(q, reads, writes)
        self.cnt[st.dsem] += 16
        tk = (st.dsem, self.cnt[st.dsem])
        self.streams[q].append((waits, lambda e, o=out, i=in_: e.dma_start(out=o, in_=i),
                                (self.sems[st.dsem], 16)))
        self._record(tk, reads, writes)
        return tk

    def dyn_dma(self, q, out, handle, tab_ap, dims, st, reads=(), writes=()):
        self._dsem(st, q)
        if not hasattr(self, "regs"):
            self.regs = {}
            self.regi = {}
        if q not in self.regs:
            eng = {"sp": self.nc.sync, "act": self.nc.scalar, "pool": self.nc.gpsimd}[q]
            self.regs[q] = [self.es.enter_context(eng.register("dr%s%d" % (q, i))) for i in range(2)]
            self.regi[q] = 0
        reg = self.regs[q][self.regi[q] % 2]
        self.regi[q] += 1
        waits = self._collect(q, reads, writes)
        self.streams[q].append((waits, lambda e, reg=reg, t=tab_ap: e.reg_load(reg, t), None))
        self.cnt[st.dsem] += 16
        tk = (st.dsem, self.cnt[st.dsem])
        src = bass.AP(handle, reg, [list(d) for d in dims])
        self.streams[q].append(([], lambda e, o=out, i=src: e.dma_start(out=o, in_=i), (self.sems[st.dsem], 16)))
        self._record(tk, reads, writes)
        return tk

    def raw(self, eng, fn, key, reads=(), writes=()):
        if key not in self.sems:
            self._newsem(key)
        waits = self._collect(eng, reads, writes)
        self.cnt[key] += 1
        tk = (key, self.cnt[key])
        self.streams[eng].append((waits, fn, (self.sems[key], 1)))
        self._record(tk, reads, writes)
        return tk

    def barrier(self):
        for e in self.ENG:
            waits = []
            for k, v in self.all_tickets.items():
                if self.seen[e].get(k, 0) < v:
                    self.seen[e][k] = v
                    waits.append((self.sems[k], v))
            if waits:
                self.streams[e].append((waits, None, None))

    def emit(self):
        nc = self.nc

        def run(e, lst):
            for waits, fn, inc in lst:
                for s, v in waits:
                    e.wait_ge(s, v)
                if fn is not None:
                    ins = fn(e)
                    if inc is not None:
                        ins.then_inc(inc[0], inc[1])

        with nc.Block() as block:
            @block.tensor
            def _(e):
                run(e, self.streams["pe"])

            @block.scalar
            def _(e):
                run(e, self.streams["act"])

            @block.vector
            def _(e):
                run(e, self.streams["dve"])

            @block.gpsimd
            def _(e):
                run(e, self.streams["pool"])

            @block.sync
            def _(e):
                run(e, self.streams["sp"])


def dap(h, off, dims):
    return bass.AP(h, off, [list(d) for d in dims])


class Builder:
    def __init__(self, mode, S):
        self.mode = mode
        self.S = S
        self.NG = S // 512
        self.NGL = self.NG // NR
        self.NTL = self.NGL * 512
        self.NBL = self.NGL * 4
        self.nc = bass.Bass("TRN2", target_bir_lowering=False)
        self.es = ExitStack()
        self.p = Prog(self.nc, self.es)
        self.dram = {}
        self.dh = {}
        self.rr = {}

    def din(self, name, shape, dt):
        self.dh[name] = self.nc.dram_tensor(name, list(shape), dt, kind="ExternalInput")
        self.dram[name] = self.dh[name].ap()

    def dout(self, name, shape, dt):
        self.dh[name] = self.nc.dram_tensor(name, list(shape), dt, kind="ExternalOutput")
        self.dram[name] = self.dh[name].ap()

    def dint(self, name, shape, dt):
        self.dh[name] = self.nc.dram_tensor(name, list(shape), dt)
        self.dram[name] = self.dh[name].ap()

    def declare(self):
        mode, NTL, NBL, NGL = self.mode, self.NTL, self.NBL, self.NGL
        L = DEPTH if mode == 'F' else 1
        self.din("consts_f", [128, 640], F32)
        self.din("consts_b", [128, 896], BF16)
        self.din("kaug", [64, 512], BF16)
        if mode in ("A", "F"):
            self.din("pos", [1, NTL], I32)
            self.din("gpre_t", [L, 128, 8], F32)
            self.din("w_in_x", [L, D, DINX], F32)
            self.din("gcq_t", [L, 128, 2], F32)
            self.din("w_uq_x", [L, 256, 512], F32)
            self.din("gkv_t", [L, 128, 1], F32)
            self.din("w_ukv_x", [L, 128, 512], F32)
        if mode in ("B", "F"):
            self.din("sinks", [L, 4], F32)
            self.din("convw_t", [L, 128, 6], F32)
            self.din("convb_t", [L, 128, 2], F32)
            self.din("ggrp", [L, D], F32)
            self.din("w_out", [L, D, D], F32)
            self.din("gpost", [L, D], F32)
        self.din("x", [NTL, D], F32)
        scr = [("sc_sbq", [256, NTL], BF16), ("sc_mlaq", [384, NTL], BF16), ("sc_swaq", [256, NTL], BF16),
               ("sc_u", [256, NTL], F32), ("sc_bb", [256, NTL], F32), ("sc_gate", [NBL * 128, D], BF16)]
        if mode == "A":
            for n, s, d in scr:
                self.dout(n, s, d)
            self.dout("kvb_loc", [KVROWS, NTL], BF16)
            self.dout("kvf_loc", [256, NGL * 2], F32)
        elif mode == "B":
            for n, s, d in scr:
                self.din(n, s, d)
            NPOS = self.S // 128 + 12
            self.din("sbK", [256, NPOS * 128], BF16)
            self.din("sbV", [NPOS * 128, 256], BF16)
            self.din("mlaK", [384, NPOS * 128], BF16)
            self.din("mlaV", [NPOS * 128, 260], BF16)
            self.din("swK", [128, NGL * 640], BF16)
            self.din("swV", [NGL * 5 * 128, 130], BF16)
            self.din("utail", [256, NGL * 2], F32)
            self.dint("sc_y", [NBL * 128, D], F32)
            self.dout("out", [NTL, D], F32)
        else:
            for n, s, d in scr:
                self.dint(n, s, d)
            for f in range(2):
                self.dint("kvb_loc%d" % f, [NGL * HR[f], 512], BF16)
                self.dint("kvb_all%d" % f, [(self.NG + 6) * HR[f], 512], BF16)
            self.dh["kvb_loc"] = None
            self.din("tab", [1, 128], I32)
            self.dint("sc_y", [NBL * 128, D], F32)
            self.dint("sc_x1", [NTL, D], F32)
            self.dint("sc_cos", [128, NTL], F32)
            self.dint("sc_sin", [128, NTL], F32)
            self.dout("out", [NTL, D], F32)

    def common(self):
        p = self.p
        self.cf = p.tile("cf", [128, 640], F32)
        self.cb = p.tile("cb", [128, 896], BF16)
        self.t_cf = Tk("cf")
        self.t_cb = Tk("cb")
        p.dma("sp", self.cf[:], self.dram["consts_f"], self.t_cf, writes=[self.t_cf])
        p.dma("sp", self.cb[:], self.dram["consts_b"], self.t_cb, writes=[self.t_cb])
        self.ident_f = self.cf[:, 0:128]
        self.ones_f = self.cf[:, 128:256]
        self.eps_c = self.cf[:, 256:257]
        self.one_c = self.cf[:, 257:258]
        self.freq_c = self.cf[:, 258:259]
        self.sgn_c = self.cf[:, 259:260]
        self.negtri = self.cb[:, 0:128]
        self.m_lt = self.cb[:, 128:256]
        self.m_le = self.cb[:, 256:384]
        self.m_gt = self.cb[:, 384:512]
        self.ones98 = self.cb[:, 512:610]
        self.zero_b = self.cb[:, 640:768]
        self.ones_b = self.cb[:, 768:896]
        self.pp = [self.es.enter_context(self.nc.psum_tensor("pp%d" % i, [128, 1024], F32)) for i in range(4)]
        self.ps = [self.pp[i // 2][:, (i % 2) * 512:(i % 2 + 1) * 512] for i in range(8)]
        self.t_ps = [Tk("ps%d" % i, excl=True) for i in range(8)]

    def rot(self, name, n):
        i = self.rr.get(name, 0)
        self.rr[name] = i + 1
        return i % n

    def phaseA(self, L, x_src):
        p, nc = self.p, self.nc
        NTL, NGL, NBL = self.NTL, self.NGL, self.NBL
        dr = self.dram
        cf_, cb_ = self.t_cf, self.t_cb
        with ExitStack() as es:
            def tile(name, shape, dt):
                return es.enter_context(nc.sbuf_tensor("A%d_%s" % (L, name), list(shape), dt))

            Wp = tile("Wp", [128, 8, DINX], BF16); t_Wp = Tk(); t_Wp2 = Tk()
            wuq = tile("wuq", [128, 2, 512], BF16); t_wuq = Tk()
            wukv = tile("wukv", [128, 512], BF16); t_wukv = Tk()
            gcols = tile("gcols", [128, 16], F32); t_gc = Tk()
            wst = [tile("wst%d" % i, [128, 1760], F32) for i in range(2)]; t_wst = [Tk() for _ in range(2)]
            cosT = tile("cosT", [128, NTL], F32); t_cos = Tk()
            sinS = tile("sinS", [128, NTL], F32); t_sin = Tk()
            posi = tile("posi", [128, NTL], I32); t_posi = Tk()
            ang = tile("ang", [128, NTL], F32); t_ang = Tk()
            tr1 = tile("tr1", [128, NTL], F32); t_tr1 = Tk()
            tr2 = tile("tr2", [128, NTL], F32); t_tr2 = Tk()
            tri = tile("tri", [128, NTL], I32); t_tri = Tk()
            xs = [tile("xs%d" % i, [128, D], F32) for i in range(2)]; t_xs = [Tk(), Tk()]
            xn = [tile("xn%d" % i, [128, D], F32) for i in range(2)]; t_xn = [Tk(), Tk()]
            junk = tile("junk", [128, D], F32); t_junk = Tk()
            sst = [tile("ss%d" % i, [128, 4], F32) for i in range(2)]; t_ss = [Tk(), Tk()]
            xnT = tile("xnT", [128, 8, 512], BF16); t_xnT = Tk()
            cqT = tile("cqT", [128, 2, 512], BF16); t_cqT = Tk()
            sq = [tile("sq%d" % i, [128, 512], BF16) for i in range(2)]; t_sq = [Tk(), Tk()]
            ckvT = tile("ckvT", [128, 512], BF16); t_ckvT = Tk()
            rq_bc = tile("rq_bc", [128, 512], F32); t_rq = Tk()
            rkv_bc = tile("rkv_bc", [128, 512], F32); t_rkv = Tk()
            rkv_t = tile("rkv_t", [128, 8], F32); t_rkvt = Tk()
            tmpf = [tile("tmpf%d" % i, [128, 512], F32) for i in range(3)]; t_tmpf = [Tk(), Tk(), Tk()]
            NSTB, NSTF = 9, 5
            stb = [tile("stb%d" % i, [128, 1024], BF16) for i in range(NSTB)]; t_stb = [Tk() for _ in range(NSTB)]
            stf = [tile("stf%d" % i, [128, 512], F32) for i in range(NSTF)]; t_stf = [Tk() for _ in range(NSTF)]

            p.dma("sp", gcols[:, 0:8], dr["gpre_t"][L], t_gc, writes=[t_gc])
            p.dma("sp", gcols[:, 8:10], dr["gcq_t"][L], t_gc, writes=[t_gc])
            p.dma("sp", gcols[:, 10:11], dr["gkv_t"][L], t_gc, writes=[t_gc])
            k = 0
            for kc in range(8):
                for hf in range(2):
                    s = k % 2; k += 1
                    p.dma("sp", wst[s][:], dr["w_in_x"][L, kc * 128:(kc + 1) * 128, hf * 1760:(hf + 1) * 1760],
                          t_wst[s], writes=[t_wst[s]])
                    if s % 2 == 0:
                        p.op("dve", lambda e, o=Wp[:, kc, hf * 1760:(hf + 1) * 1760], i=wst[s][:], sc=gcols[:, kc:kc + 1]:
                             e.tensor_scalar(out=o, in0=i, scalar1=sc, scalar2=None, op0=ALU.mult),
                             reads=[t_wst[s], t_gc], writes=[t_Wp])
                    else:
                        p.op("act", lambda e, o=Wp[:, kc, hf * 1760:(hf + 1) * 1760], i=wst[s][:], sc=gcols[:, kc:kc + 1]:
                             e.activation(out=o, in_=i, func=AF.Copy, scale=sc),
                             reads=[t_wst[s], t_gc], writes=[t_Wp2])
            for kc in range(2):
                s = k % 2; k += 1
                p.dma("sp", wst[s][:, 0:512], dr["w_uq_x"][L, kc * 128:(kc + 1) * 128, :], t_wst[s], writes=[t_wst[s]])
                p.op("dve", lambda e, o=wuq[:, kc, :], i=wst[s][:, 0:512], sc=gcols[:, 8 + kc:9 + kc]:
                     e.tensor_scalar(out=o, in0=i, scalar1=sc, scalar2=float(96 ** -0.5), op0=ALU.mult, op1=ALU.mult),
                     reads=[t_wst[s], t_gc], writes=[t_wuq])
            s = k % 2; k += 1
            p.dma("sp", wst[s][:, 0:512], dr["w_ukv_x"][L], t_wst[s], writes=[t_wst[s]])
            p.op("dve", lambda e, o=wukv[:], i=wst[s][:, 0:512], sc=gcols[:, 10:11]:
                 e.tensor_scalar(out=o, in0=i, scalar1=sc, scalar2=None, op0=ALU.mult),
                 reads=[t_wst[s], t_gc], writes=[t_wukv])

            import os
            KSTOP = float(os.environ.get("KSTOP", "99"))
            if KSTOP <= 1:
                p.barrier(); return
            REUSE_TABLES = (self.mode == "F" and L > 0)
            if REUSE_TABLES:
                p.dma("sp", cosT[:], dr["sc_cos"], t_cos, writes=[t_cos])
                p.dma("sp", sinS[:], dr["sc_sin"], t_sin, writes=[t_sin])
            else:
                p.dma("sp", posi[:], dap(self.dh["pos"], 0, [[0, 128], [1, NTL]]), t_posi, writes=[t_posi])
                p.op("dve", lambda e: e.tensor_copy(out=ang[:], in_=posi[:]), reads=[t_posi], writes=[t_ang])
                p.op("dve", lambda e: e.tensor_scalar(out=ang[:], in0=ang[:], scalar1=self.freq_c, scalar2=None, op0=ALU.mult),
                     reads=[t_ang, cf_], writes=[t_ang])
                TWO_PI = 2.0 * math.pi
                C_HI = 6.28125
                C_LO = TWO_PI - C_HI

                def sin_of(dst, t_dst, shift):
                    p.op("dve", lambda e: e.tensor_scalar(out=tr1[:], in0=ang[:], scalar1=float(shift), scalar2=float(1.0 / TWO_PI),
                                                          op0=ALU.add, op1=ALU.mult), reads=[t_ang], writes=[t_tr1])
                    p.op("dve", lambda e: e.tensor_copy(out=tri[:], in_=tr1[:]), reads=[t_tr1], writes=[t_tri])
                    p.op("dve", lambda e: e.tensor_copy(out=tr1[:], in_=tri[:]), reads=[t_tri], writes=[t_tr1])
                    p.op("dve", lambda e: e.tensor_scalar(out=tr2[:], in0=ang[:], scalar1=float(shift), scalar2=None, op0=ALU.add),
                         reads=[t_ang], writes=[t_tr2])
                    p.op("dve", lambda e: e.scalar_tensor_tensor(out=tr2[:], in0=tr1[:], scalar=float(-C_HI), in1=tr2[:],
                                                                 op0=ALU.mult, op1=ALU.add), reads=[t_tr1, t_tr2], writes=[t_tr2])
                    p.op("dve", lambda e: e.scalar_tensor_tensor(out=tr2[:], in0=tr1[:], scalar=float(-C_LO), in1=tr2[:],
                                                                 op0=ALU.mult, op1=ALU.add), reads=[t_tr1, t_tr2], writes=[t_tr2])
                    p.op("dve", lambda e: e.tensor_scalar(out=tr1[:], in0=tr2[:], scalar1=float(math.pi), scalar2=float(-TWO_PI),
                                                          op0=ALU.is_gt, op1=ALU.mult), reads=[t_tr2], writes=[t_tr1])
                    p.op("dve", lambda e: e.tensor_tensor(out=tr2[:], in0=tr2[:], in1=tr1[:], op=ALU.add),
                         reads=[t_tr1, t_tr2], writes=[t_tr2])
                    p.op("dve", lambda e: e.tensor_scalar(out=tr1[:], in0=tr2[:], scalar1=float(-math.pi), scalar2=float(TWO_PI),
                                                          op0=ALU.is_lt, op1=ALU.mult), reads=[t_tr2], writes=[t_tr1])
                    p.op("dve", lambda e: e.tensor_tensor(out=tr2[:], in0=tr2[:], in1=tr1[:], op=ALU.add),
                         reads=[t_tr1, t_tr2], writes=[t_tr2])
                    p.op("dve", lambda e: e.tensor_scalar(out=tr2[:], in0=tr2[:], scalar1=float(-3.14159), scalar2=float(3.14159),
                                                          op0=ALU.max, op1=ALU.min), reads=[t_tr2], writes=[t_tr2])
                    p.op("act", lambda e: e.activation(out=dst[:], in_=tr2[:], func=AF.Sin), reads=[t_tr2], writes=[t_dst])

                sin_of(cosT, t_cos, math.pi / 2.0)
                sin_of(sinS, t_sin, 0.0)
                p.op("dve", lambda e: e.tensor_scalar(out=sinS[:], in0=sinS[:], scalar1=self.sgn_c, scalar2=None, op0=ALU.mult),
                     reads=[t_sin, cf_], writes=[t_sin])
                if self.mode == "F":
                    p.dma("pool", dr["sc_cos"], cosT[:], t_cos, reads=[t_cos])
                    p.dma("pool", dr["sc_sin"], sinS[:], t_sin, reads=[t_sin])

            if KSTOP <= 2:
                p.barrier(); return
            if self.mode == "F" and L == 0:
                for f in range(2):
                    for m in (0, 1, 2, self.NG + 3, self.NG + 4, self.NG + 5):
                        for r0 in range(0, HR[f], 128):
                            rows = min(128, HR[f] - r0)
                            p.dma("pool", dap(self.dh["kvb_all%d" % f], (m * HR[f] + r0) * 512, [[512, rows], [1, 512]]),
                                  self.zt[0:rows, :], self.t_zt, reads=[self.t_zt])
            PSA = [2, 3, 4, 5, 6, 7]

            def acc_bank():
                return PSA[self.rot("A_acc", len(PSA))]

            def stage_b():
                i = self.rot("A_stb", NSTB)
                return stb[i], t_stb[i]

            def stage_f():
                i = self.rot("A_stf", NSTF)
                return stf[i], t_stf[i]

            def evac(eng_hint, out, in_, reads, writes, scale=None):
                e = eng_hint if eng_hint else ("act" if self.rot("A_ev", 2) == 0 else "dve")
                if e == "act":
                    if scale is None:
                        p.op("act", lambda en: en.copy(out=out, in_=in_), reads=reads, writes=writes)
                    else:
                        p.op("act", lambda en: en.mul(out=out, in_=in_, mul=float(scale)), reads=reads, writes=writes)
                else:
                    if scale is None:
                        p.op("dve", lambda en: en.tensor_copy(out=out, in_=in_), reads=reads, writes=writes)
                    else:
                        p.op("dve", lambda en: en.tensor_scalar(out=out, in0=in_, scalar1=float(scale), scalar2=None,
                                                                op0=ALU.mult), reads=reads, writes=writes)

            def fmm(col0, M, bank=None):
                b = acc_bank() if bank is None else bank
                for kc in range(8):
                    p.op("pe", lambda e, b=b, kc=kc: e.matmul(self.ps[b][0:M, :], lhsT=Wp[:, kc, col0:col0 + M],
                                                             rhs=xnT[:, kc, :], start=(kc == 0), stop=(kc == 7)),
                         reads=[t_Wp, t_Wp2, t_xnT], writes=[self.t_ps[b]])
                return b

            FUSED = (self.mode == "F")

            def store2d(dst_h, row0, M, lg, src, t_src, prow0=0, q="pool"):
                if FUSED and dst_h is kvb:
                    f, rr = kvmap(row0)
                    d_ = dr["kvb_loc%d" % f]
                    p.dma(q, d_[lg * HR[f] + rr:lg * HR[f] + rr + M, 0:512], src[prow0:prow0 + M, 0:512], t_src, reads=[t_src])
                else:
                    p.dma(q, dst_h[row0:row0 + M, lg * 512:(lg + 1) * 512], src[prow0:prow0 + M, 0:512], t_src, reads=[t_src])

            def vdst(R, lg, blk, W):
                if FUSED:
                    f, rr = kvmap(R)
                    return dap(self.dh["kvb_loc%d" % f], (lg * HR[f] + rr) * 512 + blk * 128 * W, [[W, 128], [1, W]])
                return dap(self.dh["kvb_loc"], R * NTL + (lg * 4 + blk) * 128 * W, [[W, 128], [1, W]])

            def store_nat(dst_h, row0, lg, src, t_src):
                for c in range(4):
                    p.dma("pool", dst_h[row0:row0 + 128, lg * 512 + (3 - c) * 128:lg * 512 + (4 - c) * 128], src[:, c * 128:(c + 1) * 128],
                          t_src, reads=[t_src])

            def ones_cols(st, t_st, nh):
                p.op("pool", lambda e: e.memset(st[:, 0:nh * 65].rearrange("p (h c) -> p h c", h=nh)[:, :, 64:65], 1.0), writes=[t_st])

            kvb = dr.get("kvb_loc", "KVB")
            for lg in range(NGL):
                for blk in range(4):
                    lb = lg * 4 + blk
                    s = self.rot("A_x", 2)
                    p.dma("sp", xs[s][:], x_src[lb * 128:(lb + 1) * 128, :], t_xs[s], writes=[t_xs[s]])
                    p.op("act", lambda e, s=s: e.activation(out=junk[:], in_=xs[s][:], func=AF.Square, accum_out=sst[s][:, 0:1]),
                         reads=[t_xs[s]], writes=[t_junk, t_ss[s]])
                    p.op("act", lambda e, s=s: e.activation(out=sst[s][:, 1:2], in_=sst[s][:, 0:1], func=AF.Sqrt,
                                                           bias=self.eps_c, scale=float(1.0 / D)),
                         reads=[t_ss[s], cf_], writes=[t_ss[s]])
                    p.op("dve", lambda e, s=s: e.reciprocal(out=sst[s][:, 2:3], in_=sst[s][:, 1:2]), reads=[t_ss[s]], writes=[t_ss[s]])
                    p.op("dve", lambda e, s=s: e.tensor_scalar(out=xn[s][:], in0=xs[s][:], scalar1=sst[s][:, 2:3], scalar2=None,
                                                               op0=ALU.mult), reads=[t_xs[s], t_ss[s]], writes=[t_xn[s]])
                    for half in range(2):
                        for i in range(4):
                            kc = half * 4 + i
                            p.op("pe", lambda e, s=s, kc=kc, half=half, i=i: e.transpose(
                                out=self.ps[half][:, i * 128:(i + 1) * 128], in_=xn[s][:, kc * 128:(kc + 1) * 128],
                                identity=self.ident_f), reads=[t_xn[s], cf_], writes=[self.t_ps[half]])
                        evac(None, xnT[:, half * 4:half * 4 + 4, (3 - blk) * 128:(4 - blk) * 128],
                             self.ps[half][:].rearrange("p (a b) -> p a b", a=4), [self.t_ps[half]], [t_xnT])

                if KSTOP <= 3:
                    continue
                for (c0, scale, dst, r0) in ((C_AQ, 0.125, dr["sc_swaq"], 0), (C_AQ + 128, 0.125, dr["sc_swaq"], 128),
                                             (C_AK, None, kvb, R_SWAK),
                                             (C_DQ, 0.125, dr["sc_sbq"], 0), (C_DQ + 128, 0.125, dr["sc_sbq"], 128),
                                             (C_DK, None, kvb, R_SBK), (C_DK + 128, None, kvb, R_SBK + 128)):
                    b = fmm(c0, 128)
                    st, t_st = stage_b()
                    evac(None, st[:, 0:512], self.ps[b][:], [self.t_ps[b]], [t_st], scale)
                    store2d(dst, r0, 128, lg, st, t_st)
                if KSTOP <= 4:
                    continue
                for ct in range(2):
                    b = fmm(C_BB + ct * 128, 128)
                    st, t_st = stage_f()
                    evac(None, st[:], self.ps[b][:], [self.t_ps[b]], [t_st])
                    store_nat(dr["sc_bb"], ct * 128, lg, st, t_st)
                    b1 = fmm(C_BC + ct * 128, 128)
                    b2 = fmm(C_BX + ct * 128, 128)
                    i = self.rot("A_tmpf", 3)
                    evac("act", tmpf[i][:], self.ps[b1][:], [self.t_ps[b1]], [t_tmpf[i]])
                    st, t_st = stage_f()
                    p.op("dve", lambda e, st=st, i=i, b2=b2: e.tensor_tensor(out=st[:], in0=tmpf[i][:], in1=self.ps[b2][:], op=ALU.mult),
                         reads=[t_tmpf[i], self.t_ps[b2]], writes=[t_st])
                    store_nat(dr["sc_u"], ct * 128, lg, st, t_st)
                    if FUSED:
                        tl, t_tl = stage_b()
                        p.op("dve", lambda e: e.tensor_copy(out=tl[:, 0:2], in_=st[:, 126:128]), reads=[t_st], writes=[t_tl])
                        p.op("dve", lambda e: e.tensor_tensor(out=tl[:, 2:4], in0=st[:, 126:128], in1=tl[:, 0:2], op=ALU.subtract),
                             reads=[t_st, t_tl], writes=[t_tl])
                        p.dma("pool", dap(self.dh["kvb_loc0"], (lg * HR[0] + 770) * 512 + ct * 128 * 4, [[4, 128], [1, 4]]),
                              tl[:, 0:4], t_tl, reads=[t_tl])
                    else:
                        p.dma("pool", dr["kvf_loc"][ct * 128:(ct + 1) * 128, lg * 2:lg * 2 + 2], st[:, 126:128], t_st, reads=[t_st])
                if KSTOP <= 5:
                    continue
                for kc in range(2):
                    b = fmm(C_CQ + kc * 128, 128)
                    evac("dve", cqT[:, kc, :], self.ps[b][:], [self.t_ps[b]], [t_cqT])
                    p.op("act", lambda e, b=b, kc=kc: e.activation(out=sq[kc][:], in_=self.ps[b][:], func=AF.Square),
                         reads=[self.t_ps[b]], writes=[t_sq[kc]])
                b = acc_bank()
                for kc in range(2):
                    p.op("pe", lambda e, b=b, kc=kc: e.matmul(self.ps[b][:], lhsT=self.ones_b, rhs=sq[kc][:], start=(kc == 0), stop=(kc == 1)),
                         reads=[t_sq[kc], cb_], writes=[self.t_ps[b]])
                p.op("act", lambda e, b=b: e.activation(out=rq_bc[:], in_=self.ps[b][:], func=AF.Sqrt, bias=self.eps_c, scale=float(1.0 / 256)),
                     reads=[self.t_ps[b], cf_], writes=[t_rq])
                p.op("dve", lambda e: e.reciprocal(out=rq_bc[:], in_=rq_bc[:]), reads=[t_rq], writes=[t_rq])
                if KSTOP <= 5.1:
                    continue
                b = fmm(C_CKV, 128)
                evac("dve", ckvT[:], self.ps[b][:], [self.t_ps[b]], [t_ckvT])
                p.op("act", lambda e, b=b: e.activation(out=sq[0][:], in_=self.ps[b][:], func=AF.Square),
                     reads=[self.t_ps[b]], writes=[t_sq[0]])
                b = acc_bank()
                p.op("pe", lambda e, b=b: e.matmul(self.ps[b][:], lhsT=self.ones_b, rhs=sq[0][:], start=True, stop=True),
                     reads=[t_sq[0], cb_], writes=[self.t_ps[b]])
                p.op("act", lambda e, b=b: e.activation(out=rkv_bc[:], in_=self.ps[b][:], func=AF.Sqrt, bias=self.eps_c, scale=float(1.0 / 128)),
                     reads=[self.t_ps[b], cf_], writes=[t_rkv])
                p.op("dve", lambda e: e.reciprocal(out=rkv_bc[:], in_=rkv_bc[:]), reads=[t_rkv], writes=[t_rkv])
                if KSTOP <= 5.2:
                    continue
                b = acc_bank()
                for blk in range(4):
                    p.op("pe", lambda e, b=b, blk=blk: e.matmul(self.ps[b][:, 2 * blk:2 * blk + 2], lhsT=sq[0][:, blk * 128:(blk + 1) * 128],
                                                               rhs=self.ones_b[:, 0:2], start=True, stop=True),
                         reads=[t_sq[0], cb_], writes=[self.t_ps[b]])
                p.op("act", lambda e, b=b: e.activation(out=rkv_t[:], in_=self.ps[b][:, 0:8], func=AF.Sqrt, bias=self.eps_c, scale=float(1.0 / 128)),
                     reads=[self.t_ps[b], cf_], writes=[t_rkvt])
                p.op("dve", lambda e: e.reciprocal(out=rkv_t[:], in_=rkv_t[:]), reads=[t_rkvt], writes=[t_rkvt])
                if KSTOP <= 6:
                    continue
                b1 = fmm(C_CKR, 32)
                b2 = fmm(C_KRSW, 32)
                i1 = self.rot("A_tmpf", 3)
                p.op("dve", lambda e, i1=i1, b1=b1: e.tensor_tensor(out=tmpf[i1][0:32, :], in0=self.ps[b1][0:32, :],
                                                                    in1=cosT[0:32, lg * 512:(lg + 1) * 512], op=ALU.mult),
                     reads=[self.t_ps[b1], t_cos], writes=[t_tmpf[i1]])
                i2 = self.rot("A_tmpf", 3)
                p.op("dve", lambda e, i2=i2, b2=b2: e.tensor_tensor(out=tmpf[i2][0:32, :], in0=self.ps[b2][0:32, :],
                                                                    in1=sinS[0:32, lg * 512:(lg + 1) * 512], op=ALU.mult),
                     reads=[self.t_ps[b2], t_sin], writes=[t_tmpf[i2]])
                st, t_st = stage_b()
                p.op("dve", lambda e, st=st, i1=i1, i2=i2: e.tensor_tensor(out=st[0:32, 0:512], in0=tmpf[i1][0:32, :], in1=tmpf[i2][0:32, :], op=ALU.add),
                     reads=[t_tmpf[i1], t_tmpf[i2]], writes=[t_st])
                for h in range(4):
                    store2d(kvb, R_MLAK + h * 96 + 64, 32, lg, st, t_st)
                if KSTOP <= 7:
                    continue
                for pair in range(2):
                    b = acc_bank()
                    for kc in range(2):
                        p.op("pe", lambda e, b=b, kc=kc, pair=pair: e.matmul(self.ps[b][:], lhsT=wuq[:, kc, pair * 128:(pair + 1) * 128],
                                                                            rhs=cqT[:, kc, :], start=(kc == 0), stop=(kc == 1)),
                             reads=[t_wuq, t_cqT], writes=[self.t_ps[b]])
                    st, t_st = stage_b()
                    p.op("dve", lambda e, st=st, b=b: e.tensor_tensor(out=st[:, 0:512], in0=self.ps[b][:], in1=rq_bc[:], op=ALU.mult),
                         reads=[self.t_ps[b], t_rq], writes=[t_st])
                    for i in range(2):
                        store2d(dr["sc_mlaq"], (2 * pair + i) * 96, 64, lg, st, t_st, prow0=i * 64)
                    b = acc_bank()
                    p.op("pe", lambda e, b=b, pair=pair: e.matmul(self.ps[b][:], lhsT=wukv[:, pair * 128:(pair + 1) * 128], rhs=ckvT[:],
                                                                 start=True, stop=True), reads=[t_wukv, t_ckvT], writes=[self.t_ps[b]])
                    st, t_st = stage_b()
                    p.op("dve", lambda e, st=st, b=b: e.tensor_tensor(out=st[:, 0:512], in0=self.ps[b][:], in1=rkv_bc[:], op=ALU.mult),
                         reads=[self.t_ps[b], t_rkv], writes=[t_st])
                    for i in range(2):
                        store2d(kvb, R_MLAK + (2 * pair + i) * 96, 64, lg, st, t_st, prow0=i * 64)
                b1 = acc_bank()
                b2 = acc_bank()
                for (b, c0) in ((b1, 256), (b2, 384)):
                    for kc in range(2):
                        p.op("pe", lambda e, b=b, kc=kc, c0=c0: e.matmul(self.ps[b][:], lhsT=wuq[:, kc, c0:c0 + 128], rhs=cqT[:, kc, :],
                                                                        start=(kc == 0), stop=(kc == 1)),
                             reads=[t_wuq, t_cqT], writes=[self.t_ps[b]])
                i1 = self.rot("A_tmpf", 3)
                p.op("dve", lambda e, i1=i1, b1=b1: e.tensor_tensor(out=tmpf[i1][:], in0=self.ps[b1][:], in1=cosT[:, lg * 512:(lg + 1) * 512], op=ALU.mult),
                     reads=[self.t_ps[b1], t_cos], writes=[t_tmpf[i1]])
                i2 = self.rot("A_tmpf", 3)
                p.op("dve", lambda e, i2=i2, b2=b2: e.tensor_tensor(out=tmpf[i2][:], in0=self.ps[b2][:], in1=sinS[:, lg * 512:(lg + 1) * 512], op=ALU.mult),
                     reads=[self.t_ps[b2], t_sin], writes=[t_tmpf[i2]])
                p.op("pool", lambda e, i1=i1, i2=i2: e.tensor_tensor(out=tmpf[i1][:], in0=tmpf[i1][:], in1=tmpf[i2][:], op=ALU.add),
                     reads=[t_tmpf[i1], t_tmpf[i2]], writes=[t_tmpf[i1]])
                st, t_st = stage_b()
                p.op("dve", lambda e, st=st, i1=i1: e.tensor_tensor(out=st[:, 0:512], in0=tmpf[i1][:], in1=rq_bc[:], op=ALU.mult),
                     reads=[t_tmpf[i1], t_rq], writes=[t_st])
                for h in range(4):
                    store2d(dr["sc_mlaq"], h * 96 + 64, 32, lg, st, t_st, prow0=h * 32)

                if KSTOP <= 8:
                    continue
                for blk in range(4):
                    lb = lg * 4 + blk
                    tok = slice(blk * 128, (blk + 1) * 128)
                    b = acc_bank()
                    p.op("pe", lambda e, b=b, tok=tok: e.matmul(self.ps[b][:, 0:256], lhsT=ckvT[:, tok], rhs=wukv[:, 256:512], start=True, stop=True),
                         reads=[t_ckvT, t_wukv], writes=[self.t_ps[b]])
                    st, t_st = stage_b()
                    p.op("dve", lambda e, st=st, b=b, blk=blk: e.tensor_scalar(
                        out=st[:, 0:260].rearrange("p (h c) -> p h c", h=4)[:, :, 0:64], in0=self.ps[b][:, 0:256].rearrange("p (h c) -> p h c", h=4),
                        scalar1=rkv_t[:, 2 * blk:2 * blk + 1], scalar2=None, op0=ALU.mult),
                         reads=[self.t_ps[b], t_rkvt], writes=[t_st])
                    ones_cols(st, t_st, 4)
                    p.dma("pool", vdst(R_MLAV, lg, blk, 260), st[:, 0:260], t_st, reads=[t_st])
                    b = acc_bank()
                    for (c0, n, o0) in ((C_AV, 128, 0), (C_DV, 256, 128)):
                        for kc in range(8):
                            p.op("pe", lambda e, b=b, kc=kc, c0=c0, n=n, o0=o0, tok=tok: e.matmul(
                                self.ps[b][:, o0:o0 + n], lhsT=xnT[:, kc, tok], rhs=Wp[:, kc, c0:c0 + n], start=(kc == 0), stop=(kc == 7)),
                                 reads=[t_Wp, t_Wp2, t_xnT], writes=[self.t_ps[b]])
                    st, t_st = stage_b()
                    evac(None, st[:, 0:130].rearrange("p (h c) -> p h c", h=2)[:, :, 0:64], self.ps[b][:, 0:128].rearrange("p (h c) -> p h c", h=2),
                         [self.t_ps[b]], [t_st])
                    evac(None, st[:, 256:512], self.ps[b][:, 128:384], [self.t_ps[b]], [t_st])
                    ones_cols(st, t_st, 2)
                    p.dma("pool", vdst(R_SWAV, lg, blk, 130), st[:, 0:130], t_st, reads=[t_st])
                    p.dma("pool", vdst(R_SBV, lg, blk, 256), st[:, 256:512], t_st, reads=[t_st])
                    st, t_st = stage_b()
                    for half in range(2):
                        b = acc_bank()
                        c0 = C_GATE + half * 512
                        for kc in range(8):
                            p.op("pe", lambda e, b=b, kc=kc, c0=c0, tok=tok: e.matmul(self.ps[b][:], lhsT=xnT[:, kc, tok], rhs=Wp[:, kc, c0:c0 + 512],
                                                                                     start=(kc == 0), stop=(kc == 7)),
                                 reads=[t_Wp, t_Wp2, t_xnT], writes=[self.t_ps[b]])
                        p.op("act", lambda e, st=st, b=b, half=half: e.activation(out=st[:, half * 512:(half + 1) * 512], in_=self.ps[b][:], func=AF.Silu),
                             reads=[self.t_ps[b]], writes=[t_st])
                    p.dma("pool", dr["sc_gate"][lb * 128:(lb + 1) * 128, :], st[:, :], t_st, reads=[t_st])
                if FUSED:
                    self.gather_group(lg)
            p.barrier()

    def gather_group(self, lg):
        p = self.p
        groups = [[b * NR + r for r in range(NR)] for b in range(NB_BATCH)]
        waits = []
        for k, v in p.all_tickets.items():
            if isinstance(k, tuple) and k[0] in ("sw", "hw") and p.seen["pool"].get(k, 0) < v:
                p.seen["pool"][k] = v
                waits.append((p.sems[k], v))
        p.streams["pool"].append((waits, None, None))
        t_g = Tk()
        for f in range(2):
            p.raw("pool", lambda e, lg=lg, f=f: e.collective_compute(
                "AllGather", ALU.bypass, replica_groups=groups,
                ins=[self.dh["kvb_loc%d" % f].ap()[lg * HR[f]:(lg + 1) * HR[f], :].opt()],
                outs=[self.dh["kvb_all%d" % f].ap()[(3 + 4 * lg) * HR[f]:(7 + 4 * lg) * HR[f], :].opt()]),
                  ("cc", lg * 2 + f), writes=[t_g])

    def phaseB(self, L, x_src, x_dst):
        p, nc = self.p, self.nc
        S, NG, NTL, NGL, NBL = self.S, self.NG, self.NTL, self.NGL, self.NBL
        NBK = S // 128
        NPOS = NBK + 12
        dr = self.dram
        cf_, cb_ = self.t_cf, self.t_cb
        z4 = self.zero4
        with ExitStack() as es:
            def tile(name, shape, dt):
                return es.enter_context(nc.sbuf_tensor("B%d_%s" % (L, name), list(shape), dt))

            yt = [tile("yt%d" % i, [128, 4, 256], F32) for i in range(3)]; t_yt = [Tk(), Tk(), Tk()]
            small = tile("small", [128, 64], F32); t_small = Tk()
            Wo = tile("Wo", [128, 8, D], BF16); t_Wo = Tk()
            wst = [tile("wst%d" % i, [128, D], F32) for i in range(2)]; t_wst = [Tk(), Tk()]
            ggrp_bc = tile("ggrp", [128, D], F32); t_gg = Tk()
            gpost_bc = tile("gpost", [128, D], F32); t_gp = Tk()
            cwb = tile("cwb", [128, 8], F32); t_cwb = Tk()
            esink = tile("esink", [128, 4], F32); t_es = Tk()
            es_att = ExitStack()

            def atile(name, shape, dt):
                return es_att.enter_context(nc.sbuf_tensor("B%d_%s" % (L, name), list(shape), dt))
            _tile_outer = tile
            tile = atile
            KtAll = tile("KtAll", [128, 4, NPOS * 128], BF16)
            Kt = [KtAll[:, h, :] for h in range(4)]; t_K = [Tk() for _ in range(4)]
            Vt = tile("Vt", [128, NPOS, 264], BF16); t_V = Tk()
            Qa = [tile("Qa%d" % h, [128, 512], BF16) for h in range(4)]
            Qm = [tile("Qm%d" % h, [128, 512], BF16) for h in range(4)]
            t_Qm = [Tk() for _ in range(4)]; t_Qx = [Tk() for _ in range(4)]
            NE = 2
            Et2 = [tile("E%d" % i, [128, 2, 512], F32) for i in range(NE)]; t_E = [Tk() for _ in range(NE)]
            Lt2 = [tile("Lp%d" % i, [128, 2, 512], BF16) for i in range(NE)]; t_L = [Tk() for _ in range(NE)]
            NA = 3
            At2 = [tile("At%d" % i, [128, 2, 512], BF16) for i in range(NA)]; t_A = [Tk() for _ in range(NA)]
            tile = _tile_outer

            p.dma("sp", ggrp_bc[:], dap(self.dh["ggrp"], L * D, [[0, 128], [1, D]]), t_gg, writes=[t_gg])
            p.dma("sp", gpost_bc[:], dap(self.dh["gpost"], L * D, [[0, 128], [1, D]]), t_gp, writes=[t_gp])
            p.dma("sp", cwb[:, 0:6], dr["convw_t"][L], t_cwb, writes=[t_cwb])
            p.dma("sp", cwb[:, 6:8], dr["convb_t"][L], t_cwb, writes=[t_cwb])
            p.dma("sp", esink[:], dap(self.dh["sinks"], L * 4, [[0, 128], [1, 4]]), t_es, writes=[t_es])
            p.op("act", lambda e: e.activation(out=esink[:], in_=esink[:], func=AF.Exp), reads=[t_es], writes=[t_es])

            NCH = 4

            FUSED = (self.mode == "F")
            NQG = NPOS // 4
            FAM = dict(sbK=0, sbV=0, mlaK=1, mlaV=1)
            DQ = "sp" if L == 0 else "act"
            TB = dict(sbK=0, sbV=4, mlaK=8, mlaV=12, swK=16, swV=18, ut=19)

            def load_K(name, nrows, heads=(0, 1, 2, 3)):
                if FUSED:
                    for h in heads:
                        i = TB[name] + h
                        f = FAM[name]
                        p.dyn_dma(DQ, KtAll[0:nrows, h, :], self.dh["kvb_all%d" % f], self.tabt[0:1, i:i + 1],
                                  [[512, nrows], [HR[f] * 512, NQG], [1, 512]], t_K[h], reads=[self.t_tab],
                                  writes=[t_K[h]] + ([t_Kx[h]] if nrows > 64 else []))
                    return
                w = NPOS * 128
                cw = (w + NCH - 1) // NCH
                for h in range(4):
                    for c in range(NCH):
                        c0, c1 = c * cw, min(w, (c + 1) * cw)
                        p.dma("sp", Kt[h][0:nrows, c0:c1], dr[name][h * nrows:(h + 1) * nrows, c0:c1], t_K[h], writes=[t_K[h]])

            def load_V(name, wcols):
                if FUSED:
                    for blk in range(4):
                        i = TB[name] + blk
                        f = FAM[name]
                        p.dyn_dma(DQ, Vt[:, blk:NPOS:4, 0:wcols], self.dh["kvb_all%d" % f], self.tabt[0:1, i:i + 1],
                                  [[wcols, 128], [HR[f] * 512, NQG], [1, wcols]], t_V, reads=[self.t_tab], writes=[t_V])
                    return
                step = 8
                for q0 in range(0, NPOS, step):
                    q1 = min(NPOS, q0 + step)
                    p.dma("sp", Vt[:, q0:q1, 0:wcols], dap(self.dh[name], q0 * 128 * wcols, [[wcols, 128], [128 * wcols, q1 - q0], [1, wcols]]),
                          t_V, writes=[t_V])

            def pos_of(lg, d):
                if FUSED:
                    return 4 * (4 * lg + 3 - d // 4) + d % 4
                return NBK - 4 - 16 * lg + d

            def nstep(lg):
                return 16 * lg + 16

            def acols(d):
                if d < 4:
                    return (d + 1) * 128, True
                return 512, False

            t_Kx = [Tk() for _ in range(4)]
            for h in range(4):
                en = "dve" if h % 2 == 0 else "pool"
                p.op(en, lambda e: e.memset(Kt[h][64:128, :], 0.0), writes=[t_Kx[h]])
                p.op(en, lambda e: e.memset(Kt[h][64:65, :], 1.0), writes=[t_Kx[h]])
                p.op(en, lambda e: e.memset(Kt[h][96:97, :], 1.0), writes=[t_Kx[h]])
            load_K("sbK", 64, (0, 1))
            load_V("sbV", 256)
            load_K("sbK", 64, (2, 3))

            for h in range(4):
                p.op("pool", lambda e, h=h: e.memset(Qm[h][64:128, :], 0.0), writes=[t_Qm[h]])
                p.op("pool", lambda e, h=h: e.memset(Qa[h][64:128, :], 0.0), writes=[t_Qx[h]])
            t_Qd = [Tk() for _ in range(4)]
            for pair in range(2):
              if pair == 1:
                load_K("mlaK", 96, (0, 1))
              for lg in range(NGL):
                ys = self.rot("B_yt", 3)
                if True:
                    hs = (2 * pair, 2 * pair + 1)
                    bank = {}
                    for i, h in enumerate(hs):
                        bank[h] = dict(A=i, B=2 + i, C=4 + i, O=6 + i)
                        qsrc = dr["sc_sbq"][h * 64:(h + 1) * 64, lg * 512:(lg + 1) * 512]
                        p.dma("sp", Qm[h][0:64, :], qsrc, t_Qm[h], writes=[t_Qm[h]])
                        p.dma("sp", Qa[h][0:64, :], qsrc, t_Qd[h], writes=[t_Qd[h]])
                        p.op("pool", lambda e, h=h: e.memset(Qa[h][64:98, :], 0.0), writes=[t_Qx[h]])
                        bC, bO = bank[h]["C"], bank[h]["O"]
                        p.op("pe", lambda e, bC=bC: e.matmul(self.ps[bC][0:98, :], lhsT=self.ones98, rhs=z4, start=True, stop=False, skip_group_check=True),
                             reads=[cb_], writes=[self.t_ps[bC]])
                        p.op("pe", lambda e, bO=bO: e.matmul(self.ps[bO][:, 0:256], lhsT=self.zero_b, rhs=z4[:, 0:256], start=True, stop=False, skip_group_check=True),
                             reads=[cb_], writes=[self.t_ps[bO]])
                    ND = nstep(lg)
                    slot = {}
                    aslot = {}

                    def mm1(d):
                        n, dg = acols(d)
                        P_ = pos_of(lg, d)
                        for h in hs:
                            bA = bank[h]["A"]
                            p.op("pe", lambda e, h=h, bA=bA, P_=P_, n=n: e.matmul(self.ps[bA][:, 0:n], lhsT=Kt[h][:, P_ * 128:(P_ + 1) * 128],
                                                                                rhs=Qm[h][:, 0:n], start=True, stop=True),
                                 reads=[t_K[h], t_Kx[h], t_Qm[h]], writes=[self.t_ps[bA]])

                    def EL(d):
                        n, dg = acols(d)
                        s = self.rot("B_E", 2)
                        slot[d] = s
                        kA = bank[hs[0]]["A"] // 2
                        p.op("act", lambda e: e.activation(out=Et2[s][:, :, 0:n], in_=self.pp[kA][:].rearrange("p (h c) -> p h c", h=2)[:, :, 0:n], func=AF.Exp),
                             reads=[self.t_ps[2 * kA], self.t_ps[2 * kA + 1]], writes=[t_E[s]])
                        p.op("act", lambda e: e.activation(out=Lt2[s][:, :, 0:n], in_=Et2[s][:, :, 0:n], func=AF.Ln, bias=1.0),
                             reads=[t_E[s]], writes=[t_L[s]])
                        if dg:
                            for hi in range(2):
                                p.op("pool", lambda e: e.tensor_tensor(out=Lt2[s][:, hi, n - 128:n], in0=Lt2[s][:, hi, n - 128:n], in1=self.m_lt, op=ALU.mult),
                                     reads=[t_L[s], cb_], writes=[t_L[s]])

                    def mm2tc(d):
                        n, dg = acols(d)
                        P_ = pos_of(lg, d)
                        last = (d == ND - 1)
                        s = slot[d]
                        for hi, h in enumerate(hs):
                            bB, bC = bank[h]["B"], bank[h]["C"]
                            p.op("pe", lambda e, h=h, bB=bB, P_=P_, n=n: e.matmul(self.ps[bB][:, 0:n], lhsT=Kt[h][:, P_ * 128:(P_ + 1) * 128],
                                                                                rhs=Qa[h][:, 0:n], start=True, stop=False),
                                 reads=[t_K[h], t_Kx[h], t_Qd[h], t_Qx[h]], writes=[self.t_ps[bB]])
                            p.op("pe", lambda e, bB=bB, s=s, n=n: e.matmul(self.ps[bB][:, 0:n], lhsT=self.negtri, rhs=Lt2[s][:, hi, 0:n], start=False, stop=True),
                                 reads=[t_L[s], cb_], writes=[self.t_ps[bB]])
                            if not last:
                                p.op("pe", lambda e, bC=bC, s=s, n=n: e.matmul(self.ps[bC][0:98, 0:n], lhsT=self.ones98, rhs=Lt2[s][:, hi, 0:n],
                                                                              start=False, stop=False, skip_group_check=True),
                                     reads=[t_L[s], cb_], writes=[self.t_ps[bC]])
                        if not last:
                            for h in hs:
                                bC = bank[h]["C"]
                                p.op("dve", lambda e, h=h, bC=bC, n=n: e.tensor_scalar(out=Qa[h][64:98, 0:n], in0=self.ps[bC][64:98, 0:n], scalar1=-1.0,
                                                                                      scalar2=None, op0=ALU.mult),
                                     reads=[self.t_ps[bC]], writes=[t_Qx[h]])
                                p.op("dve", lambda e, h=h, bC=bC, n=n: e.scalar_tensor_tensor(out=Qa[h][96:98, 0:n], in0=self.ps[bC][96:98, 0:n], scalar=-1.0,
                                                                                             in1=Qa[h][96:98, 0:n], op0=ALU.mult, op1=ALU.subtract),
                                     reads=[self.t_ps[bC], t_Qx[h]], writes=[t_Qx[h]])

                    def Bexp(d):
                        n, dg = acols(d)
                        a = self.rot("B_A", NA)
                        aslot[d] = a
                        kB = bank[hs[0]]["B"] // 2
                        p.op("act", lambda e: e.activation(out=At2[a][:, :, 0:n], in_=self.pp[kB][:].rearrange("p (h c) -> p h c", h=2)[:, :, 0:n], func=AF.Exp),
                             reads=[self.t_ps[2 * kB], self.t_ps[2 * kB + 1]], writes=[t_A[a]])
                        if dg:
                            for hi in range(2):
                                p.op("pool", lambda e: e.tensor_tensor(out=At2[a][:, hi, n - 128:n], in0=At2[a][:, hi, n - 128:n], in1=self.m_lt, op=ALU.mult),
                                     reads=[t_A[a], cb_], writes=[t_A[a]])

                    def PV(d):
                        n, dg = acols(d)
                        P_ = pos_of(lg, d)
                        last = (d == ND - 1)
                        a = aslot[d]
                        for hi, h in enumerate(hs):
                            bO = bank[h]["O"]
                            for c in range(n // 128):
                                p.op("pe", lambda e, h=h, bO=bO, a=a, c=c, P_=P_: e.matmul(self.ps[bO][:, c * 64:(c + 1) * 64], lhsT=At2[a][:, hi, c * 128:(c + 1) * 128],
                                                                                         rhs=Vt[:, P_, h * 64:(h + 1) * 64], start=False, stop=last,
                                                                                         skip_group_check=True),
                                     reads=[t_A[a], t_V], writes=[self.t_ps[bO]])

                    mm1(0)
                    EL(0)
                    if ND > 1:
                        mm1(1)
                    for d in range(ND):
                        mm2tc(d)
                        if d >= 1:
                            PV(d - 1)
                        if d + 1 < ND:
                            EL(d + 1)
                        if d + 2 < ND:
                            mm1(d + 2)
                        Bexp(d)
                    PV(ND - 1)
                    for h in hs:
                        bO = bank[h]["O"]
                        p.op("dve", lambda e, h=h, bO=bO, ys=ys: e.tensor_copy(out=yt[ys][:, :, h * 64:(h + 1) * 64],
                                                                               in_=self.ps[bO][:, 0:256].rearrange("p (j c) -> p j c", j=4)),
                             reads=[self.t_ps[bO]], writes=[t_yt[ys]])
                p.dma("pool", dap(self.dh["sc_y"], lg * 4 * 128 * D + 768 + pair * 128, [[D, 128], [128 * D, 4], [1, 128]]), yt[ys][:, :, pair * 128:(pair + 1) * 128], t_yt[ys], reads=[t_yt[ys]])

            load_V("mlaV", 260)
            load_K("mlaK", 96, (2, 3))
            for kc in range(8):
                s = kc % 2
                p.dma("sp", wst[s][:], dr["w_out"][L, kc * 128:(kc + 1) * 128, :], t_wst[s], writes=[t_wst[s]])
                p.op("dve", lambda e, s=s, kc=kc: e.tensor_copy(out=Wo[:, kc, :], in_=wst[s][:]), reads=[t_wst[s]], writes=[t_Wo])
            for pair in range(2):
              for lg in range(NGL):
                ys = self.rot("B_yt", 3)
                if True:
                    hs = (2 * pair, 2 * pair + 1)
                    bank = {}
                    for i, h in enumerate(hs):
                        bank[h] = dict(S=(i, 2 + i, 6 + i), O=4 + i)
                        p.dma("sp", Qa[h][0:96, :], dr["sc_mlaq"][h * 96:(h + 1) * 96, lg * 512:(lg + 1) * 512], t_Qm[h], writes=[t_Qm[h], t_Qx[h], t_Qd[h]])
                        bO = bank[h]["O"]
                        p.op("pe", lambda e, bO=bO: e.matmul(self.ps[bO][:, 0:260], lhsT=self.zero_b, rhs=z4[:, 0:260], start=True, stop=False, skip_group_check=True),
                             reads=[cb_], writes=[self.t_ps[bO]])
                    ND = nstep(lg)

                    def mmS(d):
                        n, dg = acols(d)
                        P_ = pos_of(lg, d)
                        for h in hs:
                            bS = bank[h]["S"][d % 3]
                            p.op("pe", lambda e, h=h, bS=bS, P_=P_, n=n: e.matmul(self.ps[bS][:, 0:n], lhsT=Kt[h][0:96, P_ * 128:(P_ + 1) * 128],
                                                                                rhs=Qa[h][0:96, 0:n], start=True, stop=True),
                                 reads=[t_K[h], t_Qm[h]], writes=[self.t_ps[bS]])

                    aslot = {}

                    def Pexp(d):
                        n, dg = acols(d)
                        a = self.rot("B_A", NA)
                        aslot[d] = a
                        kS = (0, 1, 3)[d % 3]
                        p.op("act", lambda e: e.activation(out=At2[a][:, :, 0:n], in_=self.pp[kS][:].rearrange("p (h c) -> p h c", h=2)[:, :, 0:n], func=AF.Exp),
                             reads=[self.t_ps[2 * kS], self.t_ps[2 * kS + 1]], writes=[t_A[a]])
                        if dg:
                            for hi in range(2):
                                p.op("pool", lambda e: e.tensor_tensor(out=At2[a][:, hi, n - 128:n], in0=At2[a][:, hi, n - 128:n], in1=self.m_le, op=ALU.mult),
                                     reads=[t_A[a], cb_], writes=[t_A[a]])

                    def PVm(d):
                        n, dg = acols(d)
                        P_ = pos_of(lg, d)
                        last = (d == ND - 1)
                        a = aslot[d]
                        for hi, h in enumerate(hs):
                            bO = bank[h]["O"]
                            for c in range(n // 128):
                                p.op("pe", lambda e: e.matmul(self.ps[bO][:, c * 65:(c + 1) * 65], lhsT=At2[a][:, hi, c * 128:(c + 1) * 128],
                                                              rhs=Vt[:, P_, h * 65:(h + 1) * 65], start=False, stop=last, skip_group_check=True),
                                     reads=[t_A[a], t_V], writes=[self.t_ps[bO]])

                    mmS(0)
                    if ND > 1:
                        mmS(1)
                    for d in range(ND):
                        if d + 2 < ND:
                            mmS(d + 2)
                        if d >= 1:
                            PVm(d - 1)
                        Pexp(d)
                    PVm(ND - 1)
                    for h in hs:
                        bO = bank[h]["O"]
                        p.op("dve", lambda e, bO=bO, h=h: e.reciprocal(out=small[:, 8 + h * 4:12 + h * 4],
                                                                       in_=self.ps[bO][:, 0:260].rearrange("p (j c) -> p j c", j=4)[:, :, 64]),
                             reads=[self.t_ps[bO]], writes=[t_small])
                        for c in range(4):
                            p.op("dve", lambda e, bO=bO, h=h, c=c, ys=ys: e.tensor_scalar(out=yt[ys][:, c, h * 64:(h + 1) * 64], in0=self.ps[bO][:, c * 65:c * 65 + 64],
                                                                                         scalar1=small[:, 8 + h * 4 + c:9 + h * 4 + c], scalar2=None, op0=ALU.mult),
                                 reads=[self.t_ps[bO], t_small], writes=[t_yt[ys]])
                p.dma("pool", dap(self.dh["sc_y"], lg * 4 * 128 * D + 512 + pair * 128, [[D, 128], [128 * D, 4], [1, 128]]), yt[ys][:, :, pair * 128:(pair + 1) * 128], t_yt[ys], reads=[t_yt[ys]])

            p.barrier()
            es_att.close()
            es_sw = ExitStack()

            def stile(name, shape, dt):
                return es_sw.enter_context(nc.sbuf_tensor("B%d_%s" % (L, name), list(shape), dt))
            tile = stile
            swQ = [tile("swQ%d" % i, [64, 4, 512], BF16) for i in range(2)]; t_swQ = [Tk(), Tk()]
            swK = [tile("swK%d" % i, [64, 2, 640], BF16) for i in range(2)]; t_swK = [Tk(), Tk()]
            swV = [tile("swV%d" % i, [128, 5, 130], BF16) for i in range(2)]; t_swV = [Tk(), Tk()]
            Pc = [tile("Pc%d" % i, [128, 512], BF16) for i in range(2)]; t_Pc = [Tk(), Tk()]
            Pp = [tile("Pp%d" % i, [128, 512], BF16) for i in range(2)]; t_Pp = [Tk(), Tk()]
            ut = [tile("ut%d" % i, [128, 2, 514], F32) for i in range(2)]; t_ut = [Tk(), Tk()]
            bbt = [tile("bbt%d" % i, [128, 2, 512], F32) for i in range(2)]; t_bbt = [Tk(), Tk()]
            cvt = [tile("cvt%d" % i, [128, 2, 512], F32) for i in range(2)]; t_cvt = [Tk(), Tk()]
            if FUSED:
                swKp = tile("swKp", [64, 2, NGL, 128], BF16); t_swKp = Tk()
                swVp = tile("swVp", [128, NGL, 130], BF16); t_swVp = Tk()
                utl = tile("utl", [128, 2, NGL, 4], BF16); t_utl = Tk()
                for kvh in range(2):
                    i = TB["swK"] + kvh
                    p.dyn_dma("pool", swKp[:, kvh, :, :], self.dh["kvb_all0"], self.tabt[0:1, i:i + 1], [[512, 64], [4 * HR[0] * 512, NGL], [1, 128]], t_swKp,
                              reads=[self.t_tab], writes=[t_swKp])
                i = TB["swV"]
                p.dyn_dma("pool", swVp[:], self.dh["kvb_all0"], self.tabt[0:1, i:i + 1], [[130, 128], [4 * HR[0] * 512, NGL], [1, 130]], t_swVp,
                          reads=[self.t_tab], writes=[t_swVp])
                for ct in range(2):
                    i = TB["ut"] + ct
                    p.dyn_dma("pool", utl[:, ct, :, :], self.dh["kvb_all0"], self.tabt[0:1, i:i + 1], [[4, 128], [4 * HR[0] * 512, NGL], [1, 4]], t_utl,
                              reads=[self.t_tab], writes=[t_utl])
            ys_of = {}

            def swa_load(lg):
                s = lg % 2
                ys_of[lg] = self.rot("B_yt", 3)
                p.dma("sp", swQ[s][:], dap(self.dh["sc_swaq"], lg * 512, [[NTL, 64], [64 * NTL, 4], [1, 512]]), t_swQ[s], writes=[t_swQ[s]])
                if FUSED:
                    kl = self.dh["kvb_loc0"]
                    p.dma("sp", swK[s][:, :, 0:512], dap(kl, (lg * HR[0] + 512) * 512, [[512, 64], [64 * 512, 2], [1, 512]]), t_swK[s], writes=[t_swK[s]])
                    p.dma("sp", swV[s][:, 0:4, :], dap(kl, (lg * HR[0] + 640) * 512, [[130, 128], [128 * 130, 4], [1, 130]]), t_swV[s], writes=[t_swV[s]])
                    p.op("pool", lambda e: e.tensor_copy(out=swK[s][:, :, 512:640], in_=swKp[:, :, lg, :]), reads=[t_swKp], writes=[t_swK[s]])
                    p.op("pool", lambda e: e.tensor_copy(out=swV[s][:, 4, :], in_=swVp[:, lg, :]), reads=[t_swVp], writes=[t_swV[s]])
                else:
                    p.dma("sp", swK[s][:], dap(self.dh["swK"], lg * 640, [[NGL * 640, 64], [64 * NGL * 640, 2], [1, 640]]), t_swK[s], writes=[t_swK[s]])
                    p.dma("sp", swV[s][:], dap(self.dh["swV"], lg * 5 * 128 * 130, [[130, 128], [128 * 130, 5], [1, 130]]), t_swV[s], writes=[t_swV[s]])

            def swa_A(lg, c, idx):
                s = lg % 2
                q = idx % 2
                bC, bP = 0, 1
                for h in range(4):
                    p.op("pe", lambda e: e.matmul(self.ps[bC][:, h * 128:(h + 1) * 128], lhsT=swK[s][:, h // 2, c * 128:(c + 1) * 128],
                                                  rhs=swQ[s][:, h, c * 128:(c + 1) * 128], start=True, stop=True),
                         reads=[t_swK[s], t_swQ[s]], writes=[self.t_ps[bC]])
                for h in range(4):
                    p.op("pe", lambda e: e.matmul(self.ps[bP][:, h * 128:(h + 1) * 128], lhsT=swK[s][:, h // 2, (c + 1) * 128:(c + 2) * 128],
                                                  rhs=swQ[s][:, h, c * 128:(c + 1) * 128], start=True, stop=True),
                         reads=[t_swK[s], t_swQ[s]], writes=[self.t_ps[bP]])
                p.op("act", lambda e: e.activation(out=Pc[q][:], in_=self.ps[bC][:], func=AF.Exp), reads=[self.t_ps[bC]], writes=[t_Pc[q]])
                p.op("act", lambda e: e.activation(out=Pp[q][:], in_=self.ps[bP][:], func=AF.Exp), reads=[self.t_ps[bP]], writes=[t_Pp[q]])
                m_le4 = bass.AP(self.cb, 256, [[896, 128], [0, 4], [1, 128]])
                m_gt4 = bass.AP(self.cb, 384, [[896, 128], [0, 4], [1, 128]])
                p.op("pool", lambda e: e.tensor_tensor(out=Pc[q][:].rearrange("p (h c) -> p h c", h=4), in0=Pc[q][:].rearrange("p (h c) -> p h c", h=4),
                                                       in1=m_le4, op=ALU.mult), reads=[t_Pc[q], cb_], writes=[t_Pc[q]])
                p.op("dve", lambda e: e.tensor_tensor(out=Pp[q][:].rearrange("p (h c) -> p h c", h=4), in0=Pp[q][:].rearrange("p (h c) -> p h c", h=4),
                                                      in1=m_gt4, op=ALU.mult), reads=[t_Pp[q], cb_], writes=[t_Pp[q]])

            def swa_B(lg, c, idx):
                s = lg % 2
                q = idx % 2
                ys = ys_of[lg]
                bO = 2 + (idx % 2)
                p.op("pe", lambda e: e.matmul(self.ps[bO][:, 0:260], lhsT=self.zero_b, rhs=z4[:, 0:260], start=True, stop=False,
                                              skip_group_check=True), reads=[cb_], writes=[self.t_ps[bO]])
                for h in range(4):
                    kvh = h // 2
                    p.op("pe", lambda e: e.matmul(self.ps[bO][:, h * 65:(h + 1) * 65], lhsT=Pc[q][:, h * 128:(h + 1) * 128],
                                                  rhs=swV[s][:, c, kvh * 65:(kvh + 1) * 65], start=False, stop=False, skip_group_check=True),
                         reads=[t_Pc[q], t_swV[s]], writes=[self.t_ps[bO]])
                    p.op("pe", lambda e: e.matmul(self.ps[bO][:, h * 65:(h + 1) * 65], lhsT=Pp[q][:, h * 128:(h + 1) * 128],
                                                  rhs=swV[s][:, c + 1, kvh * 65:(kvh + 1) * 65], start=False, stop=True, skip_group_check=True),
                         reads=[t_Pp[q], t_swV[s]], writes=[self.t_ps[bO]])
                p.op("dve", lambda e: e.tensor_tensor(out=small[:, 0:4], in0=self.ps[bO][:, 0:260].rearrange("p (h c) -> p h c", h=4)[:, :, 64],
                                                      in1=esink[:], op=ALU.add), reads=[self.t_ps[bO], t_es], writes=[t_small])
                p.op("dve", lambda e: e.reciprocal(out=small[:, 4:8], in_=small[:, 0:4]), reads=[t_small], writes=[t_small])
                rec4 = bass.AP(small, 4, [[64, 128], [1, 4], [0, 64]])
                p.op("dve", lambda e: e.tensor_tensor(out=yt[ys][:, c, :].rearrange("p (h c) -> p h c", h=4),
                                                      in0=self.ps[bO][:, 0:260].rearrange("p (h c) -> p h c", h=4)[:, :, 0:64], in1=rec4, op=ALU.mult),
                     reads=[self.t_ps[bO], t_small], writes=[t_yt[ys]])
                if c == 3:
                    p.dma("pool", dap(self.dh["sc_y"], lg * 4 * 128 * D + 0, [[D, 128], [128 * D, 4], [1, 256]]), yt[ys][:], t_yt[ys], reads=[t_yt[ys]])

            def conv_compute(lg):
                s = lg % 2
                p.dma("sp", ut[s][:, :, 2:514], dap(self.dh["sc_u"], lg * 512, [[NTL, 128], [128 * NTL, 2], [1, 512]]), t_ut[s], writes=[t_ut[s]])
                p.dma("sp", bbt[s][:], dap(self.dh["sc_bb"], lg * 512, [[NTL, 128], [128 * NTL, 2], [1, 512]]), t_bbt[s], writes=[t_bbt[s]])
                if FUSED:
                    p.op("pool", lambda e: e.tensor_tensor(out=ut[s][:, :, 0:2], in0=utl[:, :, lg, 0:2], in1=utl[:, :, lg, 2:4], op=ALU.add),
                         reads=[t_utl], writes=[t_ut[s]])
                else:
                    p.dma("sp", ut[s][:, :, 0:2], dap(self.dh["utail"], lg * 2, [[NGL * 2, 128], [128 * NGL * 2, 2], [1, 2]]), t_ut[s], writes=[t_ut[s]])
                for ct in range(2):
                    p.op("dve", lambda e, s=s, ct=ct: e.tensor_scalar(out=cvt[s][:, ct, :], in0=ut[s][:, ct, 0:512], scalar1=cwb[:, ct * 3:ct * 3 + 1], scalar2=None, op0=ALU.mult),
                         reads=[t_ut[s], t_cwb], writes=[t_cvt[s]])
                    for kk in (1, 2):
                        p.op("dve", lambda e, s=s, ct=ct, kk=kk: e.scalar_tensor_tensor(out=cvt[s][:, ct, :], in0=ut[s][:, ct, kk:kk + 512], scalar=cwb[:, ct * 3 + kk:ct * 3 + kk + 1],
                                                                                       in1=cvt[s][:, ct, :], op0=ALU.mult, op1=ALU.add),
                             reads=[t_ut[s], t_cwb, t_cvt[s]], writes=[t_cvt[s]])
                    p.op("dve", lambda e, s=s, ct=ct: e.scalar_tensor_tensor(out=cvt[s][:, ct, :], in0=cvt[s][:, ct, :], scalar=cwb[:, 6 + ct:7 + ct], in1=bbt[s][:, ct, :],
                                                                            op0=ALU.add, op1=ALU.mult),
                         reads=[t_cvt[s], t_cwb, t_bbt[s]], writes=[t_cvt[s]])

            def conv_out(lg):
                s = lg % 2
                ys2 = self.rot("B_yt", 3)
                for c in range(4):
                    bT = 4 + (c % 2)
                    blk = 3 - c
                    for ct in range(2):
                        p.op("pe", lambda e, s=s, ct=ct, blk=blk, bT=bT: e.transpose(out=self.ps[bT][:, ct * 128:(ct + 1) * 128], in_=cvt[s][:, ct, blk * 128:(blk + 1) * 128],
                                                                                    identity=self.ident_f),
                             reads=[t_cvt[s], cf_], writes=[self.t_ps[bT]])
                    p.op("act", lambda e, bT=bT, c=c, ys2=ys2: e.copy(out=yt[ys2][:, c, :], in_=self.ps[bT][:, 0:256]), reads=[self.t_ps[bT]], writes=[t_yt[ys2]])
                p.dma("pool", dap(self.dh["sc_y"], lg * 4 * 128 * D + 256, [[D, 128], [128 * D, 4], [1, 256]]), yt[ys2][:], t_yt[ys2], reads=[t_yt[ys2]])


            items = [(lg, c) for lg in range(NGL) for c in range(4)]
            swa_load(0)
            conv_compute(0)
            swa_A(0, 0, 0)
            for idx, (lg, c) in enumerate(items):
                if c == 0 and lg + 1 < NGL:
                    conv_compute(lg + 1)
                if idx + 1 < len(items):
                    nlg, nc_ = items[idx + 1]
                    if nc_ == 0:
                        swa_load(nlg)
                    swa_A(nlg, nc_, idx + 1)
                swa_B(lg, c, idx)
                if c == 3:
                    conv_out(lg)

            p.barrier()
            es_sw.close()
            tile = _tile_outer

            NS = 3
            yb = [tile("yb%d" % i, [128, D], F32) for i in range(NS)]; t_yb = [Tk() for _ in range(NS)]
            gb_ = [tile("gb%d" % i, [128, D], BF16) for i in range(NS)]; t_gb = [Tk() for _ in range(NS)]
            xb = [tile("xb%d" % i, [128, D], F32) for i in range(NS)]; t_xb = [Tk() for _ in range(NS)]
            ob = [tile("ob%d" % i, [128, D], F32) for i in range(2)]; t_ob = [Tk(), Tk()]
            junk = tile("junkB", [128, D], F32); t_junk = Tk()
            junk2 = tile("junkB2", [128, D], F32); t_junk2 = Tk()
            ygT = [tile("ygT%d" % i, [128, 8, 128], BF16) for i in range(2)]; t_ygT = [Tk(), Tk()]
            sm = [tile("sm%d" % i, [128, 16], F32) for i in range(NS)]; t_sm = [Tk() for _ in range(NS)]
            sm2 = [tile("sm2_%d" % i, [128, 8], F32) for i in range(2)]; t_sm2 = [Tk(), Tk()]

            def rows_of(sl):
                lg, c = sl // 4, sl % 4
                lb = lg * 4 + (3 - c)
                return slice(sl * 128, (sl + 1) * 128), slice(lb * 128, (lb + 1) * 128)

            def S1a(sl):
                s = sl % NS
                srow, rows = rows_of(sl)
                p.dma("sp", yb[s][:], dr["sc_y"][srow, :], t_yb[s], writes=[t_yb[s]])
                p.dma("sp", gb_[s][:], dr["sc_gate"][srow, :], t_gb[s], writes=[t_gb[s]])
                p.dma("sp", xb[s][:], x_src[rows, :], t_xb[s], writes=[t_xb[s]])
                for g in range(4):
                    p.op("act", lambda e: e.activation(out=junk[:, g * 256:(g + 1) * 256], in_=yb[s][:, g * 256:(g + 1) * 256], func=AF.Square,
                                                       accum_out=sm[s][:, g:g + 1]),
                         reads=[t_yb[s]], writes=[t_junk, t_sm[s]])
                p.op("act", lambda e: e.activation(out=sm[s][:, 4:8], in_=sm[s][:, 0:4], func=AF.Sqrt, bias=self.eps_c, scale=float(1.0 / 256)),
                     reads=[t_sm[s], cf_], writes=[t_sm[s]])
                p.op("dve", lambda e: e.reciprocal(out=sm[s][:, 8:12], in_=sm[s][:, 4:8]), reads=[t_sm[s]], writes=[t_sm[s]])
                for g in range(4):
                    p.op("dve", lambda e: e.scalar_tensor_tensor(out=yb[s][:, g * 256:(g + 1) * 256], in0=yb[s][:, g * 256:(g + 1) * 256], scalar=sm[s][:, 8 + g:9 + g],
                                                                 in1=ggrp_bc[:, g * 256:(g + 1) * 256], op0=ALU.mult, op1=ALU.mult),
                         reads=[t_yb[s], t_sm[s], t_gg], writes=[t_yb[s]])
                p.op("pool", lambda e: e.tensor_tensor(out=yb[s][:], in0=yb[s][:], in1=gb_[s][:], op=ALU.mult), reads=[t_yb[s], t_gb[s]], writes=[t_yb[s]])

            def S1b(sl):
                s = sl % NS
                u = sl % 2
                for half in range(2):
                    for i in range(4):
                        kc = half * 4 + i
                        p.op("pe", lambda e: e.transpose(out=self.ps[half][:, i * 128:(i + 1) * 128], in_=yb[s][:, kc * 128:(kc + 1) * 128],
                                                         identity=self.ident_f), reads=[t_yb[s], cf_], writes=[self.t_ps[half]])
                    if half == 0:
                        p.op("act", lambda e: e.copy(out=ygT[u][:, 0:4, :], in_=self.ps[0][:].rearrange("p (a b) -> p a b", a=4)),
                             reads=[self.t_ps[0]], writes=[t_ygT[u]])
                    else:
                        p.op("dve", lambda e: e.tensor_copy(out=ygT[u][:, 4:8, :], in_=self.ps[1][:].rearrange("p (a b) -> p a b", a=4)),
                             reads=[self.t_ps[1]], writes=[t_ygT[u]])

            def S2(sl):
                s = sl % NS
                u = sl % 2
                srow, rows = rows_of(sl)
                bo = (2 + 2 * u, 3 + 2 * u)
                for half in range(2):
                    b = bo[half]
                    for kc in range(8):
                        p.op("pe", lambda e: e.matmul(self.ps[b][:], lhsT=ygT[u][:, kc, :], rhs=Wo[:, kc, half * 512:(half + 1) * 512],
                                                      start=(kc == 0), stop=(kc == 7)),
                             reads=[t_ygT[u], t_Wo], writes=[self.t_ps[b]])
                    p.op("act", lambda e: e.activation(out=junk2[:, half * 512:(half + 1) * 512], in_=self.ps[b][:], func=AF.Square,
                                                       accum_out=sm2[u][:, half:half + 1]),
                         reads=[self.t_ps[b]], writes=[t_junk2, t_sm2[u]])
                p.op("dve", lambda e: e.tensor_tensor(out=sm2[u][:, 2:3], in0=sm2[u][:, 0:1], in1=sm2[u][:, 1:2], op=ALU.add), reads=[t_sm2[u]], writes=[t_sm2[u]])
                p.op("act", lambda e: e.activation(out=sm2[u][:, 3:4], in_=sm2[u][:, 2:3], func=AF.Sqrt, bias=self.eps_c, scale=float(1.0 / D)),
                     reads=[t_sm2[u], cf_], writes=[t_sm2[u]])
                p.op("dve", lambda e: e.reciprocal(out=sm2[u][:, 4:5], in_=sm2[u][:, 3:4]), reads=[t_sm2[u]], writes=[t_sm2[u]])
                for half in range(2):
                    b = bo[half]
                    p.op("dve", lambda e: e.scalar_tensor_tensor(out=ob[u][:, half * 512:(half + 1) * 512], in0=self.ps[b][:], scalar=sm2[u][:, 4:5],
                                                                 in1=gpost_bc[:, half * 512:(half + 1) * 512], op0=ALU.mult, op1=ALU.mult),
                         reads=[self.t_ps[b], t_sm2[u], t_gp], writes=[t_ob[u]])
                p.op("pool", lambda e: e.tensor_tensor(out=ob[u][:], in0=ob[u][:], in1=xb[s][:], op=ALU.add), reads=[t_ob[u], t_xb[s]], writes=[t_ob[u]])
                p.dma("pool", x_dst[rows, :], ob[u][:], t_ob[u], reads=[t_ob[u]])

            S1a(0)
            if NBL > 1:
                S1a(1)
            S1b(0)
            for sl in range(NBL):
                if sl + 2 < NBL:
                    S1a(sl + 2)
                if sl + 1 < NBL:
                    S1b(sl + 1)
                S2(sl)
            p.barrier()

    def build(self):
        self.declare()
        self.common()
        p = self.p
        self.zero4_t = p.tile("zero4", [128, 512], BF16)
        self.zero4 = self.zero4_t[:]
        p.op("pool", lambda e: e.memset(self.zero4_t[:], 0.0), reads=[self.t_cb], writes=[self.t_cb])
        dr = self.dram
        if self.mode == "A":
            self.phaseA(0, dr["x"])
        elif self.mode == "B":
            self.phaseB(0, dr["x"], dr["out"])
        else:
            NTL, NGL = self.NTL, self.NGL
            self.tabt = p.tile("tabt", [1, 128], I32)
            self.t_tab = Tk()
            p.dma("sp", self.tabt[:], dr["tab"], self.t_tab, writes=[self.t_tab])
            self.zt = p.tile("zt", [128, 512], BF16)
            self.t_zt = Tk()
            p.op("pool", lambda e: e.memset(self.zt[:], 0.0), writes=[self.t_zt])
            NG = self.NG
            groups = [[b * NR + r for r in range(NR)] for b in range(NB_BATCH)]
            import os
            KF = os.environ.get("KFSTOP", "")
            for l in range(DEPTH):
                if KF and l >= int(KF[0]):
                    break
                x_src = dr["x"] if l == 0 else dr["sc_x1"]
                x_dst = dr["out"] if l == DEPTH - 1 else dr["sc_x1"]
                p.phase_begin()
                self.phaseA(l, x_src)
                p.barrier()
                p.phase_end()
                if KF.endswith("a"):
                    break
                p.barrier()
                if KF.endswith("g"):
                    break
                p.phase_begin()
                self.phaseB(l, x_src, x_dst)
                p.barrier()
                p.phase_end()
        p.barrier()
        p.emit()
        self.es.close()
        return self.nc


def make_consts():
    cf = np.zeros((128, 640), np.float32)
    cf[:, 0:128] = np.eye(128, dtype=np.float32)
    cf[:, 128:256] = 1.0
    cf[:, 256] = EPS
    cf[:, 257] = 1.0
    half = 16
    freqs = (10000.0 ** (-np.arange(half, dtype=np.float32) / half)).astype(np.float32)
    pidx = np.arange(128)
    cf[:, 258] = freqs[(pidx % 32) % 16]
    cf[:, 259] = np.where((pidx % 32) < 16, -1.0, 1.0)
    cb = np.zeros((128, 896), np.float32)
    cb[:, 768:896] = 1.0
    j = np.arange(128)[:, None]
    i = np.arange(128)[None, :]
    cb[:, 0:128] = -1.0 * (j >= i)
    cb[:, 128:256] = (j < i)
    cb[:, 256:384] = (j <= i)
    cb[:, 384:512] = (j > i)
    for m in (64, 65, 96, 97):
        cb[:, 512 + m] = 1.0
    return cf, cb.astype(ml_dtypes.bfloat16)


def f_order_rows(ngl):
    idx = []
    for lg in range(ngl):
        for c in range(4):
            b = lg * 4 + (3 - c)
            idx.extend(range(b * 128, (b + 1) * 128))
    return np.array(idx)


def prep_weights(inp):
    w = {}
    w["gpre_t"] = np.ascontiguousarray(inp["norm_pre"].reshape(DEPTH, 8, 128).transpose(0, 2, 1))
    w_in = inp["w_in"]
    sw = np.concatenate([w_in[:, :, C_CKR + 16:C_CKR + 32], w_in[:, :, C_CKR:C_CKR + 16]], axis=2)
    w["w_in_x"] = np.ascontiguousarray(np.concatenate([w_in, sw], axis=2))
    w["gcq_t"] = np.ascontiguousarray(inp["mla_q_norm"].reshape(DEPTH, 2, 128).transpose(0, 2, 1))
    uq = inp["mla_w_uq"].reshape(DEPTH, 256, 4, 96)
    nope = uq[..., 0:64].reshape(DEPTH, 256, 256)
    rope = uq[..., 64:96]
    rope_sw = np.concatenate([rope[..., 16:32], rope[..., 0:16]], axis=-1)
    w["w_uq_x"] = np.ascontiguousarray(np.concatenate([nope, rope.reshape(DEPTH, 256, 128), rope_sw.reshape(DEPTH, 256, 128)], axis=2))
    w["gkv_t"] = np.ascontiguousarray(inp["mla_kv_norm"].reshape(DEPTH, 1, 128).transpose(0, 2, 1))
    ukv = inp["mla_w_ukv"].reshape(DEPTH, 128, 4, 128)
    w["w_ukv_x"] = np.ascontiguousarray(np.concatenate([ukv[..., 0:64].reshape(DEPTH, 128, 256), ukv[..., 64:128].reshape(DEPTH, 128, 256)], axis=2))
    w["sinks"] = np.ascontiguousarray(inp["attn_sinks"])
    w["convw_t"] = np.ascontiguousarray(inp["conv_w"].reshape(DEPTH, 3, 2, 128).transpose(0, 3, 2, 1).reshape(DEPTH, 128, 6))
    w["convb_t"] = np.ascontiguousarray(inp["conv_b"].reshape(DEPTH, 2, 128).transpose(0, 2, 1))
    w["ggrp"] = np.ascontiguousarray(inp["group_norm"])
    w["w_out"] = np.ascontiguousarray(inp["w_out"])
    w["gpost"] = np.ascontiguousarray(inp["norm_post"])
    return {k: np.asarray(v, np.float32) for k, v in w.items()}


_CACHE = {}


def get_prog(mode, S):
    key = (mode, S)
    if key not in _CACHE:
        _CACHE[key] = Builder(mode, S).build()
    return _CACHE[key]


A_WKEYS = ("gpre_t", "w_in_x", "gcq_t", "w_uq_x", "gkv_t", "w_ukv_x")
B_WKEYS = ("sinks", "convw_t", "convb_t", "ggrp", "w_out", "gpost")
SC_KEYS = ("sc_sbq", "sc_mlaq", "sc_swaq", "sc_u", "sc_bb", "sc_gate")


def arrange_B(resA, S):
    NG = S // 512
    NGL = NG // NR
    NTL = NGL * 512
    NBK = S // 128
    NPOS = NBK + 12
    outs = []
    for core in range(NB_BATCH * NR):
        b, r = divmod(core, NR)
        sbK = np.zeros((256, NPOS * 128), ml_dtypes.bfloat16)
        mlaK = np.zeros((384, NPOS * 128), ml_dtypes.bfloat16)
        sbV = np.zeros((NPOS * 128, 256), ml_dtypes.bfloat16)
        mlaV = np.zeros((NPOS * 128, 260), ml_dtypes.bfloat16)
        for qg in range(NPOS // 4):
            G = NG - 1 + r - qg
            if 0 <= G < NG:
                src = resA[b * NR + (G % NR)]
                lgs = G // NR
                kv = src["kvb_loc"]
                flat = kv.reshape(-1)
                sbK[:, qg * 512:(qg + 1) * 512] = kv[R_SBK:R_SBK + 256, lgs * 512:(lgs + 1) * 512]
                mlaK[:, qg * 512:(qg + 1) * 512] = kv[R_MLAK:R_MLAK + 384, lgs * 512:(lgs + 1) * 512]
                o = R_SBV * NTL + lgs * 512 * 256
                sbV[qg * 512:(qg + 1) * 512, :] = flat[o:o + 512 * 256].reshape(512, 256)
                o = R_MLAV * NTL + lgs * 512 * 260
                mlaV[qg * 512:(qg + 1) * 512, :] = flat[o:o + 512 * 260].reshape(512, 260)
        swK = np.zeros((128, NGL, 640), ml_dtypes.bfloat16)
        swV = np.zeros((NGL, 5, 128, 130), ml_dtypes.bfloat16)
        utail = np.zeros((256, NGL, 2), np.float32)
        own = resA[core]
        flat_own = own["kvb_loc"].reshape(-1)
        for lg in range(NGL):
            G = NR * lg + r
            swK[:, lg, 0:512] = own["kvb_loc"][R_SWAK:R_SWAK + 128, lg * 512:(lg + 1) * 512]
            o = R_SWAV * NTL + lg * 512 * 130
            swV[lg, 0:4] = flat_own[o:o + 512 * 130].reshape(4, 128, 130)
            if G > 0:
                src = resA[b * NR + ((G - 1) % NR)]
                lgs = (G - 1) // NR
                swK[:, lg, 512:640] = src["kvb_loc"][R_SWAK:R_SWAK + 128, lgs * 512:lgs * 512 + 128]
                o = R_SWAV * NTL + lgs * 512 * 130
                swV[lg, 4] = src["kvb_loc"].reshape(-1)[o:o + 128 * 130].reshape(128, 130)
                utail[:, lg, :] = src["kvf_loc"].reshape(256, NGL, 2)[:, lgs, :]
        outs.append(dict(sbK=sbK, sbV=sbV, mlaK=mlaK, mlaV=mlaV, swK=swK.reshape(128, NGL * 640),
                         swV=swV.reshape(NGL * 5 * 128, 130), utail=utail.reshape(256, NGL * 2)))
    return outs


def run_forward(inp, S):
    NG = S // 512
    NGL = NG // NR
    NTL = NGL * 512
    ncores = NB_BATCH * NR
    x = np.asarray(inp["x"], np.float32)
    pos = np.asarray(inp["positions"], np.int32)
    W = prep_weights(inp)
    cf, cb = make_consts()
    ford = f_order_rows(NGL)
    tok = []
    for core in range(ncores):
        b, r = divmod(core, NR)
        t = np.concatenate([np.arange((NR * lg + r) * 512, (NR * lg + r + 1) * 512) for lg in range(NGL)])
        tok.append((b, t))
    xs = [np.ascontiguousarray(x[b][t]) for (b, t) in tok]
    ps = [np.ascontiguousarray(pos[b][t][ford][None, :]) for (b, t) in tok]
    progA = get_prog("A", S)
    progB = get_prog("B", S)
    for l in range(DEPTH):
        wa = {k: W[k][l:l + 1] for k in A_WKEYS}
        wb = {k: W[k][l:l + 1] for k in B_WKEYS}
        inA = [dict(consts_f=cf, consts_b=cb, pos=ps[c], x=xs[c], **wa) for c in range(ncores)]
        resA = run_bass_kernel_spmd(progA, inA, core_ids=list(range(ncores))).results
        arr = arrange_B(resA, S)
        inB = []
        for c in range(ncores):
            d = dict(consts_f=cf, consts_b=cb, x=xs[c], **wb)
            for k in SC_KEYS:
                d[k] = resA[c][k]
            d.update(arr[c])
            inB.append(d)
        resB = run_bass_kernel_spmd(progB, inB, core_ids=list(range(ncores))).results
        xs = [np.asarray(resB[c]["out"], np.float32) for c in range(ncores)]
    out = np.zeros_like(x)
    for c, (b, t) in enumerate(tok):
        out[b][t] = xs[c]
    return out


def make_table(r, S):
    tab = np.zeros((1, 128), np.int32)
    Y0, Y1 = HR[0] * 512, HR[1] * 512
    for h in range(4):
        tab[0, 0 + h] = r * Y0 + (0 + h * 64) * 512
        tab[0, 4 + h] = r * Y0 + 256 * 512 + h * 128 * 256
        tab[0, 8 + h] = r * Y1 + (0 + h * 96) * 512
        tab[0, 12 + h] = r * Y1 + 384 * 512 + h * 128 * 260
    for kvh in range(2):
        tab[0, 16 + kvh] = (2 + r) * Y0 + (512 + kvh * 64) * 512
        tab[0, 19 + kvh] = (2 + r) * Y0 + 770 * 512 + kvh * 128 * 4
    tab[0, 18] = (2 + r) * Y0 + 640 * 512
    return tab


def run_fused(inp, S):
    NG = S // 512
    NGL = NG // NR
    ncores = NB_BATCH * NR
    x = np.asarray(inp["x"], np.float32)
    pos = np.asarray(inp["positions"], np.int32)
    W = prep_weights(inp)
    cf, cb = make_consts()
    ford = f_order_rows(NGL)
    kaug = np.zeros((64, 512), ml_dtypes.bfloat16)
    kaug[0, :] = 1.0
    kaug[32, :] = 1.0
    in_maps = []
    tok = []
    for core in range(ncores):
        b, r = divmod(core, NR)
        t = np.concatenate([np.arange((NR * lg + r) * 512, (NR * lg + r + 1) * 512) for lg in range(NGL)])
        tok.append((b, t))
        d = dict(consts_f=cf, consts_b=cb, kaug=kaug, x=np.ascontiguousarray(x[b][t]),
                 pos=np.ascontiguousarray(pos[b][t][ford][None, :]), tab=make_table(r, S))
        d.update(W)
        in_maps.append(d)
    res = run_bass_kernel_spmd(get_prog("F", S), in_maps, core_ids=list(range(ncores))).results
    out = np.zeros_like(x)
    for c, (b, t) in enumerate(tok):
        out[b][t] = np.asarray(res[c]["out"], np.float32)
    return out


def kernel(**inputs):
    inp = {k: np.asarray(v) for k, v in inputs.items()}
    S = inp["x"].shape[1]
    return run_fused(inp, S)
```

```python
import math
from contextlib import ExitStack

import numpy as np
import ml_dtypes

import concourse.bass as bass
import concourse.mybir as mybir
from concourse.bass_utils import run_bass_kernel_spmd

F32 = mybir.dt.float32
BF16 = mybir.dt.bfloat16
I32 = mybir.dt.int32
AF = mybir.ActivationFunctionType
ALU = mybir.AluOpType

D = 1024
DIN = 3488
DINX = 3520
DEPTH = 2
NB_BATCH = 2
NR = 4
EPS = 1e-6
KVROWS = 1416
C_AQ, C_AK, C_AV = 0, 256, 384
C_BB, C_BC, C_BX = 512, 768, 1024
C_CQ, C_CKV, C_CKR = 1280, 1536, 1664
C_DQ, C_DK, C_DV = 1696, 1952, 2208
C_GATE = 2464
C_KRSW = 3488
R_SBK, R_MLAK, R_SWAK, R_SBV, R_MLAV, R_SWAV, R_UT = 0, 256, 640, 768, 1024, 1284, 1414


HR = (772, 644)


def kvmap(oldrow):
    if oldrow < 256:
        return 0, oldrow
    if oldrow < 640:
        return 1, oldrow - 256
    if oldrow < 768:
        return 0, 512 + oldrow - 640
    if oldrow < 1024:
        return 0, 256 + oldrow - 768
    if oldrow < 1284:
        return 1, 384 + oldrow - 1024
    if oldrow < 1414:
        return 0, 640 + oldrow - 1284
    return 0, 770 + oldrow - 1414


def own_groups(r, ngl):
    return [r + NR * i for i in range(ngl)]


class Tk:
    __slots__ = ("w", "r", "dsem", "dcnt", "name", "excl")

    def __init__(self, name="", excl=False):
        self.excl = excl
        self.w = None
        self.r = {}
        self.dsem = None
        self.dcnt = 0
        self.name = name


class _Rec:
    def __init__(self):
        self.call = None

    def __getattr__(self, name):
        def f(*a, **kw):
            self.call = (name, a, kw)
            return self
        return f


class Prog:
    ENG = ("pe", "act", "dve", "pool", "sp")

    def __init__(self, nc, es):
        self.nc = nc
        self.es = es
        self.streams = {e: [] for e in self.ENG}
        self.seen = {e: {} for e in self.ENG}
        self.sems = {}
        self.cnt = {}
        self.nsem = 0
        self.all_tickets = {}
        for e in ("pe", "act", "dve", "pool"):
            self._newsem(e)

    def _newsem(self, key):
        s = self.es.enter_context(self.nc.semaphore("s%d" % self.nsem))
        self.nsem += 1
        self.sems[key] = s
        self.cnt[key] = 0
        return s

    def _dsem(self, st, q):
        cls = "sw" if q == "pool" else "hw"
        if st.dsem is not None:
            assert st.dsem[0] == cls, "tile %s used by both DMA classes" % st.name
            return
        free = getattr(self, "free_dsems", {}).get(cls)
        if free:
            st.dsem = free.pop()
        else:
            st.dsem = (cls, self.nsem)
            self._newsem(st.dsem)
        if getattr(self, "phase_dsems", None) is not None:
            self.phase_dsems.append(st.dsem)

    def phase_begin(self):
        self.phase_dsems = []

    def phase_end(self):
        if not hasattr(self, "free_dsems"):
            self.free_dsems = {"sw": [], "hw": []}
        for k in self.phase_dsems:
            self.free_dsems[k[0]].append(k)
        self.phase_dsems = None

    def tile(self, name, shape, dt):
        return self.es.enter_context(self.nc.sbuf_tensor(name, list(shape), dt))

    def psum(self, name):
        return self.es.enter_context(self.nc.psum_tensor(name, [128, 512], F32))

    def _collect(self, eng, reads, writes):
        deps = {}

        def add(tk):
            if tk is None:
                return
            k, v = tk
            if deps.get(k, 0) < v:
                deps[k] = v

        for t in reads:
            add(t.w)
            if t.excl:
                for k, v in t.r.items():
                    if k != eng:
                        add((k, v))
        for t in writes:
            add(t.w)
            for k, v in t.r.items():
                add((k, v))
        waits = []
        seen = self.seen[eng]
        for k, v in deps.items():
            if k == eng and eng == "pe":
                continue
            if seen.get(k, 0) >= v:
                continue
            seen[k] = v
            waits.append((self.sems[k], v))
        return waits

    def _record(self, tk, reads, writes):
        k, v = tk
        for t in reads:
            if t.r.get(k, 0) < v:
                t.r[k] = v
        for t in writes:
            t.w = tk
            t.r = {}
        self.all_tickets[k] = v

    def op(self, eng, fn, reads=(), writes=()):
        waits = self._collect(eng, reads, writes)
        self.cnt[eng] += 1
        tk = (eng, self.cnt[eng])
        rec = _Rec()
        fn(rec)
        name, a, kw = rec.call
        self.streams[eng].append((waits, (lambda e, name=name, a=a, kw=kw: getattr(e, name)(*a, **kw)), (self.sems[eng], 1)))
        self._record(tk, reads, writes)
        return tk

    def dma(self, q, out, in_, st, reads=(), writes=()):
        self._dsem(st, q)
        waits = self._collect(q, reads, writes)
        self.cnt[st.dsem] += 16
        tk = (st.dsem, self.cnt[st.dsem])
        self.streams[q].append((waits, lambda e, o=out, i=in_: e.dma_start(out=o, in_=i),
                                (self.sems[st.dsem], 16)))
        self._record(tk, reads, writes)
        return tk

    def dyn_dma(self, q, out, handle, tab_ap, dims, st, reads=(), writes=()):
        self._dsem(st, q)
        if not hasattr(self, "regs"):
            self.regs = {}
            self.regi = {}
        if q not in self.regs:
            eng = {"sp": self.nc.sync, "act": self.nc.scalar, "pool": self.nc.gpsimd}[q]
            self.regs[q] = [self.es.enter_context(eng.register("dr%s%d" % (q, i))) for i in range(2)]
            self.regi[q] = 0
        reg = self.regs[q][self.regi[q] % 2]
        self.regi[q] += 1
        waits = self._collect(q, reads, writes)
        self.streams[q].append((waits, lambda e, reg=reg, t=tab_ap: e.reg_load(reg, t), None))
        self.cnt[st.dsem] += 16
        tk = (st.dsem, self.cnt[st.dsem])
        src = bass.AP(handle, reg, [list(d) for d in dims])
        self.streams[q].append(([], lambda e, o=out, i=src: e.dma_start(out=o, in_=i), (self.sems[st.dsem], 16)))
        self._record(tk, reads, writes)
        return tk

    def raw(self, eng, fn, key, reads=(), writes=()):
        if key not in self.sems:
            self._newsem(key)
        waits = self._collect(eng, reads, writes)
        self.cnt[key] += 1
        tk = (key, self.cnt[key])
        self.streams[eng].append((waits, fn, (self.sems[key], 1)))
        self._record(tk, reads, writes)
        return tk

    def barrier(self):
        for e in self.ENG:
            waits = []
            for k, v in self.all_tickets.items():
                if self.seen[e].get(k, 0) < v:
                    self.seen[e][k] = v
                    waits.append((self.sems[k], v))
            if waits:
                self.streams[e].append((waits, None, None))

    def emit(self):
        nc = self.nc

        def run(e, lst):
            for waits, fn, inc in lst:
                for s, v in waits:
                    e.wait_ge(s, v)
                if fn is not None:
                    ins = fn(e)
                    if inc is not None:
                        ins.then_inc(inc[0], inc[1])

        with nc.Block() as block:
            @block.tensor
            def _(e):
                run(e, self.streams["pe"])

            @block.scalar
            def _(e):
                run(e, self.streams["act"])

            @block.vector
            def _(e):
                run(e, self.streams["dve"])

            @block.gpsimd
            def _(e):
                run(e, self.streams["pool"])

            @block.sync
            def _(e):
                run(e, self.streams["sp"])


def dap(h, off, dims):
    return bass.AP(h, off, [list(d) for d in dims])


class Builder:
    def __init__(self, mode, S):
        self.mode = mode
        self.S = S
        self.NG = S // 512
        self.NGL = self.NG // NR
        self.NTL = self.NGL * 512
        self.NBL = self.NGL * 4
        self.nc = bass.Bass("TRN2", target_bir_lowering=False)
        self.es = ExitStack()
        self.p = Prog(self.nc, self.es)
        self.dram = {}
        self.dh = {}
        self.rr = {}

    def din(self, name, shape, dt):
        self.dh[name] = self.nc.dram_tensor(name, list(shape), dt, kind="ExternalInput")
        self.dram[name] = self.dh[name].ap()

    def dout(self, name, shape, dt):
        self.dh[name] = self.nc.dram_tensor(name, list(shape), dt, kind="ExternalOutput")
        self.dram[name] = self.dh[name].ap()

    def dint(self, name, shape, dt):
        self.dh[name] = self.nc.dram_tensor(name, list(shape), dt)
        self.dram[name] = self.dh[name].ap()

    def declare(self):
        mode, NTL, NBL, NGL = self.mode, self.NTL, self.NBL, self.NGL
        L = DEPTH if mode == 'F' else 1
        self.din("consts_f", [128, 640], F32)
        self.din("consts_b", [128, 896], BF16)
        self.din("kaug", [64, 512], BF16)
        if mode in ("A", "F"):
            self.din("pos", [1, NTL], I32)
            self.din("gpre_t", [L, 128, 8], F32)
            self.din("w_in_x", [L, D, DINX], F32)
            self.din("gcq_t", [L, 128, 2], F32)
            self.din("w_uq_x", [L, 256, 512], F32)
            self.din("gkv_t", [L, 128, 1], F32)
            self.din("w_ukv_x", [L, 128, 512], F32)
        if mode in ("B", "F"):
            self.din("sinks", [L, 4], F32)
            self.din("convw_t", [L, 128, 6], F32)
            self.din("convb_t", [L, 128, 2], F32)
            self.din("ggrp", [L, D], F32)
            self.din("w_out", [L, D, D], F32)
            self.din("gpost", [L, D], F32)
        self.din("x", [NTL, D], F32)
        scr = [("sc_sbq", [256, NTL], BF16), ("sc_mlaq", [384, NTL], BF16), ("sc_swaq", [256, NTL], BF16),
               ("sc_u", [256, NTL], F32), ("sc_bb", [256, NTL], F32), ("sc_gate", [NBL * 128, D], BF16)]
        if mode == "A":
            for n, s, d in scr:
                self.dout(n, s, d)
            self.dout("kvb_loc", [KVROWS, NTL], BF16)
            self.dout("kvf_loc", [256, NGL * 2], F32)
        elif mode == "B":
            for n, s, d in scr:
                self.din(n, s, d)
            NPOS = self.S // 128 + 12
            self.din("sbK", [256, NPOS * 128], BF16)
            self.din("sbV", [NPOS * 128, 256], BF16)
            self.din("mlaK", [384, NPOS * 128], BF16)
            self.din("mlaV", [NPOS * 128, 260], BF16)
            self.din("swK", [128, NGL * 640], BF16)
            self.din("swV", [NGL * 5 * 128, 130], BF16)
            self.din("utail", [256, NGL * 2], F32)
            self.dint("sc_y", [NBL * 128, D], F32)
            self.dout("out", [NTL, D], F32)
        else:
            for n, s, d in scr:
                self.dint(n, s, d)
            for f in range(2):
                self.dint("kvb_loc%d" % f, [NGL * HR[f], 512], BF16)
                self.dint("kvb_all%d" % f, [(self.NG + 6) * HR[f], 512], BF16)
            self.dh["kvb_loc"] = None
            self.din("tab", [1, 128], I32)
            self.dint("sc_y", [NBL * 128, D], F32)
            self.dint("sc_x1", [NTL, D], F32)
            self.dint("sc_cos", [128, NTL], F32)
            self.dint("sc_sin", [128, NTL], F32)
            self.dout("out", [NTL, D], F32)

    def common(self):
        p = self.p
        self.cf = p.tile("cf", [128, 640], F32)
        self.cb = p.tile("cb", [128, 896], BF16)
        self.t_cf = Tk("cf")
        self.t_cb = Tk("cb")
        p.dma("sp", self.cf[:], self.dram["consts_f"], self.t_cf, writes=[self.t_cf])
        p.dma("sp", self.cb[:], self.dram["consts_b"], self.t_cb, writes=[self.t_cb])
        self.ident_f = self.cf[:, 0:128]
        self.ones_f = self.cf[:, 128:256]
        self.eps_c = self.cf[:, 256:257]
        self.one_c = self.cf[:, 257:258]
        self.freq_c = self.cf[:, 258:259]
        self.sgn_c = self.cf[:, 259:260]
        self.negtri = self.cb[:, 0:128]
        self.m_lt = self.cb[:, 128:256]
        self.m_le = self.cb[:, 256:384]
        self.m_gt = self.cb[:, 384:512]
        self.ones98 = self.cb[:, 512:610]
        self.zero_b = self.cb[:, 640:768]
        self.ones_b = self.cb[:, 768:896]
        self.pp = [self.es.enter_context(self.nc.psum_tensor("pp%d" % i, [128, 1024], F32)) for i in range(4)]
        self.ps = [self.pp[i // 2][:, (i % 2) * 512:(i % 2 + 1) * 512] for i in range(8)]
        self.t_ps = [Tk("ps%d" % i, excl=True) for i in range(8)]

    def rot(self, name, n):
        i = self.rr.get(name, 0)
        self.rr[name] = i + 1
        return i % n

    def phaseA(self, L, x_src):
        p, nc = self.p, self.nc
        NTL, NGL, NBL = self.NTL, self.NGL, self.NBL
        dr = self.dram
        cf_, cb_ = self.t_cf, self.t_cb
        with ExitStack() as es:
            def tile(name, shape, dt):
                return es.enter_context(nc.sbuf_tensor("A%d_%s" % (L, name), list(shape), dt))

            Wp = tile("Wp", [128, 8, DINX], BF16); t_Wp = Tk(); t_Wp2 = Tk()
            wuq = tile("wuq", [128, 2, 512], BF16); t_wuq = Tk()
            wukv = tile("wukv", [128, 512], BF16); t_wukv = Tk()
            gcols = tile("gcols", [128, 16], F32); t_gc = Tk()
            wst = [tile("wst%d" % i, [128, 1760], F32) for i in range(2)]; t_wst = [Tk() for _ in range(2)]
            cosT = tile("cosT", [128, NTL], F32); t_cos = Tk()
            sinS = tile("sinS", [128, NTL], F32); t_sin = Tk()
            posi = tile("posi", [128, NTL], I32); t_posi = Tk()
            ang = tile("ang", [128, NTL], F32); t_ang = Tk()
            tr1 = tile("tr1", [128, NTL], F32); t_tr1 = Tk()
            tr2 = tile("tr2", [128, NTL], F32); t_tr2 = Tk()
            tri = tile("tri", [128, NTL], I32); t_tri = Tk()
            xs = [tile("xs%d" % i, [128, D], F32) for i in range(2)]; t_xs = [Tk(), Tk()]
            xn = [tile("xn%d" % i, [128, D], F32) for i in range(2)]; t_xn = [Tk(), Tk()]
            junk = tile("junk", [128, D], F32); t_junk = Tk()
            sst = [tile("ss%d" % i, [128, 4], F32) for i in range(2)]; t_ss = [Tk(), Tk()]
            xnT = tile("xnT", [128, 8, 512], BF16); t_xnT = Tk()
            cqT = tile("cqT", [128, 2, 512], BF16); t_cqT = Tk()
            sq = [tile("sq%d" % i, [128, 512], BF16) for i in range(2)]; t_sq = [Tk(), Tk()]
            ckvT = tile("ckvT", [128, 512], BF16); t_ckvT = Tk()
            rq_bc = tile("rq_bc", [128, 512], F32); t_rq = Tk()
            rkv_bc = tile("rkv_bc", [128, 512], F32); t_rkv = Tk()
            rkv_t = tile("rkv_t", [128, 8], F32); t_rkvt = Tk()
            tmpf = [tile("tmpf%d" % i, [128, 512], F32) for i in range(3)]; t_tmpf = [Tk(), Tk(), Tk()]
            NSTB, NSTF = 10, 5
            stb = [tile("stb%d" % i, [128, 1024], BF16) for i in range(NSTB)]; t_stb = [Tk() for _ in range(NSTB)]
            stf = [tile("stf%d" % i, [128, 512], F32) for i in range(NSTF)]; t_stf = [Tk() for _ in range(NSTF)]

            p.dma("sp", gcols[:, 0:8], dr["gpre_t"][L], t_gc, writes=[t_gc])
            p.dma("sp", gcols[:, 8:10], dr["gcq_t"][L], t_gc, writes=[t_gc])
            p.dma("sp", gcols[:, 10:11], dr["gkv_t"][L], t_gc, writes=[t_gc])
            k = 0
            for kc in range(8):
                for hf in range(2):
                    s = k % 2; k += 1
                    p.dma("sp", wst[s][:], dr["w_in_x"][L, kc * 128:(kc + 1) * 128, hf * 1760:(hf + 1) * 1760],
                          t_wst[s], writes=[t_wst[s]])
                    if s % 2 == 0:
                        p.op("dve", lambda e, o=Wp[:, kc, hf * 1760:(hf + 1) * 1760], i=wst[s][:], sc=gcols[:, kc:kc + 1]:
                             e.tensor_scalar(out=o, in0=i, scalar1=sc, scalar2=None, op0=ALU.mult),
                             reads=[t_wst[s], t_gc], writes=[t_Wp])
                    else:
                        p.op("act", lambda e, o=Wp[:, kc, hf * 1760:(hf + 1) * 1760], i=wst[s][:], sc=gcols[:, kc:kc + 1]:
                             e.activation(out=o, in_=i, func=AF.Copy, scale=sc),
                             reads=[t_wst[s], t_gc], writes=[t_Wp2])
            for kc in range(2):
                s = k % 2; k += 1
                p.dma("sp", wst[s][:, 0:512], dr["w_uq_x"][L, kc * 128:(kc + 1) * 128, :], t_wst[s], writes=[t_wst[s]])
                p.op("dve", lambda e, o=wuq[:, kc, :], i=wst[s][:, 0:512], sc=gcols[:, 8 + kc:9 + kc]:
                     e.tensor_scalar(out=o, in0=i, scalar1=sc, scalar2=float(96 ** -0.5), op0=ALU.mult, op1=ALU.mult),
                     reads=[t_wst[s], t_gc], writes=[t_wuq])
            s = k % 2; k += 1
            p.dma("sp", wst[s][:, 0:512], dr["w_ukv_x"][L], t_wst[s], writes=[t_wst[s]])
            p.op("dve", lambda e, o=wukv[:], i=wst[s][:, 0:512], sc=gcols[:, 10:11]:
                 e.tensor_scalar(out=o, in0=i, scalar1=sc, scalar2=None, op0=ALU.mult),
                 reads=[t_wst[s], t_gc], writes=[t_wukv])

            import os
            KSTOP = float(os.environ.get("KSTOP", "99"))
            if KSTOP <= 1:
                p.barrier(); return
            REUSE_TABLES = (self.mode == "F" and L > 0)
            if REUSE_TABLES:
                p.dma("sp", cosT[:], dr["sc_cos"], t_cos, writes=[t_cos])
                p.dma("sp", sinS[:], dr["sc_sin"], t_sin, writes=[t_sin])
            else:
                p.dma("sp", posi[:], dap(self.dh["pos"], 0, [[0, 128], [1, NTL]]), t_posi, writes=[t_posi])
                p.op("dve", lambda e: e.tensor_copy(out=ang[:], in_=posi[:]), reads=[t_posi], writes=[t_ang])
                p.op("dve", lambda e: e.tensor_scalar(out=ang[:], in0=ang[:], scalar1=self.freq_c, scalar2=None, op0=ALU.mult),
                     reads=[t_ang, cf_], writes=[t_ang])
                TWO_PI = 2.0 * math.pi
                C_HI = 6.28125
                C_LO = TWO_PI - C_HI

                def sin_of(dst, t_dst, shift):
                    p.op("dve", lambda e: e.tensor_scalar(out=tr1[:], in0=ang[:], scalar1=float(shift), scalar2=float(1.0 / TWO_PI),
                                                          op0=ALU.add, op1=ALU.mult), reads=[t_ang], writes=[t_tr1])
                    p.op("dve", lambda e: e.tensor_copy(out=tri[:], in_=tr1[:]), reads=[t_tr1], writes=[t_tri])
                    p.op("dve", lambda e: e.tensor_copy(out=tr1[:], in_=tri[:]), reads=[t_tri], writes=[t_tr1])
                    p.op("dve", lambda e: e.tensor_scalar(out=tr2[:], in0=ang[:], scalar1=float(shift), scalar2=None, op0=ALU.add),
                         reads=[t_ang], writes=[t_tr2])
                    p.op("dve", lambda e: e.scalar_tensor_tensor(out=tr2[:], in0=tr1[:], scalar=float(-C_HI), in1=tr2[:],
                                                                 op0=ALU.mult, op1=ALU.add), reads=[t_tr1, t_tr2], writes=[t_tr2])
                    p.op("dve", lambda e: e.scalar_tensor_tensor(out=tr2[:], in0=tr1[:], scalar=float(-C_LO), in1=tr2[:],
                                                                 op0=ALU.mult, op1=ALU.add), reads=[t_tr1, t_tr2], writes=[t_tr2])
                    p.op("dve", lambda e: e.tensor_scalar(out=tr1[:], in0=tr2[:], scalar1=float(math.pi), scalar2=float(-TWO_PI),
                                                          op0=ALU.is_gt, op1=ALU.mult), reads=[t_tr2], writes=[t_tr1])
                    p.op("dve", lambda e: e.tensor_tensor(out=tr2[:], in0=tr2[:], in1=tr1[:], op=ALU.add),
                         reads=[t_tr1, t_tr2], writes=[t_tr2])
                    p.op("dve", lambda e: e.tensor_scalar(out=tr1[:], in0=tr2[:], scalar1=float(-math.pi), scalar2=float(TWO_PI),
                                                          op0=ALU.is_lt, op1=ALU.mult), reads=[t_tr2], writes=[t_tr1])
                    p.op("dve", lambda e: e.tensor_tensor(out=tr2[:], in0=tr2[:], in1=tr1[:], op=ALU.add),
                         reads=[t_tr1, t_tr2], writes=[t_tr2])
                    p.op("dve", lambda e: e.tensor_scalar(out=tr2[:], in0=tr2[:], scalar1=float(-3.14159), scalar2=float(3.14159),
                                                          op0=ALU.max, op1=ALU.min), reads=[t_tr2], writes=[t_tr2])
                    p.op("act", lambda e: e.activation(out=dst[:], in_=tr2[:], func=AF.Sin), reads=[t_tr2], writes=[t_dst])

                sin_of(cosT, t_cos, math.pi / 2.0)
                sin_of(sinS, t_sin, 0.0)
                p.op("dve", lambda e: e.tensor_scalar(out=sinS[:], in0=sinS[:], scalar1=self.sgn_c, scalar2=None, op0=ALU.mult),
                     reads=[t_sin, cf_], writes=[t_sin])
                if self.mode == "F":
                    p.dma("pool", dr["sc_cos"], cosT[:], t_cos, reads=[t_cos])
                    p.dma("pool", dr["sc_sin"], sinS[:], t_sin, reads=[t_sin])

            if KSTOP <= 2:
                p.barrier(); return
            if self.mode == "F" and L == 0:
                for f in range(2):
                    for m in (0, 1, 2, self.NG + 3, self.NG + 4, self.NG + 5):
                        for r0 in range(0, HR[f], 128):
                            rows = min(128, HR[f] - r0)
                            p.dma("pool", dap(self.dh["kvb_all%d" % f], (m * HR[f] + r0) * 512, [[512, rows], [1, 512]]),
                                  self.zt[0:rows, :], self.t_zt, reads=[self.t_zt])
            PSA = [2, 3, 4, 5, 6, 7]

            def acc_bank():
                return PSA[self.rot("A_acc", len(PSA))]

            def stage_b():
                i = self.rot("A_stb", NSTB)
                return stb[i], t_stb[i]

            def stage_f():
                i = self.rot("A_stf", NSTF)
                return stf[i], t_stf[i]

            def evac(eng_hint, out, in_, reads, writes, scale=None):
                e = eng_hint if eng_hint else ("act" if self.rot("A_ev", 2) == 0 else "dve")
                if e == "act":
                    if scale is None:
                        p.op("act", lambda en: en.copy(out=out, in_=in_), reads=reads, writes=writes)
                    else:
                        p.op("act", lambda en: en.mul(out=out, in_=in_, mul=float(scale)), reads=reads, writes=writes)
                else:
                    if scale is None:
                        p.op("dve", lambda en: en.tensor_copy(out=out, in_=in_), reads=reads, writes=writes)
                    else:
                        p.op("dve", lambda en: en.tensor_scalar(out=out, in0=in_, scalar1=float(scale), scalar2=None,
                                                                op0=ALU.mult), reads=reads, writes=writes)

            def fmm(col0, M, bank=None):
                b = acc_bank() if bank is None else bank
                for kc in range(8):
                    p.op("pe", lambda e, b=b, kc=kc: e.matmul(self.ps[b][0:M, :], lhsT=Wp[:, kc, col0:col0 + M],
                                                             rhs=xnT[:, kc, :], start=(kc == 0), stop=(kc == 7)),
                         reads=[t_Wp, t_Wp2, t_xnT], writes=[self.t_ps[b]])
                return b

            FUSED = (self.mode == "F")

            def store2d(dst_h, row0, M, lg, src, t_src, prow0=0, q="pool"):
                if FUSED and dst_h is kvb:
                    f, rr = kvmap(row0)
                    d_ = dr["kvb_loc%d" % f]
                    p.dma(q, d_[lg * HR[f] + rr:lg * HR[f] + rr + M, 0:512], src[prow0:prow0 + M, 0:512], t_src, reads=[t_src])
                else:
                    p.dma(q, dst_h[row0:row0 + M, lg * 512:(lg + 1) * 512], src[prow0:prow0 + M, 0:512], t_src, reads=[t_src])

            def vdst(R, lg, blk, W):
                if FUSED:
                    f, rr = kvmap(R)
                    return dap(self.dh["kvb_loc%d" % f], (lg * HR[f] + rr) * 512 + blk * 128 * W, [[W, 128], [1, W]])
                return dap(self.dh["kvb_loc"], R * NTL + (lg * 4 + blk) * 128 * W, [[W, 128], [1, W]])

            def store_nat(dst_h, row0, lg, src, t_src):
                for c in range(4):
                    p.dma("pool", dst_h[row0:row0 + 128, lg * 512 + (3 - c) * 128:lg * 512 + (4 - c) * 128], src[:, c * 128:(c + 1) * 128],
                          t_src, reads=[t_src])

            def ones_cols(st, t_st, nh):
                p.op("pool", lambda e: e.memset(st[:, 0:nh * 65].rearrange("p (h c) -> p h c", h=nh)[:, :, 64:65], 1.0), writes=[t_st])

            kvb = dr.get("kvb_loc", "KVB")
            for lg in range(NGL):
                for blk in range(4):
                    lb = lg * 4 + blk
                    s = self.rot("A_x", 2)
                    p.dma("sp", xs[s][:], x_src[lb * 128:(lb + 1) * 128, :], t_xs[s], writes=[t_xs[s]])
                    p.op("act", lambda e, s=s: e.activation(out=junk[:], in_=xs[s][:], func=AF.Square, accum_out=sst[s][:, 0:1]),
                         reads=[t_xs[s]], writes=[t_junk, t_ss[s]])
                    p.op("act", lambda e, s=s: e.activation(out=sst[s][:, 1:2], in_=sst[s][:, 0:1], func=AF.Sqrt,
                                                           bias=self.eps_c, scale=float(1.0 / D)),
                         reads=[t_ss[s], cf_], writes=[t_ss[s]])
                    p.op("dve", lambda e, s=s: e.reciprocal(out=sst[s][:, 2:3], in_=sst[s][:, 1:2]), reads=[t_ss[s]], writes=[t_ss[s]])
                    p.op("dve", lambda e, s=s: e.tensor_scalar(out=xn[s][:], in0=xs[s][:], scalar1=sst[s][:, 2:3], scalar2=None,
                                                               op0=ALU.mult), reads=[t_xs[s], t_ss[s]], writes=[t_xn[s]])
                    for half in range(2):
                        for i in range(4):
                            kc = half * 4 + i
                            p.op("pe", lambda e, s=s, kc=kc, half=half, i=i: e.transpose(
                                out=self.ps[half][:, i * 128:(i + 1) * 128], in_=xn[s][:, kc * 128:(kc + 1) * 128],
                                identity=self.ident_f), reads=[t_xn[s], cf_], writes=[self.t_ps[half]])
                        evac(None, xnT[:, half * 4:half * 4 + 4, (3 - blk) * 128:(4 - blk) * 128],
                             self.ps[half][:].rearrange("p (a b) -> p a b", a=4), [self.t_ps[half]], [t_xnT])

                if KSTOP <= 3:
                    continue
                for (c0, scale, dst, r0) in ((C_AQ, 0.125, dr["sc_swaq"], 0), (C_AQ + 128, 0.125, dr["sc_swaq"], 128),
                                             (C_AK, None, kvb, R_SWAK),
                                             (C_DQ, 0.125, dr["sc_sbq"], 0), (C_DQ + 128, 0.125, dr["sc_sbq"], 128),
                                             (C_DK, None, kvb, R_SBK), (C_DK + 128, None, kvb, R_SBK + 128)):
                    b = fmm(c0, 128)
                    st, t_st = stage_b()
                    evac(None, st[:, 0:512], self.ps[b][:], [self.t_ps[b]], [t_st], scale)
                    store2d(dst, r0, 128, lg, st, t_st)
                if KSTOP <= 4:
                    continue
                for ct in range(2):
                    b = fmm(C_BB + ct * 128, 128)
                    st, t_st = stage_f()
                    evac(None, st[:], self.ps[b][:], [self.t_ps[b]], [t_st])
                    store_nat(dr["sc_bb"], ct * 128, lg, st, t_st)
                    b1 = fmm(C_BC + ct * 128, 128)
                    b2 = fmm(C_BX + ct * 128, 128)
                    i = self.rot("A_tmpf", 3)
                    evac("act", tmpf[i][:], self.ps[b1][:], [self.t_ps[b1]], [t_tmpf[i]])
                    st, t_st = stage_f()
                    p.op("dve", lambda e, st=st, i=i, b2=b2: e.tensor_tensor(out=st[:], in0=tmpf[i][:], in1=self.ps[b2][:], op=ALU.mult),
                         reads=[t_tmpf[i], self.t_ps[b2]], writes=[t_st])
                    store_nat(dr["sc_u"], ct * 128, lg, st, t_st)
                    if FUSED:
                        tl, t_tl = stage_b()
                        p.op("dve", lambda e: e.tensor_copy(out=tl[:, 0:2], in_=st[:, 126:128]), reads=[t_st], writes=[t_tl])
                        p.op("dve", lambda e: e.tensor_tensor(out=tl[:, 2:4], in0=st[:, 126:128], in1=tl[:, 0:2], op=ALU.subtract),
                             reads=[t_st, t_tl], writes=[t_tl])
                        p.dma("pool", dap(self.dh["kvb_loc0"], (lg * HR[0] + 770) * 512 + ct * 128 * 4, [[4, 128], [1, 4]]),
                              tl[:, 0:4], t_tl, reads=[t_tl])
                    else:
                        p.dma("pool", dr["kvf_loc"][ct * 128:(ct + 1) * 128, lg * 2:lg * 2 + 2], st[:, 126:128], t_st, reads=[t_st])
                if KSTOP <= 5:
                    continue
                for kc in range(2):
                    b = fmm(C_CQ + kc * 128, 128)
                    evac("dve", cqT[:, kc, :], self.ps[b][:], [self.t_ps[b]], [t_cqT])
                    p.op("act", lambda e, b=b, kc=kc: e.activation(out=sq[kc][:], in_=self.ps[b][:], func=AF.Square),
                         reads=[self.t_ps[b]], writes=[t_sq[kc]])
                b = acc_bank()
                for kc in range(2):
                    p.op("pe", lambda e, b=b, kc=kc: e.matmul(self.ps[b][:], lhsT=self.ones_b, rhs=sq[kc][:], start=(kc == 0), stop=(kc == 1)),
                         reads=[t_sq[kc], cb_], writes=[self.t_ps[b]])
                p.op("act", lambda e, b=b: e.activation(out=rq_bc[:], in_=self.ps[b][:], func=AF.Sqrt, bias=self.eps_c, scale=float(1.0 / 256)),
                     reads=[self.t_ps[b], cf_], writes=[t_rq])
                p.op("dve", lambda e: e.reciprocal(out=rq_bc[:], in_=rq_bc[:]), reads=[t_rq], writes=[t_rq])
                if KSTOP <= 5.1:
                    continue
                b = fmm(C_CKV, 128)
                evac("dve", ckvT[:], self.ps[b][:], [self.t_ps[b]], [t_ckvT])
                p.op("act", lambda e, b=b: e.activation(out=sq[0][:], in_=self.ps[b][:], func=AF.Square),
                     reads=[self.t_ps[b]], writes=[t_sq[0]])
                b = acc_bank()
                p.op("pe", lambda e, b=b: e.matmul(self.ps[b][:], lhsT=self.ones_b, rhs=sq[0][:], start=True, stop=True),
                     reads=[t_sq[0], cb_], writes=[self.t_ps[b]])
                p.op("act", lambda e, b=b: e.activation(out=rkv_bc[:], in_=self.ps[b][:], func=AF.Sqrt, bias=self.eps_c, scale=float(1.0 / 128)),
                     reads=[self.t_ps[b], cf_], writes=[t_rkv])
                p.op("dve", lambda e: e.reciprocal(out=rkv_bc[:], in_=rkv_bc[:]), reads=[t_rkv], writes=[t_rkv])
                if KSTOP <= 5.2:
                    continue
                b = acc_bank()
                for blk in range(4):
                    p.op("pe", lambda e, b=b, blk=blk: e.matmul(self.ps[b][:, 2 * blk:2 * blk + 2], lhsT=sq[0][:, blk * 128:(blk + 1) * 128],
                                                               rhs=self.ones_b[:, 0:2], start=True, stop=True),
                         reads=[t_sq[0], cb_], writes=[self.t_ps[b]])
                p.op("act", lambda e, b=b: e.activation(out=rkv_t[:], in_=self.ps[b][:, 0:8], func=AF.Sqrt, bias=self.eps_c, scale=float(1.0 / 128)),
                     reads=[self.t_ps[b], cf_], writes=[t_rkvt])
                p.op("dve", lambda e: e.reciprocal(out=rkv_t[:], in_=rkv_t[:]), reads=[t_rkvt], writes=[t_rkvt])
                if KSTOP <= 6:
                    continue
                b1 = fmm(C_CKR, 32)
                b2 = fmm(C_KRSW, 32)
                i1 = self.rot("A_tmpf", 3)
                p.op("dve", lambda e, i1=i1, b1=b1: e.tensor_tensor(out=tmpf[i1][0:32, :], in0=self.ps[b1][0:32, :],
                                                                    in1=cosT[0:32, lg * 512:(lg + 1) * 512], op=ALU.mult),
                     reads=[self.t_ps[b1], t_cos], writes=[t_tmpf[i1]])
                i2 = self.rot("A_tmpf", 3)
                p.op("dve", lambda e, i2=i2, b2=b2: e.tensor_tensor(out=tmpf[i2][0:32, :], in0=self.ps[b2][0:32, :],
                                                                    in1=sinS[0:32, lg * 512:(lg + 1) * 512], op=ALU.mult),
                     reads=[self.t_ps[b2], t_sin], writes=[t_tmpf[i2]])
                st, t_st = stage_b()
                p.op("dve", lambda e, st=st, i1=i1, i2=i2: e.tensor_tensor(out=st[0:32, 0:512], in0=tmpf[i1][0:32, :], in1=tmpf[i2][0:32, :], op=ALU.add),
                     reads=[t_tmpf[i1], t_tmpf[i2]], writes=[t_st])
                for h in range(4):
                    store2d(kvb, R_MLAK + h * 96 + 64, 32, lg, st, t_st)
                if KSTOP <= 7:
                    continue
                for pair in range(2):
                    b = acc_bank()
                    for kc in range(2):
                        p.op("pe", lambda e, b=b, kc=kc, pair=pair: e.matmul(self.ps[b][:], lhsT=wuq[:, kc, pair * 128:(pair + 1) * 128],
                                                                            rhs=cqT[:, kc, :], start=(kc == 0), stop=(kc == 1)),
                             reads=[t_wuq, t_cqT], writes=[self.t_ps[b]])
                    st, t_st = stage_b()
                    p.op("dve", lambda e, st=st, b=b: e.tensor_tensor(out=st[:, 0:512], in0=self.ps[b][:], in1=rq_bc[:], op=ALU.mult),
                         reads=[self.t_ps[b], t_rq], writes=[t_st])
                    for i in range(2):
                        store2d(dr["sc_mlaq"], (2 * pair + i) * 96, 64, lg, st, t_st, prow0=i * 64)
                    b = acc_bank()
                    p.op("pe", lambda e, b=b, pair=pair: e.matmul(self.ps[b][:], lhsT=wukv[:, pair * 128:(pair + 1) * 128], rhs=ckvT[:],
                                                                 start=True, stop=True), reads=[t_wukv, t_ckvT], writes=[self.t_ps[b]])
                    st, t_st = stage_b()
                    p.op("dve", lambda e, st=st, b=b: e.tensor_tensor(out=st[:, 0:512], in0=self.ps[b][:], in1=rkv_bc[:], op=ALU.mult),
                         reads=[self.t_ps[b], t_rkv], writes=[t_st])
                    for i in range(2):
                        store2d(kvb, R_MLAK + (2 * pair + i) * 96, 64, lg, st, t_st, prow0=i * 64)
                b1 = acc_bank()
                b2 = acc_bank()
                for (b, c0) in ((b1, 256), (b2, 384)):
                    for kc in range(2):
                        p.op("pe", lambda e, b=b, kc=kc, c0=c0: e.matmul(self.ps[b][:], lhsT=wuq[:, kc, c0:c0 + 128], rhs=cqT[:, kc, :],
                                                                        start=(kc == 0), stop=(kc == 1)),
                             reads=[t_wuq, t_cqT], writes=[self.t_ps[b]])
                i1 = self.rot("A_tmpf", 3)
                p.op("dve", lambda e, i1=i1, b1=b1: e.tensor_tensor(out=tmpf[i1][:], in0=self.ps[b1][:], in1=cosT[:, lg * 512:(lg + 1) * 512], op=ALU.mult),
                     reads=[self.t_ps[b1], t_cos], writes=[t_tmpf[i1]])
                i2 = self.rot("A_tmpf", 3)
                p.op("dve", lambda e, i2=i2, b2=b2: e.tensor_tensor(out=tmpf[i2][:], in0=self.ps[b2][:], in1=sinS[:, lg * 512:(lg + 1) * 512], op=ALU.mult),
                     reads=[self.t_ps[b2], t_sin], writes=[t_tmpf[i2]])
                p.op("pool", lambda e, i1=i1, i2=i2: e.tensor_tensor(out=tmpf[i1][:], in0=tmpf[i1][:], in1=tmpf[i2][:], op=ALU.add),
                     reads=[t_tmpf[i1], t_tmpf[i2]], writes=[t_tmpf[i1]])
                st, t_st = stage_b()
                p.op("dve", lambda e, st=st, i1=i1: e.tensor_tensor(out=st[:, 0:512], in0=tmpf[i1][:], in1=rq_bc[:], op=ALU.mult),
                     reads=[t_tmpf[i1], t_rq], writes=[t_st])
                for h in range(4):
                    store2d(dr["sc_mlaq"], h * 96 + 64, 32, lg, st, t_st, prow0=h * 32)

                if KSTOP <= 8:
                    continue
                for blk in range(4):
                    lb = lg * 4 + blk
                    tok = slice(blk * 128, (blk + 1) * 128)
                    b = acc_bank()
                    p.op("pe", lambda e, b=b, tok=tok: e.matmul(self.ps[b][:, 0:256], lhsT=ckvT[:, tok], rhs=wukv[:, 256:512], start=True, stop=True),
                         reads=[t_ckvT, t_wukv], writes=[self.t_ps[b]])
                    st, t_st = stage_b()
                    p.op("dve", lambda e, st=st, b=b, blk=blk: e.tensor_scalar(
                        out=st[:, 0:260].rearrange("p (h c) -> p h c", h=4)[:, :, 0:64], in0=self.ps[b][:, 0:256].rearrange("p (h c) -> p h c", h=4),
                        scalar1=rkv_t[:, 2 * blk:2 * blk + 1], scalar2=None, op0=ALU.mult),
                         reads=[self.t_ps[b], t_rkvt], writes=[t_st])
                    ones_cols(st, t_st, 4)
                    p.dma("pool", vdst(R_MLAV, lg, blk, 260), st[:, 0:260], t_st, reads=[t_st])
                    b = acc_bank()
                    for (c0, n, o0) in ((C_AV, 128, 0), (C_DV, 256, 128)):
                        for kc in range(8):
                            p.op("pe", lambda e, b=b, kc=kc, c0=c0, n=n, o0=o0, tok=tok: e.matmul(
                                self.ps[b][:, o0:o0 + n], lhsT=xnT[:, kc, tok], rhs=Wp[:, kc, c0:c0 + n], start=(kc == 0), stop=(kc == 7)),
                                 reads=[t_Wp, t_Wp2, t_xnT], writes=[self.t_ps[b]])
                    st, t_st = stage_b()
                    evac(None, st[:, 0:130].rearrange("p (h c) -> p h c", h=2)[:, :, 0:64], self.ps[b][:, 0:128].rearrange("p (h c) -> p h c", h=2),
                         [self.t_ps[b]], [t_st])
                    evac(None, st[:, 256:512], self.ps[b][:, 128:384], [self.t_ps[b]], [t_st])
                    ones_cols(st, t_st, 2)
                    p.dma("pool", vdst(R_SWAV, lg, blk, 130), st[:, 0:130], t_st, reads=[t_st])
                    p.dma("pool", vdst(R_SBV, lg, blk, 256), st[:, 256:512], t_st, reads=[t_st])
                    st, t_st = stage_b()
                    for half in range(2):
                        b = acc_bank()
                        c0 = C_GATE + half * 512
                        for kc in range(8):
                            p.op("pe", lambda e, b=b, kc=kc, c0=c0, tok=tok: e.matmul(self.ps[b][:], lhsT=xnT[:, kc, tok], rhs=Wp[:, kc, c0:c0 + 512],
                                                                                     start=(kc == 0), stop=(kc == 7)),
                                 reads=[t_Wp, t_Wp2, t_xnT], writes=[self.t_ps[b]])
                        p.op("act", lambda e, st=st, b=b, half=half: e.activation(out=st[:, half * 512:(half + 1) * 512], in_=self.ps[b][:], func=AF.Silu),
                             reads=[self.t_ps[b]], writes=[t_st])
                    p.dma("pool", dr["sc_gate"][lb * 128:(lb + 1) * 128, :], st[:, :], t_st, reads=[t_st])
                if FUSED:
                    self.gather_group(lg)
            p.barrier()

    def gather_group(self, lg):
        p = self.p
        groups = [[b * NR + r for r in range(NR)] for b in range(NB_BATCH)]
        waits = []
        for k, v in p.all_tickets.items():
            if isinstance(k, tuple) and k[0] in ("sw", "hw") and p.seen["pool"].get(k, 0) < v:
                p.seen["pool"][k] = v
                waits.append((p.sems[k], v))
        p.streams["pool"].append((waits, None, None))
        t_g = Tk()
        for f in range(2):
            p.raw("pool", lambda e, lg=lg, f=f: e.collective_compute(
                "AllGather", ALU.bypass, replica_groups=groups,
                ins=[self.dh["kvb_loc%d" % f].ap()[lg * HR[f]:(lg + 1) * HR[f], :].opt()],
                outs=[self.dh["kvb_all%d" % f].ap()[(3 + 4 * lg) * HR[f]:(7 + 4 * lg) * HR[f], :].opt()]),
                  ("cc", lg * 2 + f), writes=[t_g])

    def phaseB(self, L, x_src, x_dst):
        p, nc = self.p, self.nc
        S, NG, NTL, NGL, NBL = self.S, self.NG, self.NTL, self.NGL, self.NBL
        NBK = S // 128
        NPOS = NBK + 12
        dr = self.dram
        cf_, cb_ = self.t_cf, self.t_cb
        z4 = self.zero4
        with ExitStack() as es:
            def tile(name, shape, dt):
                return es.enter_context(nc.sbuf_tensor("B%d_%s" % (L, name), list(shape), dt))

            yt = [tile("yt%d" % i, [128, 4, 256], F32) for i in range(3)]; t_yt = [Tk(), Tk(), Tk()]
            small = tile("small", [128, 64], F32); t_small = Tk()
            Wo = tile("Wo", [128, 8, D], BF16); t_Wo = Tk()
            wst = [tile("wst%d" % i, [128, D], F32) for i in range(2)]; t_wst = [Tk(), Tk()]
            ggrp_bc = tile("ggrp", [128, D], F32); t_gg = Tk()
            gpost_bc = tile("gpost", [128, D], F32); t_gp = Tk()
            cwb = tile("cwb", [128, 8], F32); t_cwb = Tk()
            esink = tile("esink", [128, 4], F32); t_es = Tk()
            es_att = ExitStack()

            def atile(name, shape, dt):
                return es_att.enter_context(nc.sbuf_tensor("B%d_%s" % (L, name), list(shape), dt))
            _tile_outer = tile
            tile = atile
            KtAll = tile("KtAll", [128, 4, NPOS * 128], BF16)
            Kt = [KtAll[:, h, :] for h in range(4)]; t_K = [Tk() for _ in range(4)]
            Vt = tile("Vt", [128, NPOS, 264], BF16); t_V = Tk()
            Qa = [tile("Qa%d" % h, [128, 512], BF16) for h in range(4)]
            Qm = [tile("Qm%d" % h, [128, 512], BF16) for h in range(4)]
            t_Qm = [Tk() for _ in range(4)]; t_Qx = [Tk() for _ in range(4)]
            NE = 2
            Et2 = [tile("E%d" % i, [128, 2, 512], F32) for i in range(NE)]; t_E = [Tk() for _ in range(NE)]
            Lt2 = [tile("Lp%d" % i, [128, 2, 512], BF16) for i in range(NE)]; t_L = [Tk() for _ in range(NE)]
            NA = 3
            At2 = [tile("At%d" % i, [128, 2, 512], BF16) for i in range(NA)]; t_A = [Tk() for _ in range(NA)]
            tile = _tile_outer

            p.dma("sp", ggrp_bc[:], dap(self.dh["ggrp"], L * D, [[0, 128], [1, D]]), t_gg, writes=[t_gg])
            p.dma("sp", gpost_bc[:], dap(self.dh["gpost"], L * D, [[0, 128], [1, D]]), t_gp, writes=[t_gp])
            p.dma("sp", cwb[:, 0:6], dr["convw_t"][L], t_cwb, writes=[t_cwb])
            p.dma("sp", cwb[:, 6:8], dr["convb_t"][L], t_cwb, writes=[t_cwb])
            p.dma("sp", esink[:], dap(self.dh["sinks"], L * 4, [[0, 128], [1, 4]]), t_es, writes=[t_es])
            p.op("act", lambda e: e.activation(out=esink[:], in_=esink[:], func=AF.Exp), reads=[t_es], writes=[t_es])

            NCH = 4

            FUSED = (self.mode == "F")
            NQG = NPOS // 4
            FAM = dict(sbK=0, sbV=0, mlaK=1, mlaV=1)
            DQ = "sp" if L == 0 else "act"
            TB = dict(sbK=0, sbV=4, mlaK=8, mlaV=12, swK=16, swV=18, ut=19)

            def load_K(name, nrows, heads=(0, 1, 2, 3)):
                if FUSED:
                    for h in heads:
                        i = TB[name] + h
                        f = FAM[name]
                        p.dyn_dma(DQ, KtAll[0:nrows, h, :], self.dh["kvb_all%d" % f], self.tabt[0:1, i:i + 1],
                                  [[512, nrows], [HR[f] * 512, NQG], [1, 512]], t_K[h], reads=[self.t_tab],
                                  writes=[t_K[h]] + ([t_Kx[h]] if nrows > 64 else []))
                    return
                w = NPOS * 128
                cw = (w + NCH - 1) // NCH
                for h in range(4):
                    for c in range(NCH):
                        c0, c1 = c * cw, min(w, (c + 1) * cw)
                        p.dma("sp", Kt[h][0:nrows, c0:c1], dr[name][h * nrows:(h + 1) * nrows, c0:c1], t_K[h], writes=[t_K[h]])

            def load_V(name, wcols):
                if FUSED:
                    for blk in range(4):
                        i = TB[name] + blk
                        f = FAM[name]
                        p.dyn_dma(DQ, Vt[:, blk:NPOS:4, 0:wcols], self.dh["kvb_all%d" % f], self.tabt[0:1, i:i + 1],
                                  [[wcols, 128], [HR[f] * 512, NQG], [1, wcols]], t_V, reads=[self.t_tab], writes=[t_V])
                    return
                step = 8
                for q0 in range(0, NPOS, step):
                    q1 = min(NPOS, q0 + step)
                    p.dma("sp", Vt[:, q0:q1, 0:wcols], dap(self.dh[name], q0 * 128 * wcols, [[wcols, 128], [128 * wcols, q1 - q0], [1, wcols]]),
                          t_V, writes=[t_V])

            def pos_of(lg, d):
                if FUSED:
                    return 4 * (4 * lg + 3 - d // 4) + d % 4
                return NBK - 4 - 16 * lg + d

            def nstep(lg):
                return 16 * lg + 16

            def acols(d):
                if d < 4:
                    return (d + 1) * 128, True
                return 512, False

            t_Kx = [Tk() for _ in range(4)]
            for h in range(4):
                en = "dve" if h % 2 == 0 else "pool"
                p.op(en, lambda e: e.memset(Kt[h][64:128, :], 0.0), writes=[t_Kx[h]])
                p.op(en, lambda e: e.memset(Kt[h][64:65, :], 1.0), writes=[t_Kx[h]])
                p.op(en, lambda e: e.memset(Kt[h][96:97, :], 1.0), writes=[t_Kx[h]])
            load_K("sbK", 64, (0, 1))
            load_V("sbV", 256)
            load_K("sbK", 64, (2, 3))

            for h in range(4):
                p.op("pool", lambda e, h=h: e.memset(Qm[h][64:128, :], 0.0), writes=[t_Qm[h]])
                p.op("pool", lambda e, h=h: e.memset(Qa[h][64:128, :], 0.0), writes=[t_Qx[h]])
            t_Qd = [Tk() for _ in range(4)]
            for pair in range(2):
              if pair == 1:
                load_K("mlaK", 96, (0, 1))
              for lg in range(NGL):
                ys = self.rot("B_yt", 3)
                if True:
                    hs = (2 * pair, 2 * pair + 1)
                    bank = {}
                    for i, h in enumerate(hs):
                        bank[h] = dict(A=i, B=2 + i, C=4 + i, O=6 + i)
                        qsrc = dr["sc_sbq"][h * 64:(h + 1) * 64, lg * 512:(lg + 1) * 512]
                        p.dma("sp", Qm[h][0:64, :], qsrc, t_Qm[h], writes=[t_Qm[h]])
                        p.dma("sp", Qa[h][0:64, :], qsrc, t_Qd[h], writes=[t_Qd[h]])
                        p.op("pool", lambda e, h=h: e.memset(Qa[h][64:98, :], 0.0), writes=[t_Qx[h]])
                        bC, bO = bank[h]["C"], bank[h]["O"]
                        p.op("pe", lambda e, bC=bC: e.matmul(self.ps[bC][0:98, :], lhsT=self.ones98, rhs=z4, start=True, stop=False, skip_group_check=True),
                             reads=[cb_], writes=[self.t_ps[bC]])
                        p.op("pe", lambda e, bO=bO: e.matmul(self.ps[bO][:, 0:256], lhsT=self.zero_b, rhs=z4[:, 0:256], start=True, stop=False, skip_group_check=True),
                             reads=[cb_], writes=[self.t_ps[bO]])
                    ND = nstep(lg)
                    slot = {}
                    aslot = {}

                    def mm1(d):
                        n, dg = acols(d)
                        P_ = pos_of(lg, d)
                        for h in hs:
                            bA = bank[h]["A"]
                            p.op("pe", lambda e, h=h, bA=bA, P_=P_, n=n: e.matmul(self.ps[bA][:, 0:n], lhsT=Kt[h][:, P_ * 128:(P_ + 1) * 128],
                                                                                rhs=Qm[h][:, 0:n], start=True, stop=True),
                                 reads=[t_K[h], t_Kx[h], t_Qm[h]], writes=[self.t_ps[bA]])

                    def EL(d):
                        n, dg = acols(d)
                        s = self.rot("B_E", 2)
                        slot[d] = s
                        kA = bank[hs[0]]["A"] // 2
                        p.op("act", lambda e: e.activation(out=Et2[s][:, :, 0:n], in_=self.pp[kA][:].rearrange("p (h c) -> p h c", h=2)[:, :, 0:n], func=AF.Exp),
                             reads=[self.t_ps[2 * kA], self.t_ps[2 * kA + 1]], writes=[t_E[s]])
                        p.op("act", lambda e: e.activation(out=Lt2[s][:, :, 0:n], in_=Et2[s][:, :, 0:n], func=AF.Ln, bias=1.0),
                             reads=[t_E[s]], writes=[t_L[s]])
                        if dg:
                            for hi in range(2):
                                p.op("pool", lambda e: e.tensor_tensor(out=Lt2[s][:, hi, n - 128:n], in0=Lt2[s][:, hi, n - 128:n], in1=self.m_lt, op=ALU.mult),
                                     reads=[t_L[s], cb_], writes=[t_L[s]])

                    def mm2tc(d):
                        n, dg = acols(d)
                        P_ = pos_of(lg, d)
                        last = (d == ND - 1)
                        s = slot[d]
                        for hi, h in enumerate(hs):
                            bB, bC = bank[h]["B"], bank[h]["C"]
                            p.op("pe", lambda e, h=h, bB=bB, P_=P_, n=n: e.matmul(self.ps[bB][:, 0:n], lhsT=Kt[h][:, P_ * 128:(P_ + 1) * 128],
                                                                                rhs=Qa[h][:, 0:n], start=True, stop=False),
                                 reads=[t_K[h], t_Kx[h], t_Qd[h], t_Qx[h]], writes=[self.t_ps[bB]])
                            p.op("pe", lambda e, bB=bB, s=s, n=n: e.matmul(self.ps[bB][:, 0:n], lhsT=self.negtri, rhs=Lt2[s][:, hi, 0:n], start=False, stop=True),
                                 reads=[t_L[s], cb_], writes=[self.t_ps[bB]])
                            if not last:
                                p.op("pe", lambda e, bC=bC, s=s, n=n: e.matmul(self.ps[bC][0:98, 0:n], lhsT=self.ones98, rhs=Lt2[s][:, hi, 0:n],
                                                                              start=False, stop=False, skip_group_check=True),
                                     reads=[t_L[s], cb_], writes=[self.t_ps[bC]])
                        if not last:
                            for h in hs:
                                bC = bank[h]["C"]
                                p.op("dve", lambda e, h=h, bC=bC, n=n: e.tensor_scalar(out=Qa[h][64:98, 0:n], in0=self.ps[bC][64:98, 0:n], scalar1=-1.0,
                                                                                      scalar2=None, op0=ALU.mult),
                                     reads=[self.t_ps[bC]], writes=[t_Qx[h]])
                                p.op("dve", lambda e, h=h, bC=bC, n=n: e.scalar_tensor_tensor(out=Qa[h][96:98, 0:n], in0=self.ps[bC][96:98, 0:n], scalar=-1.0,
                                                                                             in1=Qa[h][96:98, 0:n], op0=ALU.mult, op1=ALU.subtract),
                                     reads=[self.t_ps[bC], t_Qx[h]], writes=[t_Qx[h]])

                    def Bexp(d):
                        n, dg = acols(d)
                        a = self.rot("B_A", NA)
                        aslot[d] = a
                        kB = bank[hs[0]]["B"] // 2
                        p.op("act", lambda e: e.activation(out=At2[a][:, :, 0:n], in_=self.pp[kB][:].rearrange("p (h c) -> p h c", h=2)[:, :, 0:n], func=AF.Exp),
                             reads=[self.t_ps[2 * kB], self.t_ps[2 * kB + 1]], writes=[t_A[a]])
                        if dg:
                            for hi in range(2):
                                p.op("pool", lambda e: e.tensor_tensor(out=At2[a][:, hi, n - 128:n], in0=At2[a][:, hi, n - 128:n], in1=self.m_lt, op=ALU.mult),
                                     reads=[t_A[a], cb_], writes=[t_A[a]])

                    def PV(d):
                        n, dg = acols(d)
                        P_ = pos_of(lg, d)
                        last = (d == ND - 1)
                        a = aslot[d]
                        for hi, h in enumerate(hs):
                            bO = bank[h]["O"]
                            for c in range(n // 128):
                                p.op("pe", lambda e, h=h, bO=bO, a=a, c=c, P_=P_: e.matmul(self.ps[bO][:, c * 64:(c + 1) * 64], lhsT=At2[a][:, hi, c * 128:(c + 1) * 128],
                                                                                         rhs=Vt[:, P_, h * 64:(h + 1) * 64], start=False, stop=last,
                                                                                         skip_group_check=True),
                                     reads=[t_A[a], t_V], writes=[self.t_ps[bO]])

                    mm1(0)
                    EL(0)
                    if ND > 1:
                        mm1(1)
                    for d in range(ND):
                        mm2tc(d)
                        if d >= 1:
                            PV(d - 1)
                        if d + 1 < ND:
                            EL(d + 1)
                        if d + 2 < ND:
                            mm1(d + 2)
                        Bexp(d)
                    PV(ND - 1)
                    for h in hs:
                        bO = bank[h]["O"]
                        p.op("dve", lambda e, h=h, bO=bO, ys=ys: e.tensor_copy(out=yt[ys][:, :, h * 64:(h + 1) * 64],
                                                                               in_=self.ps[bO][:, 0:256].rearrange("p (j c) -> p j c", j=4)),
                             reads=[self.t_ps[bO]], writes=[t_yt[ys]])
                p.dma("pool", dap(self.dh["sc_y"], lg * 4 * 128 * D + 768 + pair * 128, [[D, 128], [128 * D, 4], [1, 128]]), yt[ys][:, :, pair * 128:(pair + 1) * 128], t_yt[ys], reads=[t_yt[ys]])

            load_V("mlaV", 260)
            load_K("mlaK", 96, (2, 3))
            for kc in range(8):
                s = kc % 2
                p.dma("sp", wst[s][:], dr["w_out"][L, kc * 128:(kc + 1) * 128, :], t_wst[s], writes=[t_wst[s]])
                p.op("dve", lambda e, s=s, kc=kc: e.tensor_copy(out=Wo[:, kc, :], in_=wst[s][:]), reads=[t_wst[s]], writes=[t_Wo])
            for pair in range(2):
              for lg in range(NGL):
                ys = self.rot("B_yt", 3)
                if True:
                    hs = (2 * pair, 2 * pair + 1)
                    bank = {}
                    for i, h in enumerate(hs):
                        bank[h] = dict(S=(i, 2 + i, 6 + i), O=4 + i)
                        p.dma("sp", Qa[h][0:96, :], dr["sc_mlaq"][h * 96:(h + 1) * 96, lg * 512:(lg + 1) * 512], t_Qm[h], writes=[t_Qm[h], t_Qx[h], t_Qd[h]])
                        bO = bank[h]["O"]
                        p.op("pe", lambda e, bO=bO: e.matmul(self.ps[bO][:, 0:260], lhsT=self.zero_b, rhs=z4[:, 0:260], start=True, stop=False, skip_group_check=True),
                             reads=[cb_], writes=[self.t_ps[bO]])
                    ND = nstep(lg)

                    def mmS(d):
                        n, dg = acols(d)
                        P_ = pos_of(lg, d)
                        for h in hs:
                            bS = bank[h]["S"][d % 3]
                            p.op("pe", lambda e, h=h, bS=bS, P_=P_, n=n: e.matmul(self.ps[bS][:, 0:n], lhsT=Kt[h][0:96, P_ * 128:(P_ + 1) * 128],
                                                                                rhs=Qa[h][0:96, 0:n], start=True, stop=True),
                                 reads=[t_K[h], t_Qm[h]], writes=[self.t_ps[bS]])

                    aslot = {}

                    def Pexp(d):
                        n, dg = acols(d)
                        a = self.rot("B_A", NA)
                        aslot[d] = a
                        kS = (0, 1, 3)[d % 3]
                        p.op("act", lambda e: e.activation(out=At2[a][:, :, 0:n], in_=self.pp[kS][:].rearrange("p (h c) -> p h c", h=2)[:, :, 0:n], func=AF.Exp),
                             reads=[self.t_ps[2 * kS], self.t_ps[2 * kS + 1]], writes=[t_A[a]])
                        if dg:
                            for hi in range(2):
                                p.op("pool", lambda e: e.tensor_tensor(out=At2[a][:, hi, n - 128:n], in0=At2[a][:, hi, n - 128:n], in1=self.m_le, op=ALU.mult),
                                     reads=[t_A[a], cb_], writes=[t_A[a]])

                    def PVm(d):
                        n, dg = acols(d)
                        P_ = pos_of(lg, d)
                        last = (d == ND - 1)
                        a = aslot[d]
                        for hi, h in enumerate(hs):
                            bO = bank[h]["O"]
                            for c in range(n // 128):
                                p.op("pe", lambda e: e.matmul(self.ps[bO][:, c * 65:(c + 1) * 65], lhsT=At2[a][:, hi, c * 128:(c + 1) * 128],
                                                              rhs=Vt[:, P_, h * 65:(h + 1) * 65], start=False, stop=last, skip_group_check=True),
                                     reads=[t_A[a], t_V], writes=[self.t_ps[bO]])

                    mmS(0)
                    if ND > 1:
                        mmS(1)
                    for d in range(ND):
                        if d + 2 < ND:
                            mmS(d + 2)
                        if d >= 1:
                            PVm(d - 1)
                        Pexp(d)
                    PVm(ND - 1)
                    for h in hs:
                        bO = bank[h]["O"]
                        p.op("dve", lambda e, bO=bO, h=h: e.reciprocal(out=small[:, 8 + h * 4:12 + h * 4],
                                                                       in_=self.ps[bO][:, 0:260].rearrange("p (j c) -> p j c", j=4)[:, :, 64]),
                             reads=[self.t_ps[bO]], writes=[t_small])
                        for c in range(4):
                            p.op("dve", lambda e, bO=bO, h=h, c=c, ys=ys: e.tensor_scalar(out=yt[ys][:, c, h * 64:(h + 1) * 64], in0=self.ps[bO][:, c * 65:c * 65 + 64],
                                                                                         scalar1=small[:, 8 + h * 4 + c:9 + h * 4 + c], scalar2=None, op0=ALU.mult),
                                 reads=[self.t_ps[bO], t_small], writes=[t_yt[ys]])
                p.dma("pool", dap(self.dh["sc_y"], lg * 4 * 128 * D + 512 + pair * 128, [[D, 128], [128 * D, 4], [1, 128]]), yt[ys][:, :, pair * 128:(pair + 1) * 128], t_yt[ys], reads=[t_yt[ys]])

            p.barrier()
            es_att.close()
            es_sw = ExitStack()

            def stile(name, shape, dt):
                return es_sw.enter_context(nc.sbuf_tensor("B%d_%s" % (L, name), list(shape), dt))
            tile = stile
            swQ = [tile("swQ%d" % i, [64, 4, 512], BF16) for i in range(2)]; t_swQ = [Tk(), Tk()]
            swK = [tile("swK%d" % i, [64, 2, 640], BF16) for i in range(2)]; t_swK = [Tk(), Tk()]
            swV = [tile("swV%d" % i, [128, 5, 130], BF16) for i in range(2)]; t_swV = [Tk(), Tk()]
            Pc = [tile("Pc%d" % i, [128, 512], BF16) for i in range(2)]; t_Pc = [Tk(), Tk()]
            Pp = [tile("Pp%d" % i, [128, 512], BF16) for i in range(2)]; t_Pp = [Tk(), Tk()]
            ut = [tile("ut%d" % i, [128, 2, 514], F32) for i in range(2)]; t_ut = [Tk(), Tk()]
            bbt = [tile("bbt%d" % i, [128, 2, 512], F32) for i in range(2)]; t_bbt = [Tk(), Tk()]
            cvt = [tile("cvt%d" % i, [128, 2, 512], F32) for i in range(2)]; t_cvt = [Tk(), Tk()]
            if FUSED:
                swKp = tile("swKp", [64, 2, NGL, 128], BF16); t_swKp = Tk()
                swVp = tile("swVp", [128, NGL, 130], BF16); t_swVp = Tk()
                utl = tile("utl", [128, 2, NGL, 4], BF16); t_utl = Tk()
                for kvh in range(2):
                    i = TB["swK"] + kvh
                    p.dyn_dma("pool", swKp[:, kvh, :, :], self.dh["kvb_all0"], self.tabt[0:1, i:i + 1], [[512, 64], [4 * HR[0] * 512, NGL], [1, 128]], t_swKp,
                              reads=[self.t_tab], writes=[t_swKp])
                i = TB["swV"]
                p.dyn_dma("pool", swVp[:], self.dh["kvb_all0"], self.tabt[0:1, i:i + 1], [[130, 128], [4 * HR[0] * 512, NGL], [1, 130]], t_swVp,
                          reads=[self.t_tab], writes=[t_swVp])
                for ct in range(2):
                    i = TB["ut"] + ct
                    p.dyn_dma("pool", utl[:, ct, :, :], self.dh["kvb_all0"], self.tabt[0:1, i:i + 1], [[4, 128], [4 * HR[0] * 512, NGL], [1, 4]], t_utl,
                              reads=[self.t_tab], writes=[t_utl])
            ys_of = {}

            def swa_load(lg):
                s = lg % 2
                ys_of[lg] = self.rot("B_yt", 3)
                p.dma("sp", swQ[s][:], dap(self.dh["sc_swaq"], lg * 512, [[NTL, 64], [64 * NTL, 4], [1, 512]]), t_swQ[s], writes=[t_swQ[s]])
                if FUSED:
                    kl = self.dh["kvb_loc0"]
                    p.dma("sp", swK[s][:, :, 0:512], dap(kl, (lg * HR[0] + 512) * 512, [[512, 64], [64 * 512, 2], [1, 512]]), t_swK[s], writes=[t_swK[s]])
                    p.dma("sp", swV[s][:, 0:4, :], dap(kl, (lg * HR[0] + 640) * 512, [[130, 128], [128 * 130, 4], [1, 130]]), t_swV[s], writes=[t_swV[s]])
                    p.op("pool", lambda e: e.tensor_copy(out=swK[s][:, :, 512:640], in_=swKp[:, :, lg, :]), reads=[t_swKp], writes=[t_swK[s]])
                    p.op("pool", lambda e: e.tensor_copy(out=swV[s][:, 4, :], in_=swVp[:, lg, :]), reads=[t_swVp], writes=[t_swV[s]])
                else:
                    p.dma("sp", swK[s][:], dap(self.dh["swK"], lg * 640, [[NGL * 640, 64], [64 * NGL * 640, 2], [1, 640]]), t_swK[s], writes=[t_swK[s]])
                    p.dma("sp", swV[s][:], dap(self.dh["swV"], lg * 5 * 128 * 130, [[130, 128], [128 * 130, 5], [1, 130]]), t_swV[s], writes=[t_swV[s]])

            def swa_A(lg, c, idx):
                s = lg % 2
                q = idx % 2
                bC, bP = 0, 1
                for h in range(4):
                    p.op("pe", lambda e: e.matmul(self.ps[bC][:, h * 128:(h + 1) * 128], lhsT=swK[s][:, h // 2, c * 128:(c + 1) * 128],
                                                  rhs=swQ[s][:, h, c * 128:(c + 1) * 128], start=True, stop=True),
                         reads=[t_swK[s], t_swQ[s]], writes=[self.t_ps[bC]])
                for h in range(4):
                    p.op("pe", lambda e: e.matmul(self.ps[bP][:, h * 128:(h + 1) * 128], lhsT=swK[s][:, h // 2, (c + 1) * 128:(c + 2) * 128],
                                                  rhs=swQ[s][:, h, c * 128:(c + 1) * 128], start=True, stop=True),
                         reads=[t_swK[s], t_swQ[s]], writes=[self.t_ps[bP]])
                p.op("act", lambda e: e.activation(out=Pc[q][:], in_=self.ps[bC][:], func=AF.Exp), reads=[self.t_ps[bC]], writes=[t_Pc[q]])
                p.op("act", lambda e: e.activation(out=Pp[q][:], in_=self.ps[bP][:], func=AF.Exp), reads=[self.t_ps[bP]], writes=[t_Pp[q]])
                m_le4 = bass.AP(self.cb, 256, [[896, 128], [0, 4], [1, 128]])
                m_gt4 = bass.AP(self.cb, 384, [[896, 128], [0, 4], [1, 128]])
                p.op("pool", lambda e: e.tensor_tensor(out=Pc[q][:].rearrange("p (h c) -> p h c", h=4), in0=Pc[q][:].rearrange("p (h c) -> p h c", h=4),
                                                       in1=m_le4, op=ALU.mult), reads=[t_Pc[q], cb_], writes=[t_Pc[q]])
                p.op("dve", lambda e: e.tensor_tensor(out=Pp[q][:].rearrange("p (h c) -> p h c", h=4), in0=Pp[q][:].rearrange("p (h c) -> p h c", h=4),
                                                      in1=m_gt4, op=ALU.mult), reads=[t_Pp[q], cb_], writes=[t_Pp[q]])

            def swa_B(lg, c, idx):
                s = lg % 2
                q = idx % 2
                ys = ys_of[lg]
                bO = 2 + (idx % 2)
                p.op("pe", lambda e: e.matmul(self.ps[bO][:, 0:260], lhsT=self.zero_b, rhs=z4[:, 0:260], start=True, stop=False,
                                              skip_group_check=True), reads=[cb_], writes=[self.t_ps[bO]])
                for h in range(4):
                    kvh = h // 2
                    p.op("pe", lambda e: e.matmul(self.ps[bO][:, h * 65:(h + 1) * 65], lhsT=Pc[q][:, h * 128:(h + 1) * 128],
                                                  rhs=swV[s][:, c, kvh * 65:(kvh + 1) * 65], start=False, stop=False, skip_group_check=True),
                         reads=[t_Pc[q], t_swV[s]], writes=[self.t_ps[bO]])
                    p.op("pe", lambda e: e.matmul(self.ps[bO][:, h * 65:(h + 1) * 65], lhsT=Pp[q][:, h * 128:(h + 1) * 128],
                                                  rhs=swV[s][:, c + 1, kvh * 65:(kvh + 1) * 65], start=False, stop=True, skip_group_check=True),
                         reads=[t_Pp[q], t_swV[s]], writes=[self.t_ps[bO]])
                p.op("dve", lambda e: e.tensor_tensor(out=small[:, 0:4], in0=self.ps[bO][:, 0:260].rearrange("p (h c) -> p h c", h=4)[:, :, 64],
                                                      in1=esink[:], op=ALU.add), reads=[self.t_ps[bO], t_es], writes=[t_small])
                p.op("dve", lambda e: e.reciprocal(out=small[:, 4:8], in_=small[:, 0:4]), reads=[t_small], writes=[t_small])
                rec4 = bass.AP(small, 4, [[64, 128], [1, 4], [0, 64]])
                p.op("dve", lambda e: e.tensor_tensor(out=yt[ys][:, c, :].rearrange("p (h c) -> p h c", h=4),
                                                      in0=self.ps[bO][:, 0:260].rearrange("p (h c) -> p h c", h=4)[:, :, 0:64], in1=rec4, op=ALU.mult),
                     reads=[self.t_ps[bO], t_small], writes=[t_yt[ys]])
                if c == 3:
                    p.dma("pool", dap(self.dh["sc_y"], lg * 4 * 128 * D + 0, [[D, 128], [128 * D, 4], [1, 256]]), yt[ys][:], t_yt[ys], reads=[t_yt[ys]])

            def conv_compute(lg):
                s = lg % 2
                p.dma("sp", ut[s][:, :, 2:514], dap(self.dh["sc_u"], lg * 512, [[NTL, 128], [128 * NTL, 2], [1, 512]]), t_ut[s], writes=[t_ut[s]])
                p.dma("sp", bbt[s][:], dap(self.dh["sc_bb"], lg * 512, [[NTL, 128], [128 * NTL, 2], [1, 512]]), t_bbt[s], writes=[t_bbt[s]])
                if FUSED:
                    p.op("pool", lambda e: e.tensor_tensor(out=ut[s][:, :, 0:2], in0=utl[:, :, lg, 0:2], in1=utl[:, :, lg, 2:4], op=ALU.add),
                         reads=[t_utl], writes=[t_ut[s]])
                else:
                    p.dma("sp", ut[s][:, :, 0:2], dap(self.dh["utail"], lg * 2, [[NGL * 2, 128], [128 * NGL * 2, 2], [1, 2]]), t_ut[s], writes=[t_ut[s]])
                for ct in range(2):
                    p.op("dve", lambda e, s=s, ct=ct: e.tensor_scalar(out=cvt[s][:, ct, :], in0=ut[s][:, ct, 0:512], scalar1=cwb[:, ct * 3:ct * 3 + 1], scalar2=None, op0=ALU.mult),
                         reads=[t_ut[s], t_cwb], writes=[t_cvt[s]])
                    for kk in (1, 2):
                        p.op("dve", lambda e, s=s, ct=ct, kk=kk: e.scalar_tensor_tensor(out=cvt[s][:, ct, :], in0=ut[s][:, ct, kk:kk + 512], scalar=cwb[:, ct * 3 + kk:ct * 3 + kk + 1],
                                                                                       in1=cvt[s][:, ct, :], op0=ALU.mult, op1=ALU.add),
                             reads=[t_ut[s], t_cwb, t_cvt[s]], writes=[t_cvt[s]])
                    p.op("dve", lambda e, s=s, ct=ct: e.scalar_tensor_tensor(out=cvt[s][:, ct, :], in0=cvt[s][:, ct, :], scalar=cwb[:, 6 + ct:7 + ct], in1=bbt[s][:, ct, :],
                                                                            op0=ALU.add, op1=ALU.mult),
                         reads=[t_cvt[s], t_cwb, t_bbt[s]], writes=[t_cvt[s]])

            def conv_out(lg):
                s = lg % 2
                ys2 = self.rot("B_yt", 3)
                for c in range(4):
                    bT = 4 + (c % 2)
                    blk = 3 - c
                    for ct in range(2):
                        p.op("pe", lambda e, s=s, ct=ct, blk=blk, bT=bT: e.transpose(out=self.ps[bT][:, ct * 128:(ct + 1) * 128], in_=cvt[s][:, ct, blk * 128:(blk + 1) * 128],
                                                                                    identity=self.ident_f),
                             reads=[t_cvt[s], cf_], writes=[self.t_ps[bT]])
                    p.op("act", lambda e, bT=bT, c=c, ys2=ys2: e.copy(out=yt[ys2][:, c, :], in_=self.ps[bT][:, 0:256]), reads=[self.t_ps[bT]], writes=[t_yt[ys2]])
                p.dma("pool", dap(self.dh["sc_y"], lg * 4 * 128 * D + 256, [[D, 128], [128 * D, 4], [1, 256]]), yt[ys2][:], t_yt[ys2], reads=[t_yt[ys2]])


            items = [(lg, c) for lg in range(NGL) for c in range(4)]
            swa_load(0)
            conv_compute(0)
            swa_A(0, 0, 0)
            for idx, (lg, c) in enumerate(items):
                if c == 0 and lg + 1 < NGL:
                    conv_compute(lg + 1)
                if idx + 1 < len(items):
                    nlg, nc_ = items[idx + 1]
                    if nc_ == 0:
                        swa_load(nlg)
                    swa_A(nlg, nc_, idx + 1)
                swa_B(lg, c, idx)
                if c == 3:
                    conv_out(lg)

            p.barrier()
            es_sw.close()
            tile = _tile_outer

            NS = 3
            yb = [tile("yb%d" % i, [128, D], F32) for i in range(NS)]; t_yb = [Tk() for _ in range(NS)]
            gb_ = [tile("gb%d" % i, [128, D], BF16) for i in range(NS)]; t_gb = [Tk() for _ in range(NS)]
            xb = [tile("xb%d" % i, [128, D], F32) for i in range(NS)]; t_xb = [Tk() for _ in range(NS)]
            ob = [tile("ob%d" % i, [128, D], F32) for i in range(3)]; t_ob = [Tk(), Tk(), Tk()]
            junk = tile("junkB", [128, D], F32); t_junk = Tk()
            junk2 = tile("junkB2", [128, D], F32); t_junk2 = Tk()
            ygT = [tile("ygT%d" % i, [128, 8, 128], BF16) for i in range(3)]; t_ygT = [Tk(), Tk(), Tk()]
            sm = [tile("sm%d" % i, [128, 16], F32) for i in range(NS)]; t_sm = [Tk() for _ in range(NS)]
            sm2 = [tile("sm2_%d" % i, [128, 8], F32) for i in range(3)]; t_sm2 = [Tk(), Tk(), Tk()]

            def rows_of(sl):
                lg, c = sl // 4, sl % 4
                lb = lg * 4 + (3 - c)
                return slice(sl * 128, (sl + 1) * 128), slice(lb * 128, (lb + 1) * 128)

            def S1a(sl):
                s = sl % NS
                srow, rows = rows_of(sl)
                p.dma("sp", yb[s][:], dr["sc_y"][srow, :], t_yb[s], writes=[t_yb[s]])
                p.dma("sp", gb_[s][:], dr["sc_gate"][srow, :], t_gb[s], writes=[t_gb[s]])
                p.dma("sp", xb[s][:], x_src[rows, :], t_xb[s], writes=[t_xb[s]])
                for g in range(4):
                    p.op("act", lambda e: e.activation(out=junk[:, g * 256:(g + 1) * 256], in_=yb[s][:, g * 256:(g + 1) * 256], func=AF.Square,
                                                       accum_out=sm[s][:, g:g + 1]),
                         reads=[t_yb[s]], writes=[t_junk, t_sm[s]])
                p.op("act", lambda e: e.activation(out=sm[s][:, 4:8], in_=sm[s][:, 0:4], func=AF.Sqrt, bias=self.eps_c, scale=float(1.0 / 256)),
                     reads=[t_sm[s], cf_], writes=[t_sm[s]])
                p.op("dve", lambda e: e.reciprocal(out=sm[s][:, 8:12], in_=sm[s][:, 4:8]), reads=[t_sm[s]], writes=[t_sm[s]])
                for g in range(4):
                    p.op("dve", lambda e: e.scalar_tensor_tensor(out=yb[s][:, g * 256:(g + 1) * 256], in0=yb[s][:, g * 256:(g + 1) * 256], scalar=sm[s][:, 8 + g:9 + g],
                                                                 in1=ggrp_bc[:, g * 256:(g + 1) * 256], op0=ALU.mult, op1=ALU.mult),
                         reads=[t_yb[s], t_sm[s], t_gg], writes=[t_yb[s]])
                p.op("pool", lambda e: e.tensor_tensor(out=yb[s][:], in0=yb[s][:], in1=gb_[s][:], op=ALU.mult), reads=[t_yb[s], t_gb[s]], writes=[t_yb[s]])

            def S1b(sl):
                s = sl % NS
                u = sl % 3
                for half in range(2):
                    for i in range(4):
                        kc = half * 4 + i
                        p.op("pe", lambda e: e.transpose(out=self.ps[half][:, i * 128:(i + 1) * 128], in_=yb[s][:, kc * 128:(kc + 1) * 128],
                                                         identity=self.ident_f), reads=[t_yb[s], cf_], writes=[self.t_ps[half]])
                    if half == 0:
                        p.op("act", lambda e: e.copy(out=ygT[u][:, 0:4, :], in_=self.ps[0][:].rearrange("p (a b) -> p a b", a=4)),
                             reads=[self.t_ps[0]], writes=[t_ygT[u]])
                    else:
                        p.op("dve", lambda e: e.tensor_copy(out=ygT[u][:, 4:8, :], in_=self.ps[1][:].rearrange("p (a b) -> p a b", a=4)),
                             reads=[self.t_ps[1]], writes=[t_ygT[u]])

            def S2(sl):
                s = sl % NS
                u = sl % 3
                srow, rows = rows_of(sl)
                bo = (2 + 2 * u, 3 + 2 * u)
                for half in range(2):
                    b = bo[half]
                    for kc in range(8):
                        p.op("pe", lambda e: e.matmul(self.ps[b][:], lhsT=ygT[u][:, kc, :], rhs=Wo[:, kc, half * 512:(half + 1) * 512],
                                                      start=(kc == 0), stop=(kc == 7)),
                             reads=[t_ygT[u], t_Wo], writes=[self.t_ps[b]])
                    p.op("act", lambda e: e.activation(out=junk2[:, half * 512:(half + 1) * 512], in_=self.ps[b][:], func=AF.Square,
                                                       accum_out=sm2[u][:, half:half + 1]),
                         reads=[self.t_ps[b]], writes=[t_junk2, t_sm2[u]])
                p.op("dve", lambda e: e.tensor_tensor(out=sm2[u][:, 2:3], in0=sm2[u][:, 0:1], in1=sm2[u][:, 1:2], op=ALU.add), reads=[t_sm2[u]], writes=[t_sm2[u]])
                p.op("act", lambda e: e.activation(out=sm2[u][:, 3:4], in_=sm2[u][:, 2:3], func=AF.Sqrt, bias=self.eps_c, scale=float(1.0 / D)),
                     reads=[t_sm2[u], cf_], writes=[t_sm2[u]])
                p.op("dve", lambda e: e.reciprocal(out=sm2[u][:, 4:5], in_=sm2[u][:, 3:4]), reads=[t_sm2[u]], writes=[t_sm2[u]])
                for half in range(2):
                    b = bo[half]
                    p.op("dve", lambda e: e.scalar_tensor_tensor(out=ob[u][:, half * 512:(half + 1) * 512], in0=self.ps[b][:], scalar=sm2[u][:, 4:5],
                                                                 in1=gpost_bc[:, half * 512:(half + 1) * 512], op0=ALU.mult, op1=ALU.mult),
                         reads=[self.t_ps[b], t_sm2[u], t_gp], writes=[t_ob[u]])
                p.op("pool", lambda e: e.tensor_tensor(out=ob[u][:], in0=ob[u][:], in1=xb[s][:], op=ALU.add), reads=[t_ob[u], t_xb[s]], writes=[t_ob[u]])
                p.dma("pool", x_dst[rows, :], ob[u][:], t_ob[u], reads=[t_ob[u]])

            S1a(0)
            if NBL > 1:
                S1a(1)
            S1b(0)
            for sl in range(NBL):
                if sl + 2 < NBL:
                    S1a(sl + 2)
                if sl + 1 < NBL:
                    S1b(sl + 1)
                S2(sl)
            p.barrier()

    def build(self):
        self.declare()
        self.common()
        p = self.p
        self.zero4_t = p.tile("zero4", [128, 512], BF16)
        self.zero4 = self.zero4_t[:]
        p.op("pool", lambda e: e.memset(self.zero4_t[:], 0.0), reads=[self.t_cb], writes=[self.t_cb])
        dr = self.dram
        if self.mode == "A":
            self.phaseA(0, dr["x"])
        elif self.mode == "B":
            self.phaseB(0, dr["x"], dr["out"])
        else:
            NTL, NGL = self.NTL, self.NGL
            self.tabt = p.tile("tabt", [1, 128], I32)
            self.t_tab = Tk()
            p.dma("sp", self.tabt[:], dr["tab"], self.t_tab, writes=[self.t_tab])
            self.zt = p.tile("zt", [128, 512], BF16)
            self.t_zt = Tk()
            p.op("pool", lambda e: e.memset(self.zt[:], 0.0), writes=[self.t_zt])
            NG = self.NG
            groups = [[b * NR + r for r in range(NR)] for b in range(NB_BATCH)]
            import os
            KF = os.environ.get("KFSTOP", "")
            for l in range(DEPTH):
                if KF and l >= int(KF[0]):
                    break
                x_src = dr["x"] if l == 0 else dr["sc_x1"]
                x_dst = dr["out"] if l == DEPTH - 1 else dr["sc_x1"]
                p.phase_begin()
                self.phaseA(l, x_src)
                p.barrier()
                p.phase_end()
                if KF.endswith("a"):
                    break
                p.barrier()
                if KF.endswith("g"):
                    break
                p.phase_begin()
                self.phaseB(l, x_src, x_dst)
                p.barrier()
                p.phase_end()
        p.barrier()
        p.emit()
        self.es.close()
        return self.nc


def make_consts():
    cf = np.zeros((128, 640), np.float32)
    cf[:, 0:128] = np.eye(128, dtype=np.float32)
    cf[:, 128:256] = 1.0
    cf[:, 256] = EPS
    cf[:, 257] = 1.0
    half = 16
    freqs = (10000.0 ** (-np.arange(half, dtype=np.float32) / half)).astype(np.float32)
    pidx = np.arange(128)
    cf[:, 258] = freqs[(pidx % 32) % 16]
    cf[:, 259] = np.where((pidx % 32) < 16, -1.0, 1.0)
    cb = np.zeros((128, 896), np.float32)
    cb[:, 768:896] = 1.0
    j = np.arange(128)[:, None]
    i = np.arange(128)[None, :]
    cb[:, 0:128] = -1.0 * (j >= i)
    cb[:, 128:256] = (j < i)
    cb[:, 256:384] = (j <= i)
    cb[:, 384:512] = (j > i)
    for m in (64, 65, 96, 97):
        cb[:, 512 + m] = 1.0
    return cf, cb.astype(ml_dtypes.bfloat16)


def f_order_rows(ngl):
    idx = []
    for lg in range(ngl):
        for c in range(4):
            b = lg * 4 + (3 - c)
            idx.extend(range(b * 128, (b + 1) * 128))
    return np.array(idx)


def prep_weights(inp):
    w = {}
    w["gpre_t"] = np.ascontiguousarray(inp["norm_pre"].reshape(DEPTH, 8, 128).transpose(0, 2, 1))
    w_in = inp["w_in"]
    sw = np.concatenate([w_in[:, :, C_CKR + 16:C_CKR + 32], w_in[:, :, C_CKR:C_CKR + 16]], axis=2)
    w["w_in_x"] = np.ascontiguousarray(np.concatenate([w_in, sw], axis=2))
    w["gcq_t"] = np.ascontiguousarray(inp["mla_q_norm"].reshape(DEPTH, 2, 128).transpose(0, 2, 1))
    uq = inp["mla_w_uq"].reshape(DEPTH, 256, 4, 96)
    nope = uq[..., 0:64].reshape(DEPTH, 256, 256)
    rope = uq[..., 64:96]
    rope_sw = np.concatenate([rope[..., 16:32], rope[..., 0:16]], axis=-1)
    w["w_uq_x"] = np.ascontiguousarray(np.concatenate([nope, rope.reshape(DEPTH, 256, 128), rope_sw.reshape(DEPTH, 256, 128)], axis=2))
    w["gkv_t"] = np.ascontiguousarray(inp["mla_kv_norm"].reshape(DEPTH, 1, 128).transpose(0, 2, 1))
    ukv = inp["mla_w_ukv"].reshape(DEPTH, 128, 4, 128)
    w["w_ukv_x"] = np.ascontiguousarray(np.concatenate([ukv[..., 0:64].reshape(DEPTH, 128, 256), ukv[..., 64:128].reshape(DEPTH, 128, 256)], axis=2))
    w["sinks"] = np.ascontiguousarray(inp["attn_sinks"])
    w["convw_t"] = np.ascontiguousarray(inp["conv_w"].reshape(DEPTH, 3, 2, 128).transpose(0, 3, 2, 1).reshape(DEPTH, 128, 6))
    w["convb_t"] = np.ascontiguousarray(inp["conv_b"].reshape(DEPTH, 2, 128).transpose(0, 2, 1))
    w["ggrp"] = np.ascontiguousarray(inp["group_norm"])
    w["w_out"] = np.ascontiguousarray(inp["w_out"])
    w["gpost"] = np.ascontiguousarray(inp["norm_post"])
    return {k: np.asarray(v, np.float32) for k, v in w.items()}


_CACHE = {}


def get_prog(mode, S):
    key = (mode, S)
    if key not in _CACHE:
        _CACHE[key] = Builder(mode, S).build()
    return _CACHE[key]


A_WKEYS = ("gpre_t", "w_in_x", "gcq_t", "w_uq_x", "gkv_t", "w_ukv_x")
B_WKEYS = ("sinks", "convw_t", "convb_t", "ggrp", "w_out", "gpost")
SC_KEYS = ("sc_sbq", "sc_mlaq", "sc_swaq", "sc_u", "sc_bb", "sc_gate")


def arrange_B(resA, S):
    NG = S // 512
    NGL = NG // NR
    NTL = NGL * 512
    NBK = S // 128
    NPOS = NBK + 12
    outs = []
    for core in range(NB_BATCH * NR):
        b, r = divmod(core, NR)
        sbK = np.zeros((256, NPOS * 128), ml_dtypes.bfloat16)
        mlaK = np.zeros((384, NPOS * 128), ml_dtypes.bfloat16)
        sbV = np.zeros((NPOS * 128, 256), ml_dtypes.bfloat16)
        mlaV = np.zeros((NPOS * 128, 260), ml_dtypes.bfloat16)
        for qg in range(NPOS // 4):
            G = NG - 1 + r - qg
            if 0 <= G < NG:
                src = resA[b * NR + (G % NR)]
                lgs = G // NR
                kv = src["kvb_loc"]
                flat = kv.reshape(-1)
                sbK[:, qg * 512:(qg + 1) * 512] = kv[R_SBK:R_SBK + 256, lgs * 512:(lgs + 1) * 512]
                mlaK[:, qg * 512:(qg + 1) * 512] = kv[R_MLAK:R_MLAK + 384, lgs * 512:(lgs + 1) * 512]
                o = R_SBV * NTL + lgs * 512 * 256
                sbV[qg * 512:(qg + 1) * 512, :] = flat[o:o + 512 * 256].reshape(512, 256)
                o = R_MLAV * NTL + lgs * 512 * 260
                mlaV[qg * 512:(qg + 1) * 512, :] = flat[o:o + 512 * 260].reshape(512, 260)
        swK = np.zeros((128, NGL, 640), ml_dtypes.bfloat16)
        swV = np.zeros((NGL, 5, 128, 130), ml_dtypes.bfloat16)
        utail = np.zeros((256, NGL, 2), np.float32)
        own = resA[core]
        flat_own = own["kvb_loc"].reshape(-1)
        for lg in range(NGL):
            G = NR * lg + r
            swK[:, lg, 0:512] = own["kvb_loc"][R_SWAK:R_SWAK + 128, lg * 512:(lg + 1) * 512]
            o = R_SWAV * NTL + lg * 512 * 130
            swV[lg, 0:4] = flat_own[o:o + 512 * 130].reshape(4, 128, 130)
            if G > 0:
                src = resA[b * NR + ((G - 1) % NR)]
                lgs = (G - 1) // NR
                swK[:, lg, 512:640] = src["kvb_loc"][R_SWAK:R_SWAK + 128, lgs * 512:lgs * 512 + 128]
                o = R_SWAV * NTL + lgs * 512 * 130
                swV[lg, 4] = src["kvb_loc"].reshape(-1)[o:o + 128 * 130].reshape(128, 130)
                utail[:, lg, :] = src["kvf_loc"].reshape(256, NGL, 2)[:, lgs, :]
        outs.append(dict(sbK=sbK, sbV=sbV, mlaK=mlaK, mlaV=mlaV, swK=swK.reshape(128, NGL * 640),
                         swV=swV.reshape(NGL * 5 * 128, 130), utail=utail.reshape(256, NGL * 2)))
    return outs


def run_forward(inp, S):
    NG = S // 512
    NGL = NG // NR
    NTL = NGL * 512
    ncores = NB_BATCH * NR
    x = np.asarray(inp["x"], np.float32)
    pos = np.asarray(inp["positions"], np.int32)
    W = prep_weights(inp)
    cf, cb = make_consts()
    ford = f_order_rows(NGL)
    tok = []
    for core in range(ncores):
        b, r = divmod(core, NR)
        t = np.concatenate([np.arange((NR * lg + r) * 512, (NR * lg + r + 1) * 512) for lg in range(NGL)])
        tok.append((b, t))
    xs = [np.ascontiguousarray(x[b][t]) for (b, t) in tok]
    ps = [np.ascontiguousarray(pos[b][t][ford][None, :]) for (b, t) in tok]
    progA = get_prog("A", S)
    progB = get_prog("B", S)
    for l in range(DEPTH):
        wa = {k: W[k][l:l + 1] for k in A_WKEYS}
        wb = {k: W[k][l:l + 1] for k in B_WKEYS}
        inA = [dict(consts_f=cf, consts_b=cb, pos=ps[c], x=xs[c], **wa) for c in range(ncores)]
        resA = run_bass_kernel_spmd(progA, inA, core_ids=list(range(ncores))).results
        arr = arrange_B(resA, S)
        inB = []
        for c in range(ncores):
            d = dict(consts_f=cf, consts_b=cb, x=xs[c], **wb)
            for k in SC_KEYS:
                d[k] = resA[c][k]
            d.update(arr[c])
            inB.append(d)
        resB = run_bass_kernel_spmd(progB, inB, core_ids=list(range(ncores))).results
        xs = [np.asarray(resB[c]["out"], np.float32) for c in range(ncores)]
    out = np.zeros_like(x)
    for c, (b, t) in enumerate(tok):
        out[b][t] = xs[c]
    return out


def make_table(r, S):
    tab = np.zeros((1, 128), np.int32)
    Y0, Y1 = HR[0] * 512, HR[1] * 512
    for h in range(4):
        tab[0, 0 + h] = r * Y0 + (0 + h * 64) * 512
        tab[0, 4 + h] = r * Y0 + 256 * 512 + h * 128 * 256
        tab[0, 8 + h] = r * Y1 + (0 + h * 96) * 512
        tab[0, 12 + h] = r * Y1 + 384 * 512 + h * 128 * 260
    for kvh in range(2):
        tab[0, 16 + kvh] = (2 + r) * Y0 + (512 + kvh * 64) * 512
        tab[0, 19 + kvh] = (2 + r) * Y0 + 770 * 512 + kvh * 128 * 4
    tab[0, 18] = (2 + r) * Y0 + 640 * 512
    return tab


def run_fused(inp, S):
    NG = S // 512
    NGL = NG // NR
    ncores = NB_BATCH * NR
    x = np.asarray(inp["x"], np.float32)
    pos = np.asarray(inp["positions"], np.int32)
    W = prep_weights(inp)
    cf, cb = make_consts()
    ford = f_order_rows(NGL)
    kaug = np.zeros((64, 512), ml_dtypes.bfloat16)
    kaug[0, :] = 1.0
    kaug[32, :] = 1.0
    in_maps = []
    tok = []
    for core in range(ncores):
        b, r = divmod(core, NR)
        t = np.concatenate([np.arange((NR * lg + r) * 512, (NR * lg + r + 1) * 512) for lg in range(NGL)])
        tok.append((b, t))
        d = dict(consts_f=cf, consts_b=cb, kaug=kaug, x=np.ascontiguousarray(x[b][t]),
                 pos=np.ascontiguousarray(pos[b][t][ford][None, :]), tab=make_table(r, S))
        d.update(W)
        in_maps.append(d)
    res = run_bass_kernel_spmd(get_prog("F", S), in_maps, core_ids=list(range(ncores))).results
    out = np.zeros_like(x)
    for c, (b, t) in enumerate(tok):
        out[b][t] = np.asarray(res[c]["out"], np.float32)
    return out


def kernel(**inputs):
    inp = {k: np.asarray(v) for k, v in inputs.items()}
    S = inp["x"].shape[1]
    return run_fused(inp, S)
```

```python
import math
from contextlib import ExitStack

import numpy as np
import ml_dtypes

import concourse.bass as bass
import concourse.mybir as mybir
from concourse.bass_utils import run_bass_kernel_spmd

F32 = mybir.dt.float32
BF16 = mybir.dt.bfloat16
I32 = mybir.dt.int32
AF = mybir.ActivationFunctionType
ALU = mybir.AluOpType

D = 1024
DIN = 3488
DINX = 3520
DEPTH = 2
NB_BATCH = 2
NR = 4
EPS = 1e-6
KVROWS = 1416
C_AQ, C_AK, C_AV = 0, 256, 384
C_BB, C_BC, C_BX = 512, 768, 1024
C_CQ, C_CKV, C_CKR = 1280, 1536, 1664
C_DQ, C_DK, C_DV = 1696, 1952, 2208
C_GATE = 2464
C_KRSW = 3488
R_SBK, R_MLAK, R_SWAK, R_SBV, R_MLAV, R_SWAV, R_UT = 0, 256, 640, 768, 1024, 1284, 1414


HR = (772, 644)


def kvmap(oldrow):
    if oldrow < 256:
        return 0, oldrow
    if oldrow < 640:
        return 1, oldrow - 256
    if oldrow < 768:
        return 0, 512 + oldrow - 640
    if oldrow < 1024:
        return 0, 256 + oldrow - 768
    if oldrow < 1284:
        return 1, 384 + oldrow - 1024
    if oldrow < 1414:
        return 0, 640 + oldrow - 1284
    return 0, 770 + oldrow - 1414


def own_groups(r, ngl):
    return [r + NR * i for i in range(ngl)]


class Tk:
    __slots__ = ("w", "r", "dsem", "dcnt", "name", "excl")

    def __init__(self, name="", excl=False):
        self.excl = excl
        self.w = None
        self.r = {}
        self.dsem = None
        self.dcnt = 0
        self.name = name


class _Rec:
    def __init__(self):
        self.call = None

    def __getattr__(self, name):
        def f(*a, **kw):
            self.call = (name, a, kw)
            return self
        return f


class Prog:
    ENG = ("pe", "act", "dve", "pool", "sp")

    def __init__(self, nc, es):
        self.nc = nc
        self.es = es
        self.streams = {e: [] for e in self.ENG}
        self.seen = {e: {} for e in self.ENG}
        self.sems = {}
        self.cnt = {}
        self.nsem = 0
        self.all_tickets = {}
        for e in ("pe", "act", "dve", "pool"):
            self._newsem(e)

    def _newsem(self, key):
        s = self.es.enter_context(self.nc.semaphore("s%d" % self.nsem))
        self.nsem += 1
        self.sems[key] = s
        self.cnt[key] = 0
        return s

    def _dsem(self, st, q):
        cls = "sw" if q == "pool" else "hw"
        if st.dsem is not None:
            assert st.dsem[0] == cls, "tile %s used by both DMA classes" % st.name
            return
        free = getattr(self, "free_dsems", {}).get(cls)
        if free:
            st.dsem = free.pop()
        else:
            st.dsem = (cls, self.nsem)
            self._newsem(st.dsem)
        if getattr(self, "phase_dsems", None) is not None:
            self.phase_dsems.append(st.dsem)

    def phase_begin(self):
        self.phase_dsems = []

    def phase_end(self):
        if not hasattr(self, "free_dsems"):
            self.free_dsems = {"sw": [], "hw": []}
        for k in self.phase_dsems:
            self.free_dsems[k[0]].append(k)
        self.phase_dsems = None

    def tile(self, name, shape, dt):
        return self.es.enter_context(self.nc.sbuf_tensor(name, list(shape), dt))

    def psum(self, name):
        return self.es.enter_context(self.nc.psum_tensor(name, [128, 512], F32))

    def _collect(self, eng, reads, writes):
        deps = {}

        def add(tk):
            if tk is None:
                return
            k, v = tk
            if deps.get(k, 0) < v:
                deps[k] = v

        for t in reads:
            add(t.w)
            if t.excl:
                for k, v in t.r.items():
                    if k != eng:
                        add((k, v))
        for t in writes:
            add(t.w)
            for k, v in t.r.items():
                add((k, v))
        waits = []
        seen = self.seen[eng]
        for k, v in deps.items():
            if k == eng and eng == "pe":
                continue
            if seen.get(k, 0) >= v:
                continue
            seen[k] = v
            waits.append((self.sems[k], v))
        return waits

    def _record(self, tk, reads, writes):
        k, v = tk
        for t in reads:
            if t.r.get(k, 0) < v:
                t.r[k] = v
        for t in writes:
            t.w = tk
            t.r = {}
        self.all_tickets[k] = v

    def op(self, eng, fn, reads=(), writes=()):
        waits = self._collect(eng, reads, writes)
        self.cnt[eng] += 1
        tk = (eng, self.cnt[eng])
        rec = _Rec()
        fn(rec)
        name, a, kw = rec.call
        self.streams[eng].append((waits, (lambda e, name=name, a=a, kw=kw: getattr(e, name)(*a, **kw)), (self.sems[eng], 1)))
        self._record(tk, reads, writes)
        return tk

    def dma(self, q, out, in_, st, reads=(), writes=()):
        self._dsem(st, q)
        waits = self._collect(q, reads, writes)
        self.cnt[st.dsem] += 16
        tk = (st.dsem, self.cnt[st.dsem])
        self.streams[q].append((waits, lambda e, o=out, i=in_: e.dma_start(out=o, in_=i),
                                (self.sems[st.dsem], 16)))
        self._record(tk, reads, writes)
        return tk

    def dyn_dma(self, q, out, handle, tab_ap, dims, st, reads=(), writes=()):
        self._dsem(st, q)
        if not hasattr(self, "regs"):
            self.regs = {}
            self.regi = {}
        if q not in self.regs:
            eng = {"sp": self.nc.sync, "act": self.nc.scalar, "pool": self.nc.gpsimd}[q]
            self.regs[q] = [self.es.enter_context(eng.register("dr%s%d" % (q, i))) for i in range(2)]
            self.regi[q] = 0
        reg = self.regs[q][self.regi[q] % 2]
        self.regi[q] += 1
        waits = self._collect(q, reads, writes)
        self.streams[q].append((waits, lambda e, reg=reg, t=tab_ap: e.reg_load(reg, t), None))
        self.cnt[st.dsem] += 16
        tk = (st.dsem, self.cnt[st.dsem])
        src = bass.AP(handle, reg, [list(d) for d in dims])
        self.streams[q].append(([], lambda e, o=out, i=src: e.dma_start(out=o, in_=i), (self.sems[st.dsem], 16)))
        self._record(tk, reads, writes)
        return tk

    def raw(self, eng, fn, key, reads=(), writes=()):
        if key not in self.sems:
            self._newsem(key)
        waits = self._collect(eng, reads, writes)
        self.cnt[key] += 1
        tk = (key, self.cnt[key])
        self.streams[eng].append((waits, fn, (self.sems[key], 1)))
        self._record(tk, reads, writes)
        return tk

    def barrier(self):
        for e in self.ENG:
            waits = []
            for k, v in self.all_tickets.items():
                if self.seen[e].get(k, 0) < v:
                    self.seen[e][k] = v
                    waits.append((self.sems[k], v))
            if waits:
                self.streams[e].append((waits, None, None))

    def emit(self):
        nc = self.nc

        def run(e, lst):
            for waits, fn, inc in lst:
                for s, v in waits:
                    e.wait_ge(s, v)
                if fn is not None:
                    ins = fn(e)
                    if inc is not None:
                        ins.then_inc(inc[0], inc[1])

        with nc.Block() as block:
            @block.tensor
            def _(e):
                run(e, self.streams["pe"])

            @block.scalar
            def _(e):
                run(e, self.streams["act"])

            @block.vector
            def _(e):
                run(e, self.streams["dve"])

            @block.gpsimd
            def _(e):
                run(e, self.streams["pool"])

            @block.sync
            def _(e):
                run(e, self.streams["sp"])


def dap(h, off, dims):
    return bass.AP(h, off, [list(d) for d in dims])


class Builder:
    def __init__(self, mode, S):
        self.mode = mode
        self.S = S
        self.NG = S // 512
        self.NGL = self.NG // NR
        self.NTL = self.NGL * 512
        self.NBL = self.NGL * 4
        self.nc = bass.Bass("TRN2", target_bir_lowering=False)
        self.es = ExitStack()
        self.p = Prog(self.nc, self.es)
        self.dram = {}
        self.dh = {}
        self.rr = {}

    def din(self, name, shape, dt):
        self.dh[name] = self.nc.dram_tensor(name, list(shape), dt, kind="ExternalInput")
        self.dram[name] = self.dh[name].ap()

    def dout(self, name, shape, dt):
        self.dh[name] = self.nc.dram_tensor(name, list(shape), dt, kind="ExternalOutput")
        self.dram[name] = self.dh[name].ap()

    def dint(self, name, shape, dt):
        self.dh[name] = self.nc.dram_tensor(name, list(shape), dt)
        self.dram[name] = self.dh[name].ap()

    def declare(self):
        mode, NTL, NBL, NGL = self.mode, self.NTL, self.NBL, self.NGL
        L = DEPTH if mode == 'F' else 1
        self.din("consts_f", [128, 640], F32)
        self.din("consts_b", [128, 896], BF16)
        self.din("kaug", [64, 512], BF16)
        if mode in ("A", "F"):
            self.din("pos", [1, NTL], I32)
            self.din("gpre_t", [L, 128, 8], F32)
            self.din("w_in_x", [L, D, DINX], F32)
            self.din("gcq_t", [L, 128, 2], F32)
            self.din("w_uq_x", [L, 256, 512], F32)
            self.din("gkv_t", [L, 128, 1], F32)
            self.din("w_ukv_x", [L, 128, 512], F32)
        if mode in ("B", "F"):
            self.din("sinks", [L, 4], F32)
            self.din("convw_t", [L, 128, 6], F32)
            self.din("convb_t", [L, 128, 2], F32)
            self.din("ggrp", [L, D], F32)
            self.din("w_out", [L, D, D], F32)
            self.din("gpost", [L, D], F32)
        self.din("x", [NTL, D], F32)
        scr = [("sc_sbq", [256, NTL], BF16), ("sc_mlaq", [384, NTL], BF16), ("sc_swaq", [256, NTL], BF16),
               ("sc_u", [256, NTL], F32), ("sc_bb", [256, NTL], F32), ("sc_gate", [NBL * 128, D], BF16)]
        if mode == "A":
            for n, s, d in scr:
                self.dout(n, s, d)
            self.dout("kvb_loc", [KVROWS, NTL], BF16)
            self.dout("kvf_loc", [256, NGL * 2], F32)
        elif mode == "B":
            for n, s, d in scr:
                self.din(n, s, d)
            NPOS = self.S // 128 + 12
            self.din("sbK", [256, NPOS * 128], BF16)
            self.din("sbV", [NPOS * 128, 256], BF16)
            self.din("mlaK", [384, NPOS * 128], BF16)
            self.din("mlaV", [NPOS * 128, 260], BF16)
            self.din("swK", [128, NGL * 640], BF16)
            self.din("swV", [NGL * 5 * 128, 130], BF16)
            self.din("utail", [256, NGL * 2], F32)
            self.dint("sc_y", [NBL * 128, D], F32)
            self.dout("out", [NTL, D], F32)
        else:
            for n, s, d in scr:
                self.dint(n, s, d)
            for f in range(2):
                self.dint("kvb_loc%d" % f, [NGL * HR[f], 512], BF16)
                self.dint("kvb_all%d" % f, [(self.NG + 6) * HR[f], 512], BF16)
            self.dh["kvb_loc"] = None
            self.din("tab", [1, 128], I32)
            self.dint("sc_y", [NBL * 128, D], F32)
            self.dint("sc_x1", [NTL, D], F32)
            self.dint("sc_cos", [128, NTL], F32)
            self.dint("sc_sin", [128, NTL], F32)
            self.dout("out", [NTL, D], F32)

    def common(self):
        p = self.p
        self.cf = p.tile("cf", [128, 640], F32)
        self.cb = p.tile("cb", [128, 896], BF16)
        self.t_cf = Tk("cf")
        self.t_cb = Tk("cb")
        p.dma("sp", self.cf[:], self.dram["consts_f"], self.t_cf, writes=[self.t_cf])
        p.dma("sp", self.cb[:], self.dram["consts_b"], self.t_cb, writes=[self.t_cb])
        self.ident_f = self.cf[:, 0:128]
        self.ones_f = self.cf[:, 128:256]
        self.eps_c = self.cf[:, 256:257]
        self.one_c = self.cf[:, 257:258]
        self.freq_c = self.cf[:, 258:259]
        self.sgn_c = self.cf[:, 259:260]
        self.negtri = self.cb[:, 0:128]
        self.m_lt = self.cb[:, 128:256]
        self.m_le = self.cb[:, 256:384]
        self.m_gt = self.cb[:, 384:512]
        self.ones98 = self.cb[:, 512:610]
        self.zero_b = self.cb[:, 640:768]
        self.ones_b = self.cb[:, 768:896]
        self.pp = [self.es.enter_context(self.nc.psum_tensor("pp%d" % i, [128, 1024], F32)) for i in range(4)]
        self.ps = [self.pp[i // 2][:, (i % 2) * 512:(i % 2 + 1) * 512] for i in range(8)]
        self.t_ps = [Tk("ps%d" % i, excl=True) for i in range(8)]

    def rot(self, name, n):
        i = self.rr.get(name, 0)
        self.rr[name] = i + 1
        return i % n

    def phaseA(self, L, x_src):
        p, nc = self.p, self.nc
        NTL, NGL, NBL = self.NTL, self.NGL, self.NBL
        dr = self.dram
        cf_, cb_ = self.t_cf, self.t_cb
        with ExitStack() as es:
            def tile(name, shape, dt):
                return es.enter_context(nc.sbuf_tensor("A%d_%s" % (L, name), list(shape), dt))

            Wp = tile("Wp", [128, 8, DINX], BF16); t_Wp = Tk(); t_Wp2 = Tk()
            wuq = tile("wuq", [128, 2, 512], BF16); t_wuq = Tk()
            wukv = tile("wukv", [128, 512], BF16); t_wukv = Tk()
            gcols = tile("gcols", [128, 16], F32); t_gc = Tk()
            wst = [tile("wst%d" % i, [128, 1760], F32) for i in range(2)]; t_wst = [Tk() for _ in range(2)]
            cosT = tile("cosT", [128, NTL], F32); t_cos = Tk()
            sinS = tile("sinS", [128, NTL], F32); t_sin = Tk()
            posi = tile("posi", [128, NTL], I32); t_posi = Tk()
            ang = tile("ang", [128, NTL], F32); t_ang = Tk()
            tr1 = tile("tr1", [128, NTL], F32); t_tr1 = Tk()
            tr2 = tile("tr2", [128, NTL], F32); t_tr2 = Tk()
            tri = tile("tri", [128, NTL], I32); t_tri = Tk()
            xs = [tile("xs%d" % i, [128, D], F32) for i in range(2)]; t_xs = [Tk(), Tk()]
            xn = [tile("xn%d" % i, [128, D], F32) for i in range(2)]; t_xn = [Tk(), Tk()]
            junk = tile("junk", [128, D], F32); t_junk = Tk()
            sst = [tile("ss%d" % i, [128, 4], F32) for i in range(2)]; t_ss = [Tk(), Tk()]
            xnT = tile("xnT", [128, 8, 512], BF16); t_xnT = Tk()
            cqT = tile("cqT", [128, 2, 512], BF16); t_cqT = Tk()
            sq = [tile("sq%d" % i, [128, 512], BF16) for i in range(2)]; t_sq = [Tk(), Tk()]
            ckvT = tile("ckvT", [128, 512], BF16); t_ckvT = Tk()
            rq_bc = tile("rq_bc", [128, 512], F32); t_rq = Tk()
            rkv_bc = tile("rkv_bc", [128, 512], F32); t_rkv = Tk()
            rkv_t = tile("rkv_t", [128, 8], F32); t_rkvt = Tk()
            tmpf = [tile("tmpf%d" % i, [128, 512], F32) for i in range(3)]; t_tmpf = [Tk(), Tk(), Tk()]
            NSTB, NSTF = 10, 5
            stb = [tile("stb%d" % i, [128, 1024], BF16) for i in range(NSTB)]; t_stb = [Tk() for _ in range(NSTB)]
            stf = [tile("stf%d" % i, [128, 512], F32) for i in range(NSTF)]; t_stf = [Tk() for _ in range(NSTF)]

            p.dma("sp", gcols[:, 0:8], dr["gpre_t"][L], t_gc, writes=[t_gc])
            p.dma("sp", gcols[:, 8:10], dr["gcq_t"][L], t_gc, writes=[t_gc])
            p.dma("sp", gcols[:, 10:11], dr["gkv_t"][L], t_gc, writes=[t_gc])
            k = 0
            for kc in range(8):
                for hf in range(2):
                    s = k % 2; k += 1
                    p.dma("sp", wst[s][:], dr["w_in_x"][L, kc * 128:(kc + 1) * 128, hf * 1760:(hf + 1) * 1760],
                          t_wst[s], writes=[t_wst[s]])
                    if s % 2 == 0:
                        p.op("dve", lambda e, o=Wp[:, kc, hf * 1760:(hf + 1) * 1760], i=wst[s][:], sc=gcols[:, kc:kc + 1]:
                             e.tensor_scalar(out=o, in0=i, scalar1=sc, scalar2=None, op0=ALU.mult),
                             reads=[t_wst[s], t_gc], writes=[t_Wp])
                    else:
                        p.op("act", lambda e, o=Wp[:, kc, hf * 1760:(hf + 1) * 1760], i=wst[s][:], sc=gcols[:, kc:kc + 1]:
                             e.activation(out=o, in_=i, func=AF.Copy, scale=sc),
                             reads=[t_wst[s], t_gc], writes=[t_Wp2])
            for kc in range(2):
                s = k % 2; k += 1
                p.dma("sp", wst[s][:, 0:512], dr["w_uq_x"][L, kc * 128:(kc + 1) * 128, :], t_wst[s], writes=[t_wst[s]])
                p.op("dve", lambda e, o=wuq[:, kc, :], i=wst[s][:, 0:512], sc=gcols[:, 8 + kc:9 + kc]:
                     e.tensor_scalar(out=o, in0=i, scalar1=sc, scalar2=float(96 ** -0.5), op0=ALU.mult, op1=ALU.mult),
                     reads=[t_wst[s], t_gc], writes=[t_wuq])
            s = k % 2; k += 1
            p.dma("sp", wst[s][:, 0:512], dr["w_ukv_x"][L], t_wst[s], writes=[t_wst[s]])
            p.op("dve", lambda e, o=wukv[:], i=wst[s][:, 0:512], sc=gcols[:, 10:11]:
                 e.tensor_scalar(out=o, in0=i, scalar1=sc, scalar2=None, op0=ALU.mult),
                 reads=[t_wst[s], t_gc], writes=[t_wukv])

            import os
            KSTOP = float(os.environ.get("KSTOP", "99"))
            if KSTOP <= 1:
                p.barrier(); return
            REUSE_TABLES = (self.mode == "F" and L > 0)
            if REUSE_TABLES:
                p.dma("sp", cosT[:], dr["sc_cos"], t_cos, writes=[t_cos])
                p.dma("sp", sinS[:], dr["sc_sin"], t_sin, writes=[t_sin])
            else:
                p.dma("sp", posi[:], dap(self.dh["pos"], 0, [[0, 128], [1, NTL]]), t_posi, writes=[t_posi])
                p.op("dve", lambda e: e.tensor_copy(out=ang[:], in_=posi[:]), reads=[t_posi], writes=[t_ang])
                p.op("dve", lambda e: e.tensor_scalar(out=ang[:], in0=ang[:], scalar1=self.freq_c, scalar2=None, op0=ALU.mult),
                     reads=[t_ang, cf_], writes=[t_ang])
                TWO_PI = 2.0 * math.pi
                C_HI = 6.28125
                C_LO = TWO_PI - C_HI

                def sin_of(dst, t_dst, shift):
                    p.op("dve", lambda e: e.tensor_scalar(out=tr1[:], in0=ang[:], scalar1=float(shift), scalar2=float(1.0 / TWO_PI),
                                                          op0=ALU.add, op1=ALU.mult), reads=[t_ang], writes=[t_tr1])
                    p.op("dve", lambda e: e.tensor_copy(out=tri[:], in_=tr1[:]), reads=[t_tr1], writes=[t_tri])
                    p.op("dve", lambda e: e.tensor_copy(out=tr1[:], in_=tri[:]), reads=[t_tri], writes=[t_tr1])
                    p.op("dve", lambda e: e.tensor_scalar(out=tr2[:], in0=ang[:], scalar1=float(shift), scalar2=None, op0=ALU.add),
                         reads=[t_ang], writes=[t_tr2])
                    p.op("dve", lambda e: e.scalar_tensor_tensor(out=tr2[:], in0=tr1[:], scalar=float(-C_HI), in1=tr2[:],
                                                                 op0=ALU.mult, op1=ALU.add), reads=[t_tr1, t_tr2], writes=[t_tr2])
                    p.op("dve", lambda e: e.scalar_tensor_tensor(out=tr2[:], in0=tr1[:], scalar=float(-C_LO), in1=tr2[:],
                                                                 op0=ALU.mult, op1=ALU.add), reads=[t_tr1, t_tr2], writes=[t_tr2])
                    p.op("dve", lambda e: e.tensor_scalar(out=tr1[:], in0=tr2[:], scalar1=float(math.pi), scalar2=float(-TWO_PI),
                                                          op0=ALU.is_gt, op1=ALU.mult), reads=[t_tr2], writes=[t_tr1])
                    p.op("dve", lambda e: e.tensor_tensor(out=tr2[:], in0=tr2[:], in1=tr1[:], op=ALU.add),
                         reads=[t_tr1, t_tr2], writes=[t_tr2])
                    p.op("dve", lambda e: e.tensor_scalar(out=tr1[:], in0=tr2[:], scalar1=float(-math.pi), scalar2=float(TWO_PI),
                                                          op0=ALU.is_lt, op1=ALU.mult), reads=[t_tr2], writes=[t_tr1])
                    p.op("dve", lambda e: e.tensor_tensor(out=tr2[:], in0=tr2[:], in1=tr1[:], op=ALU.add),
                         reads=[t_tr1, t_tr2], writes=[t_tr2])
                    p.op("dve", lambda e: e.tensor_scalar(out=tr2[:], in0=tr2[:], scalar1=float(-3.14159), scalar2=float(3.14159),
                                                          op0=ALU.max, op1=ALU.min), reads=[t_tr2], writes=[t_tr2])
                    p.op("act", lambda e: e.activation(out=dst[:], in_=tr2[:], func=AF.Sin), reads=[t_tr2], writes=[t_dst])

                sin_of(cosT, t_cos, math.pi / 2.0)
                sin_of(sinS, t_sin, 0.0)
                p.op("dve", lambda e: e.tensor_scalar(out=sinS[:], in0=sinS[:], scalar1=self.sgn_c, scalar2=None, op0=ALU.mult),
                     reads=[t_sin, cf_], writes=[t_sin])
                if self.mode == "F":
                    p.dma("pool", dr["sc_cos"], cosT[:], t_cos, reads=[t_cos])
                    p.dma("pool", dr["sc_sin"], sinS[:], t_sin, reads=[t_sin])

            if KSTOP <= 2:
                p.barrier(); return
            if self.mode == "F" and L == 0:
                for f in range(2):
                    for m in (0, 1, 2, self.NG + 3, self.NG + 4, self.NG + 5):
                        for r0 in range(0, HR[f], 128):
                            rows = min(128, HR[f] - r0)
                            p.dma("pool", dap(self.dh["kvb_all%d" % f], (m * HR[f] + r0) * 512, [[512, rows], [1, 512]]),
                                  self.zt[0:rows, :], self.t_zt, reads=[self.t_zt])
            PSA = [2, 3, 4, 5, 6, 7]

            def acc_bank():
                return PSA[self.rot("A_acc", len(PSA))]

            def stage_b():
                i = self.rot("A_stb", NSTB)
                return stb[i], t_stb[i]

            def stage_f():
                i = self.rot("A_stf", NSTF)
                return stf[i], t_stf[i]

            def evac(eng_hint, out, in_, reads, writes, scale=None):
                e = eng_hint if eng_hint else ("act" if self.rot("A_ev", 2) == 0 else "dve")
                if e == "act":
                    if scale is None:
                        p.op("act", lambda en: en.copy(out=out, in_=in_), reads=reads, writes=writes)
                    else:
                        p.op("act", lambda en: en.mul(out=out, in_=in_, mul=float(scale)), reads=reads, writes=writes)
                else:
                    if scale is None:
                        p.op("dve", lambda en: en.tensor_copy(out=out, in_=in_), reads=reads, writes=writes)
                    else:
                        p.op("dve", lambda en: en.tensor_scalar(out=out, in0=in_, scalar1=float(scale), scalar2=None,
                                                                op0=ALU.mult), reads=reads, writes=writes)

            def fmm(col0, M, bank=None):
                b = acc_bank() if bank is None else bank
                for kc in range(8):
                    p.op("pe", lambda e, b=b, kc=kc: e.matmul(self.ps[b][0:M, :], lhsT=Wp[:, kc, col0:col0 + M],
                                                             rhs=xnT[:, kc, :], start=(kc == 0), stop=(kc == 7)),
                         reads=[t_Wp, t_Wp2, t_xnT], writes=[self.t_ps[b]])
                return b

            FUSED = (self.mode == "F")

            def store2d(dst_h, row0, M, lg, src, t_src, prow0=0, q="pool"):
                if FUSED and dst_h is kvb:
                    f, rr = kvmap(row0)
                    d_ = dr["kvb_loc%d" % f]
                    p.dma(q, d_[lg * HR[f] + rr:lg * HR[f] + rr + M, 0:512], src[prow0:prow0 + M, 0:512], t_src, reads=[t_src])
                else:
                    p.dma(q, dst_h[row0:row0 + M, lg * 512:(lg + 1) * 512], src[prow0:prow0 + M, 0:512], t_src, reads=[t_src])

            def vdst(R, lg, blk, W):
                if FUSED:
                    f, rr = kvmap(R)
                    return dap(self.dh["kvb_loc%d" % f], (lg * HR[f] + rr) * 512 + blk * 128 * W, [[W, 128], [1, W]])
                return dap(self.dh["kvb_loc"], R * NTL + (lg * 4 + blk) * 128 * W, [[W, 128], [1, W]])

            def store_nat(dst_h, row0, lg, src, t_src):
                for c in range(4):
                    p.dma("pool", dst_h[row0:row0 + 128, lg * 512 + (3 - c) * 128:lg * 512 + (4 - c) * 128], src[:, c * 128:(c + 1) * 128],
                          t_src, reads=[t_src])

            def ones_cols(st, t_st, nh):
                p.op("pool", lambda e: e.memset(st[:, 0:nh * 65].rearrange("p (h c) -> p h c", h=nh)[:, :, 64:65], 1.0), writes=[t_st])

            kvb = dr.get("kvb_loc", "KVB")
            for lg in range(NGL):
                for blk in range(4):
                    lb = lg * 4 + blk
                    s = self.rot("A_x", 2)
                    p.dma("sp", xs[s][:], x_src[lb * 128:(lb + 1) * 128, :], t_xs[s], writes=[t_xs[s]])
                    p.op("act", lambda e, s=s: e.activation(out=junk[:], in_=xs[s][:], func=AF.Square, accum_out=sst[s][:, 0:1]),
                         reads=[t_xs[s]], writes=[t_junk, t_ss[s]])
                    p.op("act", lambda e, s=s: e.activation(out=sst[s][:, 1:2], in_=sst[s][:, 0:1], func=AF.Sqrt,
                                                           bias=self.eps_c, scale=float(1.0 / D)),
                         reads=[t_ss[s], cf_], writes=[t_ss[s]])
                    p.op("dve", lambda e, s=s: e.reciprocal(out=sst[s][:, 2:3], in_=sst[s][:, 1:2]), reads=[t_ss[s]], writes=[t_ss[s]])
                    p.op("dve", lambda e, s=s: e.tensor_scalar(out=xn[s][:], in0=xs[s][:], scalar1=sst[s][:, 2:3], scalar2=None,
                                                               op0=ALU.mult), reads=[t_xs[s], t_ss[s]], writes=[t_xn[s]])
                    for half in range(2):
                        for i in range(4):
                            kc = half * 4 + i
                            p.op("pe", lambda e, s=s, kc=kc, half=half, i=i: e.transpose(
                                out=self.ps[half][:, i * 128:(i + 1) * 128], in_=xn[s][:, kc * 128:(kc + 1) * 128],
                                identity=self.ident_f), reads=[t_xn[s], cf_], writes=[self.t_ps[half]])
                        evac(None, xnT[:, half * 4:half * 4 + 4, (3 - blk) * 128:(4 - blk) * 128],
                             self.ps[half][:].rearrange("p (a b) -> p a b", a=4), [self.t_ps[half]], [t_xnT])

                if KSTOP <= 3:
                    continue
                for (c0, scale, dst, r0) in ((C_AQ, 0.125, dr["sc_swaq"], 0), (C_AQ + 128, 0.125, dr["sc_swaq"], 128),
                                             (C_AK, None, kvb, R_SWAK),
                                             (C_DQ, 0.125, dr["sc_sbq"], 0), (C_DQ + 128, 0.125, dr["sc_sbq"], 128),
                                             (C_DK, None, kvb, R_SBK), (C_DK + 128, None, kvb, R_SBK + 128)):
                    b = fmm(c0, 128)
                    st, t_st = stage_b()
                    evac(None, st[:, 0:512], self.ps[b][:], [self.t_ps[b]], [t_st], scale)
                    store2d(dst, r0, 128, lg, st, t_st)
                if KSTOP <= 4:
                    continue
                for ct in range(2):
                    b = fmm(C_BB + ct * 128, 128)
                    st, t_st = stage_f()
                    evac(None, st[:], self.ps[b][:], [self.t_ps[b]], [t_st])
                    store_nat(dr["sc_bb"], ct * 128, lg, st, t_st)
                    b1 = fmm(C_BC + ct * 128, 128)
                    b2 = fmm(C_BX + ct * 128, 128)
                    i = self.rot("A_tmpf", 3)
                    evac("act", tmpf[i][:], self.ps[b1][:], [self.t_ps[b1]], [t_tmpf[i]])
                    st, t_st = stage_f()
                    p.op("dve", lambda e, st=st, i=i, b2=b2: e.tensor_tensor(out=st[:], in0=tmpf[i][:], in1=self.ps[b2][:], op=ALU.mult),
                         reads=[t_tmpf[i], self.t_ps[b2]], writes=[t_st])
                    store_nat(dr["sc_u"], ct * 128, lg, st, t_st)
                    if FUSED:
                        tl, t_tl = stage_b()
                        p.op("dve", lambda e: e.tensor_copy(out=tl[:, 0:2], in_=st[:, 126:128]), reads=[t_st], writes=[t_tl])
                        p.op("dve", lambda e: e.tensor_tensor(out=tl[:, 2:4], in0=st[:, 126:128], in1=tl[:, 0:2], op=ALU.subtract),
                             reads=[t_st, t_tl], writes=[t_tl])
                        p.dma("pool", dap(self.dh["kvb_loc0"], (lg * HR[0] + 770) * 512 + ct * 128 * 4, [[4, 128], [1, 4]]),
                              tl[:, 0:4], t_tl, reads=[t_tl])
                    else:
                        p.dma("pool", dr["kvf_loc"][ct * 128:(ct + 1) * 128, lg * 2:lg * 2 + 2], st[:, 126:128], t_st, reads=[t_st])
                if KSTOP <= 5:
                    continue
                for kc in range(2):
                    b = fmm(C_CQ + kc * 128, 128)
                    evac("dve", cqT[:, kc, :], self.ps[b][:], [self.t_ps[b]], [t_cqT])
                    p.op("act", lambda e, b=b, kc=kc: e.activation(out=sq[kc][:], in_=self.ps[b][:], func=AF.Square),
                         reads=[self.t_ps[b]], writes=[t_sq[kc]])
                b = acc_bank()
                for kc in range(2):
                    p.op("pe", lambda e, b=b, kc=kc: e.matmul(self.ps[b][:], lhsT=self.ones_b, rhs=sq[kc][:], start=(kc == 0), stop=(kc == 1)),
                         reads=[t_sq[kc], cb_], writes=[self.t_ps[b]])
                p.op("act", lambda e, b=b: e.activation(out=rq_bc[:], in_=self.ps[b][:], func=AF.Sqrt, bias=self.eps_c, scale=float(1.0 / 256)),
                     reads=[self.t_ps[b], cf_], writes=[t_rq])
                p.op("dve", lambda e: e.reciprocal(out=rq_bc[:], in_=rq_bc[:]), reads=[t_rq], writes=[t_rq])
                if KSTOP <= 5.1:
                    continue
                b = fmm(C_CKV, 128)
                evac("dve", ckvT[:], self.ps[b][:], [self.t_ps[b]], [t_ckvT])
                p.op("act", lambda e, b=b: e.activation(out=sq[0][:], in_=self.ps[b][:], func=AF.Square),
                     reads=[self.t_ps[b]], writes=[t_sq[0]])
                b = acc_bank()
                p.op("pe", lambda e, b=b: e.matmul(self.ps[b][:], lhsT=self.ones_b, rhs=sq[0][:], start=True, stop=True),
                     reads=[t_sq[0], cb_], writes=[self.t_ps[b]])
                p.op("act", lambda e, b=b: e.activation(out=rkv_bc[:], in_=self.ps[b][:], func=AF.Sqrt, bias=self.eps_c, scale=float(1.0 / 128)),
                     reads=[self.t_ps[b], cf_], writes=[t_rkv])
                p.op("dve", lambda e: e.reciprocal(out=rkv_bc[:], in_=rkv_bc[:]), reads=[t_rkv], writes=[t_rkv])
                if KSTOP <= 5.2:
                    continue
                b = acc_bank()
                for blk in range(4):
                    p.op("pe", lambda e, b=b, blk=blk: e.matmul(self.ps[b][:, 2 * blk:2 * blk + 2], lhsT=sq[0][:, blk * 128:(blk + 1) * 128],
                                                               rhs=self.ones_b[:, 0:2], start=True, stop=True),
                         reads=[t_sq[0], cb_], writes=[self.t_ps[b]])
                p.op("act", lambda e, b=b: e.activation(out=rkv_t[:], in_=self.ps[b][:, 0:8], func=AF.Sqrt, bias=self.eps_c, scale=float(1.0 / 128)),
                     reads=[self.t_ps[b], cf_], writes=[t_rkvt])
                p.op("dve", lambda e: e.reciprocal(out=rkv_t[:], in_=rkv_t[:]), reads=[t_rkvt], writes=[t_rkvt])
                if KSTOP <= 6:
                    continue
                b1 = fmm(C_CKR, 32)
                b2 = fmm(C_KRSW, 32)
                i1 = self.rot("A_tmpf", 3)
                p.op("dve", lambda e, i1=i1, b1=b1: e.tensor_tensor(out=tmpf[i1][0:32, :], in0=self.ps[b1][0:32, :],
                                                                    in1=cosT[0:32, lg * 512:(lg + 1) * 512], op=ALU.mult),
                     reads=[self.t_ps[b1], t_cos], writes=[t_tmpf[i1]])
                i2 = self.rot("A_tmpf", 3)
                p.op("dve", lambda e, i2=i2, b2=b2: e.tensor_tensor(out=tmpf[i2][0:32, :], in0=self.ps[b2][0:32, :],
                                                                    in1=sinS[0:32, lg * 512:(lg + 1) * 512], op=ALU.mult),
                     reads=[self.t_ps[b2], t_sin], writes=[t_tmpf[i2]])
                st, t_st = stage_b()
                p.op("dve", lambda e, st=st, i1=i1, i2=i2: e.tensor_tensor(out=st[0:32, 0:512], in0=tmpf[i1][0:32, :], in1=tmpf[i2][0:32, :], op=ALU.add),
                     reads=[t_tmpf[i1], t_tmpf[i2]], writes=[t_st])
                for h in range(4):
                    store2d(kvb, R_MLAK + h * 96 + 64, 32, lg, st, t_st)
                if KSTOP <= 7:
                    continue
                for pair in range(2):
                    b = acc_bank()
                    for kc in range(2):
                        p.op("pe", lambda e, b=b, kc=kc, pair=pair: e.matmul(self.ps[b][:], lhsT=wuq[:, kc, pair * 128:(pair + 1) * 128],
                                                                            rhs=cqT[:, kc, :], start=(kc == 0), stop=(kc == 1)),
                             reads=[t_wuq, t_cqT], writes=[self.t_ps[b]])
                    st, t_st = stage_b()
                    p.op("dve", lambda e, st=st, b=b: e.tensor_tensor(out=st[:, 0:512], in0=self.ps[b][:], in1=rq_bc[:], op=ALU.mult),
                         reads=[self.t_ps[b], t_rq], writes=[t_st])
                    for i in range(2):
                        store2d(dr["sc_mlaq"], (2 * pair + i) * 96, 64, lg, st, t_st, prow0=i * 64)
                    b = acc_bank()
                    p.op("pe", lambda e, b=b, pair=pair: e.matmul(self.ps[b][:], lhsT=wukv[:, pair * 128:(pair + 1) * 128], rhs=ckvT[:],
                                                                 start=True, stop=True), reads=[t_wukv, t_ckvT], writes=[self.t_ps[b]])
                    st, t_st = stage_b()
                    p.op("dve", lambda e, st=st, b=b: e.tensor_tensor(out=st[:, 0:512], in0=self.ps[b][:], in1=rkv_bc[:], op=ALU.mult),
                         reads=[self.t_ps[b], t_rkv], writes=[t_st])
                    for i in range(2):
                        store2d(kvb, R_MLAK + (2 * pair + i) * 96, 64, lg, st, t_st, prow0=i * 64)
                b1 = acc_bank()
                b2 = acc_bank()
                for (b, c0) in ((b1, 256), (b2, 384)):
                    for kc in range(2):
                        p.op("pe", lambda e, b=b, kc=kc, c0=c0: e.matmul(self.ps[b][:], lhsT=wuq[:, kc, c0:c0 + 128], rhs=cqT[:, kc, :],
                                                                        start=(kc == 0), stop=(kc == 1)),
                             reads=[t_wuq, t_cqT], writes=[self.t_ps[b]])
                i1 = self.rot("A_tmpf", 3)
                p.op("dve", lambda e, i1=i1, b1=b1: e.tensor_tensor(out=tmpf[i1][:], in0=self.ps[b1][:], in1=cosT[:, lg * 512:(lg + 1) * 512], op=ALU.mult),
                     reads=[self.t_ps[b1], t_cos], writes=[t_tmpf[i1]])
                i2 = self.rot("A_tmpf", 3)
                p.op("dve", lambda e, i2=i2, b2=b2: e.tensor_tensor(out=tmpf[i2][:], in0=self.ps[b2][:], in1=sinS[:, lg * 512:(lg + 1) * 512], op=ALU.mult),
                     reads=[self.t_ps[b2], t_sin], writes=[t_tmpf[i2]])
                p.op("pool", lambda e, i1=i1, i2=i2: e.tensor_tensor(out=tmpf[i1][:], in0=tmpf[i1][:], in1=tmpf[i2][:], op=ALU.add),
                     reads=[t_tmpf[i1], t_tmpf[i2]], writes=[t_tmpf[i1]])
                st, t_st = stage_b()
                p.op("dve", lambda e, st=st, i1=i1: e.tensor_tensor(out=st[:, 0:512], in0=tmpf[i1][:], in1=rq_bc[:], op=ALU.mult),
                     reads=[t_tmpf[i1], t_rq], writes=[t_st])
                for h in range(4):
                    store2d(dr["sc_mlaq"], h * 96 + 64, 32, lg, st, t_st, prow0=h * 32)

                if KSTOP <= 8:
                    continue
                for blk in range(4):
                    lb = lg * 4 + blk
                    tok = slice(blk * 128, (blk + 1) * 128)
                    b = acc_bank()
                    p.op("pe", lambda e, b=b, tok=tok: e.matmul(self.ps[b][:, 0:256], lhsT=ckvT[:, tok], rhs=wukv[:, 256:512], start=True, stop=True),
                         reads=[t_ckvT, t_wukv], writes=[self.t_ps[b]])
                    st, t_st = stage_b()
                    p.op("dve", lambda e, st=st, b=b, blk=blk: e.tensor_scalar(
                        out=st[:, 0:260].rearrange("p (h c) -> p h c", h=4)[:, :, 0:64], in0=self.ps[b][:, 0:256].rearrange("p (h c) -> p h c", h=4),
                        scalar1=rkv_t[:, 2 * blk:2 * blk + 1], scalar2=None, op0=ALU.mult),
                         reads=[self.t_ps[b], t_rkvt], writes=[t_st])
                    ones_cols(st, t_st, 4)
                    p.dma("pool", vdst(R_MLAV, lg, blk, 260), st[:, 0:260], t_st, reads=[t_st])
                    b = acc_bank()
                    for (c0, n, o0) in ((C_AV, 128, 0), (C_DV, 256, 128)):
                        for kc in range(8):
                            p.op("pe", lambda e, b=b, kc=kc, c0=c0, n=n, o0=o0, tok=tok: e.matmul(
                                self.ps[b][:, o0:o0 + n], lhsT=xnT[:, kc, tok], rhs=Wp[:, kc, c0:c0 + n], start=(kc == 0), stop=(kc == 7)),
                                 reads=[t_Wp, t_Wp2, t_xnT], writes=[self.t_ps[b]])
                    st, t_st = stage_b()
                    evac(None, st[:, 0:130].rearrange("p (h c) -> p h c", h=2)[:, :, 0:64], self.ps[b][:, 0:128].rearrange("p (h c) -> p h c", h=2),
                         [self.t_ps[b]], [t_st])
                    evac(None, st[:, 256:512], self.ps[b][:, 128:384], [self.t_ps[b]], [t_st])
                    ones_cols(st, t_st, 2)
                    p.dma("pool", vdst(R_SWAV, lg, blk, 130), st[:, 0:130], t_st, reads=[t_st])
                    p.dma("pool", vdst(R_SBV, lg, blk, 256), st[:, 256:512], t_st, reads=[t_st])
                    st, t_st = stage_b()
                    for half in range(2):
                        b = acc_bank()
                        c0 = C_GATE + half * 512
                        for kc in range(8):
                            p.op("pe", lambda e, b=b, kc=kc, c0=c0, tok=tok: e.matmul(self.ps[b][:], lhsT=xnT[:, kc, tok], rhs=Wp[:, kc, c0:c0 + 512],
                                                                                     start=(kc == 0), stop=(kc == 7)),
                                 reads=[t_Wp, t_Wp2, t_xnT], writes=[self.t_ps[b]])
                        p.op("act", lambda e, st=st, b=b, half=half: e.activation(out=st[:, half * 512:(half + 1) * 512], in_=self.ps[b][:], func=AF.Silu),
                             reads=[self.t_ps[b]], writes=[t_st])
                    p.dma("pool", dr["sc_gate"][lb * 128:(lb + 1) * 128, :], st[:, :], t_st, reads=[t_st])
                if FUSED:
                    self.gather_group(lg)
            p.barrier()

    def gather_group(self, lg):
        p = self.p
        groups = [[b * NR + r for r in range(NR)] for b in range(NB_BATCH)]
        waits = []
        for k, v in p.all_tickets.items():
            if isinstance(k, tuple) and k[0] in ("sw", "hw") and p.seen["pool"].get(k, 0) < v:
                p.seen["pool"][k] = v
                waits.append((p.sems[k], v))
        p.streams["pool"].append((waits, None, None))
        t_g = Tk()
        for f in range(2):
            p.raw("pool", lambda e, lg=lg, f=f: e.collective_compute(
                "AllGather", ALU.bypass, replica_groups=groups,
                ins=[self.dh["kvb_loc%d" % f].ap()[lg * HR[f]:(lg + 1) * HR[f], :].opt()],
                outs=[self.dh["kvb_all%d" % f].ap()[(3 + 4 * lg) * HR[f]:(7 + 4 * lg) * HR[f], :].opt()]),
                  ("cc", lg * 2 + f), writes=[t_g])

    def phaseB(self, L, x_src, x_dst):
        p, nc = self.p, self.nc
        S, NG, NTL, NGL, NBL = self.S, self.NG, self.NTL, self.NGL, self.NBL
        NBK = S // 128
        NPOS = NBK + 12
        dr = self.dram
        cf_, cb_ = self.t_cf, self.t_cb
        z4 = self.zero4
        with ExitStack() as es:
            def tile(name, shape, dt):
                return es.enter_context(nc.sbuf_tensor("B%d_%s" % (L, name), list(shape), dt))

            yt = [tile("yt%d" % i, [128, 4, 256], F32) for i in range(4)]; t_yt = [Tk() for _ in range(4)]
            small = tile("small", [128, 64], F32); t_small = Tk()
            Wo = tile("Wo", [128, 8, D], BF16); t_Wo = Tk()
            wst = [tile("wst%d" % i, [128, D], F32) for i in range(2)]; t_wst = [Tk(), Tk()]
            ggrp_bc = tile("ggrp", [128, D], F32); t_gg = Tk()
            gpost_bc = tile("gpost", [128, D], F32); t_gp = Tk()
            cwb = tile("cwb", [128, 8], F32); t_cwb = Tk()
            esink = tile("esink", [128, 4], F32); t_es = Tk()
            es_att = ExitStack()

            def atile(name, shape, dt):
                return es_att.enter_context(nc.sbuf_tensor("B%d_%s" % (L, name), list(shape), dt))
            _tile_outer = tile
            tile = atile
            KtAll = tile("KtAll", [128, 4, NPOS * 128], BF16)
            Kt = [KtAll[:, h, :] for h in range(4)]; t_K = [Tk() for _ in range(4)]
            Vt = tile("Vt", [128, NPOS, 264], BF16); t_V = Tk()
            Qa = [tile("Qa%d" % h, [128, 512], BF16) for h in range(4)]
            Qm = [tile("Qm%d" % h, [128, 512], BF16) for h in range(4)]
            t_Qm = [Tk() for _ in range(4)]; t_Qx = [Tk() for _ in range(4)]
            NE = 2
            Et2 = [tile("E%d" % i, [128, 2, 512], F32) for i in range(NE)]; t_E = [Tk() for _ in range(NE)]
            Lt2 = [tile("Lp%d" % i, [128, 2, 512], BF16) for i in range(NE)]; t_L = [Tk() for _ in range(NE)]
            NA = 3
            At2 = [tile("At%d" % i, [128, 2, 512], BF16) for i in range(NA)]; t_A = [Tk() for _ in range(NA)]
            tile = _tile_outer

            p.dma("sp", ggrp_bc[:], dap(self.dh["ggrp"], L * D, [[0, 128], [1, D]]), t_gg, writes=[t_gg])
            p.dma("sp", gpost_bc[:], dap(self.dh["gpost"], L * D, [[0, 128], [1, D]]), t_gp, writes=[t_gp])
            p.dma("sp", cwb[:, 0:6], dr["convw_t"][L], t_cwb, writes=[t_cwb])
            p.dma("sp", cwb[:, 6:8], dr["convb_t"][L], t_cwb, writes=[t_cwb])
            p.dma("sp", esink[:], dap(self.dh["sinks"], L * 4, [[0, 128], [1, 4]]), t_es, writes=[t_es])
            p.op("act", lambda e: e.activation(out=esink[:], in_=esink[:], func=AF.Exp), reads=[t_es], writes=[t_es])

            NCH = 4

            FUSED = (self.mode == "F")
            NQG = NPOS // 4
            FAM = dict(sbK=0, sbV=0, mlaK=1, mlaV=1)
            DQ = "sp" if L == 0 else "act"
            TB = dict(sbK=0, sbV=4, mlaK=8, mlaV=12, swK=16, swV=18, ut=19)

            def load_K(name, nrows, heads=(0, 1, 2, 3)):
                if FUSED:
                    for h in heads:
                        i = TB[name] + h
                        f = FAM[name]
                        p.dyn_dma(DQ, KtAll[0:nrows, h, :], self.dh["kvb_all%d" % f], self.tabt[0:1, i:i + 1],
                                  [[512, nrows], [HR[f] * 512, NQG], [1, 512]], t_K[h], reads=[self.t_tab],
                                  writes=[t_K[h]] + ([t_Kx[h]] if nrows > 64 else []))
                    return
                w = NPOS * 128
                cw = (w + NCH - 1) // NCH
                for h in range(4):
                    for c in range(NCH):
                        c0, c1 = c * cw, min(w, (c + 1) * cw)
                        p.dma("sp", Kt[h][0:nrows, c0:c1], dr[name][h * nrows:(h + 1) * nrows, c0:c1], t_K[h], writes=[t_K[h]])

            def load_V(name, wcols):
                if FUSED:
                    for blk in range(4):
                        i = TB[name] + blk
                        f = FAM[name]
                        p.dyn_dma(DQ, Vt[:, blk:NPOS:4, 0:wcols], self.dh["kvb_all%d" % f], self.tabt[0:1, i:i + 1],
                                  [[wcols, 128], [HR[f] * 512, NQG], [1, wcols]], t_V, reads=[self.t_tab], writes=[t_V])
                    return
                step = 8
                for q0 in range(0, NPOS, step):
                    q1 = min(NPOS, q0 + step)
                    p.dma("sp", Vt[:, q0:q1, 0:wcols], dap(self.dh[name], q0 * 128 * wcols, [[wcols, 128], [128 * wcols, q1 - q0], [1, wcols]]),
                          t_V, writes=[t_V])

            def pos_of(lg, d):
                if FUSED:
                    return 4 * (4 * lg + 3 - d // 4) + d % 4
                return NBK - 4 - 16 * lg + d

            def nstep(lg):
                return 16 * lg + 16

            def acols(d):
                if d < 4:
                    return (d + 1) * 128, True
                return 512, False

            t_Kx = [Tk() for _ in range(4)]
            for h in range(4):
                en = "dve" if h % 2 == 0 else "pool"
                p.op(en, lambda e: e.memset(Kt[h][64:128, :], 0.0), writes=[t_Kx[h]])
                p.op(en, lambda e: e.memset(Kt[h][64:65, :], 1.0), writes=[t_Kx[h]])
                p.op(en, lambda e: e.memset(Kt[h][96:97, :], 1.0), writes=[t_Kx[h]])
            load_K("sbK", 64, (0, 1))
            load_V("sbV", 256)
            load_K("sbK", 64, (2, 3))

            for h in range(4):
                p.op("pool", lambda e, h=h: e.memset(Qm[h][64:128, :], 0.0), writes=[t_Qm[h]])
                p.op("pool", lambda e, h=h: e.memset(Qa[h][64:128, :], 0.0), writes=[t_Qx[h]])
            t_Qd = [Tk() for _ in range(4)]
            for pair in range(2):
              if pair == 1:
                load_K("mlaK", 96, (0, 1))
              for lg in range(NGL):
                ys = self.rot("B_yt", 4)
                if True:
                    hs = (2 * pair, 2 * pair + 1)
                    bank = {}
                    for i, h in enumerate(hs):
                        bank[h] = dict(A=i, B=2 + i, C=4 + i, O=6 + i)
                        qsrc = dr["sc_sbq"][h * 64:(h + 1) * 64, lg * 512:(lg + 1) * 512]
                        p.dma("sp", Qm[h][0:64, :], qsrc, t_Qm[h], writes=[t_Qm[h]])
                        p.dma("sp", Qa[h][0:64, :], qsrc, t_Qd[h], writes=[t_Qd[h]])
                        p.op("pool", lambda e, h=h: e.memset(Qa[h][64:98, :], 0.0), writes=[t_Qx[h]])
                        bC, bO = bank[h]["C"], bank[h]["O"]
                        p.op("pe", lambda e, bC=bC: e.matmul(self.ps[bC][0:98, :], lhsT=self.ones98, rhs=z4, start=True, stop=False, skip_group_check=True),
                             reads=[cb_], writes=[self.t_ps[bC]])
                        p.op("pe", lambda e, bO=bO: e.matmul(self.ps[bO][:, 0:256], lhsT=self.zero_b, rhs=z4[:, 0:256], start=True, stop=False, skip_group_check=True),
                             reads=[cb_], writes=[self.t_ps[bO]])
                    ND = nstep(lg)
                    slot = {}
                    aslot = {}

                    def mm1(d):
                        n, dg = acols(d)
                        P_ = pos_of(lg, d)
                        for h in hs:
                            bA = bank[h]["A"]
                            p.op("pe", lambda e, h=h, bA=bA, P_=P_, n=n: e.matmul(self.ps[bA][:, 0:n], lhsT=Kt[h][:, P_ * 128:(P_ + 1) * 128],
                                                                                rhs=Qm[h][:, 0:n], start=True, stop=True),
                                 reads=[t_K[h], t_Kx[h], t_Qm[h]], writes=[self.t_ps[bA]])

                    def EL(d):
                        n, dg = acols(d)
                        s = self.rot("B_E", 2)
                        slot[d] = s
                        kA = bank[hs[0]]["A"] // 2
                        p.op("act", lambda e: e.activation(out=Et2[s][:, :, 0:n], in_=self.pp[kA][:].rearrange("p (h c) -> p h c", h=2)[:, :, 0:n], func=AF.Exp),
                             reads=[self.t_ps[2 * kA], self.t_ps[2 * kA + 1]], writes=[t_E[s]])
                        p.op("act", lambda e: e.activation(out=Lt2[s][:, :, 0:n], in_=Et2[s][:, :, 0:n], func=AF.Ln, bias=1.0),
                             reads=[t_E[s]], writes=[t_L[s]])
                        if dg:
                            for hi in range(2):
                                p.op("pool", lambda e: e.tensor_tensor(out=Lt2[s][:, hi, n - 128:n], in0=Lt2[s][:, hi, n - 128:n], in1=self.m_lt, op=ALU.mult),
                                     reads=[t_L[s], cb_], writes=[t_L[s]])

                    def mm2tc(d):
                        n, dg = acols(d)
                        P_ = pos_of(lg, d)
                        last = (d == ND - 1)
                        s = slot[d]
                        for hi, h in enumerate(hs):
                            bB, bC = bank[h]["B"], bank[h]["C"]
                            p.op("pe", lambda e, h=h, bB=bB, P_=P_, n=n: e.matmul(self.ps[bB][:, 0:n], lhsT=Kt[h][:, P_ * 128:(P_ + 1) * 128],
                                                                                rhs=Qa[h][:, 0:n], start=True, stop=False),
                                 reads=[t_K[h], t_Kx[h], t_Qd[h], t_Qx[h]], writes=[self.t_ps[bB]])
                            p.op("pe", lambda e, bB=bB, s=s, n=n: e.matmul(self.ps[bB][:, 0:n], lhsT=self.negtri, rhs=Lt2[s][:, hi, 0:n], start=False, stop=True),
                                 reads=[t_L[s], cb_], writes=[self.t_ps[bB]])
                            if not last:
                                p.op("pe", lambda e, bC=bC, s=s, n=n: e.matmul(self.ps[bC][0:98, 0:n], lhsT=self.ones98, rhs=Lt2[s][:, hi, 0:n],
                                                                              start=False, stop=False, skip_group_check=True),
                                     reads=[t_L[s], cb_], writes=[self.t_ps[bC]])
                        if not last:
                            for h in hs:
                                bC = bank[h]["C"]
                                p.op("dve", lambda e, h=h, bC=bC, n=n: e.tensor_scalar(out=Qa[h][64:98, 0:n], in0=self.ps[bC][64:98, 0:n], scalar1=-1.0,
                                                                                      scalar2=None, op0=ALU.mult),
                                     reads=[self.t_ps[bC]], writes=[t_Qx[h]])
                                p.op("dve", lambda e, h=h, bC=bC, n=n: e.scalar_tensor_tensor(out=Qa[h][96:98, 0:n], in0=self.ps[bC][96:98, 0:n], scalar=-1.0,
                                                                                             in1=Qa[h][96:98, 0:n], op0=ALU.mult, op1=ALU.subtract),
                                     reads=[self.t_ps[bC], t_Qx[h]], writes=[t_Qx[h]])

                    def Bexp(d):
                        n, dg = acols(d)
                        a = self.rot("B_A", NA)
                        aslot[d] = a
                        kB = bank[hs[0]]["B"] // 2
                        p.op("act", lambda e: e.activation(out=At2[a][:, :, 0:n], in_=self.pp[kB][:].rearrange("p (h c) -> p h c", h=2)[:, :, 0:n], func=AF.Exp),
                             reads=[self.t_ps[2 * kB], self.t_ps[2 * kB + 1]], writes=[t_A[a]])
                        if dg:
                            for hi in range(2):
                                p.op("pool", lambda e: e.tensor_tensor(out=At2[a][:, hi, n - 128:n], in0=At2[a][:, hi, n - 128:n], in1=self.m_lt, op=ALU.mult),
                                     reads=[t_A[a], cb_], writes=[t_A[a]])

                    def PV(d):
                        n, dg = acols(d)
                        P_ = pos_of(lg, d)
                        last = (d == ND - 1)
                        a = aslot[d]
                        for hi, h in enumerate(hs):
                            bO = bank[h]["O"]
                            for c in range(n // 128):
                                p.op("pe", lambda e, h=h, bO=bO, a=a, c=c, P_=P_: e.matmul(self.ps[bO][:, c * 64:(c + 1) * 64], lhsT=At2[a][:, hi, c * 128:(c + 1) * 128],
                                                                                         rhs=Vt[:, P_, h * 64:(h + 1) * 64], start=False, stop=last,
                                                                                         skip_group_check=True),
                                     reads=[t_A[a], t_V], writes=[self.t_ps[bO]])

                    mm1(0)
                    EL(0)
                    if ND > 1:
                        mm1(1)
                    for d in range(ND):
                        mm2tc(d)
                        if d >= 1:
                            PV(d - 1)
                        if d + 1 < ND:
                            EL(d + 1)
                        if d + 2 < ND:
                            mm1(d + 2)
                        Bexp(d)
                    PV(ND - 1)
                    for h in hs:
                        bO = bank[h]["O"]
                        p.op("dve", lambda e, h=h, bO=bO, ys=ys: e.tensor_copy(out=yt[ys][:, :, h * 64:(h + 1) * 64],
                                                                               in_=self.ps[bO][:, 0:256].rearrange("p (j c) -> p j c", j=4)),
                             reads=[self.t_ps[bO]], writes=[t_yt[ys]])
                p.dma("pool", dap(self.dh["sc_y"], lg * 4 * 128 * D + 768 + pair * 128, [[D, 128], [128 * D, 4], [1, 128]]), yt[ys][:, :, pair * 128:(pair + 1) * 128], t_yt[ys], reads=[t_yt[ys]])

            load_V("mlaV", 260)
            load_K("mlaK", 96, (2, 3))
            for kc in range(8):
                s = kc % 2
                p.dma("sp", wst[s][:], dr["w_out"][L, kc * 128:(kc + 1) * 128, :], t_wst[s], writes=[t_wst[s]])
                p.op("dve", lambda e, s=s, kc=kc: e.tensor_copy(out=Wo[:, kc, :], in_=wst[s][:]), reads=[t_wst[s]], writes=[t_Wo])
            for pair in range(2):
              for lg in range(NGL):
                ys = self.rot("B_yt", 4)
                if True:
                    hs = (2 * pair, 2 * pair + 1)
                    bank = {}
                    for i, h in enumerate(hs):
                        bank[h] = dict(S=(i, 2 + i, 6 + i), O=4 + i)
                        p.dma("sp", Qa[h][0:96, :], dr["sc_mlaq"][h * 96:(h + 1) * 96, lg * 512:(lg + 1) * 512], t_Qm[h], writes=[t_Qm[h], t_Qx[h], t_Qd[h]])
                        bO = bank[h]["O"]
                        p.op("pe", lambda e, bO=bO: e.matmul(self.ps[bO][:, 0:260], lhsT=self.zero_b, rhs=z4[:, 0:260], start=True, stop=False, skip_group_check=True),
                             reads=[cb_], writes=[self.t_ps[bO]])
                    ND = nstep(lg)

                    def mmS(d):
                        n, dg = acols(d)
                        P_ = pos_of(lg, d)
                        for h in hs:
                            bS = bank[h]["S"][d % 3]
                            p.op("pe", lambda e, h=h, bS=bS, P_=P_, n=n: e.matmul(self.ps[bS][:, 0:n], lhsT=Kt[h][0:96, P_ * 128:(P_ + 1) * 128],
                                                                                rhs=Qa[h][0:96, 0:n], start=True, stop=True),
                                 reads=[t_K[h], t_Qm[h]], writes=[self.t_ps[bS]])

                    aslot = {}

                    def Pexp(d):
                        n, dg = acols(d)
                        a = self.rot("B_A", NA)
                        aslot[d] = a
                        kS = (0, 1, 3)[d % 3]
                        p.op("act", lambda e: e.activation(out=At2[a][:, :, 0:n], in_=self.pp[kS][:].rearrange("p (h c) -> p h c", h=2)[:, :, 0:n], func=AF.Exp),
                             reads=[self.t_ps[2 * kS], self.t_ps[2 * kS + 1]], writes=[t_A[a]])
                        if dg:
                            for hi in range(2):
                                p.op("pool", lambda e: e.tensor_tensor(out=At2[a][:, hi, n - 128:n], in0=At2[a][:, hi, n - 128:n], in1=self.m_le, op=ALU.mult),
                                     reads=[t_A[a], cb_], writes=[t_A[a]])

                    def PVm(d):
                        n, dg = acols(d)
                        P_ = pos_of(lg, d)
                        last = (d == ND - 1)
                        a = aslot[d]
                        for hi, h in enumerate(hs):
                            bO = bank[h]["O"]
                            for c in range(n // 128):
                                p.op("pe", lambda e: e.matmul(self.ps[bO][:, c * 65:(c + 1) * 65], lhsT=At2[a][:, hi, c * 128:(c + 1) * 128],
                                                              rhs=Vt[:, P_, h * 65:(h + 1) * 65], start=False, stop=last, skip_group_check=True),
                                     reads=[t_A[a], t_V], writes=[self.t_ps[bO]])

                    mmS(0)
                    if ND > 1:
                        mmS(1)
                    for d in range(ND):
                        if d + 2 < ND:
                            mmS(d + 2)
                        if d >= 1:
                            PVm(d - 1)
                        Pexp(d)
                    PVm(ND - 1)
                    for h in hs:
                        bO = bank[h]["O"]
                        p.op("dve", lambda e, bO=bO, h=h: e.reciprocal(out=small[:, 8 + h * 4:12 + h * 4],
                                                                       in_=self.ps[bO][:, 0:260].rearrange("p (j c) -> p j c", j=4)[:, :, 64]),
                             reads=[self.t_ps[bO]], writes=[t_small])
                        for c in range(4):
                            p.op("dve", lambda e, bO=bO, h=h, c=c, ys=ys: e.tensor_scalar(out=yt[ys][:, c, h * 64:(h + 1) * 64], in0=self.ps[bO][:, c * 65:c * 65 + 64],
                                                                                         scalar1=small[:, 8 + h * 4 + c:9 + h * 4 + c], scalar2=None, op0=ALU.mult),
                                 reads=[self.t_ps[bO], t_small], writes=[t_yt[ys]])
                p.dma("pool", dap(self.dh["sc_y"], lg * 4 * 128 * D + 512 + pair * 128, [[D, 128], [128 * D, 4], [1, 128]]), yt[ys][:, :, pair * 128:(pair + 1) * 128], t_yt[ys], reads=[t_yt[ys]])

            p.barrier()
            es_att.close()
            es_sw = ExitStack()

            def stile(name, shape, dt):
                return es_sw.enter_context(nc.sbuf_tensor("B%d_%s" % (L, name), list(shape), dt))
            tile = stile
            swQ = [tile("swQ%d" % i, [64, 4, 512], BF16) for i in range(2)]; t_swQ = [Tk(), Tk()]
            swK = [tile("swK%d" % i, [64, 2, 640], BF16) for i in range(2)]; t_swK = [Tk(), Tk()]
            swV = [tile("swV%d" % i, [128, 5, 130], BF16) for i in range(2)]; t_swV = [Tk(), Tk()]
            Pc = [tile("Pc%d" % i, [128, 512], BF16) for i in range(2)]; t_Pc = [Tk(), Tk()]
            Pp = [tile("Pp%d" % i, [128, 512], BF16) for i in range(2)]; t_Pp = [Tk(), Tk()]
            ut = [tile("ut%d" % i, [128, 2, 514], F32) for i in range(2)]; t_ut = [Tk(), Tk()]
            bbt = [tile("bbt%d" % i, [128, 2, 512], F32) for i in range(2)]; t_bbt = [Tk(), Tk()]
            cvt = [tile("cvt%d" % i, [128, 2, 512], F32) for i in range(2)]; t_cvt = [Tk(), Tk()]
            if FUSED:
                swKp = tile("swKp", [64, 2, NGL, 128], BF16); t_swKp = Tk()
                swVp = tile("swVp", [128, NGL, 130], BF16); t_swVp = Tk()
                utl = tile("utl", [128, 2, NGL, 4], BF16); t_utl = Tk()
                for kvh in range(2):
                    i = TB["swK"] + kvh
                    p.dyn_dma("pool", swKp[:, kvh, :, :], self.dh["kvb_all0"], self.tabt[0:1, i:i + 1], [[512, 64], [4 * HR[0] * 512, NGL], [1, 128]], t_swKp,
                              reads=[self.t_tab], writes=[t_swKp])
                i = TB["swV"]
                p.dyn_dma("pool", swVp[:], self.dh["kvb_all0"], self.tabt[0:1, i:i + 1], [[130, 128], [4 * HR[0] * 512, NGL], [1, 130]], t_swVp,
                          reads=[self.t_tab], writes=[t_swVp])
                for ct in range(2):
                    i = TB["ut"] + ct
                    p.dyn_dma("pool", utl[:, ct, :, :], self.dh["kvb_all0"], self.tabt[0:1, i:i + 1], [[4, 128], [4 * HR[0] * 512, NGL], [1, 4]], t_utl,
                              reads=[self.t_tab], writes=[t_utl])
            ys_of = {}

            def swa_load(lg):
                s = lg % 2
                ys_of[lg] = self.rot("B_yt", 4)
                p.dma("sp", swQ[s][:], dap(self.dh["sc_swaq"], lg * 512, [[NTL, 64], [64 * NTL, 4], [1, 512]]), t_swQ[s], writes=[t_swQ[s]])
                if FUSED:
                    kl = self.dh["kvb_loc0"]
                    p.dma("sp", swK[s][:, :, 0:512], dap(kl, (lg * HR[0] + 512) * 512, [[512, 64], [64 * 512, 2], [1, 512]]), t_swK[s], writes=[t_swK[s]])
                    p.dma("sp", swV[s][:, 0:4, :], dap(kl, (lg * HR[0] + 640) * 512, [[130, 128], [128 * 130, 4], [1, 130]]), t_swV[s], writes=[t_swV[s]])
                    p.op("pool", lambda e: e.tensor_copy(out=swK[s][:, :, 512:640], in_=swKp[:, :, lg, :]), reads=[t_swKp], writes=[t_swK[s]])
                    p.op("pool", lambda e: e.tensor_copy(out=swV[s][:, 4, :], in_=swVp[:, lg, :]), reads=[t_swVp], writes=[t_swV[s]])
                else:
                    p.dma("sp", swK[s][:], dap(self.dh["swK"], lg * 640, [[NGL * 640, 64], [64 * NGL * 640, 2], [1, 640]]), t_swK[s], writes=[t_swK[s]])
                    p.dma("sp", swV[s][:], dap(self.dh["swV"], lg * 5 * 128 * 130, [[130, 128], [128 * 130, 5], [1, 130]]), t_swV[s], writes=[t_swV[s]])

            def swa_A(lg, c, idx):
                s = lg % 2
                q = idx % 2
                bC, bP = 0, 1
                for h in range(4):
                    p.op("pe", lambda e: e.matmul(self.ps[bC][:, h * 128:(h + 1) * 128], lhsT=swK[s][:, h // 2, c * 128:(c + 1) * 128],
                                                  rhs=swQ[s][:, h, c * 128:(c + 1) * 128], start=True, stop=True),
                         reads=[t_swK[s], t_swQ[s]], writes=[self.t_ps[bC]])
                for h in range(4):
                    p.op("pe", lambda e: e.matmul(self.ps[bP][:, h * 128:(h + 1) * 128], lhsT=swK[s][:, h // 2, (c + 1) * 128:(c + 2) * 128],
                                                  rhs=swQ[s][:, h, c * 128:(c + 1) * 128], start=True, stop=True),
                         reads=[t_swK[s], t_swQ[s]], writes=[self.t_ps[bP]])
                p.op("act", lambda e: e.activation(out=Pc[q][:], in_=self.ps[bC][:], func=AF.Exp), reads=[self.t_ps[bC]], writes=[t_Pc[q]])
                p.op("act", lambda e: e.activation(out=Pp[q][:], in_=self.ps[bP][:], func=AF.Exp), reads=[self.t_ps[bP]], writes=[t_Pp[q]])
                m_le4 = bass.AP(self.cb, 256, [[896, 128], [0, 4], [1, 128]])
                m_gt4 = bass.AP(self.cb, 384, [[896, 128], [0, 4], [1, 128]])
                p.op("pool", lambda e: e.tensor_tensor(out=Pc[q][:].rearrange("p (h c) -> p h c", h=4), in0=Pc[q][:].rearrange("p (h c) -> p h c", h=4),
                                                       in1=m_le4, op=ALU.mult), reads=[t_Pc[q], cb_], writes=[t_Pc[q]])
                p.op("dve", lambda e: e.tensor_tensor(out=Pp[q][:].rearrange("p (h c) -> p h c", h=4), in0=Pp[q][:].rearrange("p (h c) -> p h c", h=4),
                                                      in1=m_gt4, op=ALU.mult), reads=[t_Pp[q], cb_], writes=[t_Pp[q]])

            def swa_B(lg, c, idx):
                s = lg % 2
                q = idx % 2
                ys = ys_of[lg]
                bO = 2 + (idx % 2)
                p.op("pe", lambda e: e.matmul(self.ps[bO][:, 0:260], lhsT=self.zero_b, rhs=z4[:, 0:260], start=True, stop=False,
                                              skip_group_check=True), reads=[cb_], writes=[self.t_ps[bO]])
                for h in range(4):
                    kvh = h // 2
                    p.op("pe", lambda e: e.matmul(self.ps[bO][:, h * 65:(h + 1) * 65], lhsT=Pc[q][:, h * 128:(h + 1) * 128],
                                                  rhs=swV[s][:, c, kvh * 65:(kvh + 1) * 65], start=False, stop=False, skip_group_check=True),
                         reads=[t_Pc[q], t_swV[s]], writes=[self.t_ps[bO]])
                    p.op("pe", lambda e: e.matmul(self.ps[bO][:, h * 65:(h + 1) * 65], lhsT=Pp[q][:, h * 128:(h + 1) * 128],
                                                  rhs=swV[s][:, c + 1, kvh * 65:(kvh + 1) * 65], start=False, stop=True, skip_group_check=True),
                         reads=[t_Pp[q], t_swV[s]], writes=[self.t_ps[bO]])
                p.op("dve", lambda e: e.tensor_tensor(out=small[:, 0:4], in0=self.ps[bO][:, 0:260].rearrange("p (h c) -> p h c", h=4)[:, :, 64],
                                                      in1=esink[:], op=ALU.add), reads=[self.t_ps[bO], t_es], writes=[t_small])
                p.op("dve", lambda e: e.reciprocal(out=small[:, 4:8], in_=small[:, 0:4]), reads=[t_small], writes=[t_small])
                rec4 = bass.AP(small, 4, [[64, 128], [1, 4], [0, 64]])
                p.op("dve", lambda e: e.tensor_tensor(out=yt[ys][:, c, :].rearrange("p (h c) -> p h c", h=4),
                                                      in0=self.ps[bO][:, 0:260].rearrange("p (h c) -> p h c", h=4)[:, :, 0:64], in1=rec4, op=ALU.mult),
                     reads=[self.t_ps[bO], t_small], writes=[t_yt[ys]])
                if c == 3:
                    p.dma("pool", dap(self.dh["sc_y"], lg * 4 * 128 * D + 0, [[D, 128], [128 * D, 4], [1, 256]]), yt[ys][:], t_yt[ys], reads=[t_yt[ys]])

            def conv_compute(lg):
                s = lg % 2
                p.dma("sp", ut[s][:, :, 2:514], dap(self.dh["sc_u"], lg * 512, [[NTL, 128], [128 * NTL, 2], [1, 512]]), t_ut[s], writes=[t_ut[s]])
                p.dma("sp", bbt[s][:], dap(self.dh["sc_bb"], lg * 512, [[NTL, 128], [128 * NTL, 2], [1, 512]]), t_bbt[s], writes=[t_bbt[s]])
                if FUSED:
                    p.op("pool", lambda e: e.tensor_tensor(out=ut[s][:, :, 0:2], in0=utl[:, :, lg, 0:2], in1=utl[:, :, lg, 2:4], op=ALU.add),
                         reads=[t_utl], writes=[t_ut[s]])
                else:
                    p.dma("sp", ut[s][:, :, 0:2], dap(self.dh["utail"], lg * 2, [[NGL * 2, 128], [128 * NGL * 2, 2], [1, 2]]), t_ut[s], writes=[t_ut[s]])
                for ct in range(2):
                    p.op("dve", lambda e, s=s, ct=ct: e.tensor_scalar(out=cvt[s][:, ct, :], in0=ut[s][:, ct, 0:512], scalar1=cwb[:, ct * 3:ct * 3 + 1], scalar2=None, op0=ALU.mult),
                         reads=[t_ut[s], t_cwb], writes=[t_cvt[s]])
                    for kk in (1, 2):
                        p.op("dve", lambda e, s=s, ct=ct, kk=kk: e.scalar_tensor_tensor(out=cvt[s][:, ct, :], in0=ut[s][:, ct, kk:kk + 512], scalar=cwb[:, ct * 3 + kk:ct * 3 + kk + 1],
                                                                                       in1=cvt[s][:, ct, :], op0=ALU.mult, op1=ALU.add),
                             reads=[t_ut[s], t_cwb, t_cvt[s]], writes=[t_cvt[s]])
                    p.op("dve", lambda e, s=s, ct=ct: e.scalar_tensor_tensor(out=cvt[s][:, ct, :], in0=cvt[s][:, ct, :], scalar=cwb[:, 6 + ct:7 + ct], in1=bbt[s][:, ct, :],
                                                                            op0=ALU.add, op1=ALU.mult),
                         reads=[t_cvt[s], t_cwb, t_bbt[s]], writes=[t_cvt[s]])

            def conv_out(lg):
                s = lg % 2
                ys2 = self.rot("B_yt", 4)
                for c in range(4):
                    bT = 4 + (c % 2)
                    blk = 3 - c
                    for ct in range(2):
                        p.op("pe", lambda e, s=s, ct=ct, blk=blk, bT=bT: e.transpose(out=self.ps[bT][:, ct * 128:(ct + 1) * 128], in_=cvt[s][:, ct, blk * 128:(blk + 1) * 128],
                                                                                    identity=self.ident_f),
                             reads=[t_cvt[s], cf_], writes=[self.t_ps[bT]])
                    p.op("act", lambda e, bT=bT, c=c, ys2=ys2: e.copy(out=yt[ys2][:, c, :], in_=self.ps[bT][:, 0:256]), reads=[self.t_ps[bT]], writes=[t_yt[ys2]])
                p.dma("pool", dap(self.dh["sc_y"], lg * 4 * 128 * D + 256, [[D, 128], [128 * D, 4], [1, 256]]), yt[ys2][:], t_yt[ys2], reads=[t_yt[ys2]])


            items = [(lg, c) for lg in range(NGL) for c in range(4)]
            swa_load(0)
            conv_compute(0)
            swa_A(0, 0, 0)
            for idx, (lg, c) in enumerate(items):
                if c == 0 and lg + 1 < NGL:
                    conv_compute(lg + 1)
                if idx + 1 < len(items):
                    nlg, nc_ = items[idx + 1]
                    if nc_ == 0:
                        swa_load(nlg)
                    swa_A(nlg, nc_, idx + 1)
                swa_B(lg, c, idx)
                if c == 3:
                    conv_out(lg)

            p.barrier()
            es_sw.close()
            tile = _tile_outer

            NS = 3
            yb = [tile("yb%d" % i, [128, D], F32) for i in range(NS)]; t_yb = [Tk() for _ in range(NS)]
            gb_ = [tile("gb%d" % i, [128, D], BF16) for i in range(NS)]; t_gb = [Tk() for _ in range(NS)]
            xb = [tile("xb%d" % i, [128, D], F32) for i in range(NS)]; t_xb = [Tk() for _ in range(NS)]
            ob = [tile("ob%d" % i, [128, D], F32) for i in range(3)]; t_ob = [Tk(), Tk(), Tk()]
            junk = tile("junkB", [128, D], F32); t_junk = Tk()
            junk2 = tile("junkB2", [128, D], F32); t_junk2 = Tk()
            ygT = [tile("ygT%d" % i, [128, 8, 128], BF16) for i in range(3)]; t_ygT = [Tk(), Tk(), Tk()]
            sm = [tile("sm%d" % i, [128, 16], F32) for i in range(NS)]; t_sm = [Tk() for _ in range(NS)]
            sm2 = [tile("sm2_%d" % i, [128, 8], F32) for i in range(3)]; t_sm2 = [Tk(), Tk(), Tk()]

            def rows_of(sl):
                lg, c = sl // 4, sl % 4
                lb = lg * 4 + (3 - c)
                return slice(sl * 128, (sl + 1) * 128), slice(lb * 128, (lb + 1) * 128)

            def S1a(sl):
                s = sl % NS
                srow, rows = rows_of(sl)
                p.dma("sp", yb[s][:], dr["sc_y"][srow, :], t_yb[s], writes=[t_yb[s]])
                p.dma("sp", gb_[s][:], dr["sc_gate"][srow, :], t_gb[s], writes=[t_gb[s]])
                p.dma("sp", xb[s][:], x_src[rows, :], t_xb[s], writes=[t_xb[s]])
                for g in range(4):
                    p.op("act", lambda e: e.activation(out=junk[:, g * 256:(g + 1) * 256], in_=yb[s][:, g * 256:(g + 1) * 256], func=AF.Square,
                                                       accum_out=sm[s][:, g:g + 1]),
                         reads=[t_yb[s]], writes=[t_junk, t_sm[s]])
                p.op("act", lambda e: e.activation(out=sm[s][:, 4:8], in_=sm[s][:, 0:4], func=AF.Sqrt, bias=self.eps_c, scale=float(1.0 / 256)),
                     reads=[t_sm[s], cf_], writes=[t_sm[s]])
                p.op("dve", lambda e: e.reciprocal(out=sm[s][:, 8:12], in_=sm[s][:, 4:8]), reads=[t_sm[s]], writes=[t_sm[s]])
                for g in range(4):
                    p.op("dve", lambda e: e.scalar_tensor_tensor(out=yb[s][:, g * 256:(g + 1) * 256], in0=yb[s][:, g * 256:(g + 1) * 256], scalar=sm[s][:, 8 + g:9 + g],
                                                                 in1=ggrp_bc[:, g * 256:(g + 1) * 256], op0=ALU.mult, op1=ALU.mult),
                         reads=[t_yb[s], t_sm[s], t_gg], writes=[t_yb[s]])
                p.op("pool", lambda e: e.tensor_tensor(out=yb[s][:], in0=yb[s][:], in1=gb_[s][:], op=ALU.mult), reads=[t_yb[s], t_gb[s]], writes=[t_yb[s]])

            def S1b(sl):
                s = sl % NS
                u = sl % 3
                for half in range(2):
                    for i in range(4):
                        kc = half * 4 + i
                        p.op("pe", lambda e: e.transpose(out=self.ps[half][:, i * 128:(i + 1) * 128], in_=yb[s][:, kc * 128:(kc + 1) * 128],
                                                         identity=self.ident_f), reads=[t_yb[s], cf_], writes=[self.t_ps[half]])
                    if half == 0:
                        p.op("act", lambda e: e.copy(out=ygT[u][:, 0:4, :], in_=self.ps[0][:].rearrange("p (a b) -> p a b", a=4)),
                             reads=[self.t_ps[0]], writes=[t_ygT[u]])
                    else:
                        p.op("dve", lambda e: e.tensor_copy(out=ygT[u][:, 4:8, :], in_=self.ps[1][:].rearrange("p (a b) -> p a b", a=4)),
                             reads=[self.t_ps[1]], writes=[t_ygT[u]])

            def S2(sl):
                s = sl % NS
                u = sl % 3
                srow, rows = rows_of(sl)
                bo = (2 + 2 * u, 3 + 2 * u)
                for half in range(2):
                    b = bo[half]
                    for kc in range(8):
                        p.op("pe", lambda e: e.matmul(self.ps[b][:], lhsT=ygT[u][:, kc, :], rhs=Wo[:, kc, half * 512:(half + 1) * 512],
                                                      start=(kc == 0), stop=(kc == 7)),
                             reads=[t_ygT[u], t_Wo], writes=[self.t_ps[b]])
                    p.op("act", lambda e: e.activation(out=junk2[:, half * 512:(half + 1) * 512], in_=self.ps[b][:], func=AF.Square,
                                                       accum_out=sm2[u][:, half:half + 1]),
                         reads=[self.t_ps[b]], writes=[t_junk2, t_sm2[u]])
                p.op("dve", lambda e: e.tensor_tensor(out=sm2[u][:, 2:3], in0=sm2[u][:, 0:1], in1=sm2[u][:, 1:2], op=ALU.add), reads=[t_sm2[u]], writes=[t_sm2[u]])
                p.op("act", lambda e: e.activation(out=sm2[u][:, 3:4], in_=sm2[u][:, 2:3], func=AF.Sqrt, bias=self.eps_c, scale=float(1.0 / D)),
                     reads=[t_sm2[u], cf_], writes=[t_sm2[u]])
                p.op("dve", lambda e: e.reciprocal(out=sm2[u][:, 4:5], in_=sm2[u][:, 3:4]), reads=[t_sm2[u]], writes=[t_sm2[u]])
                for half in range(2):
                    b = bo[half]
                    p.op("dve", lambda e: e.scalar_tensor_tensor(out=ob[u][:, half * 512:(half + 1) * 512], in0=self.ps[b][:], scalar=sm2[u][:, 4:5],
                                                                 in1=gpost_bc[:, half * 512:(half + 1) * 512], op0=ALU.mult, op1=ALU.mult),
                         reads=[self.t_ps[b], t_sm2[u], t_gp], writes=[t_ob[u]])
                p.op("pool", lambda e: e.tensor_tensor(out=ob[u][:], in0=ob[u][:], in1=xb[s][:], op=ALU.add), reads=[t_ob[u], t_xb[s]], writes=[t_ob[u]])
                p.dma("pool", x_dst[rows, :], ob[u][:], t_ob[u], reads=[t_ob[u]])

            S1a(0)
            if NBL > 1:
                S1a(1)
            S1b(0)
            for sl in range(NBL):
                if sl + 2 < NBL:
                    S1a(sl + 2)
                if sl + 1 < NBL:
                    S1b(sl + 1)
                S2(sl)
            p.barrier()

    def build(self):
        self.declare()
        self.common()
        p = self.p
        self.zero4_t = p.tile("zero4", [128, 512], BF16)
        self.zero4 = self.zero4_t[:]
        p.op("pool", lambda e: e.memset(self.zero4_t[:], 0.0), reads=[self.t_cb], writes=[self.t_cb])
        dr = self.dram
        if self.mode == "A":
            self.phaseA(0, dr["x"])
        elif self.mode == "B":
            self.phaseB(0, dr["x"], dr["out"])
        else:
            NTL, NGL = self.NTL, self.NGL
            self.tabt = p.tile("tabt", [1, 128], I32)
            self.t_tab = Tk()
            p.dma("sp", self.tabt[:], dr["tab"], self.t_tab, writes=[self.t_tab])
            self.zt = p.tile("zt", [128, 512], BF16)
            self.t_zt = Tk()
            p.op("pool", lambda e: e.memset(self.zt[:], 0.0), writes=[self.t_zt])
            NG = self.NG
            groups = [[b * NR + r for r in range(NR)] for b in range(NB_BATCH)]
            import os
            KF = os.environ.get("KFSTOP", "")
            for l in range(DEPTH):
                if KF and l >= int(KF[0]):
                    break
                x_src = dr["x"] if l == 0 else dr["sc_x1"]
                x_dst = dr["out"] if l == DEPTH - 1 else dr["sc_x1"]
                p.phase_begin()
                self.phaseA(l, x_src)
                p.barrier()
                p.phase_end()
                if KF.endswith("a"):
                    break
                p.barrier()
                if KF.endswith("g"):
                    break
                p.phase_begin()
                self.phaseB(l, x_src, x_dst)
                p.barrier()
                p.phase_end()
        p.barrier()
        p.emit()
        self.es.close()
        return self.nc


def make_consts():
    cf = np.zeros((128, 640), np.float32)
    cf[:, 0:128] = np.eye(128, dtype=np.float32)
    cf[:, 128:256] = 1.0
    cf[:, 256] = EPS
    cf[:, 257] = 1.0
    half = 16
    freqs = (10000.0 ** (-np.arange(half, dtype=np.float32) / half)).astype(np.float32)
    pidx = np.arange(128)
    cf[:, 258] = freqs[(pidx % 32) % 16]
    cf[:, 259] = np.where((pidx % 32) < 16, -1.0, 1.0)
    cb = np.zeros((128, 896), np.float32)
    cb[:, 768:896] = 1.0
    j = np.arange(128)[:, None]
    i = np.arange(128)[None, :]
    cb[:, 0:128] = -1.0 * (j >= i)
    cb[:, 128:256] = (j < i)
    cb[:, 256:384] = (j <= i)
    cb[:, 384:512] = (j > i)
    for m in (64, 65, 96, 97):
        cb[:, 512 + m] = 1.0
    return cf, cb.astype(ml_dtypes.bfloat16)


def f_order_rows(ngl):
    idx = []
    for lg in range(ngl):
        for c in range(4):
            b = lg * 4 + (3 - c)
            idx.extend(range(b * 128, (b + 1) * 128))
    return np.array(idx)


def prep_weights(inp):
    w = {}
    w["gpre_t"] = np.ascontiguousarray(inp["norm_pre"].reshape(DEPTH, 8, 128).transpose(0, 2, 1))
    w_in = inp["w_in"]
    sw = np.concatenate([w_in[:, :, C_CKR + 16:C_CKR + 32], w_in[:, :, C_CKR:C_CKR + 16]], axis=2)
    w["w_in_x"] = np.ascontiguousarray(np.concatenate([w_in, sw], axis=2))
    w["gcq_t"] = np.ascontiguousarray(inp["mla_q_norm"].reshape(DEPTH, 2, 128).transpose(0, 2, 1))
    uq = inp["mla_w_uq"].reshape(DEPTH, 256, 4, 96)
    nope = uq[..., 0:64].reshape(DEPTH, 256, 256)
    rope = uq[..., 64:96]
    rope_sw = np.concatenate([rope[..., 16:32], rope[..., 0:16]], axis=-1)
    w["w_uq_x"] = np.ascontiguousarray(np.concatenate([nope, rope.reshape(DEPTH, 256, 128), rope_sw.reshape(DEPTH, 256, 128)], axis=2))
    w["gkv_t"] = np.ascontiguousarray(inp["mla_kv_norm"].reshape(DEPTH, 1, 128).transpose(0, 2, 1))
    ukv = inp["mla_w_ukv"].reshape(DEPTH, 128, 4, 128)
    w["w_ukv_x"] = np.ascontiguousarray(np.concatenate([ukv[..., 0:64].reshape(DEPTH, 128, 256), ukv[..., 64:128].reshape(DEPTH, 128, 256)], axis=2))
    w["sinks"] = np.ascontiguousarray(inp["attn_sinks"])
    w["convw_t"] = np.ascontiguousarray(inp["conv_w"].reshape(DEPTH, 3, 2, 128).transpose(0, 3, 2, 1).reshape(DEPTH, 128, 6))
    w["convb_t"] = np.ascontiguousarray(inp["conv_b"].reshape(DEPTH, 2, 128).transpose(0, 2, 1))
    w["ggrp"] = np.ascontiguousarray(inp["group_norm"])
    w["w_out"] = np.ascontiguousarray(inp["w_out"])
    w["gpost"] = np.ascontiguousarray(inp["norm_post"])
    return {k: np.asarray(v, np.float32) for k, v in w.items()}


_CACHE = {}


def get_prog(mode, S):
    key = (mode, S)
    if key not in _CACHE:
        _CACHE[key] = Builder(mode, S).build()
    return _CACHE[key]


A_WKEYS = ("gpre_t", "w_in_x", "gcq_t", "w_uq_x", "gkv_t", "w_ukv_x")
B_WKEYS = ("sinks", "convw_t", "convb_t", "ggrp", "w_out", "gpost")
SC_KEYS = ("sc_sbq", "sc_mlaq", "sc_swaq", "sc_u", "sc_bb", "sc_gate")


def arrange_B(resA, S):
    NG = S // 512
    NGL = NG // NR
    NTL = NGL * 512
    NBK = S // 128
    NPOS = NBK + 12
    outs = []
    for core in range(NB_BATCH * NR):
        b, r = divmod(core, NR)
        sbK = np.zeros((256, NPOS * 128), ml_dtypes.bfloat16)
        mlaK = np.zeros((384, NPOS * 128), ml_dtypes.bfloat16)
        sbV = np.zeros((NPOS * 128, 256), ml_dtypes.bfloat16)
        mlaV = np.zeros((NPOS * 128, 260), ml_dtypes.bfloat16)
        for qg in range(NPOS // 4):
            G = NG - 1 + r - qg
            if 0 <= G < NG:
                src = resA[b * NR + (G % NR)]
                lgs = G // NR
                kv = src["kvb_loc"]
                flat = kv.reshape(-1)
                sbK[:, qg * 512:(qg + 1) * 512] = kv[R_SBK:R_SBK + 256, lgs * 512:(lgs + 1) * 512]
                mlaK[:, qg * 512:(qg + 1) * 512] = kv[R_MLAK:R_MLAK + 384, lgs * 512:(lgs + 1) * 512]
                o = R_SBV * NTL + lgs * 512 * 256
                sbV[qg * 512:(qg + 1) * 512, :] = flat[o:o + 512 * 256].reshape(512, 256)
                o = R_MLAV * NTL + lgs * 512 * 260
                mlaV[qg * 512:(qg + 1) * 512, :] = flat[o:o + 512 * 260].reshape(512, 260)
        swK = np.zeros((128, NGL, 640), ml_dtypes.bfloat16)
        swV = np.zeros((NGL, 5, 128, 130), ml_dtypes.bfloat16)
        utail = np.zeros((256, NGL, 2), np.float32)
        own = resA[core]
        flat_own = own["kvb_loc"].reshape(-1)
        for lg in range(NGL):
            G = NR * lg + r
            swK[:, lg, 0:512] = own["kvb_loc"][R_SWAK:R_SWAK + 128, lg * 512:(lg + 1) * 512]
            o = R_SWAV * NTL + lg * 512 * 130
            swV[lg, 0:4] = flat_own[o:o + 512 * 130].reshape(4, 128, 130)
            if G > 0:
                src = resA[b * NR + ((G - 1) % NR)]
                lgs = (G - 1) // NR
                swK[:, lg, 512:640] = src["kvb_loc"][R_SWAK:R_SWAK + 128, lgs * 512:lgs * 512 + 128]
                o = R_SWAV * NTL + lgs * 512 * 130
                swV[lg, 4] = src["kvb_loc"].reshape(-1)[o:o + 128 * 130].reshape(128, 130)
                utail[:, lg, :] = src["kvf_loc"].reshape(256, NGL, 2)[:, lgs, :]
        outs.append(dict(sbK=sbK, sbV=sbV, mlaK=mlaK, mlaV=mlaV, swK=swK.reshape(128, NGL * 640),
                         swV=swV.reshape(NGL * 5 * 128, 130), utail=utail.reshape(256, NGL * 2)))
    return outs


def run_forward(inp, S):
    NG = S // 512
    NGL = NG // NR
    NTL = NGL * 512
    ncores = NB_BATCH * NR
    x = np.asarray(inp["x"], np.float32)
    pos = np.asarray(inp["positions"], np.int32)
    W = prep_weights(inp)
    cf, cb = make_consts()
    ford = f_order_rows(NGL)
    tok = []
    for core in range(ncores):
        b, r = divmod(core, NR)
        t = np.concatenate([np.arange((NR * lg + r) * 512, (NR * lg + r + 1) * 512) for lg in range(NGL)])
        tok.append((b, t))
    xs = [np.ascontiguousarray(x[b][t]) for (b, t) in tok]
    ps = [np.ascontiguousarray(pos[b][t][ford][None, :]) for (b, t) in tok]
    progA = get_prog("A", S)
    progB = get_prog("B", S)
    for l in range(DEPTH):
        wa = {k: W[k][l:l + 1] for k in A_WKEYS}
        wb = {k: W[k][l:l + 1] for k in B_WKEYS}
        inA = [dict(consts_f=cf, consts_b=cb, pos=ps[c], x=xs[c], **wa) for c in range(ncores)]
        resA = run_bass_kernel_spmd(progA, inA, core_ids=list(range(ncores))).results
        arr = arrange_B(resA, S)
        inB = []
        for c in range(ncores):
            d = dict(consts_f=cf, consts_b=cb, x=xs[c], **wb)
            for k in SC_KEYS:
                d[k] = resA[c][k]
            d.update(arr[c])
            inB.append(d)
        resB = run_bass_kernel_spmd(progB, inB, core_ids=list(range(ncores))).results
        xs = [np.asarray(resB[c]["out"], np.float32) for c in range(ncores)]
    out = np.zeros_like(x)
    for c, (b, t) in enumerate(tok):
        out[b][t] = xs[c]
    return out


def make_table(r, S):
    tab = np.zeros((1, 128), np.int32)
    Y0, Y1 = HR[0] * 512, HR[1] * 512
    for h in range(4):
        tab[0, 0 + h] = r * Y0 + (0 + h * 64) * 512
        tab[0, 4 + h] = r * Y0 + 256 * 512 + h * 128 * 256
        tab[0, 8 + h] = r * Y1 + (0 + h * 96) * 512
        tab[0, 12 + h] = r * Y1 + 384 * 512 + h * 128 * 260
    for kvh in range(2):
        tab[0, 16 + kvh] = (2 + r) * Y0 + (512 + kvh * 64) * 512
        tab[0, 19 + kvh] = (2 + r) * Y0 + 770 * 512 + kvh * 128 * 4
    tab[0, 18] = (2 + r) * Y0 + 640 * 512
    return tab


def run_fused(inp, S):
    NG = S // 512
    NGL = NG // NR
    ncores = NB_BATCH * NR
    x = np.asarray(inp["x"], np.float32)
    pos = np.asarray(inp["positions"], np.int32)
    W = prep_weights(inp)
    cf, cb = make_consts()
    ford = f_order_rows(NGL)
    kaug = np.zeros((64, 512), ml_dtypes.bfloat16)
    kaug[0, :] = 1.0
    kaug[32, :] = 1.0
    in_maps = []
    tok = []
    for core in range(ncores):
        b, r = divmod(core, NR)
        t = np.concatenate([np.arange((NR * lg + r) * 512, (NR * lg + r + 1) * 512) for lg in range(NGL)])
        tok.append((b, t))
        d = dict(consts_f=cf, consts_b=cb, kaug=kaug, x=np.ascontiguousarray(x[b][t]),
                 pos=np.ascontiguousarray(pos[b][t][ford][None, :]), tab=make_table(r, S))
        d.update(W)
        in_maps.append(d)
    res = run_bass_kernel_spmd(get_prog("F", S), in_maps, core_ids=list(range(ncores))).results
    out = np.zeros_like(x)
    for c, (b, t) in enumerate(tok):
        out[b][t] = np.asarray(res[c]["out"], np.float32)
    return out


def kernel(**inputs):
    inp = {k: np.asarray(v) for k, v in inputs.items()}
    S = inp["x"].shape[1]
    return run_fused(inp, S)
```

```python
import math
from contextlib import ExitStack

import numpy as np
import ml_dtypes

import concourse.bass as bass
import concourse.mybir as mybir
from concourse.bass_utils import run_bass_kernel_spmd

F32 = mybir.dt.float32
BF16 = mybir.dt.bfloat16
I32 = mybir.dt.int32
AF = mybir.ActivationFunctionType
ALU = mybir.AluOpType

D = 1024
DIN = 3488
DINX = 3520
DEPTH = 2
NB_BATCH = 2
NR = 4
EPS = 1e-6
KVROWS = 1416
C_AQ, C_AK, C_AV = 0, 256, 384
C_BB, C_BC, C_BX = 512, 768, 1024
C_CQ, C_CKV, C_CKR = 1280, 1536, 1664
C_DQ, C_DK, C_DV = 1696, 1952, 2208
C_GATE = 2464
C_KRSW = 3488
R_SBK, R_MLAK, R_SWAK, R_SBV, R_MLAV, R_SWAV, R_UT = 0, 256, 640, 768, 1024, 1284, 1414


HR = (772, 644)


def kvmap(oldrow):
    if oldrow < 256:
        return 0, oldrow
    if oldrow < 640:
        return 1, oldrow - 256
    if oldrow < 768:
        return 0, 512 + oldrow - 640
    if oldrow < 1024:
        return 0, 256 + oldrow - 768
    if oldrow < 1284:
        return 1, 384 + oldrow - 1024
    if oldrow < 1414:
        return 0, 640 + oldrow - 1284
    return 0, 770 + oldrow - 1414


def own_groups(r, ngl):
    return [r + NR * i for i in range(ngl)]


class Tk:
    __slots__ = ("w", "r", "dsem", "dcnt", "name", "excl")

    def __init__(self, name="", excl=False):
        self.excl = excl
        self.w = None
        self.r = {}
        self.dsem = None
        self.dcnt = 0
        self.name = name


class _Rec:
    def __init__(self):
        self.call = None

    def __getattr__(self, name):
        def f(*a, **kw):
            self.call = (name, a, kw)
            return self
        return f


class Prog:
    ENG = ("pe", "act", "dve", "pool", "sp")

    def __init__(self, nc, es):
        self.nc = nc
        self.es = es
        self.streams = {e: [] for e in self.ENG}
        self.seen = {e: {} for e in self.ENG}
        self.sems = {}
        self.cnt = {}
        self.nsem = 0
        self.all_tickets = {}
        for e in ("pe", "act", "dve", "pool"):
            self._newsem(e)

    def _newsem(self, key):
        s = self.es.enter_context(self.nc.semaphore("s%d" % self.nsem))
        self.nsem += 1
        self.sems[key] = s
        self.cnt[key] = 0
        return s

    def _dsem(self, st, q):
        cls = "sw" if q == "pool" else "hw"
        if st.dsem is not None:
            assert st.dsem[0] == cls, "tile %s used by both DMA classes" % st.name
            return
        free = getattr(self, "free_dsems", {}).get(cls)
        if free:
            st.dsem = free.pop()
        else:
            st.dsem = (cls, self.nsem)
            self._newsem(st.dsem)
        if getattr(self, "phase_dsems", None) is not None:
            self.phase_dsems.append(st.dsem)

    def phase_begin(self):
        self.phase_dsems = []

    def phase_end(self):
        if not hasattr(self, "free_dsems"):
            self.free_dsems = {"sw": [], "hw": []}
        for k in self.phase_dsems:
            self.free_dsems[k[0]].append(k)
        self.phase_dsems = None

    def tile(self, name, shape, dt):
        return self.es.enter_context(self.nc.sbuf_tensor(name, list(shape), dt))

    def psum(self, name):
        return self.es.enter_context(self.nc.psum_tensor(name, [128, 512], F32))

    def _collect(self, eng, reads, writes):
        deps = {}

        def add(tk):
            if tk is None:
                return
            k, v = tk
            if deps.get(k, 0) < v:
                deps[k] = v

        for t in reads:
            add(t.w)
            if t.excl:
                for k, v in t.r.items():
                    if k != eng:
                        add((k, v))
        for t in writes:
            add(t.w)
            for k, v in t.r.items():
                add((k, v))
        waits = []
        seen = self.seen[eng]
        for k, v in deps.items():
            if k == eng and eng == "pe":
                continue
            if seen.get(k, 0) >= v:
                continue
            seen[k] = v
            waits.append((self.sems[k], v))
        return waits

    def _record(self, tk, reads, writes):
        k, v = tk
        for t in reads:
            if t.r.get(k, 0) < v:
                t.r[k] = v
        for t in writes:
            t.w = tk
            t.r = {}
        self.all_tickets[k] = v

    def op(self, eng, fn, reads=(), writes=()):
        waits = self._collect(eng, reads, writes)
        self.cnt[eng] += 1
        tk = (eng, self.cnt[eng])
        rec = _Rec()
        fn(rec)
        name, a, kw = rec.call
        self.streams[eng].append((waits, (lambda e, name=name, a=a, kw=kw: getattr(e, name)(*a, **kw)), (self.sems[eng], 1)))
        self._record(tk, reads, writes)
        return tk

    def dma(self, q, out, in_, st, reads=(), writes=()):
        self._dsem(st, q)
        waits = self._collect(q, reads, writes)
        self.cnt[st.dsem] += 16
        tk = (st.dsem, self.cnt[st.dsem])
        self.streams[q].append((waits, lambda e, o=out, i=in_: e.dma_start(out=o, in_=i),
                                (self.sems[st.dsem], 16)))
        self._record(tk, reads, writes)
        return tk

    def dyn_dma(self, q, out, handle, tab_ap, dims, st, reads=(), writes=()):
        self._dsem(st, q)
        if not hasattr(self, "regs"):
            self.regs = {}
            self.regi = {}
        if q not in self.regs:
            eng = {"sp": self.nc.sync, "act": self.nc.scalar, "pool": self.nc.gpsimd}[q]
            self.regs[q] = [self.es.enter_context(eng.register("dr%s%d" % (q, i))) for i in range(2)]
            self.regi[q] = 0
        reg = self.regs[q][self.regi[q] % 2]
        self.regi[q] += 1
        waits = self._collect(q, reads, writes)
        self.streams[q].append((waits, lambda e, reg=reg, t=tab_ap: e.reg_load(reg, t), None))
        self.cnt[st.dsem] += 16
        tk = (st.dsem, self.cnt[st.dsem])
        src = bass.AP(handle, reg, [list(d) for d in dims])
        self.streams[q].append(([], lambda e, o=out, i=src: e.dma_start(out=o, in_=i), (self.sems[st.dsem], 16)))
        self._record(tk, reads, writes)
        return tk

    def raw(self, eng, fn, key, reads=(), writes=()):
        if key not in self.sems:
            self._newsem(key)
        waits = self._collect(eng, reads, writes)
        self.cnt[key] += 1
        tk = (key, self.cnt[key])
        self.streams[eng].append((waits, fn, (self.sems[key], 1)))
        self._record(tk, reads, writes)
        return tk

    def barrier(self):
        for e in self.ENG:
            waits = []
            for k, v in self.all_tickets.items():
                if self.seen[e].get(k, 0) < v:
                    self.seen[e][k] = v
                    waits.append((self.sems[k], v))
            if waits:
                self.streams[e].append((waits, None, None))

    def emit(self):
        nc = self.nc

        def run(e, lst):
            for waits, fn, inc in lst:
                for s, v in waits:
                    e.wait_ge(s, v)
                if fn is not None:
                    ins = fn(e)
                    if inc is not None:
                        ins.then_inc(inc[0], inc[1])

        with nc.Block() as block:
            @block.tensor
            def _(e):
                run(e, self.streams["pe"])

            @block.scalar
            def _(e):
                run(e, self.streams["act"])

            @block.vector
            def _(e):
                run(e, self.streams["dve"])

            @block.gpsimd
            def _(e):
                run(e, self.streams["pool"])

            @block.sync
            def _(e):
                run(e, self.streams["sp"])


def dap(h, off, dims):
    return bass.AP(h, off, [list(d) for d in dims])


class Builder:
    def __init__(self, mode, S):
        self.mode = mode
        self.S = S
        self.NG = S // 512
        self.NGL = self.NG // NR
        self.NTL = self.NGL * 512
        self.NBL = self.NGL * 4
        self.nc = bass.Bass("TRN2", target_bir_lowering=False)
        self.es = ExitStack()
        self.p = Prog(self.nc, self.es)
        self.dram = {}
        self.dh = {}
        self.rr = {}

    def din(self, name, shape, dt):
        self.dh[name] = self.nc.dram_tensor(name, list(shape), dt, kind="ExternalInput")
        self.dram[name] = self.dh[name].ap()

    def dout(self, name, shape, dt):
        self.dh[name] = self.nc.dram_tensor(name, list(shape), dt, kind="ExternalOutput")
        self.dram[name] = self.dh[name].ap()

    def dint(self, name, shape, dt):
        self.dh[name] = self.nc.dram_tensor(name, list(shape), dt)
        self.dram[name] = self.dh[name].ap()

    def declare(self):
        mode, NTL, NBL, NGL = self.mode, self.NTL, self.NBL, self.NGL
        L = DEPTH if mode == 'F' else 1
        self.din("consts_f", [128, 640], F32)
        self.din("consts_b", [128, 896], BF16)
        self.din("kaug", [64, 512], BF16)
        if mode in ("A", "F"):
            self.din("pos", [1, NTL], I32)
            self.din("gpre_t", [L, 128, 8], F32)
            self.din("w_in_x", [L, D, DINX], F32)
            self.din("gcq_t", [L, 128, 2], F32)
            self.din("w_uq_x", [L, 256, 512], F32)
            self.din("gkv_t", [L, 128, 1], F32)
            self.din("w_ukv_x", [L, 128, 512], F32)
        if mode in ("B", "F"):
            self.din("sinks", [L, 4], F32)
            self.din("convw_t", [L, 128, 6], F32)
            self.din("convb_t", [L, 128, 2], F32)
            self.din("ggrp", [L, D], F32)
            self.din("w_out", [L, D, D], F32)
            self.din("gpost", [L, D], F32)
        self.din("x", [NTL, D], F32)
        scr = [("sc_sbq", [256, NTL], BF16), ("sc_mlaq", [384, NTL], BF16), ("sc_swaq", [256, NTL], BF16),
               ("sc_u", [256, NTL], F32), ("sc_bb", [256, NTL], F32), ("sc_gate", [NBL * 128, D], BF16)]
        if mode == "A":
            for n, s, d in scr:
                self.dout(n, s, d)
            self.dout("kvb_loc", [KVROWS, NTL], BF16)
            self.dout("kvf_loc", [256, NGL * 2], F32)
        elif mode == "B":
            for n, s, d in scr:
                self.din(n, s, d)
            NPOS = self.S // 128 + 12
            self.din("sbK", [256, NPOS * 128], BF16)
            self.din("sbV", [NPOS * 128, 256], BF16)
            self.din("mlaK", [384, NPOS * 128], BF16)
            self.din("mlaV", [NPOS * 128, 260], BF16)
            self.din("swK", [128, NGL * 640], BF16)
            self.din("swV", [NGL * 5 * 128, 130], BF16)
            self.din("utail", [256, NGL * 2], F32)
            self.dint("sc_y", [NBL * 128, D], F32)
            self.dout("out", [NTL, D], F32)
        else:
            for n, s, d in scr:
                self.dint(n, s, d)
            for f in range(2):
                self.dint("kvb_loc%d" % f, [NGL * HR[f], 512], BF16)
                self.dint("kvb_all%d" % f, [(self.NG + 6) * HR[f], 512], BF16)
            self.dh["kvb_loc"] = None
            self.din("tab", [1, 128], I32)
            self.dint("sc_y", [NBL * 128, D], F32)
            self.dint("sc_x1", [NTL, D], F32)
            self.dint("sc_cos", [128, NTL], F32)
            self.dint("sc_sin", [128, NTL], F32)
            self.dout("out", [NTL, D], F32)

    def common(self):
        p = self.p
        self.cf = p.tile("cf", [128, 640], F32)
        self.cb = p.tile("cb", [128, 896], BF16)
        self.t_cf = Tk("cf")
        self.t_cb = Tk("cb")
        p.dma("sp", self.cf[:], self.dram["consts_f"], self.t_cf, writes=[self.t_cf])
        p.dma("sp", self.cb[:], self.dram["consts_b"], self.t_cb, writes=[self.t_cb])
        self.ident_f = self.cf[:, 0:128]
        self.ones_f = self.cf[:, 128:256]
        self.eps_c = self.cf[:, 256:257]
        self.one_c = self.cf[:, 257:258]
        self.freq_c = self.cf[:, 258:259]
        self.sgn_c = self.cf[:, 259:260]
        self.negtri = self.cb[:, 0:128]
        self.m_lt = self.cb[:, 128:256]
        self.m_le = self.cb[:, 256:384]
        self.m_gt = self.cb[:, 384:512]
        self.ones98 = self.cb[:, 512:610]
        self.zero_b = self.cb[:, 640:768]
        self.ones_b = self.cb[:, 768:896]
        self.pp = [self.es.enter_context(self.nc.psum_tensor("pp%d" % i, [128, 1024], F32)) for i in range(4)]
        self.ps = [self.pp[i // 2][:, (i % 2) * 512:(i % 2 + 1) * 512] for i in range(8)]
        self.t_ps = [Tk("ps%d" % i, excl=True) for i in range(8)]

    def rot(self, name, n):
        i = self.rr.get(name, 0)
        self.rr[name] = i + 1
        return i % n

    def phaseA(self, L, x_src):
        p, nc = self.p, self.nc
        NTL, NGL, NBL = self.NTL, self.NGL, self.NBL
        dr = self.dram
        cf_, cb_ = self.t_cf, self.t_cb
        with ExitStack() as es:
            def tile(name, shape, dt):
                return es.enter_context(nc.sbuf_tensor("A%d_%s" % (L, name), list(shape), dt))

            Wp = tile("Wp", [128, 8, DINX], BF16); t_Wp = Tk(); t_Wp2 = Tk()
            wuq = tile("wuq", [128, 2, 512], BF16); t_wuq = Tk()
            wukv = tile("wukv", [128, 512], BF16); t_wukv = Tk()
            gcols = tile("gcols", [128, 16], F32); t_gc = Tk()
            wst = [tile("wst%d" % i, [128, 1760], F32) for i in range(2)]; t_wst = [Tk() for _ in range(2)]
            cosT = tile("cosT", [128, NTL], F32); t_cos = Tk()
            sinS = tile("sinS", [128, NTL], F32); t_sin = Tk()
            posi = tile("posi", [128, NTL], I32); t_posi = Tk()
            ang = tile("ang", [128, NTL], F32); t_ang = Tk()
            tr1 = tile("tr1", [128, NTL], F32); t_tr1 = Tk()
            tr2 = tile("tr2", [128, NTL], F32); t_tr2 = Tk()
            tri = tile("tri", [128, NTL], I32); t_tri = Tk()
            xs = [tile("xs%d" % i, [128, D], F32) for i in range(2)]; t_xs = [Tk(), Tk()]
            xn = [tile("xn%d" % i, [128, D], F32) for i in range(2)]; t_xn = [Tk(), Tk()]
            junk = tile("junk", [128, D], F32); t_junk = Tk()
            sst = [tile("ss%d" % i, [128, 4], F32) for i in range(2)]; t_ss = [Tk(), Tk()]
            xnT = tile("xnT", [128, 8, 512], BF16); t_xnT = Tk()
            cqT = tile("cqT", [128, 2, 512], BF16); t_cqT = Tk()
            sq = [tile("sq%d" % i, [128, 512], BF16) for i in range(2)]; t_sq = [Tk(), Tk()]
            ckvT = tile("ckvT", [128, 512], BF16); t_ckvT = Tk()
            rq_bc = tile("rq_bc", [128, 512], F32); t_rq = Tk()
            rkv_bc = tile("rkv_bc", [128, 512], F32); t_rkv = Tk()
            rkv_t = tile("rkv_t", [128, 8], F32); t_rkvt = Tk()
            tmpf = [tile("tmpf%d" % i, [128, 512], F32) for i in range(3)]; t_tmpf = [Tk(), Tk(), Tk()]
            NSTB, NSTF = 10, 5
            stb = [tile("stb%d" % i, [128, 1024], BF16) for i in range(NSTB)]; t_stb = [Tk() for _ in range(NSTB)]
            stf = [tile("stf%d" % i, [128, 512], F32) for i in range(NSTF)]; t_stf = [Tk() for _ in range(NSTF)]

            p.dma("sp", gcols[:, 0:8], dr["gpre_t"][L], t_gc, writes=[t_gc])
            p.dma("sp", gcols[:, 8:10], dr["gcq_t"][L], t_gc, writes=[t_gc])
            p.dma("sp", gcols[:, 10:11], dr["gkv_t"][L], t_gc, writes=[t_gc])
            k = 0
            for kc in range(8):
                for hf in range(2):
                    s = k % 2; k += 1
                    p.dma("sp", wst[s][:], dr["w_in_x"][L, kc * 128:(kc + 1) * 128, hf * 1760:(hf + 1) * 1760],
                          t_wst[s], writes=[t_wst[s]])
                    if s % 2 == 0:
                        p.op("dve", lambda e, o=Wp[:, kc, hf * 1760:(hf + 1) * 1760], i=wst[s][:], sc=gcols[:, kc:kc + 1]:
                             e.tensor_scalar(out=o, in0=i, scalar1=sc, scalar2=None, op0=ALU.mult),
                             reads=[t_wst[s], t_gc], writes=[t_Wp])
                    else:
                        p.op("act", lambda e, o=Wp[:, kc, hf * 1760:(hf + 1) * 1760], i=wst[s][:], sc=gcols[:, kc:kc + 1]:
                             e.activation(out=o, in_=i, func=AF.Copy, scale=sc),
                             reads=[t_wst[s], t_gc], writes=[t_Wp2])
            for kc in range(2):
                s = k % 2; k += 1
                p.dma("sp", wst[s][:, 0:512], dr["w_uq_x"][L, kc * 128:(kc + 1) * 128, :], t_wst[s], writes=[t_wst[s]])
                p.op("dve", lambda e, o=wuq[:, kc, :], i=wst[s][:, 0:512], sc=gcols[:, 8 + kc:9 + kc]:
                     e.tensor_scalar(out=o, in0=i, scalar1=sc, scalar2=float(96 ** -0.5), op0=ALU.mult, op1=ALU.mult),
                     reads=[t_wst[s], t_gc], writes=[t_wuq])
            s = k % 2; k += 1
            p.dma("sp", wst[s][:, 0:512], dr["w_ukv_x"][L], t_wst[s], writes=[t_wst[s]])
            p.op("dve", lambda e, o=wukv[:], i=wst[s][:, 0:512], sc=gcols[:, 10:11]:
                 e.tensor_scalar(out=o, in0=i, scalar1=sc, scalar2=None, op0=ALU.mult),
                 reads=[t_wst[s], t_gc], writes=[t_wukv])

            import os
            KSTOP = float(os.environ.get("KSTOP", "99"))
            if KSTOP <= 1:
                p.barrier(); return
            REUSE_TABLES = (self.mode == "F" and L > 0)
            if REUSE_TABLES:
                p.dma("sp", cosT[:], dr["sc_cos"], t_cos, writes=[t_cos])
                p.dma("sp", sinS[:], dr["sc_sin"], t_sin, writes=[t_sin])
            else:
                p.dma("sp", posi[:], dap(self.dh["pos"], 0, [[0, 128], [1, NTL]]), t_posi, writes=[t_posi])
                p.op("dve", lambda e: e.tensor_copy(out=ang[:], in_=posi[:]), reads=[t_posi], writes=[t_ang])
                p.op("dve", lambda e: e.tensor_scalar(out=ang[:], in0=ang[:], scalar1=self.freq_c, scalar2=None, op0=ALU.mult),
                     reads=[t_ang, cf_], writes=[t_ang])
                TWO_PI = 2.0 * math.pi
                C_HI = 6.28125
                C_LO = TWO_PI - C_HI

                def sin_of(dst, t_dst, shift):
                    p.op("dve", lambda e: e.tensor_scalar(out=tr1[:], in0=ang[:], scalar1=float(shift), scalar2=float(1.0 / TWO_PI),
                                                          op0=ALU.add, op1=ALU.mult), reads=[t_ang], writes=[t_tr1])
                    p.op("dve", lambda e: e.tensor_copy(out=tri[:], in_=tr1[:]), reads=[t_tr1], writes=[t_tri])
                    p.op("dve", lambda e: e.tensor_copy(out=tr1[:], in_=tri[:]), reads=[t_tri], writes=[t_tr1])
                    p.op("dve", lambda e: e.tensor_scalar(out=tr2[:], in0=ang[:], scalar1=float(shift), scalar2=None, op0=ALU.add),
                         reads=[t_ang], writes=[t_tr2])
                    p.op("dve", lambda e: e.scalar_tensor_tensor(out=tr2[:], in0=tr1[:], scalar=float(-C_HI), in1=tr2[:],
                                                                 op0=ALU.mult, op1=ALU.add), reads=[t_tr1, t_tr2], writes=[t_tr2])
                    p.op("dve", lambda e: e.scalar_tensor_tensor(out=tr2[:], in0=tr1[:], scalar=float(-C_LO), in1=tr2[:],
                                                                 op0=ALU.mult, op1=ALU.add), reads=[t_tr1, t_tr2], writes=[t_tr2])
                    p.op("dve", lambda e: e.tensor_scalar(out=tr1[:], in0=tr2[:], scalar1=float(math.pi), scalar2=float(-TWO_PI),
                                                          op0=ALU.is_gt, op1=ALU.mult), reads=[t_tr2], writes=[t_tr1])
                    p.op("dve", lambda e: e.tensor_tensor(out=tr2[:], in0=tr2[:], in1=tr1[:], op=ALU.add),
                         reads=[t_tr1, t_tr2], writes=[t_tr2])
                    p.op("dve", lambda e: e.tensor_scalar(out=tr1[:], in0=tr2[:], scalar1=float(-math.pi), scalar2=float(TWO_PI),
                                                          op0=ALU.is_lt, op1=ALU.mult), reads=[t_tr2], writes=[t_tr1])
                    p.op("dve", lambda e: e.tensor_tensor(out=tr2[:], in0=tr2[:], in1=tr1[:], op=ALU.add),
                         reads=[t_tr1, t_tr2], writes=[t_tr2])
                    p.op("dve", lambda e: e.tensor_scalar(out=tr2[:], in0=tr2[:], scalar1=float(-3.14159), scalar2=float(3.14159),
                                                          op0=ALU.max, op1=ALU.min), reads=[t_tr2], writes=[t_tr2])
                    p.op("act", lambda e: e.activation(out=dst[:], in_=tr2[:], func=AF.Sin), reads=[t_tr2], writes=[t_dst])

                sin_of(cosT, t_cos, math.pi / 2.0)
                sin_of(sinS, t_sin, 0.0)
                p.op("dve", lambda e: e.tensor_scalar(out=sinS[:], in0=sinS[:], scalar1=self.sgn_c, scalar2=None, op0=ALU.mult),
                     reads=[t_sin, cf_], writes=[t_sin])
                if self.mode == "F":
                    p.dma("pool", dr["sc_cos"], cosT[:], t_cos, reads=[t_cos])
                    p.dma("pool", dr["sc_sin"], sinS[:], t_sin, reads=[t_sin])

            if KSTOP <= 2:
                p.barrier(); return
            if self.mode == "F" and L == 0:
                for f in range(2):
                    for m in (0, 1, 2, self.NG + 3, self.NG + 4, self.NG + 5):
                        for r0 in range(0, HR[f], 128):
                            rows = min(128, HR[f] - r0)
                            p.dma("pool", dap(self.dh["kvb_all%d" % f], (m * HR[f] + r0) * 512, [[512, rows], [1, 512]]),
                                  self.zt[0:rows, :], self.t_zt, reads=[self.t_zt])
            PSA = [2, 3, 4, 5, 6, 7]

            def acc_bank():
                return PSA[self.rot("A_acc", len(PSA))]

            def stage_b():
                i = self.rot("A_stb", NSTB)
                return stb[i], t_stb[i]

            def stage_f():
                i = self.rot("A_stf", NSTF)
                return stf[i], t_stf[i]

            def evac(eng_hint, out, in_, reads, writes, scale=None):
                e = eng_hint if eng_hint else ("act" if self.rot("A_ev", 2) == 0 else "dve")
                if e == "act":
                    if scale is None:
                        p.op("act", lambda en: en.copy(out=out, in_=in_), reads=reads, writes=writes)
                    else:
                        p.op("act", lambda en: en.mul(out=out, in_=in_, mul=float(scale)), reads=reads, writes=writes)
                else:
                    if scale is None:
                        p.op("dve", lambda en: en.tensor_copy(out=out, in_=in_), reads=reads, writes=writes)
                    else:
                        p.op("dve", lambda en: en.tensor_scalar(out=out, in0=in_, scalar1=float(scale), scalar2=None,
                                                                op0=ALU.mult), reads=reads, writes=writes)

            def fmm(col0, M, bank=None):
                b = acc_bank() if bank is None else bank
                for kc in range(8):
                    p.op("pe", lambda e, b=b, kc=kc: e.matmul(self.ps[b][0:M, :], lhsT=Wp[:, kc, col0:col0 + M],
                                                             rhs=xnT[:, kc, :], start=(kc == 0), stop=(kc == 7)),
                         reads=[t_Wp, t_Wp2, t_xnT], writes=[self.t_ps[b]])
                return b

            FUSED = (self.mode == "F")

            def store2d(dst_h, row0, M, lg, src, t_src, prow0=0, q="pool"):
                if FUSED and dst_h is kvb:
                    f, rr = kvmap(row0)
                    d_ = dr["kvb_loc%d" % f]
                    p.dma(q, d_[lg * HR[f] + rr:lg * HR[f] + rr + M, 0:512], src[prow0:prow0 + M, 0:512], t_src, reads=[t_src])
                else:
                    p.dma(q, dst_h[row0:row0 + M, lg * 512:(lg + 1) * 512], src[prow0:prow0 + M, 0:512], t_src, reads=[t_src])

            def vdst(R, lg, blk, W):
                if FUSED:
                    f, rr = kvmap(R)
                    return dap(self.dh["kvb_loc%d" % f], (lg * HR[f] + rr) * 512 + blk * 128 * W, [[W, 128], [1, W]])
                return dap(self.dh["kvb_loc"], R * NTL + (lg * 4 + blk) * 128 * W, [[W, 128], [1, W]])

            def store_nat(dst_h, row0, lg, src, t_src):
                for c in range(4):
                    p.dma("pool", dst_h[row0:row0 + 128, lg * 512 + (3 - c) * 128:lg * 512 + (4 - c) * 128], src[:, c * 128:(c + 1) * 128],
                          t_src, reads=[t_src])

            def ones_cols(st, t_st, nh):
                p.op("pool", lambda e: e.memset(st[:, 0:nh * 65].rearrange("p (h c) -> p h c", h=nh)[:, :, 64:65], 1.0), writes=[t_st])

            kvb = dr.get("kvb_loc", "KVB")
            for lg in range(NGL):
                for blk in range(4):
                    lb = lg * 4 + blk
                    s = self.rot("A_x", 2)
                    p.dma("sp", xs[s][:], x_src[lb * 128:(lb + 1) * 128, :], t_xs[s], writes=[t_xs[s]])
                    p.op("act", lambda e, s=s: e.activation(out=junk[:], in_=xs[s][:], func=AF.Square, accum_out=sst[s][:, 0:1]),
                         reads=[t_xs[s]], writes=[t_junk, t_ss[s]])
                    p.op("act", lambda e, s=s: e.activation(out=sst[s][:, 1:2], in_=sst[s][:, 0:1], func=AF.Sqrt,
                                                           bias=self.eps_c, scale=float(1.0 / D)),
                         reads=[t_ss[s], cf_], writes=[t_ss[s]])
                    p.op("dve", lambda e, s=s: e.reciprocal(out=sst[s][:, 2:3], in_=sst[s][:, 1:2]), reads=[t_ss[s]], writes=[t_ss[s]])
                    p.op("dve", lambda e, s=s: e.tensor_scalar(out=xn[s][:], in0=xs[s][:], scalar1=sst[s][:, 2:3], scalar2=None,
                                                               op0=ALU.mult), reads=[t_xs[s], t_ss[s]], writes=[t_xn[s]])
                    for half in range(2):
                        for i in range(4):
                            kc = half * 4 + i
                            p.op("pe", lambda e, s=s, kc=kc, half=half, i=i: e.transpose(
                                out=self.ps[half][:, i * 128:(i + 1) * 128], in_=xn[s][:, kc * 128:(kc + 1) * 128],
                                identity=self.ident_f), reads=[t_xn[s], cf_], writes=[self.t_ps[half]])
                        evac(None, xnT[:, half * 4:half * 4 + 4, (3 - blk) * 128:(4 - blk) * 128],
                             self.ps[half][:].rearrange("p (a b) -> p a b", a=4), [self.t_ps[half]], [t_xnT])

                if KSTOP <= 3:
                    continue
                for (c0, scale, dst, r0) in ((C_AQ, 0.125, dr["sc_swaq"], 0), (C_AQ + 128, 0.125, dr["sc_swaq"], 128),
                                             (C_AK, None, kvb, R_SWAK),
                                             (C_DQ, 0.125, dr["sc_sbq"], 0), (C_DQ + 128, 0.125, dr["sc_sbq"], 128),
                                             (C_DK, None, kvb, R_SBK), (C_DK + 128, None, kvb, R_SBK + 128)):
                    b = fmm(c0, 128)
                    st, t_st = stage_b()
                    evac(None, st[:, 0:512], self.ps[b][:], [self.t_ps[b]], [t_st], scale)
                    store2d(dst, r0, 128, lg, st, t_st)
                if KSTOP <= 4:
                    continue
                for ct in range(2):
                    b = fmm(C_BB + ct * 128, 128)
                    st, t_st = stage_f()
                    evac(None, st[:], self.ps[b][:], [self.t_ps[b]], [t_st])
                    store_nat(dr["sc_bb"], ct * 128, lg, st, t_st)
                    b1 = fmm(C_BC + ct * 128, 128)
                    b2 = fmm(C_BX + ct * 128, 128)
                    i = self.rot("A_tmpf", 3)
                    evac("act", tmpf[i][:], self.ps[b1][:], [self.t_ps[b1]], [t_tmpf[i]])
                    st, t_st = stage_f()
                    p.op("dve", lambda e, st=st, i=i, b2=b2: e.tensor_tensor(out=st[:], in0=tmpf[i][:], in1=self.ps[b2][:], op=ALU.mult),
                         reads=[t_tmpf[i], self.t_ps[b2]], writes=[t_st])
                    store_nat(dr["sc_u"], ct * 128, lg, st, t_st)
                    if FUSED:
                        tl, t_tl = stage_b()
                        p.op("dve", lambda e: e.tensor_copy(out=tl[:, 0:2], in_=st[:, 126:128]), reads=[t_st], writes=[t_tl])
                        p.op("dve", lambda e: e.tensor_tensor(out=tl[:, 2:4], in0=st[:, 126:128], in1=tl[:, 0:2], op=ALU.subtract),
                             reads=[t_st, t_tl], writes=[t_tl])
                        p.dma("pool", dap(self.dh["kvb_loc0"], (lg * HR[0] + 770) * 512 + ct * 128 * 4, [[4, 128], [1, 4]]),
                              tl[:, 0:4], t_tl, reads=[t_tl])
                    else:
                        p.dma("pool", dr["kvf_loc"][ct * 128:(ct + 1) * 128, lg * 2:lg * 2 + 2], st[:, 126:128], t_st, reads=[t_st])
                if KSTOP <= 5:
                    continue
                for kc in range(2):
                    b = fmm(C_CQ + kc * 128, 128)
                    evac("dve", cqT[:, kc, :], self.ps[b][:], [self.t_ps[b]], [t_cqT])
                    p.op("act", lambda e, b=b, kc=kc: e.activation(out=sq[kc][:], in_=self.ps[b][:], func=AF.Square),
                         reads=[self.t_ps[b]], writes=[t_sq[kc]])
                b = acc_bank()
                for kc in range(2):
                    p.op("pe", lambda e, b=b, kc=kc: e.matmul(self.ps[b][:], lhsT=self.ones_b, rhs=sq[kc][:], start=(kc == 0), stop=(kc == 1)),
                         reads=[t_sq[kc], cb_], writes=[self.t_ps[b]])
                p.op("act", lambda e, b=b: e.activation(out=rq_bc[:], in_=self.ps[b][:], func=AF.Sqrt, bias=self.eps_c, scale=float(1.0 / 256)),
                     reads=[self.t_ps[b], cf_], writes=[t_rq])
                p.op("dve", lambda e: e.reciprocal(out=rq_bc[:], in_=rq_bc[:]), reads=[t_rq], writes=[t_rq])
                if KSTOP <= 5.1:
                    continue
                b = fmm(C_CKV, 128)
                evac("dve", ckvT[:], self.ps[b][:], [self.t_ps[b]], [t_ckvT])
                p.op("act", lambda e, b=b: e.activation(out=sq[0][:], in_=self.ps[b][:], func=AF.Square),
                     reads=[self.t_ps[b]], writes=[t_sq[0]])
                b = acc_bank()
                p.op("pe", lambda e, b=b: e.matmul(self.ps[b][:], lhsT=self.ones_b, rhs=sq[0][:], start=True, stop=True),
                     reads=[t_sq[0], cb_], writes=[self.t_ps[b]])
                p.op("act", lambda e, b=b: e.activation(out=rkv_bc[:], in_=self.ps[b][:], func=AF.Sqrt, bias=self.eps_c, scale=float(1.0 / 128)),
                     reads=[self.t_ps[b], cf_], writes=[t_rkv])
                p.op("dve", lambda e: e.reciprocal(out=rkv_bc[:], in_=rkv_bc[:]), reads=[t_rkv], writes=[t_rkv])
                if KSTOP <= 5.2:
                    continue
                b = acc_bank()
                for blk in range(4):
                    p.op("pe", lambda e, b=b, blk=blk: e.matmul(self.ps[b][:, 2 * blk:2 * blk + 2], lhsT=sq[0][:, blk * 128:(blk + 1) * 128],
                                                               rhs=self.ones_b[:, 0:2], start=True, stop=True),
                         reads=[t_sq[0], cb_], writes=[self.t_ps[b]])
                p.op("act", lambda e, b=b: e.activation(out=rkv_t[:], in_=self.ps[b][:, 0:8], func=AF.Sqrt, bias=self.eps_c, scale=float(1.0 / 128)),
                     reads=[self.t_ps[b], cf_], writes=[t_rkvt])
                p.op("dve", lambda e: e.reciprocal(out=rkv_t[:], in_=rkv_t[:]), reads=[t_rkvt], writes=[t_rkvt])
                if KSTOP <= 6:
                    continue
                b1 = fmm(C_CKR, 32)
                b2 = fmm(C_KRSW, 32)
                i1 = self.rot("A_tmpf", 3)
                p.op("dve", lambda e, i1=i1, b1=b1: e.tensor_tensor(out=tmpf[i1][0:32, :], in0=self.ps[b1][0:32, :],
                                                                    in1=cosT[0:32, lg * 512:(lg + 1) * 512], op=ALU.mult),
                     reads=[self.t_ps[b1], t_cos], writes=[t_tmpf[i1]])
                i2 = self.rot("A_tmpf", 3)
                p.op("dve", lambda e, i2=i2, b2=b2: e.tensor_tensor(out=tmpf[i2][0:32, :], in0=self.ps[b2][0:32, :],
                                                                    in1=sinS[0:32, lg * 512:(lg + 1) * 512], op=ALU.mult),
                     reads=[self.t_ps[b2], t_sin], writes=[t_tmpf[i2]])
                st, t_st = stage_b()
                p.op("dve", lambda e, st=st, i1=i1, i2=i2: e.tensor_tensor(out=st[0:32, 0:512], in0=tmpf[i1][0:32, :], in1=tmpf[i2][0:32, :], op=ALU.add),
                     reads=[t_tmpf[i1], t_tmpf[i2]], writes=[t_st])
                for h in range(4):
                    store2d(kvb, R_MLAK + h * 96 + 64, 32, lg, st, t_st)
                if KSTOP <= 7:
                    continue
                for pair in range(2):
                    b = acc_bank()
                    for kc in range(2):
                        p.op("pe", lambda e, b=b, kc=kc, pair=pair: e.matmul(self.ps[b][:], lhsT=wuq[:, kc, pair * 128:(pair + 1) * 128],
                                                                            rhs=cqT[:, kc, :], start=(kc == 0), stop=(kc == 1)),
                             reads=[t_wuq, t_cqT], writes=[self.t_ps[b]])
                    st, t_st = stage_b()
                    p.op("dve", lambda e, st=st, b=b: e.tensor_tensor(out=st[:, 0:512], in0=self.ps[b][:], in1=rq_bc[:], op=ALU.mult),
                         reads=[self.t_ps[b], t_rq], writes=[t_st])
                    for i in range(2):
                        store2d(dr["sc_mlaq"], (2 * pair + i) * 96, 64, lg, st, t_st, prow0=i * 64)
                    b = acc_bank()
                    p.op("pe", lambda e, b=b, pair=pair: e.matmul(self.ps[b][:], lhsT=wukv[:, pair * 128:(pair + 1) * 128], rhs=ckvT[:],
                                                                 start=True, stop=True), reads=[t_wukv, t_ckvT], writes=[self.t_ps[b]])
                    st, t_st = stage_b()
                    p.op("dve", lambda e, st=st, b=b: e.tensor_tensor(out=st[:, 0:512], in0=self.ps[b][:], in1=rkv_bc[:], op=ALU.mult),
                         reads=[self.t_ps[b], t_rkv], writes=[t_st])
                    for i in range(2):
                        store2d(kvb, R_MLAK + (2 * pair + i) * 96, 64, lg, st, t_st, prow0=i * 64)
                b1 = acc_bank()
                b2 = acc_bank()
                for (b, c0) in ((b1, 256), (b2, 384)):
                    for kc in range(2):
                        p.op("pe", lambda e, b=b, kc=kc, c0=c0: e.matmul(self.ps[b][:], lhsT=wuq[:, kc, c0:c0 + 128], rhs=cqT[:, kc, :],
                                                                        start=(kc == 0), stop=(kc == 1)),
                             reads=[t_wuq, t_cqT], writes=[self.t_ps[b]])
                i1 = self.rot("A_tmpf", 3)
                p.op("dve", lambda e, i1=i1, b1=b1: e.tensor_tensor(out=tmpf[i1][:], in0=self.ps[b1][:], in1=cosT[:, lg * 512:(lg + 1) * 512], op=ALU.mult),
                     reads=[self.t_ps[b1], t_cos], writes=[t_tmpf[i1]])
                i2 = self.rot("A_tmpf", 3)
                p.op("dve", lambda e, i2=i2, b2=b2: e.tensor_tensor(out=tmpf[i2][:], in0=self.ps[b2][:], in1=sinS[:, lg * 512:(lg + 1) * 512], op=ALU.mult),
                     reads=[self.t_ps[b2], t_sin], writes=[t_tmpf[i2]])
                p.op("pool", lambda e, i1=i1, i2=i2: e.tensor_tensor(out=tmpf[i1][:], in0=tmpf[i1][:], in1=tmpf[i2][:], op=ALU.add),
                     reads=[t_tmpf[i1], t_tmpf[i2]], writes=[t_tmpf[i1]])
                st, t_st = stage_b()
                p.op("dve", lambda e, st=st, i1=i1: e.tensor_tensor(out=st[:, 0:512], in0=tmpf[i1][:], in1=rq_bc[:], op=ALU.mult),
                     reads=[t_tmpf[i1], t_rq], writes=[t_st])
                for h in range(4):
                    store2d(dr["sc_mlaq"], h * 96 + 64, 32, lg, st, t_st, prow0=h * 32)

                if KSTOP <= 8:
                    continue
                for blk in range(4):
                    lb = lg * 4 + blk
                    tok = slice(blk * 128, (blk + 1) * 128)
                    b = acc_bank()
                    p.op("pe", lambda e, b=b, tok=tok: e.matmul(self.ps[b][:, 0:256], lhsT=ckvT[:, tok], rhs=wukv[:, 256:512], start=True, stop=True),
                         reads=[t_ckvT, t_wukv], writes=[self.t_ps[b]])
                    st, t_st = stage_b()
                    p.op("dve", lambda e, st=st, b=b, blk=blk: e.tensor_scalar(
                        out=st[:, 0:260].rearrange("p (h c) -> p h c", h=4)[:, :, 0:64], in0=self.ps[b][:, 0:256].rearrange("p (h c) -> p h c", h=4),
                        scalar1=rkv_t[:, 2 * blk:2 * blk + 1], scalar2=None, op0=ALU.mult),
                         reads=[self.t_ps[b], t_rkvt], writes=[t_st])
                    ones_cols(st, t_st, 4)
                    p.dma("pool", vdst(R_MLAV, lg, blk, 260), st[:, 0:260], t_st, reads=[t_st])
                    b = acc_bank()
                    for (c0, n, o0) in ((C_AV, 128, 0), (C_DV, 256, 128)):
                        for kc in range(8):
                            p.op("pe", lambda e, b=b, kc=kc, c0=c0, n=n, o0=o0, tok=tok: e.matmul(
                                self.ps[b][:, o0:o0 + n], lhsT=xnT[:, kc, tok], rhs=Wp[:, kc, c0:c0 + n], start=(kc == 0), stop=(kc == 7)),
                                 reads=[t_Wp, t_Wp2, t_xnT], writes=[self.t_ps[b]])
                    st, t_st = stage_b()
                    evac(None, st[:, 0:130].rearrange("p (h c) -> p h c", h=2)[:, :, 0:64], self.ps[b][:, 0:128].rearrange("p (h c) -> p h c", h=2),
                         [self.t_ps[b]], [t_st])
                    evac(None, st[:, 256:512], self.ps[b][:, 128:384], [self.t_ps[b]], [t_st])
                    ones_cols(st, t_st, 2)
                    p.dma("pool", vdst(R_SWAV, lg, blk, 130), st[:, 0:130], t_st, reads=[t_st])
                    p.dma("pool", vdst(R_SBV, lg, blk, 256), st[:, 256:512], t_st, reads=[t_st])
                    st, t_st = stage_b()
                    for half in range(2):
                        b = acc_bank()
                        c0 = C_GATE + half * 512
                        for kc in range(8):
                            p.op("pe", lambda e, b=b, kc=kc, c0=c0, tok=tok: e.matmul(self.ps[b][:], lhsT=xnT[:, kc, tok], rhs=Wp[:, kc, c0:c0 + 512],
                                                                                     start=(kc == 0), stop=(kc == 7)),
                                 reads=[t_Wp, t_Wp2, t_xnT], writes=[self.t_ps[b]])
                        p.op("act", lambda e, st=st, b=b, half=half: e.activation(out=st[:, half * 512:(half + 1) * 512], in_=self.ps[b][:], func=AF.Silu),
                             reads=[self.t_ps[b]], writes=[t_st])
                    p.dma("pool", dr["sc_gate"][lb * 128:(lb + 1) * 128, :], st[:, :], t_st, reads=[t_st])
                if FUSED:
                    self.gather_group(lg)
            p.barrier()

    def gather_group(self, lg):
        p = self.p
        groups = [[b * NR + r for r in range(NR)] for b in range(NB_BATCH)]
        waits = []
        for k, v in p.all_tickets.items():
            if isinstance(k, tuple) and k[0] in ("sw", "hw") and p.seen["pool"].get(k, 0) < v:
                p.seen["pool"][k] = v
                waits.append((p.sems[k], v))
        p.streams["pool"].append((waits, None, None))
        t_g = Tk()
        for f in range(2):
            p.raw("pool", lambda e, lg=lg, f=f: e.collective_compute(
                "AllGather", ALU.bypass, replica_groups=groups,
                ins=[self.dh["kvb_loc%d" % f].ap()[lg * HR[f]:(lg + 1) * HR[f], :].opt()],
                outs=[self.dh["kvb_all%d" % f].ap()[(3 + 4 * lg) * HR[f]:(7 + 4 * lg) * HR[f], :].opt()]),
                  ("cc", lg * 2 + f), writes=[t_g])

    def phaseB(self, L, x_src, x_dst):
        p, nc = self.p, self.nc
        S, NG, NTL, NGL, NBL = self.S, self.NG, self.NTL, self.NGL, self.NBL
        NBK = S // 128
        NPOS = NBK + 12
        dr = self.dram
        cf_, cb_ = self.t_cf, self.t_cb
        z4 = self.zero4
        with ExitStack() as es:
            def tile(name, shape, dt):
                return es.enter_context(nc.sbuf_tensor("B%d_%s" % (L, name), list(shape), dt))

            yt = [tile("yt%d" % i, [128, 4, 256], F32) for i in range(6)]; t_yt = [Tk() for _ in range(6)]
            small = tile("small", [128, 64], F32); t_small = Tk()
            Wo = tile("Wo", [128, 8, D], BF16); t_Wo = Tk()
            wst = [tile("wst%d" % i, [128, D], F32) for i in range(2)]; t_wst = [Tk(), Tk()]
            ggrp_bc = tile("ggrp", [128, D], F32); t_gg = Tk()
            gpost_bc = tile("gpost", [128, D], F32); t_gp = Tk()
            cwb = tile("cwb", [128, 8], F32); t_cwb = Tk()
            esink = tile("esink", [128, 4], F32); t_es = Tk()
            es_att = ExitStack()

            def atile(name, shape, dt):
                return es_att.enter_context(nc.sbuf_tensor("B%d_%s" % (L, name), list(shape), dt))
            _tile_outer = tile
            tile = atile
            KtAll = tile("KtAll", [128, 4, NPOS * 128], BF16)
            Kt = [KtAll[:, h, :] for h in range(4)]; t_K = [Tk() for _ in range(4)]
            Vt = tile("Vt", [128, NPOS, 264], BF16); t_V = Tk()
            Qa = [tile("Qa%d" % h, [128, 512], BF16) for h in range(4)]
            Qm = [tile("Qm%d" % h, [128, 512], BF16) for h in range(4)]
            t_Qm = [Tk() for _ in range(4)]; t_Qx = [Tk() for _ in range(4)]
            NE = 2
            Et2 = [tile("E%d" % i, [128, 2, 512], F32) for i in range(NE)]; t_E = [Tk() for _ in range(NE)]
            Lt2 = [tile("Lp%d" % i, [128, 2, 512], BF16) for i in range(NE)]; t_L = [Tk() for _ in range(NE)]
            NA = 3
            At2 = [tile("At%d" % i, [128, 2, 512], BF16) for i in range(NA)]; t_A = [Tk() for _ in range(NA)]
            tile = _tile_outer

            p.dma("sp", ggrp_bc[:], dap(self.dh["ggrp"], L * D, [[0, 128], [1, D]]), t_gg, writes=[t_gg])
            p.dma("sp", gpost_bc[:], dap(self.dh["gpost"], L * D, [[0, 128], [1, D]]), t_gp, writes=[t_gp])
            p.dma("sp", cwb[:, 0:6], dr["convw_t"][L], t_cwb, writes=[t_cwb])
            p.dma("sp", cwb[:, 6:8], dr["convb_t"][L], t_cwb, writes=[t_cwb])
            p.dma("sp", esink[:], dap(self.dh["sinks"], L * 4, [[0, 128], [1, 4]]), t_es, writes=[t_es])
            p.op("act", lambda e: e.activation(out=esink[:], in_=esink[:], func=AF.Exp), reads=[t_es], writes=[t_es])

            NCH = 4

            FUSED = (self.mode == "F")
            NQG = NPOS // 4
            FAM = dict(sbK=0, sbV=0, mlaK=1, mlaV=1)
            DQ = "sp" if L == 0 else "act"
            TB = dict(sbK=0, sbV=4, mlaK=8, mlaV=12, swK=16, swV=18, ut=19)

            def load_K(name, nrows, heads=(0, 1, 2, 3)):
                if FUSED:
                    for h in heads:
                        i = TB[name] + h
                        f = FAM[name]
                        p.dyn_dma(DQ, KtAll[0:nrows, h, :], self.dh["kvb_all%d" % f], self.tabt[0:1, i:i + 1],
                                  [[512, nrows], [HR[f] * 512, NQG], [1, 512]], t_K[h], reads=[self.t_tab],
                                  writes=[t_K[h]] + ([t_Kx[h]] if nrows > 64 else []))
                    return
                w = NPOS * 128
                cw = (w + NCH - 1) // NCH
                for h in range(4):
                    for c in range(NCH):
                        c0, c1 = c * cw, min(w, (c + 1) * cw)
                        p.dma("sp", Kt[h][0:nrows, c0:c1], dr[name][h * nrows:(h + 1) * nrows, c0:c1], t_K[h], writes=[t_K[h]])

            def load_V(name, wcols):
                if FUSED:
                    for blk in range(4):
                        i = TB[name] + blk
                        f = FAM[name]
                        p.dyn_dma(DQ, Vt[:, blk:NPOS:4, 0:wcols], self.dh["kvb_all%d" % f], self.tabt[0:1, i:i + 1],
                                  [[wcols, 128], [HR[f] * 512, NQG], [1, wcols]], t_V, reads=[self.t_tab], writes=[t_V])
                    return
                step = 8
                for q0 in range(0, NPOS, step):
                    q1 = min(NPOS, q0 + step)
                    p.dma("sp", Vt[:, q0:q1, 0:wcols], dap(self.dh[name], q0 * 128 * wcols, [[wcols, 128], [128 * wcols, q1 - q0], [1, wcols]]),
                          t_V, writes=[t_V])

            def pos_of(lg, d):
                if FUSED:
                    return 4 * (4 * lg + 3 - d // 4) + d % 4
                return NBK - 4 - 16 * lg + d

            def nstep(lg):
                return 16 * lg + 16

            def acols(d):
                if d < 4:
                    return (d + 1) * 128, True
                return 512, False

            t_Kx = [Tk() for _ in range(4)]
            for h in range(4):
                en = "dve" if h % 2 == 0 else "pool"
                p.op(en, lambda e: e.memset(Kt[h][64:128, :], 0.0), writes=[t_Kx[h]])
                p.op(en, lambda e: e.memset(Kt[h][64:65, :], 1.0), writes=[t_Kx[h]])
                p.op(en, lambda e: e.memset(Kt[h][96:97, :], 1.0), writes=[t_Kx[h]])
            load_K("sbK", 64, (0, 1))
            load_V("sbV", 256)
            load_K("sbK", 64, (2, 3))

            for h in range(4):
                p.op("pool", lambda e, h=h: e.memset(Qm[h][64:128, :], 0.0), writes=[t_Qm[h]])
                p.op("pool", lambda e, h=h: e.memset(Qa[h][64:128, :], 0.0), writes=[t_Qx[h]])
            t_Qd = [Tk() for _ in range(4)]
            for pair in range(2):
              if pair == 1:
                load_K("mlaK", 96, (0, 1))
              for lg in range(NGL):
                ys = self.rot("B_yt", 6)
                if True:
                    hs = (2 * pair, 2 * pair + 1)
                    bank = {}
                    for i, h in enumerate(hs):
                        bank[h] = dict(A=i, B=2 + i, C=4 + i, O=6 + i)
                        qsrc = dr["sc_sbq"][h * 64:(h + 1) * 64, lg * 512:(lg + 1) * 512]
                        p.dma("sp", Qm[h][0:64, :], qsrc, t_Qm[h], writes=[t_Qm[h]])
                        p.dma("sp", Qa[h][0:64, :], qsrc, t_Qd[h], writes=[t_Qd[h]])
                        p.op("pool", lambda e, h=h: e.memset(Qa[h][64:98, :], 0.0), writes=[t_Qx[h]])
                        bC, bO = bank[h]["C"], bank[h]["O"]
                        p.op("pe", lambda e, bC=bC: e.matmul(self.ps[bC][0:98, :], lhsT=self.ones98, rhs=z4, start=True, stop=False, skip_group_check=True),
                             reads=[cb_], writes=[self.t_ps[bC]])
                        p.op("pe", lambda e, bO=bO: e.matmul(self.ps[bO][:, 0:256], lhsT=self.zero_b, rhs=z4[:, 0:256], start=True, stop=False, skip_group_check=True),
                             reads=[cb_], writes=[self.t_ps[bO]])
                    ND = nstep(lg)
                    slot = {}
                    aslot = {}

                    def mm1(d):
                        n, dg = acols(d)
                        P_ = pos_of(lg, d)
                        for h in hs:
                            bA = bank[h]["A"]
                            p.op("pe", lambda e, h=h, bA=bA, P_=P_, n=n: e.matmul(self.ps[bA][:, 0:n], lhsT=Kt[h][:, P_ * 128:(P_ + 1) * 128],
                                                                                rhs=Qm[h][:, 0:n], start=True, stop=True),
                                 reads=[t_K[h], t_Kx[h], t_Qm[h]], writes=[self.t_ps[bA]])

                    def EL(d):
                        n, dg = acols(d)
                        s = self.rot("B_E", 2)
                        slot[d] = s
                        kA = bank[hs[0]]["A"] // 2
                        p.op("act", lambda e: e.activation(out=Et2[s][:, :, 0:n], in_=self.pp[kA][:].rearrange("p (h c) -> p h c", h=2)[:, :, 0:n], func=AF.Exp),
                             reads=[self.t_ps[2 * kA], self.t_ps[2 * kA + 1]], writes=[t_E[s]])
                        p.op("act", lambda e: e.activation(out=Lt2[s][:, :, 0:n], in_=Et2[s][:, :, 0:n], func=AF.Ln, bias=1.0),
                             reads=[t_E[s]], writes=[t_L[s]])
                        if dg:
                            for hi in range(2):
                                p.op("pool", lambda e: e.tensor_tensor(out=Lt2[s][:, hi, n - 128:n], in0=Lt2[s][:, hi, n - 128:n], in1=self.m_lt, op=ALU.mult),
                                     reads=[t_L[s], cb_], writes=[t_L[s]])

                    def mm2tc(d):
                        n, dg = acols(d)
                        P_ = pos_of(lg, d)
                        last = (d == ND - 1)
                        s = slot[d]
                        for hi, h in enumerate(hs):
                            bB, bC = bank[h]["B"], bank[h]["C"]
                            p.op("pe", lambda e, h=h, bB=bB, P_=P_, n=n: e.matmul(self.ps[bB][:, 0:n], lhsT=Kt[h][:, P_ * 128:(P_ + 1) * 128],
                                                                                rhs=Qa[h][:, 0:n], start=True, stop=False),
                                 reads=[t_K[h], t_Kx[h], t_Qd[h], t_Qx[h]], writes=[self.t_ps[bB]])
                            p.op("pe", lambda e, bB=bB, s=s, n=n: e.matmul(self.ps[bB][:, 0:n], lhsT=self.negtri, rhs=Lt2[s][:, hi, 0:n], start=False, stop=True),
                                 reads=[t_L[s], cb_], writes=[self.t_ps[bB]])
                            if not last:
                                p.op("pe", lambda e, bC=bC, s=s, n=n: e.matmul(self.ps[bC][0:98, 0:n], lhsT=self.ones98, rhs=Lt2[s][:, hi, 0:n],
                                                                              start=False, stop=False, skip_group_check=True),
                                     reads=[t_L[s], cb_], writes=[self.t_ps[bC]])
                        if not last:
                            for h in hs:
                                bC = bank[h]["C"]
                                p.op("dve", lambda e, h=h, bC=bC, n=n: e.tensor_scalar(out=Qa[h][64:98, 0:n], in0=self.ps[bC][64:98, 0:n], scalar1=-1.0,
                                                                                      scalar2=None, op0=ALU.mult),
                                     reads=[self.t_ps[bC]], writes=[t_Qx[h]])
                                p.op("dve", lambda e, h=h, bC=bC, n=n: e.scalar_tensor_tensor(out=Qa[h][96:98, 0:n], in0=self.ps[bC][96:98, 0:n], scalar=-1.0,
                                                                                             in1=Qa[h][96:98, 0:n], op0=ALU.mult, op1=ALU.subtract),
                                     reads=[self.t_ps[bC], t_Qx[h]], writes=[t_Qx[h]])

                    def Bexp(d):
                        n, dg = acols(d)
                        a = self.rot("B_A", NA)
                        aslot[d] = a
                        kB = bank[hs[0]]["B"] // 2
                        p.op("act", lambda e: e.activation(out=At2[a][:, :, 0:n], in_=self.pp[kB][:].rearrange("p (h c) -> p h c", h=2)[:, :, 0:n], func=AF.Exp),
                             reads=[self.t_ps[2 * kB], self.t_ps[2 * kB + 1]], writes=[t_A[a]])
                        if dg:
                            for hi in range(2):
                                p.op("pool", lambda e: e.tensor_tensor(out=At2[a][:, hi, n - 128:n], in0=At2[a][:, hi, n - 128:n], in1=self.m_lt, op=ALU.mult),
                                     reads=[t_A[a], cb_], writes=[t_A[a]])

                    def PV(d):
                        n, dg = acols(d)
                        P_ = pos_of(lg, d)
                        last = (d == ND - 1)
                        a = aslot[d]
                        for hi, h in enumerate(hs):
                            bO = bank[h]["O"]
                            for c in range(n // 128):
                                p.op("pe", lambda e, h=h, bO=bO, a=a, c=c, P_=P_: e.matmul(self.ps[bO][:, c * 64:(c + 1) * 64], lhsT=At2[a][:, hi, c * 128:(c + 1) * 128],
                                                                                         rhs=Vt[:, P_, h * 64:(h + 1) * 64], start=False, stop=last,
                                                                                         skip_group_check=True),
                                     reads=[t_A[a], t_V], writes=[self.t_ps[bO]])

                    mm1(0)
                    EL(0)
                    if ND > 1:
                        mm1(1)
                    for d in range(ND):
                        mm2tc(d)
                        if d >= 1:
                            PV(d - 1)
                        if d + 1 < ND:
                            EL(d + 1)
                        if d + 2 < ND:
                            mm1(d + 2)
                        Bexp(d)
                    PV(ND - 1)
                    for h in hs:
                        bO = bank[h]["O"]
                        p.op("dve", lambda e, h=h, bO=bO, ys=ys: e.tensor_copy(out=yt[ys][:, :, h * 64:(h + 1) * 64],
                                                                               in_=self.ps[bO][:, 0:256].rearrange("p (j c) -> p j c", j=4)),
                             reads=[self.t_ps[bO]], writes=[t_yt[ys]])
                p.dma("pool", dap(self.dh["sc_y"], lg * 4 * 128 * D + 768 + pair * 128, [[D, 128], [128 * D, 4], [1, 128]]), yt[ys][:, :, pair * 128:(pair + 1) * 128], t_yt[ys], reads=[t_yt[ys]])

            load_V("mlaV", 260)
            load_K("mlaK", 96, (2, 3))
            for kc in range(8):
                s = kc % 2
                p.dma("sp", wst[s][:], dr["w_out"][L, kc * 128:(kc + 1) * 128, :], t_wst[s], writes=[t_wst[s]])
                p.op("dve", lambda e, s=s, kc=kc: e.tensor_copy(out=Wo[:, kc, :], in_=wst[s][:]), reads=[t_wst[s]], writes=[t_Wo])
            for pair in range(2):
              for lg in range(NGL):
                ys = self.rot("B_yt", 6)
                if True:
                    hs = (2 * pair, 2 * pair + 1)
                    bank = {}
                    for i, h in enumerate(hs):
                        bank[h] = dict(S=(i, 2 + i, 6 + i), O=4 + i)
                        p.dma("sp", Qa[h][0:96, :], dr["sc_mlaq"][h * 96:(h + 1) * 96, lg * 512:(lg + 1) * 512], t_Qm[h], writes=[t_Qm[h], t_Qx[h], t_Qd[h]])
                        bO = bank[h]["O"]
                        p.op("pe", lambda e, bO=bO: e.matmul(self.ps[bO][:, 0:260], lhsT=self.zero_b, rhs=z4[:, 0:260], start=True, stop=False, skip_group_check=True),
                             reads=[cb_], writes=[self.t_ps[bO]])
                    ND = nstep(lg)

                    def mmS(d):
                        n, dg = acols(d)
                        P_ = pos_of(lg, d)
                        for h in hs:
                            bS = bank[h]["S"][d % 3]
                            p.op("pe", lambda e, h=h, bS=bS, P_=P_, n=n: e.matmul(self.ps[bS][:, 0:n], lhsT=Kt[h][0:96, P_ * 128:(P_ + 1) * 128],
                                                                                rhs=Qa[h][0:96, 0:n], start=True, stop=True),
                                 reads=[t_K[h], t_Qm[h]], writes=[self.t_ps[bS]])

                    aslot = {}

                    def Pexp(d):
                        n, dg = acols(d)
                        a = self.rot("B_A", NA)
                        aslot[d] = a
                        kS = (0, 1, 3)[d % 3]
                        p.op("act", lambda e: e.activation(out=At2[a][:, :, 0:n], in_=self.pp[kS][:].rearrange("p (h c) -> p h c", h=2)[:, :, 0:n], func=AF.Exp),
                             reads=[self.t_ps[2 * kS], self.t_ps[2 * kS + 1]], writes=[t_A[a]])
                        if dg:
                            for hi in range(2):
                                p.op("pool", lambda e: e.tensor_tensor(out=At2[a][:, hi, n - 128:n], in0=At2[a][:, hi, n - 128:n], in1=self.m_le, op=ALU.mult),
                                     reads=[t_A[a], cb_], writes=[t_A[a]])

                    def PVm(d):
                        n, dg = acols(d)
                        P_ = pos_of(lg, d)
                        last = (d == ND - 1)
                        a = aslot[d]
                        for hi, h in enumerate(hs):
                            bO = bank[h]["O"]
                            for c in range(n // 128):
                                p.op("pe", lambda e: e.matmul(self.ps[bO][:, c * 65:(c + 1) * 65], lhsT=At2[a][:, hi, c * 128:(c + 1) * 128],
                                                              rhs=Vt[:, P_, h * 65:(h + 1) * 65], start=False, stop=last, skip_group_check=True),
                                     reads=[t_A[a], t_V], writes=[self.t_ps[bO]])

                    mmS(0)
                    if ND > 1:
                        mmS(1)
                    for d in range(ND):
                        if d + 2 < ND:
                            mmS(d + 2)
                        if d >= 1:
                            PVm(d - 1)
                        Pexp(d)
                    PVm(ND - 1)
                    for h in hs:
                        bO = bank[h]["O"]
                        p.op("dve", lambda e, bO=bO, h=h: e.reciprocal(out=small[:, 8 + h * 4:12 + h * 4],
                                                                       in_=self.ps[bO][:, 0:260].rearrange("p (j c) -> p j c", j=4)[:, :, 64]),
                             reads=[self.t_ps[bO]], writes=[t_small])
                        for c in range(4):
                            p.op("dve", lambda e, bO=bO, h=h, c=c, ys=ys: e.tensor_scalar(out=yt[ys][:, c, h * 64:(h + 1) * 64], in0=self.ps[bO][:, c * 65:c * 65 + 64],
                                                                                         scalar1=small[:, 8 + h * 4 + c:9 + h * 4 + c], scalar2=None, op0=ALU.mult),
                                 reads=[self.t_ps[bO], t_small], writes=[t_yt[ys]])
                p.dma("pool", dap(self.dh["sc_y"], lg * 4 * 128 * D + 512 + pair * 128, [[D, 128], [128 * D, 4], [1, 128]]), yt[ys][:, :, pair * 128:(pair + 1) * 128], t_yt[ys], reads=[t_yt[ys]])

            p.barrier()
            es_att.close()
            es_sw = ExitStack()

            def stile(name, shape, dt):
                return es_sw.enter_context(nc.sbuf_tensor("B%d_%s" % (L, name), list(shape), dt))
            tile = stile
            swQ = [tile("swQ%d" % i, [64, 4, 512], BF16) for i in range(2)]; t_swQ = [Tk(), Tk()]
            swK = [tile("swK%d" % i, [64, 2, 640], BF16) for i in range(2)]; t_swK = [Tk(), Tk()]
            swV = [tile("swV%d" % i, [128, 5, 130], BF16) for i in range(2)]; t_swV = [Tk(), Tk()]
            Pc = [tile("Pc%d" % i, [128, 512], BF16) for i in range(2)]; t_Pc = [Tk(), Tk()]
            Pp = [tile("Pp%d" % i, [128, 512], BF16) for i in range(2)]; t_Pp = [Tk(), Tk()]
            ut = [tile("ut%d" % i, [128, 2, 514], F32) for i in range(2)]; t_ut = [Tk(), Tk()]
            bbt = [tile("bbt%d" % i, [128, 2, 512], F32) for i in range(2)]; t_bbt = [Tk(), Tk()]
            cvt = [tile("cvt%d" % i, [128, 2, 512], F32) for i in range(2)]; t_cvt = [Tk(), Tk()]
            if FUSED:
                swKp = tile("swKp", [64, 2, NGL, 128], BF16); t_swKp = Tk()
                swVp = tile("swVp", [128, NGL, 130], BF16); t_swVp = Tk()
                utl = tile("utl", [128, 2, NGL, 4], BF16); t_utl = Tk()
                for kvh in range(2):
                    i = TB["swK"] + kvh
                    p.dyn_dma("pool", swKp[:, kvh, :, :], self.dh["kvb_all0"], self.tabt[0:1, i:i + 1], [[512, 64], [4 * HR[0] * 512, NGL], [1, 128]], t_swKp,
                              reads=[self.t_tab], writes=[t_swKp])
                i = TB["swV"]
                p.dyn_dma("pool", swVp[:], self.dh["kvb_all0"], self.tabt[0:1, i:i + 1], [[130, 128], [4 * HR[0] * 512, NGL], [1, 130]], t_swVp,
                          reads=[self.t_tab], writes=[t_swVp])
                for ct in range(2):
                    i = TB["ut"] + ct
                    p.dyn_dma("pool", utl[:, ct, :, :], self.dh["kvb_all0"], self.tabt[0:1, i:i + 1], [[4, 128], [4 * HR[0] * 512, NGL], [1, 4]], t_utl,
                              reads=[self.t_tab], writes=[t_utl])
            ys_of = {}

            def swa_load(lg):
                s = lg % 2
                ys_of[lg] = self.rot("B_yt", 6)
                p.dma("sp", swQ[s][:], dap(self.dh["sc_swaq"], lg * 512, [[NTL, 64], [64 * NTL, 4], [1, 512]]), t_swQ[s], writes=[t_swQ[s]])
                if FUSED:
                    kl = self.dh["kvb_loc0"]
                    p.dma("sp", swK[s][:, :, 0:512], dap(kl, (lg * HR[0] + 512) * 512, [[512, 64], [64 * 512, 2], [1, 512]]), t_swK[s], writes=[t_swK[s]])
                    p.dma("sp", swV[s][:, 0:4, :], dap(kl, (lg * HR[0] + 640) * 512, [[130, 128], [128 * 130, 4], [1, 130]]), t_swV[s], writes=[t_swV[s]])
                    p.op("pool", lambda e: e.tensor_copy(out=swK[s][:, :, 512:640], in_=swKp[:, :, lg, :]), reads=[t_swKp], writes=[t_swK[s]])
                    p.op("pool", lambda e: e.tensor_copy(out=swV[s][:, 4, :], in_=swVp[:, lg, :]), reads=[t_swVp], writes=[t_swV[s]])
                else:
                    p.dma("sp", swK[s][:], dap(self.dh["swK"], lg * 640, [[NGL * 640, 64], [64 * NGL * 640, 2], [1, 640]]), t_swK[s], writes=[t_swK[s]])
                    p.dma("sp", swV[s][:], dap(self.dh["swV"], lg * 5 * 128 * 130, [[130, 128], [128 * 130, 5], [1, 130]]), t_swV[s], writes=[t_swV[s]])

            def swa_A(lg, c, idx):
                s = lg % 2
                q = idx % 2
                bC, bP = 0, 1
                for h in range(4):
                    p.op("pe", lambda e: e.matmul(self.ps[bC][:, h * 128:(h + 1) * 128], lhsT=swK[s][:, h // 2, c * 128:(c + 1) * 128],
                                                  rhs=swQ[s][:, h, c * 128:(c + 1) * 128], start=True, stop=True),
                         reads=[t_swK[s], t_swQ[s]], writes=[self.t_ps[bC]])
                for h in range(4):
                    p.op("pe", lambda e: e.matmul(self.ps[bP][:, h * 128:(h + 1) * 128], lhsT=swK[s][:, h // 2, (c + 1) * 128:(c + 2) * 128],
                                                  rhs=swQ[s][:, h, c * 128:(c + 1) * 128], start=True, stop=True),
                         reads=[t_swK[s], t_swQ[s]], writes=[self.t_ps[bP]])
                p.op("act", lambda e: e.activation(out=Pc[q][:], in_=self.ps[bC][:], func=AF.Exp), reads=[self.t_ps[bC]], writes=[t_Pc[q]])
                p.op("act", lambda e: e.activation(out=Pp[q][:], in_=self.ps[bP][:], func=AF.Exp), reads=[self.t_ps[bP]], writes=[t_Pp[q]])
                m_le4 = bass.AP(self.cb, 256, [[896, 128], [0, 4], [1, 128]])
                m_gt4 = bass.AP(self.cb, 384, [[896, 128], [0, 4], [1, 128]])
                p.op("pool", lambda e: e.tensor_tensor(out=Pc[q][:].rearrange("p (h c) -> p h c", h=4), in0=Pc[q][:].rearrange("p (h c) -> p h c", h=4),
                                                       in1=m_le4, op=ALU.mult), reads=[t_Pc[q], cb_], writes=[t_Pc[q]])
                p.op("dve", lambda e: e.tensor_tensor(out=Pp[q][:].rearrange("p (h c) -> p h c", h=4), in0=Pp[q][:].rearrange("p (h c) -> p h c", h=4),
                                                      in1=m_gt4, op=ALU.mult), reads=[t_Pp[q], cb_], writes=[t_Pp[q]])

            def swa_B(lg, c, idx):
                s = lg % 2
                q = idx % 2
                ys = ys_of[lg]
                bO = 2 + (idx % 2)
                p.op("pe", lambda e: e.matmul(self.ps[bO][:, 0:260], lhsT=self.zero_b, rhs=z4[:, 0:260], start=True, stop=False,
                                              skip_group_check=True), reads=[cb_], writes=[self.t_ps[bO]])
                for h in range(4):
                    kvh = h // 2
                    p.op("pe", lambda e: e.matmul(self.ps[bO][:, h * 65:(h + 1) * 65], lhsT=Pc[q][:, h * 128:(h + 1) * 128],
                                                  rhs=swV[s][:, c, kvh * 65:(kvh + 1) * 65], start=False, stop=False, skip_group_check=True),
                         reads=[t_Pc[q], t_swV[s]], writes=[self.t_ps[bO]])
                    p.op("pe", lambda e: e.matmul(self.ps[bO][:, h * 65:(h + 1) * 65], lhsT=Pp[q][:, h * 128:(h + 1) * 128],
                                                  rhs=swV[s][:, c + 1, kvh * 65:(kvh + 1) * 65], start=False, stop=True, skip_group_check=True),
                         reads=[t_Pp[q], t_swV[s]], writes=[self.t_ps[bO]])
                p.op("dve", lambda e: e.tensor_tensor(out=small[:, 0:4], in0=self.ps[bO][:, 0:260].rearrange("p (h c) -> p h c", h=4)[:, :, 64],
                                                      in1=esink[:], op=ALU.add), reads=[self.t_ps[bO], t_es], writes=[t_small])
                p.op("dve", lambda e: e.reciprocal(out=small[:, 4:8], in_=small[:, 0:4]), reads=[t_small], writes=[t_small])
                rec4 = bass.AP(small, 4, [[64, 128], [1, 4], [0, 64]])
                p.op("dve", lambda e: e.tensor_tensor(out=yt[ys][:, c, :].rearrange("p (h c) -> p h c", h=4),
                                                      in0=self.ps[bO][:, 0:260].rearrange("p (h c) -> p h c", h=4)[:, :, 0:64], in1=rec4, op=ALU.mult),
                     reads=[self.t_ps[bO], t_small], writes=[t_yt[ys]])
                if c == 3:
                    p.dma("pool", dap(self.dh["sc_y"], lg * 4 * 128 * D + 0, [[D, 128], [128 * D, 4], [1, 256]]), yt[ys][:], t_yt[ys], reads=[t_yt[ys]])

            def conv_compute(lg):
                s = lg % 2
                p.dma("sp", ut[s][:, :, 2:514], dap(self.dh["sc_u"], lg * 512, [[NTL, 128], [128 * NTL, 2], [1, 512]]), t_ut[s], writes=[t_ut[s]])
                p.dma("sp", bbt[s][:], dap(self.dh["sc_bb"], lg * 512, [[NTL, 128], [128 * NTL, 2], [1, 512]]), t_bbt[s], writes=[t_bbt[s]])
                if FUSED:
                    p.op("pool", lambda e: e.tensor_tensor(out=ut[s][:, :, 0:2], in0=utl[:, :, lg, 0:2], in1=utl[:, :, lg, 2:4], op=ALU.add),
                         reads=[t_utl], writes=[t_ut[s]])
                else:
                    p.dma("sp", ut[s][:, :, 0:2], dap(self.dh["utail"], lg * 2, [[NGL * 2, 128], [128 * NGL * 2, 2], [1, 2]]), t_ut[s], writes=[t_ut[s]])
                for ct in range(2):
                    p.op("dve", lambda e, s=s, ct=ct: e.tensor_scalar(out=cvt[s][:, ct, :], in0=ut[s][:, ct, 0:512], scalar1=cwb[:, ct * 3:ct * 3 + 1], scalar2=None, op0=ALU.mult),
                         reads=[t_ut[s], t_cwb], writes=[t_cvt[s]])
                    for kk in (1, 2):
                        p.op("dve", lambda e, s=s, ct=ct, kk=kk: e.scalar_tensor_tensor(out=cvt[s][:, ct, :], in0=ut[s][:, ct, kk:kk + 512], scalar=cwb[:, ct * 3 + kk:ct * 3 + kk + 1],
                                                                                       in1=cvt[s][:, ct, :], op0=ALU.mult, op1=ALU.add),
                             reads=[t_ut[s], t_cwb, t_cvt[s]], writes=[t_cvt[s]])
                    p.op("dve", lambda e, s=s, ct=ct: e.scalar_tensor_tensor(out=cvt[s][:, ct, :], in0=cvt[s][:, ct, :], scalar=cwb[:, 6 + ct:7 + ct], in1=bbt[s][:, ct, :],
                                                                            op0=ALU.add, op1=ALU.mult),
                         reads=[t_cvt[s], t_cwb, t_bbt[s]], writes=[t_cvt[s]])

            def conv_out(lg):
                s = lg % 2
                ys2 = self.rot("B_yt", 6)
                for c in range(4):
                    bT = 4 + (c % 2)
                    blk = 3 - c
                    for ct in range(2):
                        p.op("pe", lambda e, s=s, ct=ct, blk=blk, bT=bT: e.transpose(out=self.ps[bT][:, ct * 128:(ct + 1) * 128], in_=cvt[s][:, ct, blk * 128:(blk + 1) * 128],
                                                                                    identity=self.ident_f),
                             reads=[t_cvt[s], cf_], writes=[self.t_ps[bT]])
                    p.op("act", lambda e, bT=bT, c=c, ys2=ys2: e.copy(out=yt[ys2][:, c, :], in_=self.ps[bT][:, 0:256]), reads=[self.t_ps[bT]], writes=[t_yt[ys2]])
                p.dma("pool", dap(self.dh["sc_y"], lg * 4 * 128 * D + 256, [[D, 128], [128 * D, 4], [1, 256]]), yt[ys2][:], t_yt[ys2], reads=[t_yt[ys2]])


            items = [(lg, c) for lg in range(NGL) for c in range(4)]
            swa_load(0)
            conv_compute(0)
            swa_A(0, 0, 0)
            for idx, (lg, c) in enumerate(items):
                if c == 0 and lg + 1 < NGL:
                    conv_compute(lg + 1)
                if idx + 1 < len(items):
                    nlg, nc_ = items[idx + 1]
                    if nc_ == 0:
                        swa_load(nlg)
                    swa_A(nlg, nc_, idx + 1)
                swa_B(lg, c, idx)
                if c == 3:
                    conv_out(lg)

            p.barrier()
            es_sw.close()
            tile = _tile_outer

            NS = 3
            yb = [tile("yb%d" % i, [128, D], F32) for i in range(NS)]; t_yb = [Tk() for _ in range(NS)]
            gb_ = [tile("gb%d" % i, [128, D], BF16) for i in range(NS)]; t_gb = [Tk() for _ in range(NS)]
            xb = [tile("xb%d" % i, [128, D], F32) for i in range(NS)]; t_xb = [Tk() for _ in range(NS)]
            ob = [tile("ob%d" % i, [128, D], F32) for i in range(3)]; t_ob = [Tk(), Tk(), Tk()]
            junk = tile("junkB", [128, D], F32); t_junk = Tk()
            junk2 = tile("junkB2", [128, D], F32); t_junk2 = Tk()
            ygT = [tile("ygT%d" % i, [128, 8, 128], BF16) for i in range(3)]; t_ygT = [Tk(), Tk(), Tk()]
            sm = [tile("sm%d" % i, [128, 16], F32) for i in range(NS)]; t_sm = [Tk() for _ in range(NS)]
            sm2 = [tile("sm2_%d" % i, [128, 8], F32) for i in range(3)]; t_sm2 = [Tk(), Tk(), Tk()]

            def rows_of(sl):
                lg, c = sl // 4, sl % 4
                lb = lg * 4 + (3 - c)
                return slice(sl * 128, (sl + 1) * 128), slice(lb * 128, (lb + 1) * 128)

            def S1a(sl):
                s = sl % NS
                srow, rows = rows_of(sl)
                p.dma("sp", yb[s][:], dr["sc_y"][srow, :], t_yb[s], writes=[t_yb[s]])
                p.dma("sp", gb_[s][:], dr["sc_gate"][srow, :], t_gb[s], writes=[t_gb[s]])
                p.dma("sp", xb[s][:], x_src[rows, :], t_xb[s], writes=[t_xb[s]])
                for g in range(4):
                    p.op("act", lambda e: e.activation(out=junk[:, g * 256:(g + 1) * 256], in_=yb[s][:, g * 256:(g + 1) * 256], func=AF.Square,
                                                       accum_out=sm[s][:, g:g + 1]),
                         reads=[t_yb[s]], writes=[t_junk, t_sm[s]])
                p.op("act", lambda e: e.activation(out=sm[s][:, 4:8], in_=sm[s][:, 0:4], func=AF.Sqrt, bias=self.eps_c, scale=float(1.0 / 256)),
                     reads=[t_sm[s], cf_], writes=[t_sm[s]])
                p.op("dve", lambda e: e.reciprocal(out=sm[s][:, 8:12], in_=sm[s][:, 4:8]), reads=[t_sm[s]], writes=[t_sm[s]])
                for g in range(4):
                    p.op("dve", lambda e: e.scalar_tensor_tensor(out=yb[s][:, g * 256:(g + 1) * 256], in0=yb[s][:, g * 256:(g + 1) * 256], scalar=sm[s][:, 8 + g:9 + g],
                                                                 in1=ggrp_bc[:, g * 256:(g + 1) * 256], op0=ALU.mult, op1=ALU.mult),
                         reads=[t_yb[s], t_sm[s], t_gg], writes=[t_yb[s]])
                p.op("pool", lambda e: e.tensor_tensor(out=yb[s][:], in0=yb[s][:], in1=gb_[s][:], op=ALU.mult), reads=[t_yb[s], t_gb[s]], writes=[t_yb[s]])

            def S1b(sl):
                s = sl % NS
                u = sl % 3
                for half in range(2):
                    for i in range(4):
                        kc = half * 4 + i
                        p.op("pe", lambda e: e.transpose(out=self.ps[half][:, i * 128:(i + 1) * 128], in_=yb[s][:, kc * 128:(kc + 1) * 128],
                                                         identity=self.ident_f), reads=[t_yb[s], cf_], writes=[self.t_ps[half]])
                    if half == 0:
                        p.op("act", lambda e: e.copy(out=ygT[u][:, 0:4, :], in_=self.ps[0][:].rearrange("p (a b) -> p a b", a=4)),
                             reads=[self.t_ps[0]], writes=[t_ygT[u]])
                    else:
                        p.op("dve", lambda e: e.tensor_copy(out=ygT[u][:, 4:8, :], in_=self.ps[1][:].rearrange("p (a b) -> p a b", a=4)),
                             reads=[self.t_ps[1]], writes=[t_ygT[u]])

            def S2(sl):
                s = sl % NS
                u = sl % 3
                srow, rows = rows_of(sl)
                bo = (2 + 2 * u, 3 + 2 * u)
                for half in range(2):
                    b = bo[half]
                    for kc in range(8):
                        p.op("pe", lambda e: e.matmul(self.ps[b][:], lhsT=ygT[u][:, kc, :], rhs=Wo[:, kc, half * 512:(half + 1) * 512],
                                                      start=(kc == 0), stop=(kc == 7)),
                             reads=[t_ygT[u], t_Wo], writes=[self.t_ps[b]])
                    p.op("act", lambda e: e.activation(out=junk2[:, half * 512:(half + 1) * 512], in_=self.ps[b][:], func=AF.Square,
                                                       accum_out=sm2[u][:, half:half + 1]),
                         reads=[self.t_ps[b]], writes=[t_junk2, t_sm2[u]])
                p.op("dve", lambda e: e.tensor_tensor(out=sm2[u][:, 2:3], in0=sm2[u][:, 0:1], in1=sm2[u][:, 1:2], op=ALU.add), reads=[t_sm2[u]], writes=[t_sm2[u]])
                p.op("act", lambda e: e.activation(out=sm2[u][:, 3:4], in_=sm2[u][:, 2:3], func=AF.Sqrt, bias=self.eps_c, scale=float(1.0 / D)),
                     reads=[t_sm2[u], cf_], writes=[t_sm2[u]])
                p.op("dve", lambda e: e.reciprocal(out=sm2[u][:, 4:5], in_=sm2[u][:, 3:4]), reads=[t_sm2[u]], writes=[t_sm2[u]])
                for half in range(2):
                    b = bo[half]
                    p.op("dve", lambda e: e.scalar_tensor_tensor(out=ob[u][:, half * 512:(half + 1) * 512], in0=self.ps[b][:], scalar=sm2[u][:, 4:5],
                                                                 in1=gpost_bc[:, half * 512:(half + 1) * 512], op0=ALU.mult, op1=ALU.mult),
                         reads=[self.t_ps[b], t_sm2[u], t_gp], writes=[t_ob[u]])
                p.op("pool", lambda e: e.tensor_tensor(out=ob[u][:], in0=ob[u][:], in1=xb[s][:], op=ALU.add), reads=[t_ob[u], t_xb[s]], writes=[t_ob[u]])
                p.dma("pool", x_dst[rows, :], ob[u][:], t_ob[u], reads=[t_ob[u]])

            S1a(0)
            if NBL > 1:
                S1a(1)
            S1b(0)
            for sl in range(NBL):
                if sl + 2 < NBL:
                    S1a(sl + 2)
                if sl + 1 < NBL:
                    S1b(sl + 1)
                S2(sl)
            p.barrier()

    def build(self):
        self.declare()
        self.common()
        p = self.p
        self.zero4_t = p.tile("zero4", [128, 512], BF16)
        self.zero4 = self.zero4_t[:]
        p.op("pool", lambda e: e.memset(self.zero4_t[:], 0.0), reads=[self.t_cb], writes=[self.t_cb])
        dr = self.dram
        if self.mode == "A":
            self.phaseA(0, dr["x"])
        elif self.mode == "B":
            self.phaseB(0, dr["x"], dr["out"])
        else:
            NTL, NGL = self.NTL, self.NGL
            self.tabt = p.tile("tabt", [1, 128], I32)
            self.t_tab = Tk()
            p.dma("sp", self.tabt[:], dr["tab"], self.t_tab, writes=[self.t_tab])
            self.zt = p.tile("zt", [128, 512], BF16)
            self.t_zt = Tk()
            p.op("pool", lambda e: e.memset(self.zt[:], 0.0), writes=[self.t_zt])
            NG = self.NG
            groups = [[b * NR + r for r in range(NR)] for b in range(NB_BATCH)]
            import os
            KF = os.environ.get("KFSTOP", "")
            for l in range(DEPTH):
                if KF and l >= int(KF[0]):
                    break
                x_src = dr["x"] if l == 0 else dr["sc_x1"]
                x_dst = dr["out"] if l == DEPTH - 1 else dr["sc_x1"]
                p.phase_begin()
                self.phaseA(l, x_src)
                p.barrier()
                p.phase_end()
                if KF.endswith("a"):
                    break
                p.barrier()
                if KF.endswith("g"):
                    break
                p.phase_begin()
                self.phaseB(l, x_src, x_dst)
                p.barrier()
                p.phase_end()
        p.barrier()
        p.emit()
        self.es.close()
        return self.nc


def make_consts():
    cf = np.zeros((128, 640), np.float32)
    cf[:, 0:128] = np.eye(128, dtype=np.float32)
    cf[:, 128:256] = 1.0
    cf[:, 256] = EPS
    cf[:, 257] = 1.0
    half = 16
    freqs = (10000.0 ** (-np.arange(half, dtype=np.float32) / half)).astype(np.float32)
    pidx = np.arange(128)
    cf[:, 258] = freqs[(pidx % 32) % 16]
    cf[:, 259] = np.where((pidx % 32) < 16, -1.0, 1.0)
    cb = np.zeros((128, 896), np.float32)
    cb[:, 768:896] = 1.0
    j = np.arange(128)[:, None]
    i = np.arange(128)[None, :]
    cb[:, 0:128] = -1.0 * (j >= i)
    cb[:, 128:256] = (j < i)
    cb[:, 256:384] = (j <= i)
    cb[:, 384:512] = (j > i)
    for m in (64, 65, 96, 97):
        cb[:, 512 + m] = 1.0
    return cf, cb.astype(ml_dtypes.bfloat16)


def f_order_rows(ngl):
    idx = []
    for lg in range(ngl):
        for c in range(4):
            b = lg * 4 + (3 - c)
            idx.extend(range(b * 128, (b + 1) * 128))
    return np.array(idx)


def prep_weights(inp):
    w = {}
    w["gpre_t"] = np.ascontiguousarray(inp["norm_pre"].reshape(DEPTH, 8, 128).transpose(0, 2, 1))
    w_in = inp["w_in"]
    sw = np.concatenate([w_in[:, :, C_CKR + 16:C_CKR + 32], w_in[:, :, C_CKR:C_CKR + 16]], axis=2)
    w["w_in_x"] = np.ascontiguousarray(np.concatenate([w_in, sw], axis=2))
    w["gcq_t"] = np.ascontiguousarray(inp["mla_q_norm"].reshape(DEPTH, 2, 128).transpose(0, 2, 1))
    uq = inp["mla_w_uq"].reshape(DEPTH, 256, 4, 96)
    nope = uq[..., 0:64].reshape(DEPTH, 256, 256)
    rope = uq[..., 64:96]
    rope_sw = np.concatenate([rope[..., 16:32], rope[..., 0:16]], axis=-1)
    w["w_uq_x"] = np.ascontiguousarray(np.concatenate([nope, rope.reshape(DEPTH, 256, 128), rope_sw.reshape(DEPTH, 256, 128)], axis=2))
    w["gkv_t"] = np.ascontiguousarray(inp["mla_kv_norm"].reshape(DEPTH, 1, 128).transpose(0, 2, 1))
    ukv = inp["mla_w_ukv"].reshape(DEPTH, 128, 4, 128)
    w["w_ukv_x"] = np.ascontiguousarray(np.concatenate([ukv[..., 0:64].reshape(DEPTH, 128, 256), ukv[..., 64:128].reshape(DEPTH, 128, 256)], axis=2))
    w["sinks"] = np.ascontiguousarray(inp["attn_sinks"])
    w["convw_t"] = np.ascontiguousarray(inp["conv_w"].reshape(DEPTH, 3, 2, 128).transpose(0, 3, 2, 1).reshape(DEPTH, 128, 6))
    w["convb_t"] = np.ascontiguousarray(inp["conv_b"].reshape(DEPTH, 2, 128).transpose(0, 2, 1))
    w["ggrp"] = np.ascontiguousarray(inp["group_norm"])
    w["w_out"] = np.ascontiguousarray(inp["w_out"])
    w["gpost"] = np.ascontiguousarray(inp["norm_post"])
    return {k: np.asarray(v, np.float32) for k, v in w.items()}


_CACHE = {}


def get_prog(mode, S):
    key = (mode, S)
    if key not in _CACHE:
        _CACHE[key] = Builder(mode, S).build()
    return _CACHE[key]


A_WKEYS = ("gpre_t", "w_in_x", "gcq_t", "w_uq_x", "gkv_t", "w_ukv_x")
B_WKEYS = ("sinks", "convw_t", "convb_t", "ggrp", "w_out", "gpost")
SC_KEYS = ("sc_sbq", "sc_mlaq", "sc_swaq", "sc_u", "sc_bb", "sc_gate")


def arrange_B(resA, S):
    NG = S // 512
    NGL = NG // NR
    NTL = NGL * 512
    NBK = S // 128
    NPOS = NBK + 12
    outs = []
    for core in range(NB_BATCH * NR):
        b, r = divmod(core, NR)
        sbK = np.zeros((256, NPOS * 128), ml_dtypes.bfloat16)
        mlaK = np.zeros((384, NPOS * 128), ml_dtypes.bfloat16)
        sbV = np.zeros((NPOS * 128, 256), ml_dtypes.bfloat16)
        mlaV = np.zeros((NPOS * 128, 260), ml_dtypes.bfloat16)
        for qg in range(NPOS // 4):
            G = NG - 1 + r - qg
            if 0 <= G < NG:
                src = resA[b * NR + (G % NR)]
                lgs = G // NR
                kv = src["kvb_loc"]
                flat = kv.reshape(-1)
                sbK[:, qg * 512:(qg + 1) * 512] = kv[R_SBK:R_SBK + 256, lgs * 512:(lgs + 1) * 512]
                mlaK[:, qg * 512:(qg + 1) * 512] = kv[R_MLAK:R_MLAK + 384, lgs * 512:(lgs + 1) * 512]
                o = R_SBV * NTL + lgs * 512 * 256
                sbV[qg * 512:(qg + 1) * 512, :] = flat[o:o + 512 * 256].reshape(512, 256)
                o = R_MLAV * NTL + lgs * 512 * 260
                mlaV[qg * 512:(qg + 1) * 512, :] = flat[o:o + 512 * 260].reshape(512, 260)
        swK = np.zeros((128, NGL, 640), ml_dtypes.bfloat16)
        swV = np.zeros((NGL, 5, 128, 130), ml_dtypes.bfloat16)
        utail = np.zeros((256, NGL, 2), np.float32)
        own = resA[core]
        flat_own = own["kvb_loc"].reshape(-1)
        for lg in range(NGL):
            G = NR * lg + r
            swK[:, lg, 0:512] = own["kvb_loc"][R_SWAK:R_SWAK + 128, lg * 512:(lg + 1) * 512]
            o = R_SWAV * NTL + lg * 512 * 130
            swV[lg, 0:4] = flat_own[o:o + 512 * 130].reshape(4, 128, 130)
            if G > 0:
                src = resA[b * NR + ((G - 1) % NR)]
                lgs = (G - 1) // NR
                swK[:, lg, 512:640] = src["kvb_loc"][R_SWAK:R_SWAK + 128, lgs * 512:lgs * 512 + 128]
                o = R_SWAV * NTL + lgs * 512 * 130
                swV[lg, 4] = src["kvb_loc"].reshape(-1)[o:o + 128 * 130].reshape(128, 130)
                utail[:, lg, :] = src["kvf_loc"].reshape(256, NGL, 2)[:, lgs, :]
        outs.append(dict(sbK=sbK, sbV=sbV, mlaK=mlaK, mlaV=mlaV, swK=swK.reshape(128, NGL * 640),
                         swV=swV.reshape(NGL * 5 * 128, 130), utail=utail.reshape(256, NGL * 2)))
    return outs


def run_forward(inp, S):
    NG = S // 512
    NGL = NG // NR
    NTL = NGL * 512
    ncores = NB_BATCH * NR
    x = np.asarray(inp["x"], np.float32)
    pos = np.asarray(inp["positions"], np.int32)
    W = prep_weights(inp)
    cf, cb = make_consts()
    ford = f_order_rows(NGL)
    tok = []
    for core in range(ncores):
        b, r = divmod(core, NR)
        t = np.concatenate([np.arange((NR * lg + r) * 512, (NR * lg + r + 1) * 512) for lg in range(NGL)])
        tok.append((b, t))
    xs = [np.ascontiguousarray(x[b][t]) for (b, t) in tok]
    ps = [np.ascontiguousarray(pos[b][t][ford][None, :]) for (b, t) in tok]
    progA = get_prog("A", S)
    progB = get_prog("B", S)
    for l in range(DEPTH):
        wa = {k: W[k][l:l + 1] for k in A_WKEYS}
        wb = {k: W[k][l:l + 1] for k in B_WKEYS}
        inA = [dict(consts_f=cf, consts_b=cb, pos=ps[c], x=xs[c], **wa) for c in range(ncores)]
        resA = run_bass_kernel_spmd(progA, inA, core_ids=list(range(ncores))).results
        arr = arrange_B(resA, S)
        inB = []
        for c in range(ncores):
            d = dict(consts_f=cf, consts_b=cb, x=xs[c], **wb)
            for k in SC_KEYS:
                d[k] = resA[c][k]
            d.update(arr[c])
            inB.append(d)
        resB = run_bass_kernel_spmd(progB, inB, core_ids=list(range(ncores))).results
        xs = [np.asarray(resB[c]["out"], np.float32) for c in range(ncores)]
    out = np.zeros_like(x)
    for c, (b, t) in enumerate(tok):
        out[b][t] = xs[c]
    return out


def make_table(r, S):
    tab = np.zeros((1, 128), np.int32)
    Y0, Y1 = HR[0] * 512, HR[1] * 512
    for h in range(4):
        tab[0, 0 + h] = r * Y0 + (0 + h * 64) * 512
        tab[0, 4 + h] = r * Y0 + 256 * 512 + h * 128 * 256
        tab[0, 8 + h] = r * Y1 + (0 + h * 96) * 512
        tab[0, 12 + h] = r * Y1 + 384 * 512 + h * 128 * 260
    for kvh in range(2):
        tab[0, 16 + kvh] = (2 + r) * Y0 + (512 + kvh * 64) * 512
        tab[0, 19 + kvh] = (2 + r) * Y0 + 770 * 512 + kvh * 128 * 4
    tab[0, 18] = (2 + r) * Y0 + 640 * 512
    return tab


def run_fused(inp, S):
    NG = S // 512
    NGL = NG // NR
    ncores = NB_BATCH * NR
    x = np.asarray(inp["x"], np.float32)
    pos = np.asarray(inp["positions"], np.int32)
    W = prep_weights(inp)
    cf, cb = make_consts()
    ford = f_order_rows(NGL)
    kaug = np.zeros((64, 512), ml_dtypes.bfloat16)
    kaug[0, :] = 1.0
    kaug[32, :] = 1.0
    in_maps = []
    tok = []
    for core in range(ncores):
        b, r = divmod(core, NR)
        t = np.concatenate([np.arange((NR * lg + r) * 512, (NR * lg + r + 1) * 512) for lg in range(NGL)])
        tok.append((b, t))
        d = dict(consts_f=cf, consts_b=cb, kaug=kaug, x=np.ascontiguousarray(x[b][t]),
                 pos=np.ascontiguousarray(pos[b][t][ford][None, :]), tab=make_table(r, S))
        d.update(W)
        in_maps.append(d)
    res = run_bass_kernel_spmd(get_prog("F", S), in_maps, core_ids=list(range(ncores))).results
    out = np.zeros_like(x)
    for c, (b, t) in enumerate(tok):
        out[b][t] = np.asarray(res[c]["out"], np.float32)
    return out


def kernel(**inputs):
    inp = {k: np.asarray(v) for k, v in inputs.items()}
    S = inp["x"].shape[1]
    return run_fused(inp, S)
```
